# Optimizing a Trainium2 kernel written in Bass

```python
import math
import jax
import jax.numpy as jnp
from jax import lax
import numpy as np

D_MODEL = 1024
BATCH = 8
SEQ = 2048
DEPTH = 2

HEAD_DIM = 64
N_HEADS = D_MODEL // HEAD_DIM
MOBA_HEADS = N_HEADS // 2
DIL_HEADS = N_HEADS - MOBA_HEADS
MOBA_BLOCK = 256
MOBA_TOPK = 3
MOBA_QCHUNK = 16
DIL_PAIRS = ((128, 1), (512, 4), (2048, 16))
DIL_BLOCK = 128
NSA_HEADS = N_HEADS
NSA_GROUPS = 4
NSA_HPG = NSA_HEADS // NSA_GROUPS
NSA_CMP_LEN = 32
NSA_CMP_STRIDE = 16
NSA_CMP_HIDDEN = 2 * HEAD_DIM
NSA_SEL_BLOCK = 64
NSA_SEL_TOPN = 16
NSA_WINDOW = 512
NSA_WIN_BLOCK = 128
NSA_QCHUNK = 32
NSA_FORCE = 1.0e6
REL_BUCKETS = 32
REL_MAX_DIST = 2048
FFN_HIDDEN = -((-8 * D_MODEL) // (3 * 256)) * 256
PLE_DIM = 256
N_EVEN = (DEPTH + 1) // 2
N_ODD = DEPTH // 2
RMS_EPS = 1e-6
ATTN_SCALE = HEAD_DIM ** -0.5
F32 = jnp.float32

kernel_name = 'hybrid_moba_dilated_nsa_block'


def _rms(x, g):
    xf = x.astype(F32)
    y = xf * lax.rsqrt(jnp.mean(xf * xf, axis=-1, keepdims=True) + RMS_EPS)
    return (y * g.astype(F32)).astype(x.dtype)


def _heads(t, n):
    B, S, _ = t.shape
    return t.reshape(B, S, n, HEAD_DIM).transpose(0, 2, 1, 3)


def _rel_bucket(dist):
    n = jnp.maximum(dist, 0)
    exact = REL_BUCKETS // 2
    nf = jnp.maximum(n, 1).astype(F32)
    large = exact + (jnp.log(nf / exact) / math.log(REL_MAX_DIST / exact) * (REL_BUCKETS - exact)).astype(jnp.int32)
    return jnp.where(n < exact, n, jnp.minimum(large, REL_BUCKETS - 1))


def _masked_softmax(s, mask):
    s = jnp.where(mask, s.astype(F32), -jnp.inf)
    m = jnp.max(s, axis=-1, keepdims=True)
    m = jnp.where(jnp.isfinite(m), m, 0.0)
    e = jnp.exp(s - m)
    l = jnp.sum(e, axis=-1, keepdims=True)
    p = e / jnp.where(l > 0, l, 1.0)
    return p, (m + jnp.log(l))[..., 0]


def _moba(q, k, v, tab):
    B, H, S, hd = q.shape
    nb = -(-S // MOBA_BLOCK)
    padw = ((0, 0), (0, 0), (0, nb * MOBA_BLOCK - S), (0, 0))
    kb = jnp.pad(k, padw).reshape(B, H, nb, MOBA_BLOCK, hd)
    vb = jnp.pad(v, padw).reshape(B, H, nb, MOBA_BLOCK, hd)
    kmean = jnp.mean(kb.astype(F32), axis=3)
    own = jnp.arange(S) // MOBA_BLOCK
    gate = jnp.einsum('bhsd,bhnd->bhsn', q.astype(F32), kmean)
    gate = jnp.where(jnp.arange(nb)[None, :] < own[:, None], gate, -jnp.inf)
    n_sel = min(MOBA_TOPK, nb)
    _, sel = lax.top_k(gate, n_sel)
    sel_ok = sel < own[:, None]
    nc = S // MOBA_QCHUNK

    def to_chunks(t):
        return jnp.moveaxis(t.reshape(B, H, nc, MOBA_QCHUNK, *t.shape[3:]), 2, 0)

    b_ix = jnp.arange(B)[:, None, None, None]
    h_ix = jnp.arange(H)[None, :, None, None]
    a_blk = jnp.arange(MOBA_BLOCK)

    def chunk(args):
        c, qc, sc, okc = args
        t = c * MOBA_QCHUNK + jnp.arange(MOBA_QCHUNK)
        j = (c * MOBA_QCHUNK) // MOBA_BLOCK
        k_own = lax.dynamic_index_in_dim(kb, j, axis=2, keepdims=False)
        v_own = lax.dynamic_index_in_dim(vb, j, axis=2, keepdims=False)
        d_own = t[:, None] - (j * MOBA_BLOCK + a_blk)[None, :]
        s_own = jnp.einsum('bhqd,bhkd->bhqk', qc, k_own).astype(F32) * ATTN_SCALE + tab[:, _rel_bucket(d_own)].astype(F32)
        m_own = jnp.broadcast_to(d_own >= 0, s_own.shape)
        k_sel = kb[b_ix, h_ix, sc].reshape(B, H, MOBA_QCHUNK, n_sel * MOBA_BLOCK, hd)
        v_sel = vb[b_ix, h_ix, sc].reshape(B, H, MOBA_QCHUNK, n_sel * MOBA_BLOCK, hd)
        kpos = (sc[..., None] * MOBA_BLOCK + a_blk).reshape(B, H, MOBA_QCHUNK, n_sel * MOBA_BLOCK)
        s_sel = jnp.einsum('bhqd,bhqkd->bhqk', qc, k_sel).astype(F32) * ATTN_SCALE + tab[h_ix, _rel_bucket(t[:, None] - kpos)].astype(F32)
        m_sel = jnp.repeat(okc, MOBA_BLOCK, axis=-1)
        pr, _ = _masked_softmax(jnp.concatenate([s_own, s_sel], axis=-1), jnp.concatenate([m_own, m_sel], axis=-1))
        p_own = pr[..., :MOBA_BLOCK].astype(v.dtype)
        p_sel = pr[..., MOBA_BLOCK:].astype(v.dtype)
        return jnp.einsum('bhqk,bhkd->bhqd', p_own, v_own) + jnp.einsum('bhqk,bhqkd->bhqd', p_sel, v_sel)

    out = lax.map(chunk, (jnp.arange(nc), to_chunks(q), to_chunks(sel), to_chunks(sel_ok)))
    return jnp.moveaxis(out, 0, 2).reshape(B, H, S, hd)


def _stride_split(t, dil, L):
    B, H, S, hd = t.shape
    t = jnp.pad(t, ((0, 0), (0, 0), (0, L * dil - S), (0, 0)))
    return t.reshape(B, H, L, dil, hd).transpose(0, 1, 3, 2, 4)


def _band(t):
    prev = jnp.pad(t, ((0, 0), (0, 0), (0, 0), (1, 0), (0, 0), (0, 0)))[:, :, :, :-1]
    return jnp.concatenate([prev, t], axis=4)


def _dilated(q, k, v, tab):
    B, H, S, hd = q.shape
    blk = DIL_BLOCK
    outs, lses = [], []
    for window, dil in DIL_PAIRS:
        steps = window // dil
        L = -(-S // (dil * blk)) * blk
        nbk = L // blk
        qs = _stride_split(q, dil, L).reshape(B, H, dil, nbk, blk, hd)
        ks = _stride_split(k, dil, L).reshape(B, H, dil, nbk, blk, hd)
        vs = _stride_split(v, dil, L).reshape(B, H, dil, nbk, blk, hd)
        step = jnp.arange(blk)[:, None] + blk - jnp.arange(2 * blk)[None, :]
        key_ok = (jnp.arange(nbk)[:, None] * blk - blk + jnp.arange(2 * blk)[None, :]) >= 0
        mask = ((step >= 0) & (step <= steps))[None] & key_ok[:, None, :]
        bias = tab[:, _rel_bucket(step * dil)].astype(F32)
        s = jnp.einsum('bhrjqd,bhrjkd->bhrjqk', qs, _band(ks)).astype(F32) * ATTN_SCALE + bias[None, :, None, None]
        pr, lse = _masked_softmax(s, mask)
        o = jnp.einsum('bhrjqk,bhrjkd->bhrjqd', pr.astype(v.dtype), _band(vs))
        outs.append(o.reshape(B, H, dil, L, hd).transpose(0, 1, 3, 2, 4).reshape(B, H, L * dil, hd)[:, :, :S])
        lses.append(lse.reshape(B, H, dil, L).transpose(0, 1, 3, 2).reshape(B, H, L * dil)[:, :, :S])
    w = jax.nn.softmax(jnp.stack(lses), axis=0)
    o = jnp.einsum('nbhs,nbhsd->bhsd', w, jnp.stack(outs).astype(F32))
    return o.astype(q.dtype)


def _compress(t, pos_emb, w1, b1, w2, b2):
    B, G, S, hd = t.shape
    nc = (S - NSA_CMP_LEN) // NSA_CMP_STRIDE + 1
    idx = jnp.arange(nc)[:, None] * NSA_CMP_STRIDE + jnp.arange(NSA_CMP_LEN)[None, :]
    blocks = (t[:, :, idx] + pos_emb).reshape(B, G, nc, NSA_CMP_LEN * hd)
    return jax.nn.gelu(blocks @ w1 + b1) @ w2 + b2


def _nsa(q, kc, vc, ks, vs, kw, vw, gates, tab):
    B, H, S, hd = q.shape
    G, hpg = NSA_GROUPS, NSA_HPG
    qg = q.reshape(B, G, hpg, S, hd)
    tabg = tab.reshape(G, hpg, REL_BUCKETS)
    t = jnp.arange(S)
    nc = kc.shape[2]
    dc = t[:, None] - (jnp.arange(nc) * NSA_CMP_STRIDE + NSA_CMP_LEN - 1)[None, :]
    s = jnp.einsum('bghsd,bgcd->bghsc', qg, kc).astype(F32) * ATTN_SCALE + tabg[:, :, _rel_bucket(dc)].astype(F32)
    p_cmp, _ = _masked_softmax(s, dc >= 0)
    o_cmp = jnp.einsum('bghsc,bgcd->bghsd', p_cmp.astype(vc.dtype), vc)
    n_sel = -(-S // NSA_SEL_BLOCK)
    cstart = jnp.arange(nc) * NSA_CMP_STRIDE
    sstart = jnp.arange(n_sel) * NSA_SEL_BLOCK
    overlap = jnp.maximum(jnp.minimum(cstart[:, None] + NSA_CMP_LEN, sstart[None, :] + NSA_SEL_BLOCK) - jnp.maximum(cstart[:, None], sstart[None, :]), 0).astype(F32)
    imp = jnp.einsum('bghsc,cn->bgsn', p_cmp, overlap)
    cur = (t // NSA_SEL_BLOCK)[:, None]
    blk_id = jnp.arange(n_sel)[None, :]
    forced = (blk_id == 0) | (blk_id == cur) | (blk_id == cur - 1)
    imp = jnp.where(blk_id <= cur, imp + jnp.where(forced, NSA_FORCE, 0.0), -jnp.inf)
    n_top = min(NSA_SEL_TOPN, n_sel)
    _, sel = lax.top_k(imp, n_top)
    padw = ((0, 0), (0, 0), (0, n_sel * NSA_SEL_BLOCK - S), (0, 0))
    ksb = jnp.pad(ks, padw).reshape(B, G, n_sel, NSA_SEL_BLOCK, hd)
    vsb = jnp.pad(vs, padw).reshape(B, G, n_sel, NSA_SEL_BLOCK, hd)
    nq = S // NSA_QCHUNK
    b_ix = jnp.arange(B)[:, None, None, None]
    g_ix = jnp.arange(G)[None, :, None, None]
    h_ix = jnp.arange(hpg)[None, None, :, None, None]
    a_blk = jnp.arange(NSA_SEL_BLOCK)

    def sel_chunk(args):
        c, qc, sc = args
        tq = c * NSA_QCHUNK + jnp.arange(NSA_QCHUNK)
        kk = ksb[b_ix, g_ix, sc].reshape(B, G, NSA_QCHUNK, n_top * NSA_SEL_BLOCK, hd)
        vv = vsb[b_ix, g_ix, sc].reshape(B, G, NSA_QCHUNK, n_top * NSA_SEL_BLOCK, hd)
        kpos = (sc[..., None] * NSA_SEL_BLOCK + a_blk).reshape(B, G, NSA_QCHUNK, n_top * NSA_SEL_BLOCK)
        dist = tq[:, None] - kpos
        bias = tabg[g_ix[..., None], h_ix, _rel_bucket(dist)[:, :, None]].astype(F32)
        s = jnp.einsum('bghqd,bgqkd->bghqk', qc, kk).astype(F32) * ATTN_SCALE + bias
        pr, _ = _masked_softmax(s, (dist >= 0)[:, :, None])
        return jnp.einsum('bghqk,bgqkd->bghqd', pr.astype(vv.dtype), vv)

    qch = jnp.moveaxis(qg.reshape(B, G, hpg, nq, NSA_QCHUNK, hd), 3, 0)
    sch = jnp.moveaxis(sel.reshape(B, G, nq, NSA_QCHUNK, n_top), 2, 0)
    o_slc = jnp.moveaxis(lax.map(sel_chunk, (jnp.arange(nq), qch, sch)), 0, 3).reshape(B, G, hpg, S, hd)
    nw = S // NSA_WIN_BLOCK
    span = NSA_WINDOW + NSA_WIN_BLOCK
    kwp = jnp.pad(kw, ((0, 0), (0, 0), (NSA_WINDOW, 0), (0, 0)))
    vwp = jnp.pad(vw, ((0, 0), (0, 0), (NSA_WINDOW, 0), (0, 0)))
    kidx = jnp.arange(span)[None, :]
    dw = jnp.arange(NSA_WIN_BLOCK)[:, None] + NSA_WINDOW - kidx
    bias_w = tabg[:, :, _rel_bucket(dw)].astype(F32)

    def win_block(args):
        j, qb = args
        kb_ = lax.dynamic_slice_in_dim(kwp, j * NSA_WIN_BLOCK, span, axis=2)
        vb_ = lax.dynamic_slice_in_dim(vwp, j * NSA_WIN_BLOCK, span, axis=2)
        ok = (dw >= 0) & (dw < NSA_WINDOW) & (j * NSA_WIN_BLOCK - NSA_WINDOW + kidx >= 0)
        s = jnp.einsum('bghqd,bgkd->bghqk', qb, kb_).astype(F32) * ATTN_SCALE + bias_w
        pr, _ = _masked_softmax(s, ok)
        return jnp.einsum('bghqk,bgkd->bghqd', pr.astype(vb_.dtype), vb_)

    qwb = jnp.moveaxis(qg.reshape(B, G, hpg, nw, NSA_WIN_BLOCK, hd), 3, 0)
    o_win = jnp.moveaxis(lax.map(win_block, (jnp.arange(nw), qwb)), 0, 3).reshape(B, G, hpg, S, hd)
    o = (gates[..., 0:1] * o_cmp.reshape(B, H, S, hd) + gates[..., 1:2] * o_slc.reshape(B, H, S, hd)
         + gates[..., 2:3] * o_win.reshape(B, H, S, hd))
    return o.astype(q.dtype)


def _mixer_ab(xn, w_in, w_out, qn_a, kn_a, qn_b, kn_b, rel_bias):
    B, S, _ = xn.shape
    wa = MOBA_HEADS * HEAD_DIM
    wb = DIL_HEADS * HEAD_DIM
    qa, ka, va, qb, kb, vb = jnp.split(xn @ w_in, [wa, 2 * wa, 3 * wa, 3 * wa + wb, 3 * wa + 2 * wb], axis=-1)
    oa = _moba(_rms(_heads(qa, MOBA_HEADS), qn_a), _rms(_heads(ka, MOBA_HEADS), kn_a), _heads(va, MOBA_HEADS), rel_bias[:, :MOBA_HEADS].T)
    ob = _dilated(_rms(_heads(qb, DIL_HEADS), qn_b), _rms(_heads(kb, DIL_HEADS), kn_b), _heads(vb, DIL_HEADS), rel_bias[:, MOBA_HEADS:].T)
    o = jnp.concatenate([oa, ob], axis=1).transpose(0, 2, 1, 3).reshape(B, S, D_MODEL)
    return o @ w_out


def _mixer_nsa(xn, w_in, w_out, qn, kn_c, kn_s, kn_w, cmp_k, cmp_v, rel_bias):
    B, S, _ = xn.shape
    qw = NSA_HEADS * HEAD_DIM
    kvw = NSA_GROUPS * HEAD_DIM
    q, kc, vc, ks, vs, kw, vw, g = jnp.split(xn @ w_in, [qw + i * kvw for i in range(7)], axis=-1)
    q = _rms(_heads(q, NSA_HEADS), qn)
    kc = _rms(_compress(_heads(kc, NSA_GROUPS), *cmp_k), kn_c)
    vc = _compress(_heads(vc, NSA_GROUPS), *cmp_v)
    ks = _rms(_heads(ks, NSA_GROUPS), kn_s)
    kw = _rms(_heads(kw, NSA_GROUPS), kn_w)
    gates = jax.nn.sigmoid(g.astype(F32)).reshape(B, S, NSA_HEADS, 3).transpose(0, 2, 1, 3)
    o = _nsa(q, kc, vc, ks, _heads(vs, NSA_GROUPS), kw, _heads(vw, NSA_GROUPS), gates, rel_bias.T)
    return o.transpose(0, 2, 1, 3).reshape(B, S, D_MODEL) @ w_out


def _swiglu(x, wg, wu, wd):
    return (jax.nn.silu(x @ wg) * (x @ wu)) @ wd


def setup_inputs(seed: int = 0) -> dict:
    key = jax.random.key(seed)
    keys = iter(jax.random.split(key, 40))

    def nrm(shape, scale):
        return jax.random.normal(next(keys), shape, F32) * scale

    def gain(shape):
        return 1.0 + nrm(shape, 0.02)

    D, hd, F, CH = D_MODEL, HEAD_DIM, FFN_HIDDEN, NSA_CMP_HIDDEN
    res = (2 * DEPTH) ** -0.5
    w_ab = 3 * (MOBA_HEADS + DIL_HEADS) * hd
    w_nsa = NSA_HEADS * hd + 6 * NSA_GROUPS * hd + 3 * NSA_HEADS
    cin = NSA_CMP_LEN * hd
    return {
        'x': nrm((BATCH, SEQ, D), 1.0),
        'p': nrm((DEPTH, BATCH, SEQ, PLE_DIM), 1.0),
        'rel_bias': nrm((REL_BUCKETS, N_HEADS), 0.5),
        'norm_mix': gain((DEPTH, D)),
        'norm_ffn': gain((DEPTH, D)),
        'norm_ple': gain((DEPTH, D)),
        'w_ffn_gate': nrm((DEPTH, D, F), D ** -0.5),
        'w_ffn_up': nrm((DEPTH, D, F), D ** -0.5),
        'w_ffn_down': nrm((DEPTH, F, D), F ** -0.5 * res),
        'w_ple_proj': nrm((DEPTH, PLE_DIM, D), PLE_DIM ** -0.5 * res),
        'w_ple_gate': nrm((DEPTH, D, D), D ** -0.5),
        'w_in_ab': nrm((N_EVEN, D, w_ab), D ** -0.5),
        'w_out_ab': nrm((N_EVEN, D, D), D ** -0.5 * res),
        'qn_moba': gain((N_EVEN, hd)),
        'kn_moba': gain((N_EVEN, hd)),
        'qn_dil': gain((N_EVEN, hd)),
        'kn_dil': gain((N_EVEN, hd)),
        'w_in_nsa': nrm((N_ODD, D, w_nsa), D ** -0.5),
        'w_out_nsa': nrm((N_ODD, D, D), D ** -0.5 * res),
        'qn_nsa': gain((N_ODD, hd)),
        'kn_cmp': gain((N_ODD, hd)),
        'kn_slc': gain((N_ODD, hd)),
        'kn_win': gain((N_ODD, hd)),
        'cmp_k_pos': nrm((N_ODD, NSA_CMP_LEN, hd), 0.1),
        'cmp_k_w1': nrm((N_ODD, cin, CH), cin ** -0.5),
        'cmp_k_b1': nrm((N_ODD, CH), 0.01),
        'cmp_k_w2': nrm((N_ODD, CH, hd), CH ** -0.5),
        'cmp_k_b2': nrm((N_ODD, hd), 0.01),
        'cmp_v_pos': nrm((N_ODD, NSA_CMP_LEN, hd), 0.1),
        'cmp_v_w1': nrm((N_ODD, cin, CH), cin ** -0.5),
        'cmp_v_b1': nrm((N_ODD, CH), 0.01),
        'cmp_v_w2': nrm((N_ODD, CH, hd), CH ** -0.5),
        'cmp_v_b2': nrm((N_ODD, hd), 0.01),
    }


def reference(x, p, rel_bias, norm_mix, norm_ffn, norm_ple, w_ffn_gate, w_ffn_up, w_ffn_down,
              w_ple_proj, w_ple_gate, w_in_ab, w_out_ab, qn_moba, kn_moba, qn_dil, kn_dil,
              w_in_nsa, w_out_nsa, qn_nsa, kn_cmp, kn_slc, kn_win,
              cmp_k_pos, cmp_k_w1, cmp_k_b1, cmp_k_w2, cmp_k_b2,
              cmp_v_pos, cmp_v_w1, cmp_v_b1, cmp_v_w2, cmp_v_b2):
    h = x
    for i in range(DEPTH):
        e = i // 2
        xn = _rms(h, norm_mix[i])
        if i % 2 == 0:
            mix = _mixer_ab(xn, w_in_ab[e], w_out_ab[e], qn_moba[e], kn_moba[e], qn_dil[e], kn_dil[e], rel_bias)
        else:
            cmp_k = (cmp_k_pos[e], cmp_k_w1[e], cmp_k_b1[e], cmp_k_w2[e], cmp_k_b2[e])
            cmp_v = (cmp_v_pos[e], cmp_v_w1[e], cmp_v_b1[e], cmp_v_w2[e], cmp_v_b2[e])
            mix = _mixer_nsa(xn, w_in_nsa[e], w_out_nsa[e], qn_nsa[e], kn_cmp[e], kn_slc[e], kn_win[e], cmp_k, cmp_v, rel_bias)
        h = h + mix.astype(h.dtype)
        h = h + _swiglu(_rms(h, norm_ffn[i]), w_ffn_gate[i], w_ffn_up[i], w_ffn_down[i])
        gate = jax.nn.sigmoid(_rms(h, norm_ple[i]) @ w_ple_gate[i])
        h = h + gate * (p[i] @ w_ple_proj[i])
    return h
```

```python
import math
import contextlib
import numpy as np
import concourse.bass as bass
import concourse.mybir as mybir
from concourse.bass_utils import run_bass_kernel_spmd

F32 = mybir.dt.float32
BF16 = mybir.dt.bfloat16
AF = mybir.ActivationFunctionType
ALU = mybir.AluOpType
AX = mybir.AxisListType

S = 2048
D = 1024
FH = 2816
NF = 22
WHL = 4352
EPS = 1e-6
NEG = -30000.0
BIG = 3.0e38


class Dep:
    __slots__ = ("w", "r")

    def __init__(self):
        self.w = None
        self.r = []


class FW:
    NDMA = 24

    def __init__(self, nc, es):
        self.nc = nc
        self.engs = {"pe": nc.tensor, "act": nc.scalar, "dve": nc.vector, "pool": nc.gpsimd, "sp": nc.sync}
        self.sems = {}
        self.cnt = {}
        for k in self.engs:
            self.sems[k] = es.enter_context(nc.semaphore("sem_" + k))
            self.cnt[k] = 0
        for i in range(self.NDMA):
            k = ("dma", i)
            self.sems[k] = es.enter_context(nc.semaphore("sem_dma%d" % i))
            self.cnt[k] = 0
        self.seen = {e: {} for e in self.engs}
        self.dma_rr = 0
        self.n_ins = 0
        self.n_wait = 0

    def _wait(self, eng, deps):
        seen = self.seen[eng]
        need = {}
        for d in deps:
            if d is None:
                continue
            k, v = d
            if seen.get(k, 0) >= v:
                continue
            if need.get(k, 0) < v:
                need[k] = v
        for k, v in need.items():
            self.engs[eng].wait_ge(self.sems[k], v)
            seen[k] = v
            self.n_wait += 1

    @staticmethod
    def _collect(r, w):
        deps = []
        for t in r:
            deps.append(t.w)
        for t in w:
            deps.append(t.w)
            deps.extend(t.r)
        return deps

    def _mark(self, tok, r, w):
        for t in w:
            t.w = tok
            t.r = []
        for t in r:
            t.r.append(tok)
            if len(t.r) > 64:
                best = {}
                for k, v in t.r:
                    if best.get(k, 0) < v:
                        best[k] = v
                t.r = list(best.items())

    def op(self, eng, fn, r=(), w=()):
        self._wait(eng, self._collect(r, w))
        ins = fn(self.engs[eng])
        self.cnt[eng] += 1
        ins.then_inc(self.sems[eng], 1)
        self._mark((eng, self.cnt[eng]), r, w)
        self.n_ins += 1

    def dma(self, q, out, in_, r=(), w=()):
        i = self.dma_rr
        self.dma_rr = (self.dma_rr + 1) % self.NDMA
        k = ("dma", i)
        deps = self._collect(r, w)
        if self.cnt[k] > 0:
            deps.append((k, self.cnt[k]))
        self._wait(q, deps)
        ins = self.engs[q].dma_start(out=out, in_=in_)
        self.cnt[k] += 16
        ins.then_inc(self.sems[k], 16)
        self._mark((k, self.cnt[k]), r, w)
        self.n_ins += 1

    def barrier(self):
        allk = [(k, v) for k, v in self.cnt.items() if v > 0]
        for e in self.engs:
            self._wait(e, allk)

    def finish(self, eng="sp"):
        allk = [(k, v) for k, v in self.cnt.items() if v > 0]
        self._wait(eng, allk)


class Rot:
    def __init__(self, items):
        self.items = items
        self.i = 0

    def next(self):
        t = self.items[self.i]
        self.i = (self.i + 1) % len(self.items)
        return t


def _bucket(d):
    n = np.maximum(d, 0)
    nf = np.maximum(n, 1).astype(np.float32)
    large = 16 + (np.log(nf / np.float32(16)) / np.float32(math.log(128.0)) * np.float32(16)).astype(np.int32)
    return np.where(n < 16, n, np.minimum(large, 31))


def _static_consts():
    c = {}
    m = np.arange(WHL)
    d = 2047 - m
    wm = np.zeros((3, WHL), np.float32)
    wm[0] = (d >= 0)
    wm[1] = (d >= 0) * ((d <= 128).astype(np.float32) + ((d % 4 == 0) & (d <= 512)) + ((d % 16 == 0) & (d <= 2048)))
    wm[2] = (d >= 0) & (d < 512)
    c["c_wm"] = wm
    c["c_ident"] = np.eye(128, dtype=np.float32)
    k = np.arange(S)
    c["c_blk_moba"] = (k[None, :] // 256 == np.arange(8)[:, None]).astype(np.float32)
    c["c_blk_nsa"] = (k[None, :] // 64 == np.arange(32)[:, None]).astype(np.float32)
    g = np.zeros((128, 4, 8), np.float32)
    for i, own in enumerate(range(4, 8)):
        g[:, i, own:] = -BIG
    c["c_gneg"] = g
    add = np.full((128, 16, 32), -BIG, np.float32)
    forced = np.zeros((128, 16, 32), np.float32)
    for qt in range(16):
        for q in range(128):
            cur = (qt * 128 + q) // 64
            for n in (0, cur, cur - 1):
                if n >= 0:
                    forced[q, qt, n] = 1.0
            for n in range(1, cur - 1):
                add[q, qt, n] = 0.0
    c["c_addmask"] = add
    c["c_forced"] = forced
    cs = np.arange(127) * 16
    ss = np.arange(32) * 64
    ov = np.maximum(np.minimum(cs[:, None] + 32, ss[None, :] + 64) - np.maximum(cs[:, None], ss[None, :]), 0)
    ovl = np.ones((127, 33), np.float32)
    ovl[:, :32] = ov
    c["c_ovl"] = ovl
    gs = np.zeros((48, 48, 64), np.float32)
    for i in range(48):
        gs[i, i, :] = 1.0
    c["c_gsel"] = gs
    return c


_CONST_SHAPES = None


def _dap(t, offset, ap):
    return bass.AP(tensor=t.tensor, offset=offset, ap=[list(a) for a in ap])


def build_nc(upto=99, dbg=()):
    nc = bass.Bass("TRN2", target_bir_lowering=False)
    consts = _static_consts()
    IN = {}

    def din(name, shape):
        IN[name] = nc.dram_tensor(name, list(shape), F32, kind="ExternalInput").ap()
        return IN[name]

    xT = din("xT", [D, S])
    pT = din("pT", [2, 256, S])
    whb = din("whb", [16, WHL])
    gains_d = din("gains", [128, 48])
    hg_d = din("hg", [128, 8])
    posT_d = din("posT", [64, 64])
    b1_d = din("b1c", [128, 2])
    b2k_d = din("b2k", [64, 1])
    b2v_d = din("b2v", [1, 64])
    w_in_ab = din("w_in_ab", [D, 3072])
    w_out_ab = din("w_out_ab", [D, D])
    w_in_nsa = din("w_in_nsa", [D, 2608])
    w_out_nsa = din("w_out_nsa", [D, D])
    w_g = din("w_ffn_gate", [2, D, FH])
    w_u = din("w_ffn_up", [2, D, FH])
    w_d = din("w_ffn_down", [2, FH, D])
    w_pp = din("w_ple_proj", [2, 256, D])
    w_pg = din("w_ple_gate", [2, D, D])
    ck_w1 = din("cmp_k_w1", [2048, 128])
    ck_w2 = din("cmp_k_w2", [128, 64])
    cv_w1 = din("cmp_v_w1", [2048, 128])
    cv_w2 = din("cmp_v_w2", [128, 64])
    for k, v in consts.items():
        din(k, v.shape)
    outT = nc.dram_tensor("outT", [D, S], F32, kind="ExternalOutput").ap()
    DBG = {}

    with contextlib.ExitStack() as es:
        fw = FW(nc, es)

        uniq = [0]

        def sb(name, shape, dt=F32, stack=es):
            uniq[0] += 1
            return stack.enter_context(nc.sbuf_tensor("%s_%d" % (name, uniq[0]), list(shape), dt))

        def pst(name):
            return es.enter_context(nc.psum_tensor(name, [128, 512], F32))

        psS = Rot([(pst("psS%d" % i), Dep()) for i in range(2)])
        psO = Rot([(pst("psO%d" % i), Dep()) for i in range(2)])
        psA = Rot([(pst("psA%d" % i), Dep()) for i in range(2)])
        psB = Rot([(pst("psB%d" % i), Dep()) for i in range(2)])

        def dump(name, ap, shape, deps):
            if name not in dbg:
                return
            t = nc.dram_tensor("dbg_" + name, list(shape), ap.dtype if hasattr(ap, "dtype") else F32, kind="ExternalOutput").ap()
            DBG[name] = t
            fw.dma("sp", t, ap, r=deps)

        hS = nc.dram_tensor("hS", [D, S], F32, kind="Internal").ap()
        dhS = Dep()
        dh = [[Dep() for _ in range(4)] for _ in range(8)]
        xnT = sb("xnT", [128, 8, S], BF16)
        dxn = [Dep() for _ in range(4)]
        oT = sb("oT", [128, 8, S], BF16)
        doT = [[Dep() for _ in range(4)] for _ in range(8)]
        hT = None
        gains = sb("gains_sb", [128, 48])
        hg = sb("hg_sb", [128, 8])
        dcon = Dep()
        ones_bf = sb("ones_bf", [128, 128], BF16)
        blk_ones = sb("blk_ones", [128, 128], BF16)
        ident_bf = sb("ident_bf", [128, 128], BF16)
        sqb = Rot([(sb("sqb%d" % i, [128, 512], BF16), Dep()) for i in range(2)])
        f32b = Rot([(sb("f32b%d" % i, [128, 512]), Dep()) for i in range(4)])

        fw.dma("sp", gains[:], gains_d[:, :], w=[dcon])
        fw.dma("sp", hg[:], hg_d[:, :], w=[dcon])
        fw.dma("pool", ident_bf[:], IN["c_ident"][:, :], w=[dcon])
        fw.op("dve", lambda e: e.memset(ones_bf[:], 1.0), w=[dcon])
        fw.op("dve", lambda e: e.memset(blk_ones[:], 0.0), w=[dcon])
        fw.op("dve", lambda e: e.memset(blk_ones[0:64, 0:64], 1.0), w=[dcon])
        fw.op("dve", lambda e: e.memset(blk_ones[64:128, 64:128], 1.0), w=[dcon])
        def load_h(src, dsrc):
            v = src.rearrange("(c p) s -> p c s", p=128)
            for c in range(8):
                for sc in range(4):
                    fw.dma("sp", hT[:, c, sc * 512:(sc + 1) * 512], v[:, c, sc * 512:(sc + 1) * 512], r=dsrc, w=[dh[c][sc]])

        def store_h(dst, ddst):
            v = dst.rearrange("(c p) s -> p c s", p=128)
            for c in range(8):
                for sc in range(4):
                    fw.dma("sp", v[:, c, sc * 512:(sc + 1) * 512], hT[:, c, sc * 512:(sc + 1) * 512], r=[dh[c][sc]], w=ddst)

        def cs_(sc):
            return slice(sc * 512, (sc + 1) * 512)

        def rmsnorm(gidx):
            for sc in range(4):
                cs = cs_(sc)
                pt, dp = psB.next()
                for c in range(8):
                    sq, dsq = sqb.next()
                    fw.op("act", lambda e: e.activation(out=sq[:], in_=hT[:, c, cs], func=AF.Square), r=[dh[c][sc]], w=[dsq])
                    fw.op("pe", lambda e: e.matmul(pt[:], lhsT=ones_bf[:], rhs=sq[:], start=(c == 0), stop=(c == 7)),
                          r=[dsq, dcon], w=[dp])
                rt, drt = f32b.next()
                fw.op("act", lambda e: e.activation(out=rt[:], in_=pt[:], func=AF.Sqrt, bias=EPS, scale=1.0 / D), r=[dp], w=[drt])
                fw.op("dve", lambda e: e.reciprocal(out=rt[:], in_=rt[:]), r=[drt], w=[drt])
                for c in range(8):
                    fw.op("dve", lambda e: e.scalar_tensor_tensor(
                        out=xnT[:, c, cs], in0=hT[:, c, cs], scalar=gains[:, gidx * 8 + c:gidx * 8 + c + 1], in1=rt[:],
                        op0=ALU.mult, op1=ALU.mult), r=[dh[c][sc], drt, dcon], w=[dxn[sc]])

        def proj_fm(wt, dw, sc, M, c0=0):
            pt, dp = psA.next()
            for kc in range(8):
                fw.op("pe", lambda e: e.matmul(pt[0:M, :], lhsT=wt[:, kc, c0:c0 + M], rhs=xnT[:, kc, cs_(sc)],
                                               start=(kc == 0), stop=(kc == 7)), r=[dw, dxn[sc]], w=[dp])
            return pt, dp

        def headnorm(pt, dp, M, gcol, outs):
            sq, dsq = sqb.next()
            fw.op("act", lambda e: e.activation(out=sq[0:M, :], in_=pt[0:M, :], func=AF.Square), r=[dp], w=[dsq])
            ps2, dp2 = psB.next()
            fw.op("pe", lambda e: e.matmul(ps2[0:M, :], lhsT=blk_ones[0:M, 0:M], rhs=sq[0:M, :], start=True, stop=True),
                  r=[dsq, dcon], w=[dp2])
            rt, drt = f32b.next()
            fw.op("act", lambda e: e.activation(out=rt[0:M, :], in_=ps2[0:M, :], func=AF.Sqrt, bias=EPS, scale=1.0 / 64), r=[dp2], w=[drt])
            fw.op("dve", lambda e: e.reciprocal(out=rt[0:M, :], in_=rt[0:M, :]), r=[drt], w=[drt])
            for (dst, ddst, r0) in outs:
                fw.op("dve", lambda e: e.scalar_tensor_tensor(
                    out=dst, in0=pt[r0:r0 + 64, :], scalar=hg[r0:r0 + 64, gcol:gcol + 1], in1=rt[r0:r0 + 64, :],
                    op0=ALU.mult, op1=ALU.mult), r=[dp, drt, dcon], w=[ddst])

        def load_w3(q, wt, dw, src2d, c0, ncols, dst_c0=0):
            v = src2d.rearrange("(kc p) n -> p kc n", p=128)
            fw.dma(q, wt[:, :, dst_c0:dst_c0 + ncols], v[:, :, c0:c0 + ncols], w=[dw])

        def rev(t, n, rows=128):
            a = t[0:rows, n - 1:n]
            return bass.AP(tensor=a.tensor, offset=a.offset, ap=[list(a.ap[0]), [-1, n]])

        pbuf_items = []

        def attend(qt, dq, K, ktile, dk, vt, dv, E, dE, tiles_fn, out_cb, chunks=range(4)):
            for qc in chunks:
                c0 = qc * 512
                tiles = tiles_fn(qc)
                assert tiles[0][1] == 0 and tiles[0][2] == 512
                po, dpo = psO.next()
                for idx, (kt, lo, hi) in enumerate(tiles):
                    n = hi - lo
                    pss, dps = psS.next()
                    fw.op("pe", lambda e: e.matmul(pss[:, 0:n], lhsT=ktile[0:K, kt * 128:(kt + 1) * 128],
                                                   rhs=qt[0:K, c0 + lo:c0 + hi], start=True, stop=True),
                          r=[dk[kt // 4], dq[qc]], w=[dps])
                    pb, dpb = pbuf.next()
                    fw.op("act", lambda e: e.activation(out=pb[:, 0:n], in_=pss[:, 0:n], func=AF.Exp, scale=0.125), r=[dps], w=[dpb])
                    u0 = c0 + lo - kt * 128
                    fw.op("dve", lambda e: e.tensor_tensor(out=pb[:, 0:n], in0=pb[:, 0:n], in1=E[:, u0:u0 + n], op=ALU.mult),
                          r=[dE, dpb], w=[dpb])
                    fw.op("pe", lambda e: e.matmul(po[:, lo:hi], lhsT=vt[:, kt, :], rhs=pb[:, 0:n],
                                                   start=(idx == 0), stop=(idx == len(tiles) - 1)),
                          r=[dv[kt // 4], dpb], w=[dpo])
                out_cb(qc, po, dpo)

        def causal_tiles(qc):
            res = []
            for kt in range(4 * qc + 4):
                lo = max(0, kt * 128 - qc * 512)
                res.append((kt, lo, 512))
            return res

        def win_tiles(qc):
            res = []
            order = [4 * qc] + [k for k in range(max(0, 4 * qc - 4), 4 * qc + 4) if k != 4 * qc]
            for kt in order:
                off = kt * 128 - qc * 512
                lo = max(0, off)
                hi = min(512, ((off + 638) // 128 + 1) * 128)
                res.append((kt, lo, hi))
            return res

        def build_E(H, E, dE, M, dM, stage, dstage, width=2048, pstride=1, off=0, rows=128):
            fw.dma("sp", stage[0:rows, 0:width], _dap(whb, H * WHL + off + (2048 - width), [[pstride, rows], [1, width]]), w=[dstage])
            fw.op("act", lambda e: e.activation(out=E[0:rows, 0:width], in_=rev(stage, width, rows), func=AF.Exp), r=[dstage], w=[dE])
            if M is not None:
                fw.op("dve", lambda e: e.tensor_tensor(out=E[0:rows, 0:width], in0=E[0:rows, 0:width], in1=M[0:rows, 0:width], op=ALU.mult),
                      r=[dM, dE], w=[dE])

        def build_M(kind, M, dM, stage, dstage, width=2048, pstride=1, off=0, rows=128):
            fw.dma("sp", stage[0:rows, 0:width], _dap(IN["c_wm"], kind * WHL + off + (2048 - width), [[pstride, rows], [1, width]]), w=[dstage])
            fw.op("act", lambda e: e.activation(out=M[0:rows, 0:width], in_=rev(stage, width, rows), func=AF.Copy), r=[dstage], w=[dM])

        def out_proj(w_out, st):
            wo = [(sb("wo%d" % i, [128, 8, 128], BF16, st), Dep()) for i in range(2)]
            wv = w_out.rearrange("(kc p) n -> p kc n", p=128)

            def ld(fc):
                t, d_ = wo[fc % 2]
                fw.dma("pool", t[:], wv[:, :, fc * 128:(fc + 1) * 128], w=[d_])
            ld(0)
            for fc in range(8):
                if fc + 1 < 8:
                    ld(fc + 1)
                t, d_ = wo[fc % 2]
                for sc in range(4):
                    pt, dp = psA.next()
                    for pr in range(8):
                        fw.op("pe", lambda e: e.matmul(pt[:], lhsT=t[:, pr, :], rhs=oT[:, pr, cs_(sc)], start=(pr == 0), stop=(pr == 7)),
                              r=[d_, doT[pr][sc]], w=[dp])
                    fw.op("dve", lambda e: e.tensor_tensor(out=hT[:, fc, cs_(sc)], in0=pt[:], in1=hT[:, fc, cs_(sc)], op=ALU.add),
                          r=[dp, dh[fc][sc]], w=[dh[fc][sc]])

        def ffn(layer):
            rmsnorm(layer * 3 + 1)
            with contextlib.ExitStack() as st:
                act2 = sb("ffn_act", [128, NF - 16, 1024], BF16, st)
                dact = [[Dep() for _ in range(2)] for _ in range(NF)]

                def act_ap(f, q):
                    if f < 16:
                        return oT[:, f // 2, (f % 2) * 1024 + q * 512:(f % 2) * 1024 + (q + 1) * 512]
                    return act2[:, f - 16, q * 512:(q + 1) * 512]
                wg = [(sb("wg%d" % i, [128, 8, 128], BF16, st), Dep()) for i in range(2)]
                wu = [(sb("wu%d" % i, [128, 8, 128], BF16, st), Dep()) for i in range(2)]
                wd = [(sb("wd%d" % i, [128, NF, 128], BF16, st), Dep()) for i in range(2)]
                sg = Rot([(sb("sg%d" % i, [128, 512], F32, st), Dep()) for i in range(2)])
                wgv = w_g[layer].rearrange("(kc p) n -> p kc n", p=128)
                wuv = w_u[layer].rearrange("(kc p) n -> p kc n", p=128)
                wdv = w_d[layer].rearrange("(f p) n -> p f n", p=128)

                def ld1(f):
                    fw.dma("pool", wg[f % 2][0][:], wgv[:, :, f * 128:(f + 1) * 128], w=[wg[f % 2][1]])
                    fw.dma("pool", wu[f % 2][0][:], wuv[:, :, f * 128:(f + 1) * 128], w=[wu[f % 2][1]])

                def ld2(dc):
                    fw.dma("pool", wd[dc % 2][0][:], wdv[:, :, dc * 128:(dc + 1) * 128], w=[wd[dc % 2][1]])

                for half in range(2):
                    ld1(0)
                    for f in range(NF):
                        if f + 1 < NF:
                            ld1(f + 1)
                        else:
                            ld2(0)
                        tg, dg_ = wg[f % 2]
                        tu, du_ = wu[f % 2]
                        for q in range(2):
                            sc = half * 2 + q
                            pg, dpg = psA.next()
                            pu, dpu = psB.next()
                            for kc in range(8):
                                fw.op("pe", lambda e: e.matmul(pg[:], lhsT=tg[:, kc, :], rhs=xnT[:, kc, cs_(sc)], start=(kc == 0), stop=(kc == 7)),
                                      r=[dg_, dxn[sc]], w=[dpg])
                            for kc in range(8):
                                fw.op("pe", lambda e: e.matmul(pu[:], lhsT=tu[:, kc, :], rhs=xnT[:, kc, cs_(sc)], start=(kc == 0), stop=(kc == 7)),
                                      r=[du_, dxn[sc]], w=[dpu])
                            s_, ds_ = sg.next()
                            fw.op("act", lambda e: e.activation(out=s_[:], in_=pg[:], func=AF.Silu), r=[dpg], w=[ds_])
                            fw.op("dve", lambda e: e.tensor_tensor(out=act_ap(f, q), in0=s_[:], in1=pu[:], op=ALU.mult),
                                  r=[ds_, dpu], w=[dact[f][q]])
                    for dc in range(8):
                        if dc + 1 < 8:
                            ld2(dc + 1)
                        td, dd_ = wd[dc % 2]
                        for q in range(2):
                            sc = half * 2 + q
                            pt, dp = psS.next()
                            for f in range(NF):
                                fw.op("pe", lambda e: e.matmul(pt[:], lhsT=td[:, f, :], rhs=act_ap(f, q),
                                                               start=(f == 0), stop=(f == NF - 1)), r=[dd_, dact[f][q]], w=[dp])
                            fw.op("dve", lambda e: e.tensor_tensor(out=hT[:, dc, cs_(sc)], in0=pt[:], in1=hT[:, dc, cs_(sc)], op=ALU.add),
                                  r=[dp, dh[dc][sc]], w=[dh[dc][sc]])
                fw.barrier()

        def ple(layer):
            rmsnorm(layer * 3 + 2)
            with contextlib.ExitStack() as st:
                pTs = sb("pTs", [128, 2, S], BF16, st)
                dpT = Dep()
                wpg = [(sb("wpg%d" % i, [128, 8, 128], BF16, st), Dep()) for i in range(2)]
                wpp = [(sb("wpp%d" % i, [128, 2, 128], BF16, st), Dep()) for i in range(2)]
                sg = Rot([(sb("psg%d" % i, [128, 512], F32, st), Dep()) for i in range(2)])
                fw.dma("pool", pTs[:], pT[layer].rearrange("(kc p) s -> p kc s", p=128), w=[dpT])
                wgv = w_pg[layer].rearrange("(kc p) n -> p kc n", p=128)
                wpv = w_pp[layer].rearrange("(kc p) n -> p kc n", p=128)

                def ld(fc):
                    fw.dma("pool", wpg[fc % 2][0][:], wgv[:, :, fc * 128:(fc + 1) * 128], w=[wpg[fc % 2][1]])
                    fw.dma("pool", wpp[fc % 2][0][:], wpv[:, :, fc * 128:(fc + 1) * 128], w=[wpp[fc % 2][1]])
                ld(0)
                for fc in range(8):
                    if fc + 1 < 8:
                        ld(fc + 1)
                    tg, dg_ = wpg[fc % 2]
                    tp, dp_ = wpp[fc % 2]
                    for sc in range(4):
                        pg, dpg = psA.next()
                        pp_, dpp = psB.next()
                        for kc in range(8):
                            fw.op("pe", lambda e: e.matmul(pg[:], lhsT=tg[:, kc, :], rhs=xnT[:, kc, cs_(sc)], start=(kc == 0), stop=(kc == 7)),
                                  r=[dg_, dxn[sc]], w=[dpg])
                        for kc in range(2):
                            fw.op("pe", lambda e: e.matmul(pp_[:], lhsT=tp[:, kc, :], rhs=pTs[:, kc, cs_(sc)], start=(kc == 0), stop=(kc == 1)),
                                  r=[dp_, dpT], w=[dpp])
                        s_, ds_ = sg.next()
                        fw.op("act", lambda e: e.activation(out=s_[:], in_=pg[:], func=AF.Sigmoid), r=[dpg], w=[ds_])
                        fw.op("dve", lambda e: e.tensor_tensor(out=s_[:], in0=s_[:], in1=pp_[:], op=ALU.mult), r=[ds_, dpp], w=[ds_])
                        fw.op("dve", lambda e: e.tensor_tensor(out=hT[:, fc, cs_(sc)], in0=s_[:], in1=hT[:, fc, cs_(sc)], op=ALU.add),
                              r=[ds_, dh[fc][sc]], w=[dh[fc][sc]])
                fw.barrier()

        def layer0_mixer():
            with contextlib.ExitStack() as st:
                qh = [(sb("qh%d" % i, [72, S], BF16, st), [Dep() for _ in range(4)]) for i in range(2)]
                kh = [(sb("kh%d" % i, [72, S], BF16, st), [Dep() for _ in range(4)]) for i in range(2)]
                vh = [(sb("vh%d" % i, [128, 16, 128], BF16, st), [Dep() for _ in range(4)]) for i in range(2)]
                Eh = [(sb("Eh%d" % i, [128, S], BF16, st), Dep()) for i in range(2)]
                Mt = (sb("Mt", [128, S], BF16, st), Dep())
                stage = sb("stage", [128, S], F32, st)
                dstage = Dep()
                wq = [(sb("wq%d" % i, [128, 8, 384], BF16, st), Dep()) for i in range(2)]
                global_pbuf = [(sb("pbuf%d" % i, [128, 512], BF16, st), Dep()) for i in range(4)]
                nonlocal pbuf
                pbuf = Rot(global_pbuf)
                km = [(sb("km%d" % i, [64, 8], F32, st), Dep()) for i in range(2)]
                kmb = [(sb("kmb%d" % i, [64, 8], BF16, st), Dep()) for i in range(2)]
                gneg = sb("gneg", [128, 4, 8], F32, st)
                gm = sb("gm", [128, 8], F32, st)
                dgm = Dep()
                m8 = sb("m8", [128, 8], F32, st)
                dm8 = Dep()
                nmp = [(sb("nmp%d" % i, [128, 72], BF16, st), Dep()) for i in range(4)]
                dl0 = Dep()
                fw.dma("sp", gneg[:], IN["c_gneg"][:, :, :], w=[dl0])
                for i in range(4):
                    fw.op("dve", lambda e: e.memset(nmp[i][0][:], 0.0), w=[nmp[i][1]])
                for i in range(2):
                    fw.op("dve", lambda e: e.memset(vh[i][0][:, :, 64:128], 1.0), w=vh[i][1])
                    fw.dma("pool", kh[i][0][64:72, :], IN["c_blk_moba"][:, :], w=kh[i][1])
                    fw.op("dve", lambda e: e.memset(qh[i][0][64:72, :], 0.0), w=qh[i][1])
                build_M(1, Mt[0], Mt[1], stage, dstage)

                def ldw(pair):
                    t, d_ = wq[pair % 2]
                    base = 0 if pair < 4 else 1536
                    pp = pair % 4
                    load_w3("pool", t, d_, w_in_ab, base + pp * 128, 128, 0)
                    load_w3("pool", t, d_, w_in_ab, base + 512 + pp * 128, 128, 128)
                    load_w3("pool", t, d_, w_in_ab, base + 1024 + pp * 128, 128, 256)

                ldw(0)
                for pair in range(8):
                    moba = pair < 4
                    if pair + 1 < 8:
                        ldw(pair + 1)
                    wt, dw = wq[pair % 2]
                    gq = 0 if moba else 2
                    gk = 1 if moba else 3
                    for sc in range(4):
                        pt, dp = proj_fm(wt, dw, sc, 128, 0)
                        headnorm(pt, dp, 128, gq, [(qh[0][0][0:64, cs_(sc)], qh[0][1][sc], 0), (qh[1][0][0:64, cs_(sc)], qh[1][1][sc], 64)])
                        pt, dp = proj_fm(wt, dw, sc, 128, 128)
                        headnorm(pt, dp, 128, gk, [(kh[0][0][0:64, cs_(sc)], kh[0][1][sc], 0), (kh[1][0][0:64, cs_(sc)], kh[1][1][sc], 64)])
                        if moba:
                            for hh in range(2):
                                kin = kh[hh][0][0:64, cs_(sc)]
                                kin3 = bass.AP(tensor=kin.tensor, offset=kin.offset, ap=[list(kin.ap[0]), [256, 2], [1, 256]])
                                fw.op("dve", lambda e: e.tensor_reduce(out=km[hh][0][0:64, 2 * sc:2 * sc + 2], in_=kin3, axis=AX.X, op=ALU.add),
                                      r=[kh[hh][1][sc]], w=[km[hh][1]])
                        pv, dpv = psA.next()
                        for j in range(4):
                            tt = sc * 4 + j
                            for kc in range(8):
                                fw.op("pe", lambda e: e.matmul(pv[:, j * 128:(j + 1) * 128], lhsT=xnT[:, kc, tt * 128:(tt + 1) * 128],
                                                               rhs=wt[:, kc, 256:384], start=(kc == 0), stop=(kc == 7)),
                                      r=[dw, dxn[sc]], w=[dpv])
                        for hh in range(2):
                            src = pv[:, hh * 64:hh * 64 + 1]
                            src3 = bass.AP(tensor=src.tensor, offset=src.offset, ap=[list(src.ap[0]), [128, 4], [1, 64]])
                            fw.op("act", lambda e: e.activation(out=vh[hh][0][:, sc * 4:sc * 4 + 4, 0:64], in_=src3, func=AF.Copy),
                                  r=[dpv], w=[vh[hh][1][sc]])
                    if pair == 0:
                        dump("q0", qh[0][0][0:64, :], [64, S], qh[0][1])
                        dump("k0", kh[0][0][0:64, :], [64, S], kh[0][1])
                        dump("v0", vh[0][0][:], [128, 16, 128], vh[0][1])
                    if moba:
                        for hh in range(2):
                            qt_, dq_ = qh[hh]
                            fw.op("dve", lambda e: e.tensor_copy(out=kmb[hh][0][:], in_=km[hh][0][:]), r=[km[hh][1]], w=[kmb[hh][1]])
                            fw.op("dve", lambda e: e.memset(qt_[64:72, 0:1024], 0.0), w=[dq_[0], dq_[1]])
                            for qtile in range(8, 16):
                                own = qtile // 2
                                sc = qtile // 4
                                pg, dpg = psB.next()
                                fw.op("pe", lambda e: e.matmul(pg[:, 0:8], lhsT=qt_[0:64, qtile * 128:(qtile + 1) * 128], rhs=kmb[hh][0][0:64, 0:8],
                                                               start=True, stop=True), r=[dq_[sc], kmb[hh][1]], w=[dpg])
                                fw.op("dve", lambda e: e.tensor_tensor(out=gm[:], in0=pg[:, 0:8], in1=gneg[:, own - 4, :], op=ALU.add),
                                      r=[dpg, dl0], w=[dgm])
                                fw.op("dve", lambda e: e.max(out=m8[:], in_=gm[:]), r=[dgm], w=[dm8])
                                nt, dnt = nmp[own - 4]
                                fw.op("dve", lambda e: e.tensor_scalar(out=nt[:, 64:64 + own], in0=gm[:, 0:own], scalar1=m8[:, 2:3], scalar2=NEG,
                                                                       op0=ALU.is_lt, op1=ALU.mult), r=[dgm, dm8], w=[dnt])
                                p2, dp2 = psB.next()
                                fw.op("pe", lambda e: e.matmul(p2[0:72, 0:128], lhsT=nt[:, 0:72], rhs=ident_bf[:], start=True, stop=True),
                                      r=[dnt, dcon], w=[dp2])
                                fw.op("act", lambda e: e.activation(out=qt_[64:72, qtile * 128:(qtile + 1) * 128], in_=p2[64:72, 0:128], func=AF.Copy),
                                      r=[dp2], w=[dq_[sc]])
                    for hh in range(2):
                        H = pair * 2 + hh
                        E, dE = Eh[hh]
                        M, dM = (None, None) if moba else Mt
                        build_E(H, E, dE, M, dM, stage, dstage)

                        def cb(qc, po, dpo, hh=hh, pair=pair):
                            rd, drd = f32b.next()
                            fw.op("dve", lambda e: e.reciprocal(out=rd[64:128, :], in_=po[64:128, :]), r=[dpo], w=[drd])
                            fw.op("dve", lambda e: e.tensor_tensor(out=oT[hh * 64:hh * 64 + 64, pair, cs_(qc)], in0=po[0:64, :], in1=rd[64:128, :], op=ALU.mult),
                                  r=[dpo, drd], w=[doT[pair][qc]])
                        attend(qh[hh][0], qh[hh][1], 72 if moba else 64, kh[hh][0], kh[hh][1], vh[hh][0], vh[hh][1], E, dE, causal_tiles, cb)
                pass
                fw.barrier()

        pbuf = None

        def layer1_mixer():
            with contextlib.ExitStack() as st:
                nonlocal pbuf
                pbuf = Rot([(sb("pbuf%d" % i, [128, 512], BF16, st), Dep()) for i in range(4)])
                qa = [(sb("qa%d" % i, [96, S], BF16, st), [Dep() for _ in range(4)]) for i in range(4)]
                ks = (sb("ksT", [96, S], BF16, st), [Dep() for _ in range(4)])
                kw = (sb("kwT", [64, S], BF16, st), [Dep() for _ in range(4)])
                vs = (sb("vsA", [128, 16, 128], BF16, st), [Dep() for _ in range(4)])
                vw = (sb("vwA", [128, 16, 128], BF16, st), [Dep() for _ in range(4)])
                ocmp = [(sb("ocmp%d" % i, [64, S], BF16, st), [Dep() for _ in range(4)]) for i in range(4)]
                tc_ = qa[2]
                tv_ = qa[3]
                kcT = (sb("kcT", [64, 128], BF16, st), Dep())
                vcA = (sb("vcA", [128, 128], BF16, st), Dep())
                Es = (sb("Es", [128, S], BF16, st), Dep())
                Ec = Es
                Ew = (sb("Ew", [128, 640], BF16, st), Dep())
                Mw = (sb("Mw", [128, 640], BF16, st), Dep())
                stage = sb("stage", [128, S], F32, st)
                dstage = Dep()
                gsig = (sb("gsig", [48, S], BF16, st), [Dep() for _ in range(4)])
                gsel = sb("gsel", [48, 48, 64], BF16, st)
                ovl = sb("ovl", [128, 33], BF16, st)
                addm = sb("addm", [128, 16, 32], F32, st)
                forced = sb("forced", [128, 16, 32], F32, st)
                imp = (sb("imp", [128, 16, 32], F32, st), [Dep() for _ in range(16)])
                w1s = (sb("w1s", [64, 32, 128], BF16, st), Dep())
                w1 = [w1s, w1s]
                w2 = [(sb("w2_%d" % i, [128, 64], BF16, st), Dep()) for i in range(2)]
                posT = sb("posT", [64, 64], BF16, st)
                b1c = sb("b1c", [128, 2], F32, st)
                b2k = sb("b2k", [64, 1], F32, st)
                b2v = sb("b2v", [128, 64], F32, st)
                cb1 = sb("cb1", [128, 2], F32, st)
                dcb1 = Dep()
                wA = [(sb("wA%d" % i, [128, 8, 128], BF16, st), Dep()) for i in range(3)]
                wQ = [wA[0], wA[1]]
                wG = (sb("wG", [128, 8, 48], BF16, st), Dep())
                sel_t = {}
                for nm_, shp in [("vals", [128, 32]), ("lt", [128, 32]), ("v2", [128, 32]), ("m8a", [128, 8]), ("m8b", [128, 8]), ("rdi", [128, 4])]:
                    sel_t[nm_] = (sb("sel_" + nm_, shp, F32, st), Dep())
                nmp = (sb("nmp1", [128, 96], BF16, st), Dep())
                gl = Dep()
                fw.dma("pool", gsel[:], IN["c_gsel"][:, :, :], w=[gl])
                fw.dma("pool", ovl[0:127, :], IN["c_ovl"][:, :], w=[gl])
                fw.dma("sp", addm[:], IN["c_addmask"][:, :, :], w=[gl])
                fw.dma("sp", forced[:], IN["c_forced"][:, :, :], w=[gl])
                fw.dma("pool", posT[:], posT_d[:, :], w=[gl])
                fw.dma("sp", b1c[:], b1_d[:, :], w=[gl])
                fw.dma("sp", b2k[:], b2k_d[:, :], w=[gl])
                fw.dma("sp", b2v[:], _dap(b2v_d, 0, [[0, 128], [1, 64]]), w=[gl])
                w1src = [ck_w1.rearrange("(l d) j -> d l j", d=64), cv_w1.rearrange("(l d) j -> d l j", d=64)]
                fw.dma("pool", w2[0][0][:], ck_w2[:, :], w=[w2[0][1]])
                fw.dma("pool", w2[1][0][:], cv_w2[:, :], w=[w2[1][1]])
                fw.dma("pool", ks[0][64:96, :], IN["c_blk_nsa"][:, :], w=ks[1])
                fw.op("dve", lambda e: e.memset(nmp[0][:], 0.0), w=[nmp[1]])
                fw.op("dve", lambda e: e.memset(vs[0][:, :, 64:128], 1.0), w=vs[1])
                fw.op("dve", lambda e: e.memset(vw[0][:, :, 64:128], 1.0), w=vw[1])
                fw.op("dve", lambda e: e.memset(vcA[0][:, 64:128], 1.0), w=[vcA[1]])
                build_M(2, Mw[0], Mw[1], stage, dstage, width=640)
                for i in range(2):
                    fw.dma("pool", w1s[0][:], w1src[i], w=[w1s[1]])
                    pt, dp = psB.next()
                    for l in range(32):
                        fw.op("pe", lambda e: e.matmul(pt[:, 0:1], lhsT=w1[i][0][:, l, :], rhs=posT[:, i * 32 + l:i * 32 + l + 1],
                                                       start=(l == 0), stop=(l == 31)), r=[w1[i][1], gl], w=[dp])
                    fw.op("dve", lambda e: e.tensor_tensor(out=cb1[:, i:i + 1], in0=pt[:, 0:1], in1=b1c[:, i:i + 1], op=ALU.add),
                          r=[dp, gl], w=[dcb1])
                load_w3("pool", wG[0], wG[1], w_in_nsa, 2560, 48, 0)
                for sc in range(4):
                    pt, dp = proj_fm(wG[0], wG[1], sc, 48, 0)
                    fw.op("act", lambda e: e.activation(out=gsig[0][0:48, cs_(sc)], in_=pt[0:48, :], func=AF.Sigmoid), r=[dp], w=[gsig[1][sc]])

                def gate_bc(h, j, qc):
                    pt, dp = psB.next()
                    fw.op("pe", lambda e: e.matmul(pt[0:64, :], lhsT=gsel[0:48, h * 3 + j, :], rhs=gsig[0][0:48, cs_(qc)], start=True, stop=True),
                          r=[gl, gsig[1][qc]], w=[dp])
                    return pt, dp

                for g in range(4):
                    load_w3("pool", wA[0][0], wA[0][1], w_in_nsa, 1536 + g * 64, 64, 0)
                    load_w3("pool", wA[0][0], wA[0][1], w_in_nsa, 2048 + g * 64, 64, 64)
                    load_w3("pool", wA[1][0], wA[1][1], w_in_nsa, 1024 + g * 64, 64, 0)
                    load_w3("pool", wA[1][0], wA[1][1], w_in_nsa, 1280 + g * 64, 64, 64)
                    load_w3("pool", wA[2][0], wA[2][1], w_in_nsa, 1792 + g * 64, 64, 0)
                    load_w3("pool", wA[2][0], wA[2][1], w_in_nsa, 2304 + g * 64, 64, 64)
                    for sc in range(4):
                        pt, dp = proj_fm(wA[0][0], wA[0][1], sc, 128, 0)
                        headnorm(pt, dp, 128, 5, [(ks[0][0:64, cs_(sc)], ks[1][sc], 0), (kw[0][0:64, cs_(sc)], kw[1][sc], 64)])
                        pt, dp = proj_fm(wA[1][0], wA[1][1], sc, 128, 0)
                        fw.op("act", lambda e: e.activation(out=tc_[0][0:64, cs_(sc)], in_=pt[0:64, :], func=AF.Copy), r=[dp], w=[tc_[1][sc]])
                        fw.op("act", lambda e: e.activation(out=tv_[0][0:64, cs_(sc)], in_=pt[64:128, :], func=AF.Copy), r=[dp], w=[tv_[1][sc]])
                        pv, dpv = psA.next()
                        for j in range(4):
                            tt = sc * 4 + j
                            for kc in range(8):
                                fw.op("pe", lambda e: e.matmul(pv[:, j * 128:(j + 1) * 128], lhsT=xnT[:, kc, tt * 128:(tt + 1) * 128],
                                                               rhs=wA[2][0][:, kc, :], start=(kc == 0), stop=(kc == 7)),
                                      r=[wA[2][1], dxn[sc]], w=[dpv])
                        for hh, vdst in enumerate((vs, vw)):
                            src = pv[:, hh * 64:hh * 64 + 1]
                            src3 = bass.AP(tensor=src.tensor, offset=src.offset, ap=[list(src.ap[0]), [128, 4], [1, 64]])
                            fw.op("act", lambda e: e.activation(out=vdst[0][:, sc * 4:sc * 4 + 4, 0:64], in_=src3, func=AF.Copy),
                                  r=[dpv], w=[vdst[1][sc]])
                    for i, tsrc in enumerate((tc_, tv_)):
                        fw.dma("pool", w1s[0][:], w1src[i], w=[w1s[1]])
                        ph, dph = psA.next()
                        for l in range(32):
                            a = tsrc[0][0:64, l:l + 1]
                            rhs = bass.AP(tensor=a.tensor, offset=a.offset, ap=[list(a.ap[0]), [16, 127]])
                            fw.op("pe", lambda e: e.matmul(ph[:, 0:127], lhsT=w1[i][0][:, l, :], rhs=rhs, start=(l == 0), stop=(l == 31)),
                                  r=[w1[i][1]] + tsrc[1], w=[dph])
                        xg, dxg = f32b.next()
                        tg_, dtg = f32b.next()
                        fw.op("act", lambda e: e.activation(out=xg[:, 0:127], in_=ph[:, 0:127], func=AF.Identity, bias=cb1[:, i:i + 1], scale=1.0),
                              r=[dph, dcb1], w=[dxg])
                        fw.op("dve", lambda e: e.tensor_tensor(out=tg_[:, 0:127], in0=xg[:, 0:127], in1=xg[:, 0:127], op=ALU.mult), r=[dxg], w=[dtg])
                        fw.op("dve", lambda e: e.tensor_scalar(out=tg_[:, 0:127], in0=tg_[:, 0:127], scalar1=0.044715, scalar2=1.0, op0=ALU.mult, op1=ALU.add),
                              r=[dtg], w=[dtg])
                        fw.op("dve", lambda e: e.tensor_tensor(out=tg_[:, 0:127], in0=tg_[:, 0:127], in1=xg[:, 0:127], op=ALU.mult), r=[dxg, dtg], w=[dtg])
                        fw.op("act", lambda e: e.activation(out=tg_[:, 0:127], in_=tg_[:, 0:127], func=AF.Sigmoid, scale=1.5957691216), r=[dtg], w=[dtg])
                        gb, dgb = sqb.next()
                        fw.op("dve", lambda e: e.tensor_tensor(out=gb[:, 0:127], in0=tg_[:, 0:127], in1=xg[:, 0:127], op=ALU.mult), r=[dxg, dtg], w=[dgb])
                        if i == 0:
                            pk, dpk = psA.next()
                            fw.op("pe", lambda e: e.matmul(pk[0:64, 0:127], lhsT=w2[0][0][:, :], rhs=gb[:, 0:127], start=True, stop=True),
                                  r=[w2[0][1], dgb], w=[dpk])
                            kf, dkf = f32b.next()
                            fw.op("act", lambda e: e.activation(out=kf[0:64, 0:127], in_=pk[0:64, 0:127], func=AF.Identity, bias=b2k[:, 0:1], scale=1.0),
                                  r=[dpk, gl], w=[dkf])
                            sq, dsq = sqb.next()
                            fw.op("act", lambda e: e.activation(out=sq[0:64, 0:127], in_=kf[0:64, 0:127], func=AF.Square), r=[dkf], w=[dsq])
                            ps2, dp2 = psB.next()
                            fw.op("pe", lambda e: e.matmul(ps2[0:64, 0:127], lhsT=blk_ones[0:64, 0:64], rhs=sq[0:64, 0:127], start=True, stop=True),
                                  r=[dsq, dcon], w=[dp2])
                            rt, drt = f32b.next()
                            fw.op("act", lambda e: e.activation(out=rt[0:64, 0:127], in_=ps2[0:64, 0:127], func=AF.Sqrt, bias=EPS, scale=1.0 / 64), r=[dp2], w=[drt])
                            fw.op("dve", lambda e: e.reciprocal(out=rt[0:64, 0:127], in_=rt[0:64, 0:127]), r=[drt], w=[drt])
                            fw.op("dve", lambda e: e.scalar_tensor_tensor(out=kcT[0][0:64, 0:127], in0=kf[0:64, 0:127], scalar=hg[0:64, 6:7], in1=rt[0:64, 0:127],
                                                                          op0=ALU.mult, op1=ALU.mult), r=[dkf, drt, dcon], w=[kcT[1]])
                        else:
                            pk, dpk = psA.next()
                            fw.op("pe", lambda e: e.matmul(pk[0:127, 0:64], lhsT=gb[:, 0:127], rhs=w2[1][0][:, :], start=True, stop=True),
                                  r=[w2[1][1], dgb], w=[dpk])
                            fw.op("dve", lambda e: e.tensor_tensor(out=vcA[0][0:127, 0:64], in0=pk[0:127, 0:64], in1=b2v[0:127, :], op=ALU.add),
                                  r=[dpk, gl], w=[vcA[1]])
                    if g == 0:
                        dump("kcT", kcT[0][:], [64, 128], [kcT[1]])
                        dump("vcA", vcA[0][:], [128, 128], [vcA[1]])
                        dump("ksT", ks[0][0:64, :], [64, S], ks[1])
                    for pr in range(2):
                        t, d_ = wQ[pr]
                        load_w3("pool", t, d_, w_in_nsa, (g * 4 + pr * 2) * 64, 128, 0)
                    for pr in range(2):
                        t, d_ = wQ[pr]
                        for sc in range(4):
                            pt, dp = proj_fm(t, d_, sc, 128, 0)
                            a, b_ = qa[pr * 2], qa[pr * 2 + 1]
                            headnorm(pt, dp, 128, 4, [(a[0][0:64, cs_(sc)], a[1][sc], 0), (b_[0][0:64, cs_(sc)], b_[1][sc], 64)])
                    for qd in qa:
                        fw.op("dve", lambda e: e.memset(qd[0][64:96, 0:1024], 0.0), w=[qd[1][0], qd[1][1]])
                    for qt in range(8, 16):
                        fw.op("dve", lambda e: e.memset(imp[0][:, qt, :], 0.0), w=[imp[1][qt]])
                    for hl in range(4):
                        h = g * 4 + hl
                        pair, hh = h // 2, h % 2
                        qt_, dq_ = qa[hl]
                        build_E(h, Ec[0], Ec[1], None, None, stage, dstage, pstride=16, off=31, rows=127)
                        for qc in range(4):
                            pss, dps = psS.next()
                            fw.op("pe", lambda e: e.matmul(pss[0:127, :], lhsT=kcT[0][0:64, 0:127], rhs=qt_[0:64, cs_(qc)], start=True, stop=True),
                                  r=[kcT[1], dq_[qc]], w=[dps])
                            pb, dpb = pbuf.next()
                            fw.op("act", lambda e: e.activation(out=pb[0:127, :], in_=pss[0:127, :], func=AF.Exp, scale=0.125), r=[dps], w=[dpb])
                            fw.op("dve", lambda e: e.tensor_tensor(out=pb[0:127, :], in0=pb[0:127, :], in1=Ec[0][0:127, cs_(qc)], op=ALU.mult),
                                  r=[Ec[1], dpb], w=[dpb])
                            po, dpo = psO.next()
                            fw.op("pe", lambda e: e.matmul(po[:, :], lhsT=vcA[0][0:127, :], rhs=pb[0:127, :], start=True, stop=True),
                                  r=[vcA[1], dpb], w=[dpo])
                            if qc >= 2:
                                pi, dpi = psB.next()
                                for j in range(4):
                                    fw.op("pe", lambda e: e.matmul(pi[:, j * 64:j * 64 + 33], lhsT=pb[0:127, j * 128:(j + 1) * 128], rhs=ovl[0:127, :],
                                                                   start=True, stop=True), r=[dpb, gl], w=[dpi])
                                rdi, drdi = sel_t["rdi"]
                                src = pi[:, 32:33]
                                src3 = bass.AP(tensor=src.tensor, offset=src.offset, ap=[list(src.ap[0]), [64, 4]])
                                fw.op("dve", lambda e: e.reciprocal(out=rdi[:, 0:4], in_=src3), r=[dpi], w=[drdi])
                                for j in range(4):
                                    qtile = qc * 4 + j
                                    fw.op("dve", lambda e: e.scalar_tensor_tensor(out=imp[0][:, qtile, :], in0=pi[:, j * 64:j * 64 + 32], scalar=rdi[:, j:j + 1],
                                                                                  in1=imp[0][:, qtile, :], op0=ALU.mult, op1=ALU.add),
                                          r=[dpi, drdi, imp[1][qtile]], w=[imp[1][qtile]])
                            rd, drd = f32b.next()
                            fw.op("dve", lambda e: e.tensor_scalar(out=rd[64:128, :], in0=po[64:128, :], scalar1=1e-30, scalar2=None, op0=ALU.max), r=[dpo], w=[drd])
                            fw.op("dve", lambda e: e.reciprocal(out=rd[64:128, :], in_=rd[64:128, :]), r=[drd], w=[drd])
                            fw.op("dve", lambda e: e.tensor_tensor(out=rd[0:64, :], in0=po[0:64, :], in1=rd[64:128, :], op=ALU.mult), r=[dpo, drd], w=[drd])
                            pgb, dpgb = gate_bc(h, 0, qc)
                            fw.op("dve", lambda e: e.tensor_tensor(out=ocmp[hl][0][0:64, cs_(qc)], in0=rd[0:64, :], in1=pgb[0:64, :], op=ALU.mult),
                                  r=[drd, dpgb], w=[ocmp[hl][1][qc]])
                    vals, dvals = sel_t["vals"]
                    lt, dlt = sel_t["lt"]
                    v2, dv2 = sel_t["v2"]
                    m8a, dm8a = sel_t["m8a"]
                    m8b, dm8b = sel_t["m8b"]
                    for qtile in range(8, 16):
                        sc = qtile // 4
                        fw.op("dve", lambda e: e.tensor_tensor(out=vals[:], in0=imp[0][:, qtile, :], in1=addm[:, qtile, :], op=ALU.add),
                              r=[imp[1][qtile], gl], w=[dvals])
                        fw.op("dve", lambda e: e.max(out=m8a[:], in_=vals[:]), r=[dvals], w=[dm8a])
                        fw.op("dve", lambda e: e.tensor_scalar(out=lt[:], in0=vals[:], scalar1=m8a[:, 7:8], scalar2=None, op0=ALU.is_lt), r=[dvals, dm8a], w=[dlt])
                        fw.op("dve", lambda e: e.tensor_tensor(out=v2[:], in0=vals[:], in1=lt[:], op=ALU.mult), r=[dvals, dlt], w=[dv2])
                        fw.op("dve", lambda e: e.tensor_scalar(out=lt[:], in0=lt[:], scalar1=-1.0, scalar2=BIG, op0=ALU.add, op1=ALU.mult), r=[dlt, dv2], w=[dlt])
                        fw.op("dve", lambda e: e.tensor_tensor(out=v2[:], in0=v2[:], in1=lt[:], op=ALU.add), r=[dlt, dv2], w=[dv2])
                        fw.op("dve", lambda e: e.max(out=m8b[:], in_=v2[:]), r=[dv2], w=[dm8b])
                        fw.op("dve", lambda e: e.tensor_scalar(out=lt[:], in0=vals[:], scalar1=m8b[:, 4:5], scalar2=None, op0=ALU.is_ge), r=[dvals, dm8b, dlt], w=[dlt])
                        fw.op("dve", lambda e: e.tensor_tensor(out=lt[:], in0=lt[:], in1=forced[:, qtile, :], op=ALU.max), r=[dlt, gl], w=[dlt])
                        fw.op("dve", lambda e: e.tensor_scalar(out=nmp[0][:, 64:96], in0=lt[:], scalar1=-1.0, scalar2=-NEG, op0=ALU.add, op1=ALU.mult),
                              r=[dlt], w=[nmp[1]])
                        p2, dp2 = psB.next()
                        fw.op("pe", lambda e: e.matmul(p2[0:96, 0:128], lhsT=nmp[0][:, 0:96], rhs=ident_bf[:], start=True, stop=True),
                              r=[nmp[1], dcon], w=[dp2])
                        for qd in qa:
                            fw.op("act", lambda e: e.activation(out=qd[0][64:96, qtile * 128:(qtile + 1) * 128], in_=p2[64:96, 0:128], func=AF.Copy),
                                  r=[dp2], w=[qd[1][sc]])
                    if g == 0:
                        dump("imp", imp[0][:], [128, 16, 32], imp[1])
                        dump("qa0", qa[0][0][:], [96, S], qa[0][1])
                    for hl in range(4):
                        h = g * 4 + hl
                        pair, hh = h // 2, h % 2
                        build_E(h, Es[0], Es[1], None, None, stage, dstage)
                        fw.op("dve", lambda e: e.tensor_tensor(out=Ew[0][:, :], in0=Es[0][:, 0:640], in1=Mw[0][:, :], op=ALU.mult),
                              r=[Es[1], Mw[1]], w=[Ew[1]])
                        acc = {}

                        def cb_slc(qc, po, dpo, h=h):
                            rd, drd = f32b.next()
                            fw.op("dve", lambda e: e.reciprocal(out=rd[64:128, :], in_=po[64:128, :]), r=[dpo], w=[drd])
                            fw.op("dve", lambda e: e.tensor_tensor(out=rd[0:64, :], in0=po[0:64, :], in1=rd[64:128, :], op=ALU.mult), r=[dpo, drd], w=[drd])
                            pgb, dpgb = gate_bc(h, 1, qc)
                            fw.op("dve", lambda e: e.tensor_tensor(out=rd[0:64, :], in0=rd[0:64, :], in1=pgb[0:64, :], op=ALU.mult), r=[drd, dpgb], w=[drd])
                            acc[qc] = (rd, drd)

                        def cb_win(qc, po, dpo, h=h, pair=pair, hh=hh, hl=hl):
                            rd, drd = f32b.next()
                            a_, da_ = acc[qc]
                            fw.op("dve", lambda e: e.reciprocal(out=rd[64:128, :], in_=po[64:128, :]), r=[dpo], w=[drd])
                            fw.op("dve", lambda e: e.tensor_tensor(out=rd[0:64, :], in0=po[0:64, :], in1=rd[64:128, :], op=ALU.mult), r=[dpo, drd], w=[drd])
                            pgb, dpgb = gate_bc(h, 2, qc)
                            fw.op("dve", lambda e: e.tensor_tensor(out=rd[0:64, :], in0=rd[0:64, :], in1=pgb[0:64, :], op=ALU.mult), r=[drd, dpgb], w=[drd])
                            fw.op("dve", lambda e: e.tensor_tensor(out=rd[0:64, :], in0=rd[0:64, :], in1=a_[0:64, :], op=ALU.add), r=[drd, da_], w=[drd])
                            fw.op("dve", lambda e: e.tensor_tensor(out=rd[0:64, :], in0=rd[0:64, :], in1=ocmp[hl][0][0:64, cs_(qc)], op=ALU.add),
                                  r=[drd, ocmp[hl][1][qc]], w=[drd])
                            fw.op("act", lambda e: e.activation(out=oT[hh * 64:hh * 64 + 64, pair, cs_(qc)], in_=rd[0:64, :], func=AF.Copy),
                                  r=[drd], w=[doT[pair][qc]])

                        for qc in range(4):
                            attend(qa[hl][0], qa[hl][1], 96, ks[0], ks[1], vs[0], vs[1], Es[0], Es[1], causal_tiles, cb_slc, chunks=[qc])
                            attend(qa[hl][0], qa[hl][1], 64, kw[0], kw[1], vw[0], vw[1], Ew[0], Ew[1], win_tiles, cb_win, chunks=[qc])
                pass
                fw.barrier()

        alldh = [d_ for row in dh for d_ in row]
        alldo = [d_ for row in doT for d_ in row]
        with contextlib.ExitStack() as st:
            hT = sb("hT", [128, 8, S], F32, st)
            load_h(xT, [])
            rmsnorm(0)
            dump("xn0", xnT[:], [128, 8, S], dxn)
            fw.barrier()
        if upto >= 1:
            layer0_mixer()
            dump("oT0", oT[:], [128, 8, S], alldo)
        with contextlib.ExitStack() as st:
            hT = sb("hT", [128, 8, S], F32, st)
            load_h(xT, [])
            if upto >= 1:
                out_proj(w_out_ab, st)
                dump("hmix0", hT[:], [128, 8, S], alldh)
                fw.barrier()
            if upto >= 2:
                ffn(0)
                dump("hffn0", hT[:], [128, 8, S], alldh)
            if upto >= 3:
                ple(0)
                dump("h0", hT[:], [128, 8, S], alldh)
            if upto >= 4:
                rmsnorm(3)
                store_h(hS, [dhS])
            fw.barrier()
            if upto < 4:
                store_h(outT, [])
        if upto >= 4:
            layer1_mixer()
            dump("oT1", oT[:], [128, 8, S], alldo)
            with contextlib.ExitStack() as st:
                hT = sb("hT", [128, 8, S], F32, st)
                load_h(hS, [dhS])
                out_proj(w_out_nsa, st)
                dump("hmix1", hT[:], [128, 8, S], alldh)
                fw.barrier()
                if upto >= 5:
                    ffn(1)
                if upto >= 6:
                    ple(1)
                store_h(outT, [])
                fw.barrier()
        fw.finish("sp")
        build_nc.stats = (fw.n_ins, fw.n_wait)
    return nc, consts, DBG


def host_inputs(inputs, b, consts):
    f = lambda a: np.ascontiguousarray(np.asarray(a, dtype=np.float32))
    m = {}
    m["xT"] = f(inputs["x"][b].T)
    m["pT"] = f(np.transpose(inputs["p"][:, b], (0, 2, 1)))
    rb = np.asarray(inputs["rel_bias"], np.float32)
    dd = np.maximum(2047 - np.arange(WHL), 0)
    whb = rb[_bucket(dd), :].T.copy()
    whb[:, 2048:] = NEG
    m["whb"] = f(whb)
    gl = []
    for layer in range(2):
        for nm in ("norm_mix", "norm_ffn", "norm_ple"):
            gl.append(np.asarray(inputs[nm][layer], np.float32).reshape(8, 128).T)
    m["gains"] = f(np.concatenate(gl, axis=1))
    t2 = lambda a: np.concatenate([np.asarray(a, np.float32)] * 2)
    hg = np.zeros((128, 8), np.float32)
    hg[:, 0] = t2(inputs["qn_moba"][0])
    hg[:, 1] = t2(inputs["kn_moba"][0])
    hg[:, 2] = t2(inputs["qn_dil"][0])
    hg[:, 3] = t2(inputs["kn_dil"][0])
    hg[:, 4] = t2(inputs["qn_nsa"][0])
    hg[:, 5] = np.concatenate([np.asarray(inputs["kn_slc"][0], np.float32), np.asarray(inputs["kn_win"][0], np.float32)])
    hg[:, 6] = t2(inputs["kn_cmp"][0])
    m["hg"] = hg
    m["posT"] = f(np.concatenate([np.asarray(inputs["cmp_k_pos"][0]).T, np.asarray(inputs["cmp_v_pos"][0]).T], axis=1))
    m["b1c"] = f(np.stack([inputs["cmp_k_b1"][0], inputs["cmp_v_b1"][0]], axis=1))
    m["b2k"] = f(np.asarray(inputs["cmp_k_b2"][0]).reshape(64, 1))
    m["b2v"] = f(np.asarray(inputs["cmp_v_b2"][0]).reshape(1, 64))
    m["w_in_ab"] = f(inputs["w_in_ab"][0])
    m["w_out_ab"] = f(inputs["w_out_ab"][0])
    m["w_in_nsa"] = f(inputs["w_in_nsa"][0])
    m["w_out_nsa"] = f(inputs["w_out_nsa"][0])
    for nm in ("w_ffn_gate", "w_ffn_up", "w_ffn_down", "w_ple_proj", "w_ple_gate"):
        m[nm] = f(inputs[nm])
    m["cmp_k_w1"] = f(inputs["cmp_k_w1"][0])
    m["cmp_k_w2"] = f(inputs["cmp_k_w2"][0])
    m["cmp_v_w1"] = f(inputs["cmp_v_w1"][0])
    m["cmp_v_w2"] = f(inputs["cmp_v_w2"][0])
    for k, v in consts.items():
        m[k] = v
    return m


def kernel(**inputs):
    nc, consts, _ = build_nc()
    in_maps = [host_inputs(inputs, b, consts) for b in range(8)]
    res = run_bass_kernel_spmd(nc, in_maps, core_ids=list(range(8)))
    out = np.stack([np.asarray(r["outT"], np.float32).T for r in res.results], axis=0)
    return np.ascontiguousarray(out.astype(np.float32))
```

```python
import math
import contextlib
import numpy as np
import concourse.bass as bass
import concourse.mybir as mybir
from concourse.bass_utils import run_bass_kernel_spmd

F32 = mybir.dt.float32
BF16 = mybir.dt.bfloat16
AF = mybir.ActivationFunctionType
ALU = mybir.AluOpType
AX = mybir.AxisListType

S = 2048
D = 1024
FH = 2816
NF = 22
WHL = 4352
EPS = 1e-6
NEG = -30000.0
BIG = 3.0e38


class Dep:
    __slots__ = ("w", "r")

    def __init__(self):
        self.w = None
        self.r = []


class FW:
    NDMA = 24

    def __init__(self, nc, es):
        self.nc = nc
        self.engs = {"pe": nc.tensor, "act": nc.scalar, "dve": nc.vector, "pool": nc.gpsimd, "sp": nc.sync}
        self.sems = {}
        self.cnt = {}
        for k in self.engs:
            self.sems[k] = es.enter_context(nc.semaphore("sem_" + k))
            self.cnt[k] = 0
        for i in range(self.NDMA):
            k = ("dma", i)
            self.sems[k] = es.enter_context(nc.semaphore("sem_dma%d" % i))
            self.cnt[k] = 0
        self.seen = {e: {} for e in self.engs}
        self.dma_rr = 0
        self.n_ins = 0
        self.n_wait = 0

    def _wait(self, eng, deps):
        seen = self.seen[eng]
        need = {}
        for d in deps:
            if d is None:
                continue
            k, v = d
            if k == "pe" and eng == "pe":
                continue
            if seen.get(k, 0) >= v:
                continue
            if need.get(k, 0) < v:
                need[k] = v
        for k, v in need.items():
            self.engs[eng].wait_ge(self.sems[k], v)
            seen[k] = v
            self.n_wait += 1

    @staticmethod
    def _collect(r, w):
        deps = []
        for t in r:
            deps.append(t.w)
        for t in w:
            deps.append(t.w)
            deps.extend(t.r)
        return deps

    def _mark(self, tok, r, w):
        for t in w:
            t.w = tok
            t.r = []
        for t in r:
            t.r.append(tok)
            if len(t.r) > 64:
                best = {}
                for k, v in t.r:
                    if best.get(k, 0) < v:
                        best[k] = v
                t.r = list(best.items())

    def op(self, eng, fn, r=(), w=()):
        self._wait(eng, self._collect(r, w))
        ins = fn(self.engs[eng])
        self.cnt[eng] += 1
        ins.then_inc(self.sems[eng], 1)
        self._mark((eng, self.cnt[eng]), r, w)
        self.n_ins += 1

    def dma(self, q, out, in_, r=(), w=()):
        i = self.dma_rr
        self.dma_rr = (self.dma_rr + 1) % self.NDMA
        k = ("dma", i)
        deps = self._collect(r, w)
        if self.cnt[k] > 0:
            deps.append((k, self.cnt[k]))
        self._wait(q, deps)
        ins = self.engs[q].dma_start(out=out, in_=in_)
        self.cnt[k] += 16
        ins.then_inc(self.sems[k], 16)
        self._mark((k, self.cnt[k]), r, w)
        self.n_ins += 1

    def barrier(self):
        allk = [(k, v) for k, v in self.cnt.items() if v > 0]
        for e in self.engs:
            self._wait(e, allk)

    def finish(self, eng="sp"):
        allk = [(k, v) for k, v in self.cnt.items() if v > 0]
        self._wait(eng, allk)


class Rot:
    def __init__(self, items):
        self.items = items
        self.i = 0

    def next(self):
        t = self.items[self.i]
        self.i = (self.i + 1) % len(self.items)
        return t


def _bucket(d):
    n = np.maximum(d, 0)
    nf = np.maximum(n, 1).astype(np.float32)
    large = 16 + (np.log(nf / np.float32(16)) / np.float32(math.log(128.0)) * np.float32(16)).astype(np.int32)
    return np.where(n < 16, n, np.minimum(large, 31))


def _static_consts():
    c = {}
    m = np.arange(WHL)
    d = 2047 - m
    wm = np.zeros((3, WHL), np.float32)
    wm[0] = (d >= 0)
    wm[1] = (d >= 0) * ((d <= 128).astype(np.float32) + ((d % 4 == 0) & (d <= 512)) + ((d % 16 == 0) & (d <= 2048)))
    wm[2] = (d >= 0) & (d < 512)
    c["c_wm"] = wm
    c["c_ident"] = np.eye(128, dtype=np.float32)
    k = np.arange(S)
    c["c_blk_moba"] = (k[None, :] // 256 == np.arange(8)[:, None]).astype(np.float32)
    c["c_blk_nsa"] = (k[None, :] // 64 == np.arange(32)[:, None]).astype(np.float32)
    g = np.zeros((128, 4, 8), np.float32)
    for i, own in enumerate(range(4, 8)):
        g[:, i, own:] = -BIG
    c["c_gneg"] = g
    add = np.full((128, 16, 32), -BIG, np.float32)
    forced = np.zeros((128, 16, 32), np.float32)
    for qt in range(16):
        for q in range(128):
            cur = (qt * 128 + q) // 64
            for n in (0, cur, cur - 1):
                if n >= 0:
                    forced[q, qt, n] = 1.0
            for n in range(1, cur - 1):
                add[q, qt, n] = 0.0
    c["c_addmask"] = add
    c["c_forced"] = forced
    cs = np.arange(127) * 16
    ss = np.arange(32) * 64
    ov = np.maximum(np.minimum(cs[:, None] + 32, ss[None, :] + 64) - np.maximum(cs[:, None], ss[None, :]), 0)
    ovl = np.ones((127, 33), np.float32)
    ovl[:, :32] = ov
    c["c_ovl"] = ovl
    gs = np.zeros((48, 48, 64), np.float32)
    for i in range(48):
        gs[i, i, :] = 1.0
    c["c_gsel"] = gs
    return c


_CONST_SHAPES = None


def _dap(t, offset, ap):
    return bass.AP(tensor=t.tensor, offset=offset, ap=[list(a) for a in ap])


def build_nc(upto=99, dbg=()):
    nc = bass.Bass("TRN2", target_bir_lowering=False)
    consts = _static_consts()
    IN = {}

    def din(name, shape):
        IN[name] = nc.dram_tensor(name, list(shape), F32, kind="ExternalInput").ap()
        return IN[name]

    xT = din("xT", [D, S])
    pT = din("pT", [2, 256, S])
    whb = din("whb", [16, WHL])
    gains_d = din("gains", [128, 48])
    hg_d = din("hg", [128, 8])
    posT_d = din("posT", [64, 64])
    b1_d = din("b1c", [128, 2])
    b2k_d = din("b2k", [64, 1])
    b2v_d = din("b2v", [1, 64])
    w_in_ab = din("w_in_ab", [D, 3072])
    w_out_ab = din("w_out_ab", [D, D])
    w_in_nsa = din("w_in_nsa", [D, 2608])
    w_out_nsa = din("w_out_nsa", [D, D])
    w_g = din("w_ffn_gate", [2, D, FH])
    w_u = din("w_ffn_up", [2, D, FH])
    w_d = din("w_ffn_down", [2, FH, D])
    w_pp = din("w_ple_proj", [2, 256, D])
    w_pg = din("w_ple_gate", [2, D, D])
    ck_w1 = din("cmp_k_w1", [2048, 128])
    ck_w2 = din("cmp_k_w2", [128, 64])
    cv_w1 = din("cmp_v_w1", [2048, 128])
    cv_w2 = din("cmp_v_w2", [128, 64])
    for k, v in consts.items():
        din(k, v.shape)
    outT = nc.dram_tensor("outT", [D, S], F32, kind="ExternalOutput").ap()
    DBG = {}

    with contextlib.ExitStack() as es:
        fw = FW(nc, es)

        uniq = [0]

        def sb(name, shape, dt=F32, stack=es):
            uniq[0] += 1
            return stack.enter_context(nc.sbuf_tensor("%s_%d" % (name, uniq[0]), list(shape), dt))

        def pst(name):
            return es.enter_context(nc.psum_tensor(name, [128, 512], F32))

        psS = Rot([(pst("psS%d" % i), Dep()) for i in range(2)])
        psO = Rot([(pst("psO%d" % i), Dep()) for i in range(2)])
        psA = Rot([(pst("psA%d" % i), Dep()) for i in range(2)])
        psB = Rot([(pst("psB%d" % i), Dep()) for i in range(2)])

        def dump(name, ap, shape, deps):
            if name not in dbg:
                return
            t = nc.dram_tensor("dbg_" + name, list(shape), ap.dtype if hasattr(ap, "dtype") else F32, kind="ExternalOutput").ap()
            DBG[name] = t
            fw.dma("sp", t, ap, r=deps)

        hS = nc.dram_tensor("hS", [D, S], F32, kind="Internal").ap()
        dhS = Dep()
        dh = [[Dep() for _ in range(4)] for _ in range(8)]
        xnT = sb("xnT", [128, 8, S], BF16)
        dxn = [Dep() for _ in range(4)]
        oT = sb("oT", [128, 8, S], BF16)
        doT = [[Dep() for _ in range(4)] for _ in range(8)]
        hT = None
        gains = sb("gains_sb", [128, 48])
        hg = sb("hg_sb", [128, 8])
        dcon = Dep()
        ones_bf = sb("ones_bf", [128, 128], BF16)
        blk_ones = sb("blk_ones", [128, 128], BF16)
        ident_bf = sb("ident_bf", [128, 128], BF16)
        sqb = Rot([(sb("sqb%d" % i, [128, 512], BF16), Dep()) for i in range(2)])
        f32b = Rot([(sb("f32b%d" % i, [128, 512]), Dep()) for i in range(4)])

        fw.dma("sp", gains[:], gains_d[:, :], w=[dcon])
        fw.dma("sp", hg[:], hg_d[:, :], w=[dcon])
        fw.dma("pool", ident_bf[:], IN["c_ident"][:, :], w=[dcon])
        fw.op("dve", lambda e: e.memset(ones_bf[:], 1.0), w=[dcon])
        fw.op("dve", lambda e: e.memset(blk_ones[:], 0.0), w=[dcon])
        fw.op("dve", lambda e: e.memset(blk_ones[0:64, 0:64], 1.0), w=[dcon])
        fw.op("dve", lambda e: e.memset(blk_ones[64:128, 64:128], 1.0), w=[dcon])
        def load_h(src, dsrc):
            v = src.rearrange("(c p) s -> p c s", p=128)
            for c in range(8):
                for sc in range(4):
                    fw.dma("sp", hT[:, c, sc * 512:(sc + 1) * 512], v[:, c, sc * 512:(sc + 1) * 512], r=dsrc, w=[dh[c][sc]])

        def store_h(dst, ddst):
            v = dst.rearrange("(c p) s -> p c s", p=128)
            for c in range(8):
                for sc in range(4):
                    fw.dma("sp", v[:, c, sc * 512:(sc + 1) * 512], hT[:, c, sc * 512:(sc + 1) * 512], r=[dh[c][sc]], w=ddst)

        def cs_(sc):
            return slice(sc * 512, (sc + 1) * 512)

        def rmsnorm(gidx):
            for sc in range(4):
                cs = cs_(sc)
                pt, dp = psB.next()
                for c in range(8):
                    sq, dsq = sqb.next()
                    fw.op("act", lambda e: e.activation(out=sq[:], in_=hT[:, c, cs], func=AF.Square), r=[dh[c][sc]], w=[dsq])
                    fw.op("pe", lambda e: e.matmul(pt[:], lhsT=ones_bf[:], rhs=sq[:], start=(c == 0), stop=(c == 7)),
                          r=[dsq, dcon], w=[dp])
                rt, drt = f32b.next()
                fw.op("act", lambda e: e.activation(out=rt[:], in_=pt[:], func=AF.Ln, bias=EPS, scale=1.0 / D), r=[dp], w=[drt])
                fw.op("act", lambda e: e.activation(out=rt[:], in_=rt[:], func=AF.Exp, scale=-0.5), r=[drt], w=[drt])
                for c in range(8):
                    fw.op("dve", lambda e: e.scalar_tensor_tensor(
                        out=xnT[:, c, cs], in0=hT[:, c, cs], scalar=gains[:, gidx * 8 + c:gidx * 8 + c + 1], in1=rt[:],
                        op0=ALU.mult, op1=ALU.mult), r=[dh[c][sc], drt, dcon], w=[dxn[sc]])

        def proj_fm(wt, dw, sc, M, c0=0):
            pt, dp = psA.next()
            for kc in range(8):
                fw.op("pe", lambda e: e.matmul(pt[0:M, :], lhsT=wt[:, kc, c0:c0 + M], rhs=xnT[:, kc, cs_(sc)],
                                               start=(kc == 0), stop=(kc == 7)), r=[dw, dxn[sc]], w=[dp])
            return pt, dp

        def headnorm(pt, dp, M, gcol, outs):
            sq, dsq = sqb.next()
            fw.op("act", lambda e: e.activation(out=sq[0:M, :], in_=pt[0:M, :], func=AF.Square), r=[dp], w=[dsq])
            ps2, dp2 = psB.next()
            fw.op("pe", lambda e: e.matmul(ps2[0:M, :], lhsT=blk_ones[0:M, 0:M], rhs=sq[0:M, :], start=True, stop=True),
                  r=[dsq, dcon], w=[dp2])
            rt, drt = f32b.next()
            fw.op("act", lambda e: e.activation(out=rt[0:M, :], in_=ps2[0:M, :], func=AF.Ln, bias=EPS, scale=1.0 / 64), r=[dp2], w=[drt])
            fw.op("act", lambda e: e.activation(out=rt[0:M, :], in_=rt[0:M, :], func=AF.Exp, scale=-0.5), r=[drt], w=[drt])
            for (dst, ddst, r0) in outs:
                fw.op("dve", lambda e: e.scalar_tensor_tensor(
                    out=dst, in0=pt[r0:r0 + 64, :], scalar=hg[r0:r0 + 64, gcol:gcol + 1], in1=rt[r0:r0 + 64, :],
                    op0=ALU.mult, op1=ALU.mult), r=[dp, drt, dcon], w=[ddst])

        def load_w3(q, wt, dw, src2d, c0, ncols, dst_c0=0):
            v = src2d.rearrange("(kc p) n -> p kc n", p=128)
            fw.dma(q, wt[:, :, dst_c0:dst_c0 + ncols], v[:, :, c0:c0 + ncols], w=[dw])

        def rev(t, n, rows=128):
            a = t[0:rows, n - 1:n]
            return bass.AP(tensor=a.tensor, offset=a.offset, ap=[list(a.ap[0]), [-1, n]])

        pbuf_items = []

        def attend(qt, dq, K, ktile, dk, vt, dv, E, dE, tiles_fn, out_cb, chunks=range(4)):
            for qc in chunks:
                c0 = qc * 512
                tiles = tiles_fn(qc)
                assert tiles[0][1] == 0 and tiles[0][2] == 512
                po, dpo = psO.next()
                for idx, (kt, lo, hi) in enumerate(tiles):
                    n = hi - lo
                    pss, dps = psS.next()
                    fw.op("pe", lambda e: e.matmul(pss[:, 0:n], lhsT=ktile[0:K, kt * 128:(kt + 1) * 128],
                                                   rhs=qt[0:K, c0 + lo:c0 + hi], start=True, stop=True),
                          r=[dk[kt // 4], dq[qc]], w=[dps])
                    pb, dpb = pbuf.next()
                    fw.op("act", lambda e: e.activation(out=pb[:, 0:n], in_=pss[:, 0:n], func=AF.Exp, scale=0.125), r=[dps], w=[dpb])
                    u0 = c0 + lo - kt * 128
                    fw.op("dve", lambda e: e.tensor_tensor(out=pb[:, 0:n], in0=pb[:, 0:n], in1=E[:, u0:u0 + n], op=ALU.mult),
                          r=[dE, dpb], w=[dpb])
                    fw.op("pe", lambda e: e.matmul(po[:, lo:hi], lhsT=vt[:, kt, :], rhs=pb[:, 0:n],
                                                   start=(idx == 0), stop=(idx == len(tiles) - 1)),
                          r=[dv[kt // 4], dpb], w=[dpo])
                out_cb(qc, po, dpo)

        def recip_den(rd, drd, po, dpo):
            fw.op("act", lambda e: e.activation(out=rd[64:128, :], in_=po[64:128, :], func=AF.Ln, bias=1e-30, scale=1.0), r=[dpo], w=[drd])
            fw.op("act", lambda e: e.activation(out=rd[64:128, :], in_=rd[64:128, :], func=AF.Exp, scale=-1.0), r=[drd], w=[drd])

        def causal_tiles(qc):
            res = []
            for kt in range(4 * qc + 4):
                lo = max(0, kt * 128 - qc * 512)
                res.append((kt, lo, 512))
            return res

        def win_tiles(qc):
            res = []
            order = [4 * qc] + [k for k in range(max(0, 4 * qc - 4), 4 * qc + 4) if k != 4 * qc]
            for kt in order:
                off = kt * 128 - qc * 512
                lo = max(0, off)
                hi = min(512, ((off + 638) // 128 + 1) * 128)
                res.append((kt, lo, hi))
            return res

        def build_E(H, E, dE, M, dM, stage, dstage, width=2048, pstride=1, off=0, rows=128):
            fw.dma("sp", stage[0:rows, 0:width], _dap(whb, H * WHL + off + (2048 - width), [[pstride, rows], [1, width]]), w=[dstage])
            fw.op("act", lambda e: e.activation(out=E[0:rows, 0:width], in_=rev(stage, width, rows), func=AF.Exp), r=[dstage], w=[dE])
            if M is not None:
                fw.op("dve", lambda e: e.tensor_tensor(out=E[0:rows, 0:width], in0=E[0:rows, 0:width], in1=M[0:rows, 0:width], op=ALU.mult),
                      r=[dM, dE], w=[dE])

        def build_M(kind, M, dM, stage, dstage, width=2048, pstride=1, off=0, rows=128):
            fw.dma("sp", stage[0:rows, 0:width], _dap(IN["c_wm"], kind * WHL + off + (2048 - width), [[pstride, rows], [1, width]]), w=[dstage])
            fw.op("act", lambda e: e.activation(out=M[0:rows, 0:width], in_=rev(stage, width, rows), func=AF.Copy), r=[dstage], w=[dM])

        def out_proj(w_out, st):
            wo = [(sb("wo%d" % i, [128, 8, 128], BF16, st), Dep()) for i in range(2)]
            wv = w_out.rearrange("(kc p) n -> p kc n", p=128)

            def ld(fc):
                t, d_ = wo[fc % 2]
                fw.dma("pool", t[:], wv[:, :, fc * 128:(fc + 1) * 128], w=[d_])
            ld(0)
            for fc in range(8):
                if fc + 1 < 8:
                    ld(fc + 1)
                t, d_ = wo[fc % 2]
                for sc in range(4):
                    pt, dp = psA.next()
                    for pr in range(8):
                        fw.op("pe", lambda e: e.matmul(pt[:], lhsT=t[:, pr, :], rhs=oT[:, pr, cs_(sc)], start=(pr == 0), stop=(pr == 7)),
                              r=[d_, doT[pr][sc]], w=[dp])
                    fw.op("dve", lambda e: e.tensor_tensor(out=hT[:, fc, cs_(sc)], in0=pt[:], in1=hT[:, fc, cs_(sc)], op=ALU.add),
                          r=[dp, dh[fc][sc]], w=[dh[fc][sc]])

        def ffn(layer):
            rmsnorm(layer * 3 + 1)
            with contextlib.ExitStack() as st:
                act2 = sb("ffn_act", [128, NF - 16, 1024], BF16, st)
                dact = [[Dep() for _ in range(2)] for _ in range(NF)]

                def act_ap(f, q):
                    if f < 16:
                        return oT[:, f // 2, (f % 2) * 1024 + q * 512:(f % 2) * 1024 + (q + 1) * 512]
                    return act2[:, f - 16, q * 512:(q + 1) * 512]
                wg = [(sb("wg%d" % i, [128, 8, 128], BF16, st), Dep()) for i in range(2)]
                wu = [(sb("wu%d" % i, [128, 8, 128], BF16, st), Dep()) for i in range(2)]
                wd = [(sb("wd%d" % i, [128, NF, 128], BF16, st), Dep()) for i in range(2)]
                sg = Rot([(sb("sg%d" % i, [128, 512], F32, st), Dep()) for i in range(2)])
                wgv = w_g[layer].rearrange("(kc p) n -> p kc n", p=128)
                wuv = w_u[layer].rearrange("(kc p) n -> p kc n", p=128)
                wdv = w_d[layer].rearrange("(f p) n -> p f n", p=128)

                def ld1(f):
                    fw.dma("pool", wg[f % 2][0][:], wgv[:, :, f * 128:(f + 1) * 128], w=[wg[f % 2][1]])
                    fw.dma("pool", wu[f % 2][0][:], wuv[:, :, f * 128:(f + 1) * 128], w=[wu[f % 2][1]])

                def ld2(dc):
                    fw.dma("pool", wd[dc % 2][0][:], wdv[:, :, dc * 128:(dc + 1) * 128], w=[wd[dc % 2][1]])

                for half in range(2):
                    ld1(0)
                    for f in range(NF):
                        if f + 1 < NF:
                            ld1(f + 1)
                        else:
                            ld2(0)
                        tg, dg_ = wg[f % 2]
                        tu, du_ = wu[f % 2]
                        for q in range(2):
                            sc = half * 2 + q
                            pg, dpg = psA.next()
                            pu, dpu = psB.next()
                            for kc in range(8):
                                fw.op("pe", lambda e: e.matmul(pg[:], lhsT=tg[:, kc, :], rhs=xnT[:, kc, cs_(sc)], start=(kc == 0), stop=(kc == 7)),
                                      r=[dg_, dxn[sc]], w=[dpg])
                            for kc in range(8):
                                fw.op("pe", lambda e: e.matmul(pu[:], lhsT=tu[:, kc, :], rhs=xnT[:, kc, cs_(sc)], start=(kc == 0), stop=(kc == 7)),
                                      r=[du_, dxn[sc]], w=[dpu])
                            s_, ds_ = sg.next()
                            fw.op("act", lambda e: e.activation(out=s_[:], in_=pg[:], func=AF.Silu), r=[dpg], w=[ds_])
                            fw.op("dve", lambda e: e.tensor_tensor(out=act_ap(f, q), in0=s_[:], in1=pu[:], op=ALU.mult),
                                  r=[ds_, dpu], w=[dact[f][q]])
                    for dc in range(8):
                        if dc + 1 < 8:
                            ld2(dc + 1)
                        td, dd_ = wd[dc % 2]
                        for q in range(2):
                            sc = half * 2 + q
                            pt, dp = psS.next()
                            for f in range(NF):
                                fw.op("pe", lambda e: e.matmul(pt[:], lhsT=td[:, f, :], rhs=act_ap(f, q),
                                                               start=(f == 0), stop=(f == NF - 1)), r=[dd_, dact[f][q]], w=[dp])
                            fw.op("dve", lambda e: e.tensor_tensor(out=hT[:, dc, cs_(sc)], in0=pt[:], in1=hT[:, dc, cs_(sc)], op=ALU.add),
                                  r=[dp, dh[dc][sc]], w=[dh[dc][sc]])
                fw.barrier()

        def ple(layer):
            rmsnorm(layer * 3 + 2)
            with contextlib.ExitStack() as st:
                pTs = sb("pTs", [128, 2, S], BF16, st)
                dpT = Dep()
                wpg = [(sb("wpg%d" % i, [128, 8, 128], BF16, st), Dep()) for i in range(2)]
                wpp = [(sb("wpp%d" % i, [128, 2, 128], BF16, st), Dep()) for i in range(2)]
                sg = Rot([(sb("psg%d" % i, [128, 512], F32, st), Dep()) for i in range(2)])
                fw.dma("pool", pTs[:], pT[layer].rearrange("(kc p) s -> p kc s", p=128), w=[dpT])
                wgv = w_pg[layer].rearrange("(kc p) n -> p kc n", p=128)
                wpv = w_pp[layer].rearrange("(kc p) n -> p kc n", p=128)

                def ld(fc):
                    fw.dma("pool", wpg[fc % 2][0][:], wgv[:, :, fc * 128:(fc + 1) * 128], w=[wpg[fc % 2][1]])
                    fw.dma("pool", wpp[fc % 2][0][:], wpv[:, :, fc * 128:(fc + 1) * 128], w=[wpp[fc % 2][1]])
                ld(0)
                for fc in range(8):
                    if fc + 1 < 8:
                        ld(fc + 1)
                    tg, dg_ = wpg[fc % 2]
                    tp, dp_ = wpp[fc % 2]
                    for sc in range(4):
                        pg, dpg = psA.next()
                        pp_, dpp = psB.next()
                        for kc in range(8):
                            fw.op("pe", lambda e: e.matmul(pg[:], lhsT=tg[:, kc, :], rhs=xnT[:, kc, cs_(sc)], start=(kc == 0), stop=(kc == 7)),
                                  r=[dg_, dxn[sc]], w=[dpg])
                        for kc in range(2):
                            fw.op("pe", lambda e: e.matmul(pp_[:], lhsT=tp[:, kc, :], rhs=pTs[:, kc, cs_(sc)], start=(kc == 0), stop=(kc == 1)),
                                  r=[dp_, dpT], w=[dpp])
                        s_, ds_ = sg.next()
                        fw.op("act", lambda e: e.activation(out=s_[:], in_=pg[:], func=AF.Sigmoid), r=[dpg], w=[ds_])
                        fw.op("dve", lambda e: e.tensor_tensor(out=s_[:], in0=s_[:], in1=pp_[:], op=ALU.mult), r=[ds_, dpp], w=[ds_])
                        fw.op("dve", lambda e: e.tensor_tensor(out=hT[:, fc, cs_(sc)], in0=s_[:], in1=hT[:, fc, cs_(sc)], op=ALU.add),
                              r=[ds_, dh[fc][sc]], w=[dh[fc][sc]])
                fw.barrier()

        def layer0_mixer():
            with contextlib.ExitStack() as st:
                qh = [(sb("qh%d" % i, [72, S], BF16, st), [Dep() for _ in range(4)]) for i in range(2)]
                kh = [(sb("kh%d" % i, [72, S], BF16, st), [Dep() for _ in range(4)]) for i in range(2)]
                vh = [(sb("vh%d" % i, [128, 16, 128], BF16, st), [Dep() for _ in range(4)]) for i in range(2)]
                Eh = [(sb("Eh%d" % i, [128, S], BF16, st), Dep()) for i in range(2)]
                Mt = (sb("Mt", [128, S], BF16, st), Dep())
                stage = sb("stage", [128, S], F32, st)
                dstage = Dep()
                wq = [(sb("wq%d" % i, [128, 8, 384], BF16, st), Dep()) for i in range(2)]
                global_pbuf = [(sb("pbuf%d" % i, [128, 512], BF16, st), Dep()) for i in range(4)]
                nonlocal pbuf
                pbuf = Rot(global_pbuf)
                km = [(sb("km%d" % i, [64, 8], F32, st), Dep()) for i in range(2)]
                kmb = [(sb("kmb%d" % i, [64, 8], BF16, st), Dep()) for i in range(2)]
                gneg = sb("gneg", [128, 4, 8], F32, st)
                gm = sb("gm", [128, 8], F32, st)
                dgm = Dep()
                m8 = sb("m8", [128, 8], F32, st)
                dm8 = Dep()
                nmp = [(sb("nmp%d" % i, [128, 72], BF16, st), Dep()) for i in range(4)]
                dl0 = Dep()
                fw.dma("sp", gneg[:], IN["c_gneg"][:, :, :], w=[dl0])
                for i in range(4):
                    fw.op("dve", lambda e: e.memset(nmp[i][0][:], 0.0), w=[nmp[i][1]])
                for i in range(2):
                    fw.op("dve", lambda e: e.memset(vh[i][0][:, :, 64:128], 1.0), w=vh[i][1])
                    fw.dma("pool", kh[i][0][64:72, :], IN["c_blk_moba"][:, :], w=kh[i][1])
                    fw.op("dve", lambda e: e.memset(qh[i][0][64:72, :], 0.0), w=qh[i][1])
                build_M(1, Mt[0], Mt[1], stage, dstage)

                def ldw(pair):
                    t, d_ = wq[pair % 2]
                    base = 0 if pair < 4 else 1536
                    pp = pair % 4
                    load_w3("pool", t, d_, w_in_ab, base + pp * 128, 128, 0)
                    load_w3("pool", t, d_, w_in_ab, base + 512 + pp * 128, 128, 128)
                    load_w3("pool", t, d_, w_in_ab, base + 1024 + pp * 128, 128, 256)

                ldw(0)
                for pair in range(8):
                    moba = pair < 4
                    if pair + 1 < 8:
                        ldw(pair + 1)
                    wt, dw = wq[pair % 2]
                    gq = 0 if moba else 2
                    gk = 1 if moba else 3
                    for sc in range(4):
                        pt, dp = proj_fm(wt, dw, sc, 128, 0)
                        headnorm(pt, dp, 128, gq, [(qh[0][0][0:64, cs_(sc)], qh[0][1][sc], 0), (qh[1][0][0:64, cs_(sc)], qh[1][1][sc], 64)])
                        pt, dp = proj_fm(wt, dw, sc, 128, 128)
                        headnorm(pt, dp, 128, gk, [(kh[0][0][0:64, cs_(sc)], kh[0][1][sc], 0), (kh[1][0][0:64, cs_(sc)], kh[1][1][sc], 64)])
                        if moba:
                            for hh in range(2):
                                kin = kh[hh][0][0:64, cs_(sc)]
                                kin3 = bass.AP(tensor=kin.tensor, offset=kin.offset, ap=[list(kin.ap[0]), [256, 2], [1, 256]])
                                fw.op("dve", lambda e: e.tensor_reduce(out=km[hh][0][0:64, 2 * sc:2 * sc + 2], in_=kin3, axis=AX.X, op=ALU.add),
                                      r=[kh[hh][1][sc]], w=[km[hh][1]])
                        pv, dpv = psA.next()
                        for j in range(4):
                            tt = sc * 4 + j
                            for kc in range(8):
                                fw.op("pe", lambda e: e.matmul(pv[:, j * 128:(j + 1) * 128], lhsT=xnT[:, kc, tt * 128:(tt + 1) * 128],
                                                               rhs=wt[:, kc, 256:384], start=(kc == 0), stop=(kc == 7)),
                                      r=[dw, dxn[sc]], w=[dpv])
                        for hh in range(2):
                            src = pv[:, hh * 64:hh * 64 + 1]
                            src3 = bass.AP(tensor=src.tensor, offset=src.offset, ap=[list(src.ap[0]), [128, 4], [1, 64]])
                            fw.op("act", lambda e: e.activation(out=vh[hh][0][:, sc * 4:sc * 4 + 4, 0:64], in_=src3, func=AF.Copy),
                                  r=[dpv], w=[vh[hh][1][sc]])
                    if pair == 0:
                        dump("q0", qh[0][0][0:64, :], [64, S], qh[0][1])
                        dump("k0", kh[0][0][0:64, :], [64, S], kh[0][1])
                        dump("v0", vh[0][0][:], [128, 16, 128], vh[0][1])
                    if moba:
                        for hh in range(2):
                            qt_, dq_ = qh[hh]
                            fw.op("dve", lambda e: e.tensor_copy(out=kmb[hh][0][:], in_=km[hh][0][:]), r=[km[hh][1]], w=[kmb[hh][1]])
                            fw.op("dve", lambda e: e.memset(qt_[64:72, 0:1024], 0.0), w=[dq_[0], dq_[1]])
                            for qtile in range(8, 16):
                                own = qtile // 2
                                sc = qtile // 4
                                pg, dpg = psB.next()
                                fw.op("pe", lambda e: e.matmul(pg[:, 0:8], lhsT=qt_[0:64, qtile * 128:(qtile + 1) * 128], rhs=kmb[hh][0][0:64, 0:8],
                                                               start=True, stop=True), r=[dq_[sc], kmb[hh][1]], w=[dpg])
                                fw.op("dve", lambda e: e.tensor_tensor(out=gm[:], in0=pg[:, 0:8], in1=gneg[:, own - 4, :], op=ALU.add),
                                      r=[dpg, dl0], w=[dgm])
                                fw.op("dve", lambda e: e.max(out=m8[:], in_=gm[:]), r=[dgm], w=[dm8])
                                nt, dnt = nmp[own - 4]
                                fw.op("dve", lambda e: e.tensor_scalar(out=nt[:, 64:64 + own], in0=gm[:, 0:own], scalar1=m8[:, 2:3], scalar2=NEG,
                                                                       op0=ALU.is_lt, op1=ALU.mult), r=[dgm, dm8], w=[dnt])
                                p2, dp2 = psB.next()
                                fw.op("pe", lambda e: e.matmul(p2[0:72, 0:128], lhsT=nt[:, 0:72], rhs=ident_bf[:], start=True, stop=True),
                                      r=[dnt, dcon], w=[dp2])
                                fw.op("act", lambda e: e.activation(out=qt_[64:72, qtile * 128:(qtile + 1) * 128], in_=p2[64:72, 0:128], func=AF.Copy),
                                      r=[dp2], w=[dq_[sc]])
                    for hh in range(2):
                        H = pair * 2 + hh
                        E, dE = Eh[hh]
                        M, dM = (None, None) if moba else Mt
                        build_E(H, E, dE, M, dM, stage, dstage)

                        def cb(qc, po, dpo, hh=hh, pair=pair):
                            rd, drd = f32b.next()
                            recip_den(rd, drd, po, dpo)
                            fw.op("dve", lambda e: e.tensor_tensor(out=oT[hh * 64:hh * 64 + 64, pair, cs_(qc)], in0=po[0:64, :], in1=rd[64:128, :], op=ALU.mult),
                                  r=[dpo, drd], w=[doT[pair][qc]])
                        attend(qh[hh][0], qh[hh][1], 72 if moba else 64, kh[hh][0], kh[hh][1], vh[hh][0], vh[hh][1], E, dE, causal_tiles, cb)
                pass
                fw.barrier()

        pbuf = None

        def layer1_mixer():
            with contextlib.ExitStack() as st:
                nonlocal pbuf
                pbuf = Rot([(sb("pbuf%d" % i, [128, 512], BF16, st), Dep()) for i in range(4)])
                qa = [(sb("qa%d" % i, [96, S], BF16, st), [Dep() for _ in range(4)]) for i in range(4)]
                ks = (sb("ksT", [96, S], BF16, st), [Dep() for _ in range(4)])
                kw = (sb("kwT", [64, S], BF16, st), [Dep() for _ in range(4)])
                vs = (sb("vsA", [128, 16, 128], BF16, st), [Dep() for _ in range(4)])
                vw = (sb("vwA", [128, 16, 128], BF16, st), [Dep() for _ in range(4)])
                ocmp = [(sb("ocmp%d" % i, [64, S], BF16, st), [Dep() for _ in range(4)]) for i in range(4)]
                tc_ = qa[2]
                tv_ = qa[3]
                kcT = (sb("kcT", [64, 128], BF16, st), Dep())
                vcA = (sb("vcA", [128, 128], BF16, st), Dep())
                Es = (sb("Es", [128, S], BF16, st), Dep())
                Ec = Es
                Ew = (sb("Ew", [128, 640], BF16, st), Dep())
                Mw = (sb("Mw", [128, 640], BF16, st), Dep())
                stage = sb("stage", [128, S], F32, st)
                dstage = Dep()
                gsig = (sb("gsig", [48, S], BF16, st), [Dep() for _ in range(4)])
                gsel = sb("gsel", [48, 48, 64], BF16, st)
                ovl = sb("ovl", [128, 33], BF16, st)
                addm = sb("addm", [128, 16, 32], F32, st)
                forced = sb("forced", [128, 16, 32], F32, st)
                imp = (sb("imp", [128, 16, 32], F32, st), [Dep() for _ in range(16)])
                w1s = (sb("w1s", [64, 32, 128], BF16, st), Dep())
                w1 = [w1s, w1s]
                w2 = [(sb("w2_%d" % i, [128, 64], BF16, st), Dep()) for i in range(2)]
                posT = sb("posT", [64, 64], BF16, st)
                b1c = sb("b1c", [128, 2], F32, st)
                b2k = sb("b2k", [64, 1], F32, st)
                b2v = sb("b2v", [128, 64], F32, st)
                cb1 = sb("cb1", [128, 2], F32, st)
                dcb1 = Dep()
                wA = [(sb("wA%d" % i, [128, 8, 128], BF16, st), Dep()) for i in range(3)]
                wQ = [wA[0], wA[1]]
                wG = (sb("wG", [128, 8, 48], BF16, st), Dep())
                sel_t = {}
                for nm_, shp in [("vals", [128, 32]), ("lt", [128, 32]), ("v2", [128, 32]), ("m8a", [128, 8]), ("m8b", [128, 8]), ("rdi", [128, 4])]:
                    sel_t[nm_] = (sb("sel_" + nm_, shp, F32, st), Dep())
                nmp = (sb("nmp1", [128, 96], BF16, st), Dep())
                gl = Dep()
                fw.dma("pool", gsel[:], IN["c_gsel"][:, :, :], w=[gl])
                fw.dma("pool", ovl[0:127, :], IN["c_ovl"][:, :], w=[gl])
                fw.dma("sp", addm[:], IN["c_addmask"][:, :, :], w=[gl])
                fw.dma("sp", forced[:], IN["c_forced"][:, :, :], w=[gl])
                fw.dma("pool", posT[:], posT_d[:, :], w=[gl])
                fw.dma("sp", b1c[:], b1_d[:, :], w=[gl])
                fw.dma("sp", b2k[:], b2k_d[:, :], w=[gl])
                fw.dma("sp", b2v[:], _dap(b2v_d, 0, [[0, 128], [1, 64]]), w=[gl])
                w1src = [ck_w1.rearrange("(l d) j -> d l j", d=64), cv_w1.rearrange("(l d) j -> d l j", d=64)]
                fw.dma("pool", w2[0][0][:], ck_w2[:, :], w=[w2[0][1]])
                fw.dma("pool", w2[1][0][:], cv_w2[:, :], w=[w2[1][1]])
                fw.dma("pool", ks[0][64:96, :], IN["c_blk_nsa"][:, :], w=ks[1])
                fw.op("dve", lambda e: e.memset(nmp[0][:], 0.0), w=[nmp[1]])
                fw.op("dve", lambda e: e.memset(vs[0][:, :, 64:128], 1.0), w=vs[1])
                fw.op("dve", lambda e: e.memset(vw[0][:, :, 64:128], 1.0), w=vw[1])
                fw.op("dve", lambda e: e.memset(vcA[0][:, 64:128], 1.0), w=[vcA[1]])
                build_M(2, Mw[0], Mw[1], stage, dstage, width=640)
                for i in range(2):
                    fw.dma("pool", w1s[0][:], w1src[i], w=[w1s[1]])
                    pt, dp = psB.next()
                    for l in range(32):
                        fw.op("pe", lambda e: e.matmul(pt[:, 0:1], lhsT=w1[i][0][:, l, :], rhs=posT[:, i * 32 + l:i * 32 + l + 1],
                                                       start=(l == 0), stop=(l == 31)), r=[w1[i][1], gl], w=[dp])
                    fw.op("dve", lambda e: e.tensor_tensor(out=cb1[:, i:i + 1], in0=pt[:, 0:1], in1=b1c[:, i:i + 1], op=ALU.add),
                          r=[dp, gl], w=[dcb1])
                load_w3("pool", wG[0], wG[1], w_in_nsa, 2560, 48, 0)
                for sc in range(4):
                    pt, dp = proj_fm(wG[0], wG[1], sc, 48, 0)
                    fw.op("act", lambda e: e.activation(out=gsig[0][0:48, cs_(sc)], in_=pt[0:48, :], func=AF.Sigmoid), r=[dp], w=[gsig[1][sc]])

                def gate_bc(h, j, qc):
                    pt, dp = psB.next()
                    fw.op("pe", lambda e: e.matmul(pt[0:64, :], lhsT=gsel[0:48, h * 3 + j, :], rhs=gsig[0][0:48, cs_(qc)], start=True, stop=True),
                          r=[gl, gsig[1][qc]], w=[dp])
                    return pt, dp

                for g in range(4):
                    load_w3("pool", wA[0][0], wA[0][1], w_in_nsa, 1536 + g * 64, 64, 0)
                    load_w3("pool", wA[0][0], wA[0][1], w_in_nsa, 2048 + g * 64, 64, 64)
                    load_w3("pool", wA[1][0], wA[1][1], w_in_nsa, 1024 + g * 64, 64, 0)
                    load_w3("pool", wA[1][0], wA[1][1], w_in_nsa, 1280 + g * 64, 64, 64)
                    load_w3("pool", wA[2][0], wA[2][1], w_in_nsa, 1792 + g * 64, 64, 0)
                    load_w3("pool", wA[2][0], wA[2][1], w_in_nsa, 2304 + g * 64, 64, 64)
                    for sc in range(4):
                        pt, dp = proj_fm(wA[0][0], wA[0][1], sc, 128, 0)
                        headnorm(pt, dp, 128, 5, [(ks[0][0:64, cs_(sc)], ks[1][sc], 0), (kw[0][0:64, cs_(sc)], kw[1][sc], 64)])
                        pt, dp = proj_fm(wA[1][0], wA[1][1], sc, 128, 0)
                        fw.op("act", lambda e: e.activation(out=tc_[0][0:64, cs_(sc)], in_=pt[0:64, :], func=AF.Copy), r=[dp], w=[tc_[1][sc]])
                        fw.op("act", lambda e: e.activation(out=tv_[0][0:64, cs_(sc)], in_=pt[64:128, :], func=AF.Copy), r=[dp], w=[tv_[1][sc]])
                        pv, dpv = psA.next()
                        for j in range(4):
                            tt = sc * 4 + j
                            for kc in range(8):
                                fw.op("pe", lambda e: e.matmul(pv[:, j * 128:(j + 1) * 128], lhsT=xnT[:, kc, tt * 128:(tt + 1) * 128],
                                                               rhs=wA[2][0][:, kc, :], start=(kc == 0), stop=(kc == 7)),
                                      r=[wA[2][1], dxn[sc]], w=[dpv])
                        for hh, vdst in enumerate((vs, vw)):
                            src = pv[:, hh * 64:hh * 64 + 1]
                            src3 = bass.AP(tensor=src.tensor, offset=src.offset, ap=[list(src.ap[0]), [128, 4], [1, 64]])
                            fw.op("act", lambda e: e.activation(out=vdst[0][:, sc * 4:sc * 4 + 4, 0:64], in_=src3, func=AF.Copy),
                                  r=[dpv], w=[vdst[1][sc]])
                    for i, tsrc in enumerate((tc_, tv_)):
                        fw.dma("pool", w1s[0][:], w1src[i], w=[w1s[1]])
                        ph, dph = psA.next()
                        for l in range(32):
                            a = tsrc[0][0:64, l:l + 1]
                            rhs = bass.AP(tensor=a.tensor, offset=a.offset, ap=[list(a.ap[0]), [16, 127]])
                            fw.op("pe", lambda e: e.matmul(ph[:, 0:127], lhsT=w1[i][0][:, l, :], rhs=rhs, start=(l == 0), stop=(l == 31)),
                                  r=[w1[i][1]] + tsrc[1], w=[dph])
                        xg, dxg = f32b.next()
                        tg_, dtg = f32b.next()
                        fw.op("act", lambda e: e.activation(out=xg[:, 0:127], in_=ph[:, 0:127], func=AF.Identity, bias=cb1[:, i:i + 1], scale=1.0),
                              r=[dph, dcb1], w=[dxg])
                        fw.op("dve", lambda e: e.tensor_tensor(out=tg_[:, 0:127], in0=xg[:, 0:127], in1=xg[:, 0:127], op=ALU.mult), r=[dxg], w=[dtg])
                        fw.op("dve", lambda e: e.tensor_scalar(out=tg_[:, 0:127], in0=tg_[:, 0:127], scalar1=0.044715, scalar2=1.0, op0=ALU.mult, op1=ALU.add),
                              r=[dtg], w=[dtg])
                        fw.op("dve", lambda e: e.tensor_tensor(out=tg_[:, 0:127], in0=tg_[:, 0:127], in1=xg[:, 0:127], op=ALU.mult), r=[dxg, dtg], w=[dtg])
                        fw.op("act", lambda e: e.activation(out=tg_[:, 0:127], in_=tg_[:, 0:127], func=AF.Sigmoid, scale=1.5957691216), r=[dtg], w=[dtg])
                        gb, dgb = sqb.next()
                        fw.op("dve", lambda e: e.tensor_tensor(out=gb[:, 0:127], in0=tg_[:, 0:127], in1=xg[:, 0:127], op=ALU.mult), r=[dxg, dtg], w=[dgb])
                        if i == 0:
                            pk, dpk = psA.next()
                            fw.op("pe", lambda e: e.matmul(pk[0:64, 0:127], lhsT=w2[0][0][:, :], rhs=gb[:, 0:127], start=True, stop=True),
                                  r=[w2[0][1], dgb], w=[dpk])
                            kf, dkf = f32b.next()
                            fw.op("act", lambda e: e.activation(out=kf[0:64, 0:127], in_=pk[0:64, 0:127], func=AF.Identity, bias=b2k[:, 0:1], scale=1.0),
                                  r=[dpk, gl], w=[dkf])
                            sq, dsq = sqb.next()
                            fw.op("act", lambda e: e.activation(out=sq[0:64, 0:127], in_=kf[0:64, 0:127], func=AF.Square), r=[dkf], w=[dsq])
                            ps2, dp2 = psB.next()
                            fw.op("pe", lambda e: e.matmul(ps2[0:64, 0:127], lhsT=blk_ones[0:64, 0:64], rhs=sq[0:64, 0:127], start=True, stop=True),
                                  r=[dsq, dcon], w=[dp2])
                            rt, drt = f32b.next()
                            fw.op("act", lambda e: e.activation(out=rt[0:64, 0:127], in_=ps2[0:64, 0:127], func=AF.Ln, bias=EPS, scale=1.0 / 64), r=[dp2], w=[drt])
                            fw.op("act", lambda e: e.activation(out=rt[0:64, 0:127], in_=rt[0:64, 0:127], func=AF.Exp, scale=-0.5), r=[drt], w=[drt])
                            fw.op("dve", lambda e: e.scalar_tensor_tensor(out=kcT[0][0:64, 0:127], in0=kf[0:64, 0:127], scalar=hg[0:64, 6:7], in1=rt[0:64, 0:127],
                                                                          op0=ALU.mult, op1=ALU.mult), r=[dkf, drt, dcon], w=[kcT[1]])
                        else:
                            pk, dpk = psA.next()
                            fw.op("pe", lambda e: e.matmul(pk[0:127, 0:64], lhsT=gb[:, 0:127], rhs=w2[1][0][:, :], start=True, stop=True),
                                  r=[w2[1][1], dgb], w=[dpk])
                            fw.op("dve", lambda e: e.tensor_tensor(out=vcA[0][0:127, 0:64], in0=pk[0:127, 0:64], in1=b2v[0:127, :], op=ALU.add),
                                  r=[dpk, gl], w=[vcA[1]])
                    if g == 0:
                        dump("kcT", kcT[0][:], [64, 128], [kcT[1]])
                        dump("vcA", vcA[0][:], [128, 128], [vcA[1]])
                        dump("ksT", ks[0][0:64, :], [64, S], ks[1])
                    for pr in range(2):
                        t, d_ = wQ[pr]
                        load_w3("pool", t, d_, w_in_nsa, (g * 4 + pr * 2) * 64, 128, 0)
                    for pr in range(2):
                        t, d_ = wQ[pr]
                        for sc in range(4):
                            pt, dp = proj_fm(t, d_, sc, 128, 0)
                            a, b_ = qa[pr * 2], qa[pr * 2 + 1]
                            headnorm(pt, dp, 128, 4, [(a[0][0:64, cs_(sc)], a[1][sc], 0), (b_[0][0:64, cs_(sc)], b_[1][sc], 64)])
                    for qd in qa:
                        fw.op("dve", lambda e: e.memset(qd[0][64:96, 0:1024], 0.0), w=[qd[1][0], qd[1][1]])
                    for qt in range(8, 16):
                        fw.op("dve", lambda e: e.memset(imp[0][:, qt, :], 0.0), w=[imp[1][qt]])
                    for hl in range(4):
                        h = g * 4 + hl
                        pair, hh = h // 2, h % 2
                        qt_, dq_ = qa[hl]
                        build_E(h, Ec[0], Ec[1], None, None, stage, dstage, pstride=16, off=31, rows=127)
                        for qc in range(4):
                            pss, dps = psS.next()
                            fw.op("pe", lambda e: e.matmul(pss[0:127, :], lhsT=kcT[0][0:64, 0:127], rhs=qt_[0:64, cs_(qc)], start=True, stop=True),
                                  r=[kcT[1], dq_[qc]], w=[dps])
                            pb, dpb = pbuf.next()
                            fw.op("act", lambda e: e.activation(out=pb[0:127, :], in_=pss[0:127, :], func=AF.Exp, scale=0.125), r=[dps], w=[dpb])
                            fw.op("dve", lambda e: e.tensor_tensor(out=pb[0:127, :], in0=pb[0:127, :], in1=Ec[0][0:127, cs_(qc)], op=ALU.mult),
                                  r=[Ec[1], dpb], w=[dpb])
                            po, dpo = psO.next()
                            fw.op("pe", lambda e: e.matmul(po[:, :], lhsT=vcA[0][0:127, :], rhs=pb[0:127, :], start=True, stop=True),
                                  r=[vcA[1], dpb], w=[dpo])
                            if qc >= 2:
                                pi, dpi = psB.next()
                                for j in range(4):
                                    fw.op("pe", lambda e: e.matmul(pi[:, j * 64:j * 64 + 33], lhsT=pb[0:127, j * 128:(j + 1) * 128], rhs=ovl[0:127, :],
                                                                   start=True, stop=True), r=[dpb, gl], w=[dpi])
                                rdi, drdi = sel_t["rdi"]
                                src = pi[:, 32:33]
                                src3 = bass.AP(tensor=src.tensor, offset=src.offset, ap=[list(src.ap[0]), [64, 4]])
                                fw.op("dve", lambda e: e.reciprocal(out=rdi[:, 0:4], in_=src3), r=[dpi], w=[drdi])
                                for j in range(4):
                                    qtile = qc * 4 + j
                                    fw.op("dve", lambda e: e.scalar_tensor_tensor(out=imp[0][:, qtile, :], in0=pi[:, j * 64:j * 64 + 32], scalar=rdi[:, j:j + 1],
                                                                                  in1=imp[0][:, qtile, :], op0=ALU.mult, op1=ALU.add),
                                          r=[dpi, drdi, imp[1][qtile]], w=[imp[1][qtile]])
                            rd, drd = f32b.next()
                            recip_den(rd, drd, po, dpo)
                            fw.op("dve", lambda e: e.tensor_tensor(out=rd[0:64, :], in0=po[0:64, :], in1=rd[64:128, :], op=ALU.mult), r=[dpo, drd], w=[drd])
                            pgb, dpgb = gate_bc(h, 0, qc)
                            fw.op("dve", lambda e: e.tensor_tensor(out=ocmp[hl][0][0:64, cs_(qc)], in0=rd[0:64, :], in1=pgb[0:64, :], op=ALU.mult),
                                  r=[drd, dpgb], w=[ocmp[hl][1][qc]])
                    vals, dvals = sel_t["vals"]
                    lt, dlt = sel_t["lt"]
                    v2, dv2 = sel_t["v2"]
                    m8a, dm8a = sel_t["m8a"]
                    m8b, dm8b = sel_t["m8b"]
                    for qtile in range(8, 16):
                        sc = qtile // 4
                        fw.op("dve", lambda e: e.tensor_tensor(out=vals[:], in0=imp[0][:, qtile, :], in1=addm[:, qtile, :], op=ALU.add),
                              r=[imp[1][qtile], gl], w=[dvals])
                        fw.op("dve", lambda e: e.max(out=m8a[:], in_=vals[:]), r=[dvals], w=[dm8a])
                        fw.op("dve", lambda e: e.tensor_scalar(out=lt[:], in0=vals[:], scalar1=m8a[:, 7:8], scalar2=None, op0=ALU.is_lt), r=[dvals, dm8a], w=[dlt])
                        fw.op("dve", lambda e: e.tensor_tensor(out=v2[:], in0=vals[:], in1=lt[:], op=ALU.mult), r=[dvals, dlt], w=[dv2])
                        fw.op("dve", lambda e: e.tensor_scalar(out=lt[:], in0=lt[:], scalar1=-1.0, scalar2=BIG, op0=ALU.add, op1=ALU.mult), r=[dlt, dv2], w=[dlt])
                        fw.op("dve", lambda e: e.tensor_tensor(out=v2[:], in0=v2[:], in1=lt[:], op=ALU.add), r=[dlt, dv2], w=[dv2])
                        fw.op("dve", lambda e: e.max(out=m8b[:], in_=v2[:]), r=[dv2], w=[dm8b])
                        fw.op("dve", lambda e: e.tensor_scalar(out=lt[:], in0=vals[:], scalar1=m8b[:, 4:5], scalar2=None, op0=ALU.is_ge), r=[dvals, dm8b, dlt], w=[dlt])
                        fw.op("dve", lambda e: e.tensor_tensor(out=lt[:], in0=lt[:], in1=forced[:, qtile, :], op=ALU.max), r=[dlt, gl], w=[dlt])
                        fw.op("dve", lambda e: e.tensor_scalar(out=nmp[0][:, 64:96], in0=lt[:], scalar1=-1.0, scalar2=-NEG, op0=ALU.add, op1=ALU.mult),
                              r=[dlt], w=[nmp[1]])
                        p2, dp2 = psB.next()
                        fw.op("pe", lambda e: e.matmul(p2[0:96, 0:128], lhsT=nmp[0][:, 0:96], rhs=ident_bf[:], start=True, stop=True),
                              r=[nmp[1], dcon], w=[dp2])
                        for qd in qa:
                            fw.op("act", lambda e: e.activation(out=qd[0][64:96, qtile * 128:(qtile + 1) * 128], in_=p2[64:96, 0:128], func=AF.Copy),
                                  r=[dp2], w=[qd[1][sc]])
                    if g == 0:
                        dump("imp", imp[0][:], [128, 16, 32], imp[1])
                        dump("qa0", qa[0][0][:], [96, S], qa[0][1])
                    for hl in range(4):
                        h = g * 4 + hl
                        pair, hh = h // 2, h % 2
                        build_E(h, Es[0], Es[1], None, None, stage, dstage)
                        fw.op("dve", lambda e: e.tensor_tensor(out=Ew[0][:, :], in0=Es[0][:, 0:640], in1=Mw[0][:, :], op=ALU.mult),
                              r=[Es[1], Mw[1]], w=[Ew[1]])
                        acc = {}

                        def cb_slc(qc, po, dpo, h=h):
                            rd, drd = f32b.next()
                            recip_den(rd, drd, po, dpo)
                            fw.op("dve", lambda e: e.tensor_tensor(out=rd[0:64, :], in0=po[0:64, :], in1=rd[64:128, :], op=ALU.mult), r=[dpo, drd], w=[drd])
                            pgb, dpgb = gate_bc(h, 1, qc)
                            fw.op("dve", lambda e: e.tensor_tensor(out=rd[0:64, :], in0=rd[0:64, :], in1=pgb[0:64, :], op=ALU.mult), r=[drd, dpgb], w=[drd])
                            acc[qc] = (rd, drd)

                        def cb_win(qc, po, dpo, h=h, pair=pair, hh=hh, hl=hl):
                            rd, drd = f32b.next()
                            a_, da_ = acc[qc]
                            recip_den(rd, drd, po, dpo)
                            fw.op("dve", lambda e: e.tensor_tensor(out=rd[0:64, :], in0=po[0:64, :], in1=rd[64:128, :], op=ALU.mult), r=[dpo, drd], w=[drd])
                            pgb, dpgb = gate_bc(h, 2, qc)
                            fw.op("dve", lambda e: e.tensor_tensor(out=rd[0:64, :], in0=rd[0:64, :], in1=pgb[0:64, :], op=ALU.mult), r=[drd, dpgb], w=[drd])
                            fw.op("dve", lambda e: e.tensor_tensor(out=rd[0:64, :], in0=rd[0:64, :], in1=a_[0:64, :], op=ALU.add), r=[drd, da_], w=[drd])
                            fw.op("dve", lambda e: e.tensor_tensor(out=rd[0:64, :], in0=rd[0:64, :], in1=ocmp[hl][0][0:64, cs_(qc)], op=ALU.add),
                                  r=[drd, ocmp[hl][1][qc]], w=[drd])
                            fw.op("act", lambda e: e.activation(out=oT[hh * 64:hh * 64 + 64, pair, cs_(qc)], in_=rd[0:64, :], func=AF.Copy),
                                  r=[drd], w=[doT[pair][qc]])

                        for qc in range(4):
                            attend(qa[hl][0], qa[hl][1], 96, ks[0], ks[1], vs[0], vs[1], Es[0], Es[1], causal_tiles, cb_slc, chunks=[qc])
                            attend(qa[hl][0], qa[hl][1], 64, kw[0], kw[1], vw[0], vw[1], Ew[0], Ew[1], win_tiles, cb_win, chunks=[qc])
                pass
                fw.barrier()

        alldh = [d_ for row in dh for d_ in row]
        alldo = [d_ for row in doT for d_ in row]
        with contextlib.ExitStack() as st:
            hT = sb("hT", [128, 8, S], F32, st)
            load_h(xT, [])
            rmsnorm(0)
            dump("xn0", xnT[:], [128, 8, S], dxn)
            fw.barrier()
        if upto >= 1:
            layer0_mixer()
            dump("oT0", oT[:], [128, 8, S], alldo)
        with contextlib.ExitStack() as st:
            hT = sb("hT", [128, 8, S], F32, st)
            load_h(xT, [])
            if upto >= 1:
                out_proj(w_out_ab, st)
                dump("hmix0", hT[:], [128, 8, S], alldh)
                fw.barrier()
            if upto >= 2:
                ffn(0)
                dump("hffn0", hT[:], [128, 8, S], alldh)
            if upto >= 3:
                ple(0)
                dump("h0", hT[:], [128, 8, S], alldh)
            if upto >= 4:
                rmsnorm(3)
                store_h(hS, [dhS])
            fw.barrier()
            if upto < 4:
                store_h(outT, [])
        if upto >= 4:
            layer1_mixer()
            dump("oT1", oT[:], [128, 8, S], alldo)
            with contextlib.ExitStack() as st:
                hT = sb("hT", [128, 8, S], F32, st)
                load_h(hS, [dhS])
                out_proj(w_out_nsa, st)
                dump("hmix1", hT[:], [128, 8, S], alldh)
                fw.barrier()
                if upto >= 5:
                    ffn(1)
                if upto >= 6:
                    ple(1)
                store_h(outT, [])
                fw.barrier()
        fw.finish("sp")
        build_nc.stats = (fw.n_ins, fw.n_wait)
    return nc, consts, DBG


def host_inputs(inputs, b, consts):
    f = lambda a: np.ascontiguousarray(np.asarray(a, dtype=np.float32))
    m = {}
    m["xT"] = f(inputs["x"][b].T)
    m["pT"] = f(np.transpose(inputs["p"][:, b], (0, 2, 1)))
    rb = np.asarray(inputs["rel_bias"], np.float32)
    dd = np.maximum(2047 - np.arange(WHL), 0)
    whb = rb[_bucket(dd), :].T.copy()
    whb[:, 2048:] = NEG
    m["whb"] = f(whb)
    gl = []
    for layer in range(2):
        for nm in ("norm_mix", "norm_ffn", "norm_ple"):
            gl.append(np.asarray(inputs[nm][layer], np.float32).reshape(8, 128).T)
    m["gains"] = f(np.concatenate(gl, axis=1))
    t2 = lambda a: np.concatenate([np.asarray(a, np.float32)] * 2)
    hg = np.zeros((128, 8), np.float32)
    hg[:, 0] = t2(inputs["qn_moba"][0])
    hg[:, 1] = t2(inputs["kn_moba"][0])
    hg[:, 2] = t2(inputs["qn_dil"][0])
    hg[:, 3] = t2(inputs["kn_dil"][0])
    hg[:, 4] = t2(inputs["qn_nsa"][0])
    hg[:, 5] = np.concatenate([np.asarray(inputs["kn_slc"][0], np.float32), np.asarray(inputs["kn_win"][0], np.float32)])
    hg[:, 6] = t2(inputs["kn_cmp"][0])
    m["hg"] = hg
    m["posT"] = f(np.concatenate([np.asarray(inputs["cmp_k_pos"][0]).T, np.asarray(inputs["cmp_v_pos"][0]).T], axis=1))
    m["b1c"] = f(np.stack([inputs["cmp_k_b1"][0], inputs["cmp_v_b1"][0]], axis=1))
    m["b2k"] = f(np.asarray(inputs["cmp_k_b2"][0]).reshape(64, 1))
    m["b2v"] = f(np.asarray(inputs["cmp_v_b2"][0]).reshape(1, 64))
    m["w_in_ab"] = f(inputs["w_in_ab"][0])
    m["w_out_ab"] = f(inputs["w_out_ab"][0])
    m["w_in_nsa"] = f(inputs["w_in_nsa"][0])
    m["w_out_nsa"] = f(inputs["w_out_nsa"][0])
    for nm in ("w_ffn_gate", "w_ffn_up", "w_ffn_down", "w_ple_proj", "w_ple_gate"):
        m[nm] = f(inputs[nm])
    m["cmp_k_w1"] = f(inputs["cmp_k_w1"][0])
    m["cmp_k_w2"] = f(inputs["cmp_k_w2"][0])
    m["cmp_v_w1"] = f(inputs["cmp_v_w1"][0])
    m["cmp_v_w2"] = f(inputs["cmp_v_w2"][0])
    for k, v in consts.items():
        m[k] = v
    return m


def kernel(**inputs):
    nc, consts, _ = build_nc()
    in_maps = [host_inputs(inputs, b, consts) for b in range(8)]
    res = run_bass_kernel_spmd(nc, in_maps, core_ids=list(range(8)))
    out = np.stack([np.asarray(r["outT"], np.float32).T for r in res.results], axis=0)
    return np.ascontiguousarray(out.astype(np.float32))
```

```python
import math
import contextlib
import numpy as np
import concourse.bass as bass
import concourse.mybir as mybir
from concourse.bass_utils import run_bass_kernel_spmd

F32 = mybir.dt.float32
BF16 = mybir.dt.bfloat16
AF = mybir.ActivationFunctionType
ALU = mybir.AluOpType
AX = mybir.AxisListType

S = 2048
D = 1024
FH = 2816
NF = 22
WHL = 4352
EPS = 1e-6
NEG = -30000.0
BIG = 3.0e38


class Dep:
    __slots__ = ("w", "r")

    def __init__(self):
        self.w = None
        self.r = []


class FW:
    NDMA = 24

    def __init__(self, nc, es):
        self.nc = nc
        self.engs = {"pe": nc.tensor, "act": nc.scalar, "dve": nc.vector, "pool": nc.gpsimd, "sp": nc.sync}
        self.sems = {}
        self.cnt = {}
        for k in self.engs:
            self.sems[k] = es.enter_context(nc.semaphore("sem_" + k))
            self.cnt[k] = 0
        for i in range(self.NDMA):
            k = ("dma", i)
            self.sems[k] = es.enter_context(nc.semaphore("sem_dma%d" % i))
            self.cnt[k] = 0
        self.seen = {e: {} for e in self.engs}
        self.dma_rr = {"sp": 0, "pool": 0, "act": 0}
        self.n_ins = 0
        self.n_wait = 0

    def _wait(self, eng, deps):
        seen = self.seen[eng]
        need = {}
        for d in deps:
            if d is None:
                continue
            k, v = d
            if k == "pe" and eng == "pe":
                continue
            if seen.get(k, 0) >= v:
                continue
            if need.get(k, 0) < v:
                need[k] = v
        for k, v in need.items():
            self.engs[eng].wait_ge(self.sems[k], v)
            seen[k] = v
            self.n_wait += 1

    @staticmethod
    def _collect(r, w):
        deps = []
        for t in r:
            deps.append(t.w)
        for t in w:
            deps.append(t.w)
            deps.extend(t.r)
        return deps

    def _mark(self, tok, r, w):
        for t in w:
            t.w = tok
            t.r = []
        for t in r:
            t.r.append(tok)
            if len(t.r) > 64:
                best = {}
                for k, v in t.r:
                    if best.get(k, 0) < v:
                        best[k] = v
                t.r = list(best.items())

    def op(self, eng, fn, r=(), w=()):
        self._wait(eng, self._collect(r, w))
        ins = fn(self.engs[eng])
        self.cnt[eng] += 1
        ins.then_inc(self.sems[eng], 1)
        self._mark((eng, self.cnt[eng]), r, w)
        self.n_ins += 1

    def dma(self, q, out, in_, r=(), w=()):
        half = self.NDMA // 2
        i = self.dma_rr[q]
        self.dma_rr[q] = (i + 1) % half
        k = ("dma", i + (half if q == "pool" else 0))
        deps = self._collect(r, w)
        if self.cnt[k] > 0:
            deps.append((k, self.cnt[k]))
        self._wait(q, deps)
        ins = self.engs[q].dma_start(out=out, in_=in_)
        self.cnt[k] += 16
        ins.then_inc(self.sems[k], 16)
        self._mark((k, self.cnt[k]), r, w)
        self.n_ins += 1

    def barrier(self):
        allk = [(k, v) for k, v in self.cnt.items() if v > 0]
        for e in self.engs:
            self._wait(e, allk)

    def finish(self, eng="sp"):
        allk = [(k, v) for k, v in self.cnt.items() if v > 0]
        self._wait(eng, allk)


class Rot:
    def __init__(self, items):
        self.items = items
        self.i = 0

    def next(self):
        t = self.items[self.i]
        self.i = (self.i + 1) % len(self.items)
        return t


def _bucket(d):
    n = np.maximum(d, 0)
    nf = np.maximum(n, 1).astype(np.float32)
    large = 16 + (np.log(nf / np.float32(16)) / np.float32(math.log(128.0)) * np.float32(16)).astype(np.int32)
    return np.where(n < 16, n, np.minimum(large, 31))


def _static_consts():
    c = {}
    m = np.arange(WHL)
    d = 2047 - m
    wm = np.zeros((3, WHL), np.float32)
    wm[0] = (d >= 0)
    wm[1] = (d >= 0) * ((d <= 128).astype(np.float32) + ((d % 4 == 0) & (d <= 512)) + ((d % 16 == 0) & (d <= 2048)))
    wm[2] = (d >= 0) & (d < 512)
    c["c_wm"] = wm
    c["c_ident"] = np.eye(128, dtype=np.float32)
    k = np.arange(S)
    c["c_blk_moba"] = (k[None, :] // 256 == np.arange(8)[:, None]).astype(np.float32)
    c["c_blk_nsa"] = (k[None, :] // 64 == np.arange(32)[:, None]).astype(np.float32)
    g = np.zeros((128, 4, 8), np.float32)
    for i, own in enumerate(range(4, 8)):
        g[:, i, own:] = -BIG
    c["c_gneg"] = g
    add = np.full((128, 16, 32), -BIG, np.float32)
    forced = np.zeros((128, 16, 32), np.float32)
    for qt in range(16):
        for q in range(128):
            cur = (qt * 128 + q) // 64
            for n in (0, cur, cur - 1):
                if n >= 0:
                    forced[q, qt, n] = 1.0
            for n in range(1, cur - 1):
                add[q, qt, n] = 0.0
    c["c_addmask"] = add
    c["c_forced"] = forced
    cs = np.arange(127) * 16
    ss = np.arange(32) * 64
    ov = np.maximum(np.minimum(cs[:, None] + 32, ss[None, :] + 64) - np.maximum(cs[:, None], ss[None, :]), 0)
    ovl = np.ones((127, 33), np.float32)
    ovl[:, :32] = ov
    c["c_ovl"] = ovl
    gs = np.zeros((48, 48, 64), np.float32)
    for i in range(48):
        gs[i, i, :] = 1.0
    c["c_gsel"] = gs
    return c


_CONST_SHAPES = None


def _dap(t, offset, ap):
    return bass.AP(tensor=t.tensor, offset=offset, ap=[list(a) for a in ap])


def build_nc(upto=99, dbg=()):
    nc = bass.Bass("TRN2", target_bir_lowering=False)
    consts = _static_consts()
    IN = {}

    def din(name, shape):
        IN[name] = nc.dram_tensor(name, list(shape), F32, kind="ExternalInput").ap()
        return IN[name]

    xT = din("xT", [D, S])
    pT = din("pT", [2, 256, S])
    whb = din("whb", [16, WHL])
    gains_d = din("gains", [128, 48])
    hg_d = din("hg", [128, 8])
    posT_d = din("posT", [64, 64])
    b1_d = din("b1c", [128, 2])
    b2k_d = din("b2k", [64, 1])
    b2v_d = din("b2v", [1, 64])
    w_in_ab = din("w_in_ab", [D, 3072])
    w_out_ab = din("w_out_ab", [D, D])
    w_in_nsa = din("w_in_nsa", [D, 2608])
    w_out_nsa = din("w_out_nsa", [D, D])
    w_g = din("w_ffn_gate", [2, D, FH])
    w_u = din("w_ffn_up", [2, D, FH])
    w_d = din("w_ffn_down", [2, FH, D])
    w_pp = din("w_ple_proj", [2, 256, D])
    w_pg = din("w_ple_gate", [2, D, D])
    ck_w1 = din("cmp_k_w1", [2048, 128])
    ck_w2 = din("cmp_k_w2", [128, 64])
    cv_w1 = din("cmp_v_w1", [2048, 128])
    cv_w2 = din("cmp_v_w2", [128, 64])
    for k, v in consts.items():
        din(k, v.shape)
    outT = nc.dram_tensor("outT", [D, S], F32, kind="ExternalOutput").ap()
    DBG = {}

    with contextlib.ExitStack() as es:
        fw = FW(nc, es)

        uniq = [0]

        def sb(name, shape, dt=F32, stack=es):
            uniq[0] += 1
            return stack.enter_context(nc.sbuf_tensor("%s_%d" % (name, uniq[0]), list(shape), dt))

        def pst(name):
            return es.enter_context(nc.psum_tensor(name, [128, 512], F32))

        psS = Rot([(pst("psS%d" % i), Dep()) for i in range(3)])
        psO = Rot([(pst("psO%d" % i), Dep()) for i in range(2)])
        psA = Rot([(pst("psM%d" % i), Dep()) for i in range(3)])
        psB = psA

        def dump(name, ap, shape, deps):
            if name not in dbg:
                return
            t = nc.dram_tensor("dbg_" + name, list(shape), ap.dtype if hasattr(ap, "dtype") else F32, kind="ExternalOutput").ap()
            DBG[name] = t
            fw.dma("sp", t, ap, r=deps)

        hS = nc.dram_tensor("hS", [D, S], F32, kind="Internal").ap()
        dhS = Dep()
        dh = [[Dep() for _ in range(4)] for _ in range(8)]
        xnT = sb("xnT", [128, 8, S], BF16)
        dxn = [Dep() for _ in range(4)]
        oT = sb("oT", [128, 8, S], BF16)
        doT = [[Dep() for _ in range(4)] for _ in range(8)]
        hT = None
        gains = sb("gains_sb", [128, 48])
        hg = sb("hg_sb", [128, 8])
        dcon = Dep()
        ones_bf = sb("ones_bf", [128, 128], BF16)
        blk_ones = sb("blk_ones", [128, 128], BF16)
        ident_bf = sb("ident_bf", [128, 128], BF16)
        sqb = Rot([(sb("sqb%d" % i, [128, 512], BF16), Dep()) for i in range(2)])
        f32b = Rot([(sb("f32b%d" % i, [128, 512]), Dep()) for i in range(4)])

        fw.dma("sp", gains[:], gains_d[:, :], w=[dcon])
        fw.dma("sp", hg[:], hg_d[:, :], w=[dcon])
        fw.dma("pool", ident_bf[:], IN["c_ident"][:, :], w=[dcon])
        fw.op("dve", lambda e: e.memset(ones_bf[:], 1.0), w=[dcon])
        fw.op("dve", lambda e: e.memset(blk_ones[:], 0.0), w=[dcon])
        fw.op("dve", lambda e: e.memset(blk_ones[0:64, 0:64], 1.0), w=[dcon])
        fw.op("dve", lambda e: e.memset(blk_ones[64:128, 64:128], 1.0), w=[dcon])
        def load_h(src, dsrc):
            v = src.rearrange("(c p) s -> p c s", p=128)
            for c in range(8):
                for sc in range(4):
                    fw.dma("sp", hT[:, c, sc * 512:(sc + 1) * 512], v[:, c, sc * 512:(sc + 1) * 512], r=dsrc, w=[dh[c][sc]])

        def store_h(dst, ddst):
            v = dst.rearrange("(c p) s -> p c s", p=128)
            for c in range(8):
                for sc in range(4):
                    fw.dma("sp", v[:, c, sc * 512:(sc + 1) * 512], hT[:, c, sc * 512:(sc + 1) * 512], r=[dh[c][sc]], w=ddst)

        def cs_(sc):
            return slice(sc * 512, (sc + 1) * 512)

        def rmsnorm(gidx):
            for sc in range(4):
                cs = cs_(sc)
                pt, dp = psB.next()
                for c in range(8):
                    sq, dsq = sqb.next()
                    fw.op("act", lambda e: e.activation(out=sq[:], in_=hT[:, c, cs], func=AF.Square), r=[dh[c][sc]], w=[dsq])
                    fw.op("pe", lambda e: e.matmul(pt[:], lhsT=ones_bf[:], rhs=sq[:], start=(c == 0), stop=(c == 7)),
                          r=[dsq, dcon], w=[dp])
                rt, drt = f32b.next()
                fw.op("act", lambda e: e.activation(out=rt[:], in_=pt[:], func=AF.Ln, bias=EPS, scale=1.0 / D), r=[dp], w=[drt])
                fw.op("act", lambda e: e.activation(out=rt[:], in_=rt[:], func=AF.Exp, scale=-0.5), r=[drt], w=[drt])
                for c in range(8):
                    fw.op("dve", lambda e: e.scalar_tensor_tensor(
                        out=xnT[:, c, cs], in0=hT[:, c, cs], scalar=gains[:, gidx * 8 + c:gidx * 8 + c + 1], in1=rt[:],
                        op0=ALU.mult, op1=ALU.mult), r=[dh[c][sc], drt, dcon], w=[dxn[sc]])

        def proj_fm(wt, dw, sc, M, c0=0):
            pt, dp = psA.next()
            for kc in range(8):
                fw.op("pe", lambda e: e.matmul(pt[0:M, :], lhsT=wt[:, kc, c0:c0 + M], rhs=xnT[:, kc, cs_(sc)],
                                               start=(kc == 0), stop=(kc == 7)), r=[dw, dxn[sc]], w=[dp])
            return pt, dp

        def headnorm(pt, dp, M, gcol, outs):
            sq, dsq = sqb.next()
            fw.op("act", lambda e: e.activation(out=sq[0:M, :], in_=pt[0:M, :], func=AF.Square), r=[dp], w=[dsq])
            ps2, dp2 = psB.next()
            fw.op("pe", lambda e: e.matmul(ps2[0:M, :], lhsT=blk_ones[0:M, 0:M], rhs=sq[0:M, :], start=True, stop=True),
                  r=[dsq, dcon], w=[dp2])
            rt, drt = f32b.next()
            fw.op("act", lambda e: e.activation(out=rt[0:M, :], in_=ps2[0:M, :], func=AF.Ln, bias=EPS, scale=1.0 / 64), r=[dp2], w=[drt])
            fw.op("act", lambda e: e.activation(out=rt[0:M, :], in_=rt[0:M, :], func=AF.Exp, scale=-0.5), r=[drt], w=[drt])
            for (dst, ddst, r0) in outs:
                fw.op("dve", lambda e: e.scalar_tensor_tensor(
                    out=dst, in0=pt[r0:r0 + 64, :], scalar=hg[r0:r0 + 64, gcol:gcol + 1], in1=rt[r0:r0 + 64, :],
                    op0=ALU.mult, op1=ALU.mult), r=[dp, drt, dcon], w=[ddst])

        def load_w3(q, wt, dw, src2d, c0, ncols, dst_c0=0):
            v = src2d.rearrange("(kc p) n -> p kc n", p=128)
            fw.dma(q, wt[:, :, dst_c0:dst_c0 + ncols], v[:, :, c0:c0 + ncols], w=[dw])

        def rev(t, n, rows=128):
            a = t[0:rows, n - 1:n]
            return bass.AP(tensor=a.tensor, offset=a.offset, ap=[list(a.ap[0]), [-1, n]])

        pbuf_items = []

        LOOK = 2

        def make_items(qt, dq, K, ktile, dk, vt, dv, E, dE, tiles_fn, out_cb, chunks=range(4)):
            items = []
            for qc in chunks:
                c0 = qc * 512
                tiles = tiles_fn(qc)
                assert tiles[0][1] == 0 and tiles[0][2] == 512
                for idx, (kt, lo, hi) in enumerate(tiles):
                    n = hi - lo
                    u0 = c0 + lo - kt * 128
                    items.append(dict(
                        rows=128, n=n, lo=lo, hi=hi, K=K,
                        lhsT=ktile[0:K, kt * 128:(kt + 1) * 128], rhs=qt[0:K, c0 + lo:c0 + hi], sdeps=[dk[kt // 4], dq[qc]],
                        E=E[:, u0:u0 + n], dE=dE, v=vt[:, kt, :], dv=dv[kt // 4],
                        first=(idx == 0), last=(idx == len(tiles) - 1), cb=out_cb, qc=qc, post=None))
            return items

        def run_items(items):
            staged = {}
            cur = [None]
            n_it = len(items)
            for j in range(n_it + LOOK):
                if j < n_it:
                    it = items[j]
                    pss, dps = psS.next()
                    fw.op("pe", lambda e: e.matmul(pss[0:it["rows"], 0:it["n"]], lhsT=it["lhsT"], rhs=it["rhs"], start=True, stop=True),
                          r=it["sdeps"], w=[dps])
                    staged[j] = (pss, dps)
                i = j - LOOK
                if i < 0:
                    continue
                it = items[i]
                pss, dps = staged.pop(i)
                R, n = it["rows"], it["n"]
                pb, dpb = pbuf.next()
                fw.op("act", lambda e: e.activation(out=pb[0:R, 0:n], in_=pss[0:R, 0:n], func=AF.Exp, scale=0.125), r=[dps], w=[dpb])
                fw.op("dve", lambda e: e.tensor_tensor(out=pb[0:R, 0:n], in0=pb[0:R, 0:n], in1=it["E"], op=ALU.mult),
                      r=[it["dE"], dpb], w=[dpb])
                if it["first"]:
                    cur[0] = psO.next()
                po, dpo = cur[0]
                fw.op("pe", lambda e: e.matmul(po[:, it["lo"]:it["hi"]], lhsT=it["v"], rhs=pb[0:R, 0:n], start=it["first"], stop=it["last"]),
                      r=[it["dv"], dpb], w=[dpo])
                if it["post"] is not None:
                    it["post"](pb, dpb)
                if it["last"]:
                    it["cb"](it["qc"], po, dpo)

        def attend(*args, **kw):
            run_items(make_items(*args, **kw))

        def recip_den(rd, drd, po, dpo):
            fw.op("act", lambda e: e.activation(out=rd[64:128, :], in_=po[64:128, :], func=AF.Ln, bias=1e-30, scale=1.0), r=[dpo], w=[drd])
            fw.op("act", lambda e: e.activation(out=rd[64:128, :], in_=rd[64:128, :], func=AF.Exp, scale=-1.0), r=[drd], w=[drd])

        def causal_tiles(qc):
            res = []
            for kt in range(4 * qc + 4):
                lo = max(0, kt * 128 - qc * 512)
                res.append((kt, lo, 512))
            return res

        def win_tiles(qc):
            res = []
            order = [4 * qc] + [k for k in range(max(0, 4 * qc - 4), 4 * qc + 4) if k != 4 * qc]
            for kt in order:
                off = kt * 128 - qc * 512
                lo = max(0, off)
                hi = min(512, ((off + 638) // 128 + 1) * 128)
                res.append((kt, lo, hi))
            return res

        def build_E(H, E, dE, M, dM, stage, dstage, width=2048, pstride=1, off=0, rows=128):
            fw.dma("sp", stage[0:rows, 0:width], _dap(whb, H * WHL + off + (2048 - width), [[pstride, rows], [1, width]]), w=[dstage])
            fw.op("act", lambda e: e.activation(out=E[0:rows, 0:width], in_=rev(stage, width, rows), func=AF.Exp), r=[dstage], w=[dE])
            if M is not None:
                fw.op("dve", lambda e: e.tensor_tensor(out=E[0:rows, 0:width], in0=E[0:rows, 0:width], in1=M[0:rows, 0:width], op=ALU.mult),
                      r=[dM, dE], w=[dE])

        def build_M(kind, M, dM, stage, dstage, width=2048, pstride=1, off=0, rows=128):
            fw.dma("sp", stage[0:rows, 0:width], _dap(IN["c_wm"], kind * WHL + off + (2048 - width), [[pstride, rows], [1, width]]), w=[dstage])
            fw.op("act", lambda e: e.activation(out=M[0:rows, 0:width], in_=rev(stage, width, rows), func=AF.Copy), r=[dstage], w=[dM])

        def out_proj(w_out, st):
            wo = [(sb("wo%d" % i, [128, 8, 128], BF16, st), Dep()) for i in range(2)]
            wv = w_out.rearrange("(kc p) n -> p kc n", p=128)

            def ld(fc):
                t, d_ = wo[fc % 2]
                fw.dma("pool", t[:], wv[:, :, fc * 128:(fc + 1) * 128], w=[d_])
            ld(0)
            for fc in range(8):
                if fc + 1 < 8:
                    ld(fc + 1)
                t, d_ = wo[fc % 2]
                for sc in range(4):
                    pt, dp = psA.next()
                    for pr in range(8):
                        fw.op("pe", lambda e: e.matmul(pt[:], lhsT=t[:, pr, :], rhs=oT[:, pr, cs_(sc)], start=(pr == 0), stop=(pr == 7)),
                              r=[d_, doT[pr][sc]], w=[dp])
                    fw.op("dve", lambda e: e.tensor_tensor(out=hT[:, fc, cs_(sc)], in0=pt[:], in1=hT[:, fc, cs_(sc)], op=ALU.add),
                          r=[dp, dh[fc][sc]], w=[dh[fc][sc]])

        def ffn(layer):
            rmsnorm(layer * 3 + 1)
            with contextlib.ExitStack() as st:
                act2 = sb("ffn_act", [128, NF - 16, 1024], BF16, st)
                dact = [[Dep() for _ in range(2)] for _ in range(NF)]

                def act_ap(f, q):
                    if f < 16:
                        return oT[:, f // 2, (f % 2) * 1024 + q * 512:(f % 2) * 1024 + (q + 1) * 512]
                    return act2[:, f - 16, q * 512:(q + 1) * 512]
                wg = [(sb("wg%d" % i, [128, 8, 128], BF16, st), Dep()) for i in range(2)]
                wu = [(sb("wu%d" % i, [128, 8, 128], BF16, st), Dep()) for i in range(2)]
                wd = [(sb("wd%d" % i, [128, NF, 128], BF16, st), Dep()) for i in range(2)]
                sg = Rot([(sb("sg%d" % i, [128, 512], F32, st), Dep()) for i in range(2)])
                wgv = w_g[layer].rearrange("(kc p) n -> p kc n", p=128)
                wuv = w_u[layer].rearrange("(kc p) n -> p kc n", p=128)
                wdv = w_d[layer].rearrange("(f p) n -> p f n", p=128)

                def ld1(f):
                    fw.dma("pool", wg[f % 2][0][:], wgv[:, :, f * 128:(f + 1) * 128], w=[wg[f % 2][1]])
                    fw.dma("pool", wu[f % 2][0][:], wuv[:, :, f * 128:(f + 1) * 128], w=[wu[f % 2][1]])

                def ld2(dc):
                    fw.dma("pool", wd[dc % 2][0][:], wdv[:, :, dc * 128:(dc + 1) * 128], w=[wd[dc % 2][1]])

                for half in range(2):
                    ld1(0)
                    for f in range(NF):
                        if f + 1 < NF:
                            ld1(f + 1)
                        else:
                            ld2(0)
                        tg, dg_ = wg[f % 2]
                        tu, du_ = wu[f % 2]
                        for q in range(2):
                            sc = half * 2 + q
                            pg, dpg = psA.next()
                            pu, dpu = psB.next()
                            for kc in range(8):
                                fw.op("pe", lambda e: e.matmul(pg[:], lhsT=tg[:, kc, :], rhs=xnT[:, kc, cs_(sc)], start=(kc == 0), stop=(kc == 7)),
                                      r=[dg_, dxn[sc]], w=[dpg])
                            for kc in range(8):
                                fw.op("pe", lambda e: e.matmul(pu[:], lhsT=tu[:, kc, :], rhs=xnT[:, kc, cs_(sc)], start=(kc == 0), stop=(kc == 7)),
                                      r=[du_, dxn[sc]], w=[dpu])
                            s_, ds_ = sg.next()
                            fw.op("act", lambda e: e.activation(out=s_[:], in_=pg[:], func=AF.Silu), r=[dpg], w=[ds_])
                            fw.op("dve", lambda e: e.tensor_tensor(out=act_ap(f, q), in0=s_[:], in1=pu[:], op=ALU.mult),
                                  r=[ds_, dpu], w=[dact[f][q]])
                    for dc in range(8):
                        if dc + 1 < 8:
                            ld2(dc + 1)
                        td, dd_ = wd[dc % 2]
                        for q in range(2):
                            sc = half * 2 + q
                            pt, dp = psS.next()
                            for f in range(NF):
                                fw.op("pe", lambda e: e.matmul(pt[:], lhsT=td[:, f, :], rhs=act_ap(f, q),
                                                               start=(f == 0), stop=(f == NF - 1)), r=[dd_, dact[f][q]], w=[dp])
                            fw.op("dve", lambda e: e.tensor_tensor(out=hT[:, dc, cs_(sc)], in0=pt[:], in1=hT[:, dc, cs_(sc)], op=ALU.add),
                                  r=[dp, dh[dc][sc]], w=[dh[dc][sc]])
                fw.barrier()

        def ple(layer):
            rmsnorm(layer * 3 + 2)
            with contextlib.ExitStack() as st:
                pTs = sb("pTs", [128, 2, S], BF16, st)
                dpT = Dep()
                wpg = [(sb("wpg%d" % i, [128, 8, 128], BF16, st), Dep()) for i in range(2)]
                wpp = [(sb("wpp%d" % i, [128, 2, 128], BF16, st), Dep()) for i in range(2)]
                sg = Rot([(sb("psg%d" % i, [128, 512], F32, st), Dep()) for i in range(2)])
                fw.dma("pool", pTs[:], pT[layer].rearrange("(kc p) s -> p kc s", p=128), w=[dpT])
                wgv = w_pg[layer].rearrange("(kc p) n -> p kc n", p=128)
                wpv = w_pp[layer].rearrange("(kc p) n -> p kc n", p=128)

                def ld(fc):
                    fw.dma("pool", wpg[fc % 2][0][:], wgv[:, :, fc * 128:(fc + 1) * 128], w=[wpg[fc % 2][1]])
                    fw.dma("pool", wpp[fc % 2][0][:], wpv[:, :, fc * 128:(fc + 1) * 128], w=[wpp[fc % 2][1]])
                ld(0)
                for fc in range(8):
                    if fc + 1 < 8:
                        ld(fc + 1)
                    tg, dg_ = wpg[fc % 2]
                    tp, dp_ = wpp[fc % 2]
                    for sc in range(4):
                        pg, dpg = psA.next()
                        pp_, dpp = psB.next()
                        for kc in range(8):
                            fw.op("pe", lambda e: e.matmul(pg[:], lhsT=tg[:, kc, :], rhs=xnT[:, kc, cs_(sc)], start=(kc == 0), stop=(kc == 7)),
                                  r=[dg_, dxn[sc]], w=[dpg])
                        for kc in range(2):
                            fw.op("pe", lambda e: e.matmul(pp_[:], lhsT=tp[:, kc, :], rhs=pTs[:, kc, cs_(sc)], start=(kc == 0), stop=(kc == 1)),
                                  r=[dp_, dpT], w=[dpp])
                        s_, ds_ = sg.next()
                        fw.op("act", lambda e: e.activation(out=s_[:], in_=pg[:], func=AF.Sigmoid), r=[dpg], w=[ds_])
                        fw.op("dve", lambda e: e.tensor_tensor(out=s_[:], in0=s_[:], in1=pp_[:], op=ALU.mult), r=[ds_, dpp], w=[ds_])
                        fw.op("dve", lambda e: e.tensor_tensor(out=hT[:, fc, cs_(sc)], in0=s_[:], in1=hT[:, fc, cs_(sc)], op=ALU.add),
                              r=[ds_, dh[fc][sc]], w=[dh[fc][sc]])
                fw.barrier()

        def layer0_mixer():
            with contextlib.ExitStack() as st:
                qh = [(sb("qh%d" % i, [72, S], BF16, st), [Dep() for _ in range(4)]) for i in range(2)]
                kh = [(sb("kh%d" % i, [72, S], BF16, st), [Dep() for _ in range(4)]) for i in range(2)]
                vh = [(sb("vh%d" % i, [128, 16, 128], BF16, st), [Dep() for _ in range(4)]) for i in range(2)]
                Eh = [(sb("Eh%d" % i, [128, S], BF16, st), Dep()) for i in range(2)]
                Mt = (sb("Mt", [128, S], BF16, st), Dep())
                stage = sb("stage", [128, S], F32, st)
                dstage = Dep()
                wq = [(sb("wq%d" % i, [128, 8, 384], BF16, st), Dep()) for i in range(2)]
                global_pbuf = [(sb("pbuf%d" % i, [128, 512], BF16, st), Dep()) for i in range(4)]
                nonlocal pbuf
                pbuf = Rot(global_pbuf)
                km = [(sb("km%d" % i, [64, 8], F32, st), Dep()) for i in range(2)]
                kmb = [(sb("kmb%d" % i, [72, 8], BF16, st), Dep()) for i in range(2)]
                gneg = sb("gneg", [128, 4, 8], F32, st)
                gm = sb("gm", [128, 8], F32, st)
                dgm = Dep()
                m8 = sb("m8", [128, 8], F32, st)
                dm8 = Dep()
                nmp = [(sb("nmp%d" % i, [128, 72], BF16, st), Dep()) for i in range(4)]
                dl0 = Dep()
                fw.dma("sp", gneg[:], IN["c_gneg"][:, :, :], w=[dl0])
                for i in range(4):
                    fw.op("dve", lambda e: e.memset(nmp[i][0][:], 0.0), w=[nmp[i][1]])
                for i in range(2):
                    fw.op("dve", lambda e: e.memset(kmb[i][0][:], 0.0), w=[kmb[i][1]])
                for i in range(2):
                    fw.op("dve", lambda e: e.memset(vh[i][0][:, :, 64:128], 1.0), w=vh[i][1])
                    fw.dma("pool", kh[i][0][64:72, :], IN["c_blk_moba"][:, :], w=kh[i][1])
                    fw.op("dve", lambda e: e.memset(qh[i][0][64:72, :], 0.0), w=qh[i][1])
                build_M(1, Mt[0], Mt[1], stage, dstage)

                def ldw(pair):
                    t, d_ = wq[pair % 2]
                    base = 0 if pair < 4 else 1536
                    pp = pair % 4
                    load_w3("pool", t, d_, w_in_ab, base + pp * 128, 128, 0)
                    load_w3("pool", t, d_, w_in_ab, base + 512 + pp * 128, 128, 128)
                    load_w3("pool", t, d_, w_in_ab, base + 1024 + pp * 128, 128, 256)

                ldw(0)
                for pair in range(8):
                    moba = pair < 4
                    if pair + 1 < 8:
                        ldw(pair + 1)
                    wt, dw = wq[pair % 2]
                    gq = 0 if moba else 2
                    gk = 1 if moba else 3
                    for sc in range(4):
                        pt, dp = proj_fm(wt, dw, sc, 128, 0)
                        headnorm(pt, dp, 128, gq, [(qh[0][0][0:64, cs_(sc)], qh[0][1][sc], 0), (qh[1][0][0:64, cs_(sc)], qh[1][1][sc], 64)])
                        pt, dp = proj_fm(wt, dw, sc, 128, 128)
                        headnorm(pt, dp, 128, gk, [(kh[0][0][0:64, cs_(sc)], kh[0][1][sc], 0), (kh[1][0][0:64, cs_(sc)], kh[1][1][sc], 64)])
                        if moba:
                            for hh in range(2):
                                kin = kh[hh][0][0:64, cs_(sc)]
                                kin3 = bass.AP(tensor=kin.tensor, offset=kin.offset, ap=[list(kin.ap[0]), [256, 2], [1, 256]])
                                fw.op("dve", lambda e: e.tensor_reduce(out=km[hh][0][0:64, 2 * sc:2 * sc + 2], in_=kin3, axis=AX.X, op=ALU.add),
                                      r=[kh[hh][1][sc]], w=[km[hh][1]])
                        pv, dpv = psA.next()
                        for j in range(4):
                            tt = sc * 4 + j
                            for kc in range(8):
                                fw.op("pe", lambda e: e.matmul(pv[:, j * 128:(j + 1) * 128], lhsT=xnT[:, kc, tt * 128:(tt + 1) * 128],
                                                               rhs=wt[:, kc, 256:384], start=(kc == 0), stop=(kc == 7)),
                                      r=[dw, dxn[sc]], w=[dpv])
                        for hh in range(2):
                            src = pv[:, hh * 64:hh * 64 + 1]
                            src3 = bass.AP(tensor=src.tensor, offset=src.offset, ap=[list(src.ap[0]), [128, 4], [1, 64]])
                            fw.op("act", lambda e: e.activation(out=vh[hh][0][:, sc * 4:sc * 4 + 4, 0:64], in_=src3, func=AF.Copy),
                                  r=[dpv], w=[vh[hh][1][sc]])
                    if pair == 0:
                        dump("q0", qh[0][0][0:64, :], [64, S], qh[0][1])
                        dump("k0", kh[0][0][0:64, :], [64, S], kh[0][1])
                        dump("v0", vh[0][0][:], [128, 16, 128], vh[0][1])
                    if moba:
                        for hh in range(2):
                            qt_, dq_ = qh[hh]
                            fw.op("dve", lambda e: e.tensor_copy(out=kmb[hh][0][0:64, :], in_=km[hh][0][:]), r=[km[hh][1]], w=[kmb[hh][1]])
                            fw.op("dve", lambda e: e.memset(qt_[64:72, 0:1024], 0.0), w=[dq_[0], dq_[1]])
                            for qtile in range(8, 16):
                                own = qtile // 2
                                sc = qtile // 4
                                pg, dpg = psB.next()
                                fw.op("pe", lambda e: e.matmul(pg[:, 0:8], lhsT=qt_[0:72, qtile * 128:(qtile + 1) * 128], rhs=kmb[hh][0][0:72, 0:8],
                                                               start=True, stop=True), r=[dq_[sc], kmb[hh][1]], w=[dpg])
                                fw.op("dve", lambda e: e.tensor_tensor(out=gm[:], in0=pg[:, 0:8], in1=gneg[:, own - 4, :], op=ALU.add),
                                      r=[dpg, dl0], w=[dgm])
                                fw.op("dve", lambda e: e.max(out=m8[:], in_=gm[:]), r=[dgm], w=[dm8])
                                nt, dnt = nmp[own - 4]
                                fw.op("dve", lambda e: e.tensor_scalar(out=nt[:, 64:64 + own], in0=gm[:, 0:own], scalar1=m8[:, 2:3], scalar2=NEG,
                                                                       op0=ALU.is_lt, op1=ALU.mult), r=[dgm, dm8], w=[dnt])
                                p2, dp2 = psB.next()
                                fw.op("pe", lambda e: e.matmul(p2[0:72, 0:128], lhsT=nt[:, 0:72], rhs=ident_bf[:], start=True, stop=True),
                                      r=[dnt, dcon], w=[dp2])
                                fw.op("act", lambda e: e.activation(out=qt_[64:72, qtile * 128:(qtile + 1) * 128], in_=p2[64:72, 0:128], func=AF.Copy),
                                      r=[dp2], w=[dq_[sc]])
                    if pair == 4:
                        for hh in range(2):
                            fw.op("dve", lambda e: e.memset(qh[hh][0][64:72, :], 0.0), w=qh[hh][1])
                    for hh in range(2):
                        H = pair * 2 + hh
                        E, dE = Eh[hh]
                        M, dM = (None, None) if moba else Mt
                        build_E(H, E, dE, M, dM, stage, dstage)

                        def cb(qc, po, dpo, hh=hh, pair=pair):
                            rd, drd = f32b.next()
                            recip_den(rd, drd, po, dpo)
                            fw.op("dve", lambda e: e.tensor_tensor(out=oT[hh * 64:hh * 64 + 64, pair, cs_(qc)], in0=po[0:64, :], in1=rd[64:128, :], op=ALU.mult),
                                  r=[dpo, drd], w=[doT[pair][qc]])
                        attend(qh[hh][0], qh[hh][1], 72, kh[hh][0], kh[hh][1], vh[hh][0], vh[hh][1], E, dE, causal_tiles, cb)
                pass
                fw.barrier()

        pbuf = None

        def layer1_mixer():
            with contextlib.ExitStack() as st:
                nonlocal pbuf
                pbuf = Rot([(sb("pbuf%d" % i, [128, 512], BF16, st), Dep()) for i in range(4)])
                qa = [(sb("qa%d" % i, [96, S], BF16, st), [Dep() for _ in range(4)]) for i in range(4)]
                ks = (sb("ksT", [96, S], BF16, st), [Dep() for _ in range(4)])
                kw = (sb("kwT", [96, S], BF16, st), [Dep() for _ in range(4)])
                vs = (sb("vsA", [128, 16, 128], BF16, st), [Dep() for _ in range(4)])
                vw = (sb("vwA", [128, 16, 128], BF16, st), [Dep() for _ in range(4)])
                ocmp = [(sb("ocmp%d" % i, [64, S], BF16, st), [Dep() for _ in range(4)]) for i in range(4)]
                tc_ = qa[2]
                tv_ = qa[3]
                kcT = (sb("kcT", [96, 128], BF16, st), Dep())
                vcA = (sb("vcA", [128, 128], BF16, st), Dep())
                Es = (sb("Es", [128, S], BF16, st), Dep())
                Ec = Es
                Ew = (sb("Ew", [128, 640], BF16, st), Dep())
                Mw = (sb("Mw", [128, 640], BF16, st), Dep())
                stage = sb("stage", [128, S], F32, st)
                dstage = Dep()
                gsig = (sb("gsig", [96, S], BF16, st), [Dep() for _ in range(4)])
                gsel = sb("gsel", [96, 48, 64], BF16, st)
                ovl = sb("ovl", [128, 33], BF16, st)
                addm = sb("addm", [128, 16, 32], F32, st)
                forced = sb("forced", [128, 16, 32], F32, st)
                imp = (sb("imp", [128, 16, 32], F32, st), [Dep() for _ in range(16)])
                w1s = (sb("w1s", [96, 32, 128], BF16, st), Dep())
                w1 = [w1s, w1s]
                w2 = [(sb("w2_%d" % i, [128, 64], BF16, st), Dep()) for i in range(2)]
                posT = sb("posT", [96, 64], BF16, st)
                b1c = sb("b1c", [128, 2], F32, st)
                b2k = sb("b2k", [64, 1], F32, st)
                b2v = sb("b2v", [128, 64], F32, st)
                cb1 = sb("cb1", [128, 2], F32, st)
                dcb1 = Dep()
                wA = [(sb("wA%d" % i, [128, 8, 128], BF16, st), Dep()) for i in range(3)]
                wQ = [wA[0], wA[1]]
                wG = (sb("wG", [128, 8, 48], BF16, st), Dep())
                sel_t = {}
                for nm_, shp in [("vals", [128, 32]), ("lt", [128, 32]), ("v2", [128, 32]), ("m8a", [128, 8]), ("m8b", [128, 8]), ("rdi", [128, 4])]:
                    sel_t[nm_] = (sb("sel_" + nm_, shp, F32, st), Dep())
                nmp = (sb("nmp1", [128, 96], BF16, st), Dep())
                gl = Dep()
                fw.op("dve", lambda e: e.memset(gsel[:], 0.0), w=[gl])
                fw.op("dve", lambda e: e.memset(gsig[0][:], 0.0), w=gsig[1])
                fw.op("dve", lambda e: e.memset(posT[:], 0.0), w=[gl])
                fw.op("dve", lambda e: e.memset(w1s[0][:], 0.0), w=[w1s[1]])
                fw.op("dve", lambda e: e.memset(kcT[0][:], 0.0), w=[kcT[1]])
                fw.op("dve", lambda e: e.memset(kw[0][64:96, :], 0.0), w=kw[1])
                for qd in qa:
                    fw.op("dve", lambda e: e.memset(qd[0][64:96, :], 0.0), w=qd[1])
                fw.dma("pool", gsel[0:48, :, :], IN["c_gsel"][:, :, :], w=[gl])
                fw.dma("pool", ovl[0:127, :], IN["c_ovl"][:, :], w=[gl])
                fw.dma("sp", addm[:], IN["c_addmask"][:, :, :], w=[gl])
                fw.dma("sp", forced[:], IN["c_forced"][:, :, :], w=[gl])
                fw.dma("pool", posT[0:64, :], posT_d[:, :], w=[gl])
                fw.dma("sp", b1c[:], b1_d[:, :], w=[gl])
                fw.dma("sp", b2k[:], b2k_d[:, :], w=[gl])
                fw.dma("sp", b2v[:], _dap(b2v_d, 0, [[0, 128], [1, 64]]), w=[gl])
                w1src = [ck_w1.rearrange("(l d) j -> d l j", d=64), cv_w1.rearrange("(l d) j -> d l j", d=64)]
                fw.dma("pool", w2[0][0][:], ck_w2[:, :], w=[w2[0][1]])
                fw.dma("pool", w2[1][0][:], cv_w2[:, :], w=[w2[1][1]])
                fw.dma("pool", ks[0][64:96, :], IN["c_blk_nsa"][:, :], w=ks[1])
                fw.op("dve", lambda e: e.memset(nmp[0][:], 0.0), w=[nmp[1]])
                fw.op("dve", lambda e: e.memset(vs[0][:, :, 64:128], 1.0), w=vs[1])
                fw.op("dve", lambda e: e.memset(vw[0][:, :, 64:128], 1.0), w=vw[1])
                fw.op("dve", lambda e: e.memset(vcA[0][:, 64:128], 1.0), w=[vcA[1]])
                build_M(2, Mw[0], Mw[1], stage, dstage, width=640)
                for i in range(2):
                    fw.dma("pool", w1s[0][0:64, :, :], w1src[i], w=[w1s[1]])
                    pt, dp = psB.next()
                    for l in range(32):
                        fw.op("pe", lambda e: e.matmul(pt[:, 0:1], lhsT=w1[i][0][0:96, l, :], rhs=posT[0:96, i * 32 + l:i * 32 + l + 1],
                                                       start=(l == 0), stop=(l == 31)), r=[w1[i][1], gl], w=[dp])
                    fw.op("dve", lambda e: e.tensor_tensor(out=cb1[:, i:i + 1], in0=pt[:, 0:1], in1=b1c[:, i:i + 1], op=ALU.add),
                          r=[dp, gl], w=[dcb1])
                load_w3("pool", wG[0], wG[1], w_in_nsa, 2560, 48, 0)
                for sc in range(4):
                    pt, dp = proj_fm(wG[0], wG[1], sc, 48, 0)
                    fw.op("act", lambda e: e.activation(out=gsig[0][0:48, cs_(sc)], in_=pt[0:48, :], func=AF.Sigmoid), r=[dp], w=[gsig[1][sc]])

                def gate_bc(h, j, qc):
                    pt, dp = psB.next()
                    fw.op("pe", lambda e: e.matmul(pt[0:64, :], lhsT=gsel[0:96, h * 3 + j, :], rhs=gsig[0][0:96, cs_(qc)], start=True, stop=True),
                          r=[gl, gsig[1][qc]], w=[dp])
                    return pt, dp

                for g in range(4):
                    load_w3("pool", wA[0][0], wA[0][1], w_in_nsa, 1536 + g * 64, 64, 0)
                    load_w3("pool", wA[0][0], wA[0][1], w_in_nsa, 2048 + g * 64, 64, 64)
                    load_w3("pool", wA[1][0], wA[1][1], w_in_nsa, 1024 + g * 64, 64, 0)
                    load_w3("pool", wA[1][0], wA[1][1], w_in_nsa, 1280 + g * 64, 64, 64)
                    load_w3("pool", wA[2][0], wA[2][1], w_in_nsa, 1792 + g * 64, 64, 0)
                    load_w3("pool", wA[2][0], wA[2][1], w_in_nsa, 2304 + g * 64, 64, 64)
                    for sc in range(4):
                        pt, dp = proj_fm(wA[0][0], wA[0][1], sc, 128, 0)
                        headnorm(pt, dp, 128, 5, [(ks[0][0:64, cs_(sc)], ks[1][sc], 0), (kw[0][0:64, cs_(sc)], kw[1][sc], 64)])
                        pt, dp = proj_fm(wA[1][0], wA[1][1], sc, 128, 0)
                        fw.op("act", lambda e: e.activation(out=tc_[0][0:64, cs_(sc)], in_=pt[0:64, :], func=AF.Copy), r=[dp], w=[tc_[1][sc]])
                        fw.op("act", lambda e: e.activation(out=tv_[0][0:64, cs_(sc)], in_=pt[64:128, :], func=AF.Copy), r=[dp], w=[tv_[1][sc]])
                        pv, dpv = psA.next()
                        for j in range(4):
                            tt = sc * 4 + j
                            for kc in range(8):
                                fw.op("pe", lambda e: e.matmul(pv[:, j * 128:(j + 1) * 128], lhsT=xnT[:, kc, tt * 128:(tt + 1) * 128],
                                                               rhs=wA[2][0][:, kc, :], start=(kc == 0), stop=(kc == 7)),
                                      r=[wA[2][1], dxn[sc]], w=[dpv])
                        for hh, vdst in enumerate((vs, vw)):
                            src = pv[:, hh * 64:hh * 64 + 1]
                            src3 = bass.AP(tensor=src.tensor, offset=src.offset, ap=[list(src.ap[0]), [128, 4], [1, 64]])
                            fw.op("act", lambda e: e.activation(out=vdst[0][:, sc * 4:sc * 4 + 4, 0:64], in_=src3, func=AF.Copy),
                                  r=[dpv], w=[vdst[1][sc]])
                    for i, tsrc in enumerate((tc_, tv_)):
                        fw.dma("pool", w1s[0][0:64, :, :], w1src[i], w=[w1s[1]])
                        ph, dph = psA.next()
                        for l in range(32):
                            a = tsrc[0][0:96, l:l + 1]
                            rhs = bass.AP(tensor=a.tensor, offset=a.offset, ap=[list(a.ap[0]), [16, 127]])
                            fw.op("pe", lambda e: e.matmul(ph[:, 0:127], lhsT=w1[i][0][0:96, l, :], rhs=rhs, start=(l == 0), stop=(l == 31)),
                                  r=[w1[i][1]] + tsrc[1], w=[dph])
                        xg, dxg = f32b.next()
                        tg_, dtg = f32b.next()
                        fw.op("act", lambda e: e.activation(out=xg[:, 0:127], in_=ph[:, 0:127], func=AF.Identity, bias=cb1[:, i:i + 1], scale=1.0),
                              r=[dph, dcb1], w=[dxg])
                        fw.op("dve", lambda e: e.tensor_tensor(out=tg_[:, 0:127], in0=xg[:, 0:127], in1=xg[:, 0:127], op=ALU.mult), r=[dxg], w=[dtg])
                        fw.op("dve", lambda e: e.tensor_scalar(out=tg_[:, 0:127], in0=tg_[:, 0:127], scalar1=0.044715, scalar2=1.0, op0=ALU.mult, op1=ALU.add),
                              r=[dtg], w=[dtg])
                        fw.op("dve", lambda e: e.tensor_tensor(out=tg_[:, 0:127], in0=tg_[:, 0:127], in1=xg[:, 0:127], op=ALU.mult), r=[dxg, dtg], w=[dtg])
                        fw.op("act", lambda e: e.activation(out=tg_[:, 0:127], in_=tg_[:, 0:127], func=AF.Sigmoid, scale=1.5957691216), r=[dtg], w=[dtg])
                        gb, dgb = sqb.next()
                        fw.op("dve", lambda e: e.tensor_tensor(out=gb[:, 0:127], in0=tg_[:, 0:127], in1=xg[:, 0:127], op=ALU.mult), r=[dxg, dtg], w=[dgb])
                        if i == 0:
                            pk, dpk = psA.next()
                            fw.op("pe", lambda e: e.matmul(pk[0:64, 0:127], lhsT=w2[0][0][:, :], rhs=gb[:, 0:127], start=True, stop=True),
                                  r=[w2[0][1], dgb], w=[dpk])
                            kf, dkf = f32b.next()
                            fw.op("act", lambda e: e.activation(out=kf[0:64, 0:127], in_=pk[0:64, 0:127], func=AF.Identity, bias=b2k[:, 0:1], scale=1.0),
                                  r=[dpk, gl], w=[dkf])
                            sq, dsq = sqb.next()
                            fw.op("act", lambda e: e.activation(out=sq[0:64, 0:127], in_=kf[0:64, 0:127], func=AF.Square), r=[dkf], w=[dsq])
                            ps2, dp2 = psB.next()
                            fw.op("pe", lambda e: e.matmul(ps2[0:64, 0:127], lhsT=blk_ones[0:128, 0:64], rhs=sq[0:128, 0:127], start=True, stop=True),
                                  r=[dsq, dcon], w=[dp2])
                            rt, drt = f32b.next()
                            fw.op("act", lambda e: e.activation(out=rt[0:64, 0:127], in_=ps2[0:64, 0:127], func=AF.Ln, bias=EPS, scale=1.0 / 64), r=[dp2], w=[drt])
                            fw.op("act", lambda e: e.activation(out=rt[0:64, 0:127], in_=rt[0:64, 0:127], func=AF.Exp, scale=-0.5), r=[drt], w=[drt])
                            fw.op("dve", lambda e: e.scalar_tensor_tensor(out=kcT[0][0:64, 0:127], in0=kf[0:64, 0:127], scalar=hg[0:64, 6:7], in1=rt[0:64, 0:127],
                                                                          op0=ALU.mult, op1=ALU.mult), r=[dkf, drt, dcon], w=[kcT[1]])
                        else:
                            pk, dpk = psA.next()
                            fw.op("pe", lambda e: e.matmul(pk[0:127, 0:64], lhsT=gb[:, 0:127], rhs=w2[1][0][:, :], start=True, stop=True),
                                  r=[w2[1][1], dgb], w=[dpk])
                            fw.op("dve", lambda e: e.tensor_tensor(out=vcA[0][0:127, 0:64], in0=pk[0:127, 0:64], in1=b2v[0:127, :], op=ALU.add),
                                  r=[dpk, gl], w=[vcA[1]])
                    if g == 0:
                        dump("kcT", kcT[0][:], [64, 128], [kcT[1]])
                        dump("vcA", vcA[0][:], [128, 128], [vcA[1]])
                        dump("ksT", ks[0][0:64, :], [64, S], ks[1])
                    for pr in range(2):
                        t, d_ = wQ[pr]
                        load_w3("pool", t, d_, w_in_nsa, (g * 4 + pr * 2) * 64, 128, 0)
                    for pr in range(2):
                        t, d_ = wQ[pr]
                        for sc in range(4):
                            pt, dp = proj_fm(t, d_, sc, 128, 0)
                            a, b_ = qa[pr * 2], qa[pr * 2 + 1]
                            headnorm(pt, dp, 128, 4, [(a[0][0:64, cs_(sc)], a[1][sc], 0), (b_[0][0:64, cs_(sc)], b_[1][sc], 64)])
                    for qd in qa:
                        fw.op("dve", lambda e: e.memset(qd[0][64:96, 0:1024], 0.0), w=[qd[1][0], qd[1][1]])
                    for qt in range(8, 16):
                        fw.op("dve", lambda e: e.memset(imp[0][:, qt, :], 0.0), w=[imp[1][qt]])
                    for hl in range(4):
                        h = g * 4 + hl
                        pair, hh = h // 2, h % 2
                        qt_, dq_ = qa[hl]
                        build_E(h, Ec[0], Ec[1], None, None, stage, dstage, pstride=16, off=31, rows=127)
                        items = []
                        for qc in range(4):
                            def post(pb, dpb, qc=qc):
                                if qc < 2:
                                    return
                                pi, dpi = psB.next()
                                for j in range(4):
                                    fw.op("pe", lambda e: e.matmul(pi[:, j * 64:j * 64 + 33], lhsT=pb[0:127, j * 128:(j + 1) * 128], rhs=ovl[0:127, :],
                                                                   start=True, stop=True), r=[dpb, gl], w=[dpi])
                                rdi, drdi = sel_t["rdi"]
                                src = pi[:, 32:33]
                                src3 = bass.AP(tensor=src.tensor, offset=src.offset, ap=[list(src.ap[0]), [64, 4]])
                                fw.op("dve", lambda e: e.reciprocal(out=rdi[:, 0:4], in_=src3), r=[dpi], w=[drdi])
                                for j in range(4):
                                    qtile = qc * 4 + j
                                    fw.op("dve", lambda e: e.scalar_tensor_tensor(out=imp[0][:, qtile, :], in0=pi[:, j * 64:j * 64 + 32], scalar=rdi[:, j:j + 1],
                                                                                  in1=imp[0][:, qtile, :], op0=ALU.mult, op1=ALU.add),
                                          r=[dpi, drdi, imp[1][qtile]], w=[imp[1][qtile]])

                            def cb_cmp(qc, po, dpo, h=h, hl=hl):
                                rd, drd = f32b.next()
                                recip_den(rd, drd, po, dpo)
                                fw.op("dve", lambda e: e.tensor_tensor(out=rd[0:64, :], in0=po[0:64, :], in1=rd[64:128, :], op=ALU.mult), r=[dpo, drd], w=[drd])
                                pgb, dpgb = gate_bc(h, 0, qc)
                                fw.op("dve", lambda e: e.tensor_tensor(out=ocmp[hl][0][0:64, cs_(qc)], in0=rd[0:64, :], in1=pgb[0:64, :], op=ALU.mult),
                                      r=[drd, dpgb], w=[ocmp[hl][1][qc]])
                            items.append(dict(
                                rows=127, n=512, lo=0, hi=512, K=96,
                                lhsT=kcT[0][0:96, 0:127], rhs=qt_[0:96, cs_(qc)], sdeps=[kcT[1], dq_[qc]],
                                E=Ec[0][0:127, cs_(qc)], dE=Ec[1], v=vcA[0][0:127, :], dv=vcA[1],
                                first=True, last=True, cb=cb_cmp, qc=qc, post=post))
                        run_items(items)
                    vals, dvals = sel_t["vals"]
                    lt, dlt = sel_t["lt"]
                    v2, dv2 = sel_t["v2"]
                    m8a, dm8a = sel_t["m8a"]
                    m8b, dm8b = sel_t["m8b"]
                    for qtile in range(8, 16):
                        sc = qtile // 4
                        fw.op("dve", lambda e: e.tensor_tensor(out=vals[:], in0=imp[0][:, qtile, :], in1=addm[:, qtile, :], op=ALU.add),
                              r=[imp[1][qtile], gl], w=[dvals])
                        fw.op("dve", lambda e: e.max(out=m8a[:], in_=vals[:]), r=[dvals], w=[dm8a])
                        fw.op("dve", lambda e: e.tensor_scalar(out=lt[:], in0=vals[:], scalar1=m8a[:, 7:8], scalar2=None, op0=ALU.is_lt), r=[dvals, dm8a], w=[dlt])
                        fw.op("dve", lambda e: e.tensor_tensor(out=v2[:], in0=vals[:], in1=lt[:], op=ALU.mult), r=[dvals, dlt], w=[dv2])
                        fw.op("dve", lambda e: e.tensor_scalar(out=lt[:], in0=lt[:], scalar1=-1.0, scalar2=BIG, op0=ALU.add, op1=ALU.mult), r=[dlt, dv2], w=[dlt])
                        fw.op("dve", lambda e: e.tensor_tensor(out=v2[:], in0=v2[:], in1=lt[:], op=ALU.add), r=[dlt, dv2], w=[dv2])
                        fw.op("dve", lambda e: e.max(out=m8b[:], in_=v2[:]), r=[dv2], w=[dm8b])
                        fw.op("dve", lambda e: e.tensor_scalar(out=lt[:], in0=vals[:], scalar1=m8b[:, 4:5], scalar2=None, op0=ALU.is_ge), r=[dvals, dm8b, dlt], w=[dlt])
                        fw.op("dve", lambda e: e.tensor_tensor(out=lt[:], in0=lt[:], in1=forced[:, qtile, :], op=ALU.max), r=[dlt, gl], w=[dlt])
                        fw.op("dve", lambda e: e.tensor_scalar(out=nmp[0][:, 64:96], in0=lt[:], scalar1=-1.0, scalar2=-NEG, op0=ALU.add, op1=ALU.mult),
                              r=[dlt], w=[nmp[1]])
                        p2, dp2 = psB.next()
                        fw.op("pe", lambda e: e.matmul(p2[0:96, 0:128], lhsT=nmp[0][:, 0:96], rhs=ident_bf[:], start=True, stop=True),
                              r=[nmp[1], dcon], w=[dp2])
                        for qd in qa:
                            fw.op("act", lambda e: e.activation(out=qd[0][64:96, qtile * 128:(qtile + 1) * 128], in_=p2[64:96, 0:128], func=AF.Copy),
                                  r=[dp2], w=[qd[1][sc]])
                    if g == 0:
                        dump("imp", imp[0][:], [128, 16, 32], imp[1])
                        dump("qa0", qa[0][0][:], [96, S], qa[0][1])
                    for hl in range(4):
                        h = g * 4 + hl
                        pair, hh = h // 2, h % 2
                        build_E(h, Es[0], Es[1], None, None, stage, dstage)
                        fw.op("dve", lambda e: e.tensor_tensor(out=Ew[0][:, :], in0=Es[0][:, 0:640], in1=Mw[0][:, :], op=ALU.mult),
                              r=[Es[1], Mw[1]], w=[Ew[1]])
                        acc = {}

                        def cb_slc(qc, po, dpo, h=h):
                            rd, drd = f32b.next()
                            recip_den(rd, drd, po, dpo)
                            fw.op("dve", lambda e: e.tensor_tensor(out=rd[0:64, :], in0=po[0:64, :], in1=rd[64:128, :], op=ALU.mult), r=[dpo, drd], w=[drd])
                            pgb, dpgb = gate_bc(h, 1, qc)
                            fw.op("dve", lambda e: e.tensor_tensor(out=rd[0:64, :], in0=rd[0:64, :], in1=pgb[0:64, :], op=ALU.mult), r=[drd, dpgb], w=[drd])
                            acc[qc] = (rd, drd)

                        def cb_win(qc, po, dpo, h=h, pair=pair, hh=hh, hl=hl):
                            rd, drd = f32b.next()
                            a_, da_ = acc[qc]
                            recip_den(rd, drd, po, dpo)
                            fw.op("dve", lambda e: e.tensor_tensor(out=rd[0:64, :], in0=po[0:64, :], in1=rd[64:128, :], op=ALU.mult), r=[dpo, drd], w=[drd])
                            pgb, dpgb = gate_bc(h, 2, qc)
                            fw.op("dve", lambda e: e.tensor_tensor(out=rd[0:64, :], in0=rd[0:64, :], in1=pgb[0:64, :], op=ALU.mult), r=[drd, dpgb], w=[drd])
                            fw.op("dve", lambda e: e.tensor_tensor(out=rd[0:64, :], in0=rd[0:64, :], in1=a_[0:64, :], op=ALU.add), r=[drd, da_], w=[drd])
                            fw.op("dve", lambda e: e.tensor_tensor(out=rd[0:64, :], in0=rd[0:64, :], in1=ocmp[hl][0][0:64, cs_(qc)], op=ALU.add),
                                  r=[drd, ocmp[hl][1][qc]], w=[drd])
                            fw.op("act", lambda e: e.activation(out=oT[hh * 64:hh * 64 + 64, pair, cs_(qc)], in_=rd[0:64, :], func=AF.Copy),
                                  r=[drd], w=[doT[pair][qc]])

                        items = []
                        for qc in range(4):
                            items += make_items(qa[hl][0], qa[hl][1], 96, ks[0], ks[1], vs[0], vs[1], Es[0], Es[1], causal_tiles, cb_slc, chunks=[qc])
                            items += make_items(qa[hl][0], qa[hl][1], 96, kw[0], kw[1], vw[0], vw[1], Ew[0], Ew[1], win_tiles, cb_win, chunks=[qc])
                        run_items(items)
                pass
                fw.barrier()

        alldh = [d_ for row in dh for d_ in row]
        alldo = [d_ for row in doT for d_ in row]
        with contextlib.ExitStack() as st:
            hT = sb("hT", [128, 8, S], F32, st)
            load_h(xT, [])
            rmsnorm(0)
            dump("xn0", xnT[:], [128, 8, S], dxn)
            fw.barrier()
        if upto >= 1:
            layer0_mixer()
            dump("oT0", oT[:], [128, 8, S], alldo)
        with contextlib.ExitStack() as st:
            hT = sb("hT", [128, 8, S], F32, st)
            load_h(xT, [])
            if upto >= 1:
                out_proj(w_out_ab, st)
                dump("hmix0", hT[:], [128, 8, S], alldh)
                fw.barrier()
            if upto >= 2:
                ffn(0)
                dump("hffn0", hT[:], [128, 8, S], alldh)
            if upto >= 3:
                ple(0)
                dump("h0", hT[:], [128, 8, S], alldh)
            if upto >= 4:
                rmsnorm(3)
                store_h(hS, [dhS])
            fw.barrier()
            if upto < 4:
                store_h(outT, [])
        if upto >= 4:
            layer1_mixer()
            dump("oT1", oT[:], [128, 8, S], alldo)
            with contextlib.ExitStack() as st:
                hT = sb("hT", [128, 8, S], F32, st)
                load_h(hS, [dhS])
                out_proj(w_out_nsa, st)
                dump("hmix1", hT[:], [128, 8, S], alldh)
                fw.barrier()
                if upto >= 5:
                    ffn(1)
                if upto >= 6:
                    ple(1)
                store_h(outT, [])
                fw.barrier()
        fw.finish("sp")
        build_nc.stats = (fw.n_ins, fw.n_wait)
    return nc, consts, DBG


def host_inputs(inputs, b, consts):
    f = lambda a: np.ascontiguousarray(np.asarray(a, dtype=np.float32))
    m = {}
    m["xT"] = f(inputs["x"][b].T)
    m["pT"] = f(np.transpose(inputs["p"][:, b], (0, 2, 1)))
    rb = np.asarray(inputs["rel_bias"], np.float32)
    dd = np.maximum(2047 - np.arange(WHL), 0)
    whb = rb[_bucket(dd), :].T.copy()
    whb[:, 2048:] = NEG
    m["whb"] = f(whb)
    gl = []
    for layer in range(2):
        for nm in ("norm_mix", "norm_ffn", "norm_ple"):
            gl.append(np.asarray(inputs[nm][layer], np.float32).reshape(8, 128).T)
    m["gains"] = f(np.concatenate(gl, axis=1))
    t2 = lambda a: np.concatenate([np.asarray(a, np.float32)] * 2)
    hg = np.zeros((128, 8), np.float32)
    hg[:, 0] = t2(inputs["qn_moba"][0])
    hg[:, 1] = t2(inputs["kn_moba"][0])
    hg[:, 2] = t2(inputs["qn_dil"][0])
    hg[:, 3] = t2(inputs["kn_dil"][0])
    hg[:, 4] = t2(inputs["qn_nsa"][0])
    hg[:, 5] = np.concatenate([np.asarray(inputs["kn_slc"][0], np.float32), np.asarray(inputs["kn_win"][0], np.float32)])
    hg[:, 6] = t2(inputs["kn_cmp"][0])
    m["hg"] = hg
    m["posT"] = f(np.concatenate([np.asarray(inputs["cmp_k_pos"][0]).T, np.asarray(inputs["cmp_v_pos"][0]).T], axis=1))
    m["b1c"] = f(np.stack([inputs["cmp_k_b1"][0], inputs["cmp_v_b1"][0]], axis=1))
    m["b2k"] = f(np.asarray(inputs["cmp_k_b2"][0]).reshape(64, 1))
    m["b2v"] = f(np.asarray(inputs["cmp_v_b2"][0]).reshape(1, 64))
    m["w_in_ab"] = f(inputs["w_in_ab"][0])
    m["w_out_ab"] = f(inputs["w_out_ab"][0])
    m["w_in_nsa"] = f(inputs["w_in_nsa"][0])
    m["w_out_nsa"] = f(inputs["w_out_nsa"][0])
    for nm in ("w_ffn_gate", "w_ffn_up", "w_ffn_down", "w_ple_proj", "w_ple_gate"):
        m[nm] = f(inputs[nm])
    m["cmp_k_w1"] = f(inputs["cmp_k_w1"][0])
    m["cmp_k_w2"] = f(inputs["cmp_k_w2"][0])
    m["cmp_v_w1"] = f(inputs["cmp_v_w1"][0])
    m["cmp_v_w2"] = f(inputs["cmp_v_w2"][0])
    for k, v in consts.items():
        m[k] = v
    return m


def kernel(**inputs):
    nc, consts, _ = build_nc()
    in_maps = [host_inputs(inputs, b, consts) for b in range(8)]
    res = run_bass_kernel_spmd(nc, in_maps, core_ids=list(range(8)))
    out = np.stack([np.asarray(r["outT"], np.float32).T for r in res.results], axis=0)
    return np.ascontiguousarray(out.astype(np.float32))
```

```python
import math
import contextlib
import numpy as np
import concourse.bass as bass
import concourse.mybir as mybir
from concourse.bass_utils import run_bass_kernel_spmd

F32 = mybir.dt.float32
BF16 = mybir.dt.bfloat16
AF = mybir.ActivationFunctionType
ALU = mybir.AluOpType
AX = mybir.AxisListType

S = 2048
D = 1024
FH = 2816
NF = 22
WHL = 4352
EPS = 1e-6
NEG = -30000.0
BIG = 3.0e38


class Dep:
    __slots__ = ("w", "r")

    def __init__(self):
        self.w = None
        self.r = []


class FW:
    NDMA = 24

    def __init__(self, nc, es):
        self.nc = nc
        self.engs = {"pe": nc.tensor, "act": nc.scalar, "dve": nc.vector, "pool": nc.gpsimd, "sp": nc.sync}
        self.sems = {}
        self.cnt = {}
        for k in self.engs:
            self.sems[k] = es.enter_context(nc.semaphore("sem_" + k))
            self.cnt[k] = 0
        for i in range(self.NDMA):
            k = ("dma", i)
            self.sems[k] = es.enter_context(nc.semaphore("sem_dma%d" % i))
            self.cnt[k] = 0
        self.seen = {e: {} for e in self.engs}
        self.dma_rr = {"sp": 0, "pool": 0, "act": 0}
        self.n_ins = 0
        self.n_wait = 0

    def _wait(self, eng, deps):
        seen = self.seen[eng]
        need = {}
        for d in deps:
            if d is None:
                continue
            k, v = d
            if k == "pe" and eng == "pe":
                continue
            if seen.get(k, 0) >= v:
                continue
            if need.get(k, 0) < v:
                need[k] = v
        for k, v in need.items():
            self.engs[eng].wait_ge(self.sems[k], v)
            seen[k] = v
            self.n_wait += 1

    @staticmethod
    def _collect(r, w):
        deps = []
        for t in r:
            deps.append(t.w)
        for t in w:
            deps.append(t.w)
            deps.extend(t.r)
        return deps

    def _mark(self, tok, r, w):
        for t in w:
            t.w = tok
            t.r = []
        for t in r:
            t.r.append(tok)
            if len(t.r) > 64:
                best = {}
                for k, v in t.r:
                    if best.get(k, 0) < v:
                        best[k] = v
                t.r = list(best.items())

    def op(self, eng, fn, r=(), w=()):
        self._wait(eng, self._collect(r, w))
        ins = fn(self.engs[eng])
        self.cnt[eng] += 1
        ins.then_inc(self.sems[eng], 1)
        self._mark((eng, self.cnt[eng]), r, w)
        self.n_ins += 1

    def dma(self, q, out, in_, r=(), w=()):
        half = self.NDMA // 2
        i = self.dma_rr[q]
        self.dma_rr[q] = (i + 1) % half
        k = ("dma", i + (half if q == "pool" else 0))
        deps = self._collect(r, w)
        if self.cnt[k] > 0:
            deps.append((k, self.cnt[k]))
        self._wait(q, deps)
        ins = self.engs[q].dma_start(out=out, in_=in_)
        self.cnt[k] += 16
        ins.then_inc(self.sems[k], 16)
        self._mark((k, self.cnt[k]), r, w)
        self.n_ins += 1

    def barrier(self):
        allk = [(k, v) for k, v in self.cnt.items() if v > 0]
        for e in self.engs:
            self._wait(e, allk)

    def finish(self, eng="sp"):
        allk = [(k, v) for k, v in self.cnt.items() if v > 0]
        self._wait(eng, allk)


class Rot:
    def __init__(self, items):
        self.items = items
        self.i = 0

    def next(self):
        t = self.items[self.i]
        self.i = (self.i + 1) % len(self.items)
        return t


def _bucket(d):
    n = np.maximum(d, 0)
    nf = np.maximum(n, 1).astype(np.float32)
    large = 16 + (np.log(nf / np.float32(16)) / np.float32(math.log(128.0)) * np.float32(16)).astype(np.int32)
    return np.where(n < 16, n, np.minimum(large, 31))


def _static_consts():
    c = {}
    m = np.arange(WHL)
    d = 2047 - m
    wm = np.zeros((3, WHL), np.float32)
    wm[0] = (d >= 0)
    wm[1] = (d >= 0) * ((d <= 128).astype(np.float32) + ((d % 4 == 0) & (d <= 512)) + ((d % 16 == 0) & (d <= 2048)))
    wm[2] = (d >= 0) & (d < 512)
    c["c_wm"] = wm
    c["c_ident"] = np.eye(128, dtype=np.float32)
    k = np.arange(S)
    c["c_blk_moba"] = (k[None, :] // 256 == np.arange(8)[:, None]).astype(np.float32)
    c["c_blk_nsa"] = (k[None, :] // 64 == np.arange(32)[:, None]).astype(np.float32)
    g = np.zeros((128, 4, 8), np.float32)
    for i, own in enumerate(range(4, 8)):
        g[:, i, own:] = -BIG
    c["c_gneg"] = g
    add = np.full((128, 16, 32), -BIG, np.float32)
    forced = np.zeros((128, 16, 32), np.float32)
    for qt in range(16):
        for q in range(128):
            cur = (qt * 128 + q) // 64
            for n in (0, cur, cur - 1):
                if n >= 0:
                    forced[q, qt, n] = 1.0
            for n in range(1, cur - 1):
                add[q, qt, n] = 0.0
    c["c_addmask"] = add
    c["c_forced"] = forced
    cs = np.arange(127) * 16
    ss = np.arange(32) * 64
    ov = np.maximum(np.minimum(cs[:, None] + 32, ss[None, :] + 64) - np.maximum(cs[:, None], ss[None, :]), 0)
    ovl = np.ones((127, 33), np.float32)
    ovl[:, :32] = ov
    c["c_ovl"] = ovl
    gs = np.zeros((48, 48, 64), np.float32)
    for i in range(48):
        gs[i, i, :] = 1.0
    c["c_gsel"] = gs
    return c


_CONST_SHAPES = None


def _dap(t, offset, ap):
    return bass.AP(tensor=t.tensor, offset=offset, ap=[list(a) for a in ap])


def build_nc(upto=99, dbg=()):
    nc = bass.Bass("TRN2", target_bir_lowering=False)
    consts = _static_consts()
    IN = {}

    def din(name, shape):
        IN[name] = nc.dram_tensor(name, list(shape), F32, kind="ExternalInput").ap()
        return IN[name]

    xT = din("xT", [D, S])
    pT = din("pT", [2, 256, S])
    whb = din("whb", [16, WHL])
    gains_d = din("gains", [128, 48])
    hg_d = din("hg", [128, 8])
    posT_d = din("posT", [64, 64])
    b1_d = din("b1c", [128, 2])
    b2k_d = din("b2k", [64, 1])
    b2v_d = din("b2v", [1, 64])
    w_in_ab = din("w_in_ab", [D, 3072])
    w_out_ab = din("w_out_ab", [D, D])
    w_in_nsa = din("w_in_nsa", [D, 2608])
    w_out_nsa = din("w_out_nsa", [D, D])
    w_g = din("w_ffn_gate", [2, D, FH])
    w_u = din("w_ffn_up", [2, D, FH])
    w_d = din("w_ffn_down", [2, FH, D])
    w_pp = din("w_ple_proj", [2, 256, D])
    w_pg = din("w_ple_gate", [2, D, D])
    ck_w1 = din("cmp_k_w1", [2048, 128])
    ck_w2 = din("cmp_k_w2", [128, 64])
    cv_w1 = din("cmp_v_w1", [2048, 128])
    cv_w2 = din("cmp_v_w2", [128, 64])
    for k, v in consts.items():
        din(k, v.shape)
    outT = nc.dram_tensor("outT", [D, S], F32, kind="ExternalOutput").ap()
    DBG = {}

    with contextlib.ExitStack() as es:
        fw = FW(nc, es)

        uniq = [0]

        def sb(name, shape, dt=F32, stack=es):
            uniq[0] += 1
            return stack.enter_context(nc.sbuf_tensor("%s_%d" % (name, uniq[0]), list(shape), dt))

        def pst(name):
            return es.enter_context(nc.psum_tensor(name, [128, 512], F32))

        psS = Rot([(pst("psS%d" % i), Dep()) for i in range(3)])
        psO = Rot([(pst("psO%d" % i), Dep()) for i in range(2)])
        psA = Rot([(pst("psM%d" % i), Dep()) for i in range(3)])
        psB = psA

        def dump(name, ap, shape, deps):
            if name not in dbg:
                return
            t = nc.dram_tensor("dbg_" + name, list(shape), ap.dtype if hasattr(ap, "dtype") else F32, kind="ExternalOutput").ap()
            DBG[name] = t
            fw.dma("sp", t, ap, r=deps)

        hS = nc.dram_tensor("hS", [D, S], F32, kind="Internal").ap()
        dhS = Dep()
        dh = [[Dep() for _ in range(4)] for _ in range(8)]
        xnT = sb("xnT", [128, 8, S], BF16)
        dxn = [Dep() for _ in range(4)]
        oT = sb("oT", [128, 8, S], BF16)
        doT = [[Dep() for _ in range(4)] for _ in range(8)]
        hT = None
        gains = sb("gains_sb", [128, 48])
        hg = sb("hg_sb", [128, 8])
        dcon = Dep()
        ones_bf = sb("ones_bf", [128, 128], BF16)
        blk_ones = sb("blk_ones", [128, 128], BF16)
        ident_bf = sb("ident_bf", [128, 128], BF16)
        sqb = Rot([(sb("sqb%d" % i, [128, 512], BF16), Dep()) for i in range(2)])
        f32b = Rot([(sb("f32b%d" % i, [128, 512]), Dep()) for i in range(4)])

        fw.dma("sp", gains[:], gains_d[:, :], w=[dcon])
        fw.dma("sp", hg[:], hg_d[:, :], w=[dcon])
        fw.dma("pool", ident_bf[:], IN["c_ident"][:, :], w=[dcon])
        fw.op("dve", lambda e: e.memset(ones_bf[:], 1.0), w=[dcon])
        fw.op("dve", lambda e: e.memset(blk_ones[:], 0.0), w=[dcon])
        fw.op("dve", lambda e: e.memset(blk_ones[0:64, 0:64], 1.0), w=[dcon])
        fw.op("dve", lambda e: e.memset(blk_ones[64:128, 64:128], 1.0), w=[dcon])
        def load_h(src, dsrc):
            v = src.rearrange("(c p) s -> p c s", p=128)
            for c in range(8):
                for sc in range(4):
                    fw.dma("sp", hT[:, c, sc * 512:(sc + 1) * 512], v[:, c, sc * 512:(sc + 1) * 512], r=dsrc, w=[dh[c][sc]])

        def store_h(dst, ddst):
            v = dst.rearrange("(c p) s -> p c s", p=128)
            for c in range(8):
                for sc in range(4):
                    fw.dma("sp", v[:, c, sc * 512:(sc + 1) * 512], hT[:, c, sc * 512:(sc + 1) * 512], r=[dh[c][sc]], w=ddst)

        def cs_(sc):
            return slice(sc * 512, (sc + 1) * 512)

        def rmsnorm(gidx):
            for sc in range(4):
                cs = cs_(sc)
                pt, dp = psB.next()
                for c in range(8):
                    sq, dsq = sqb.next()
                    fw.op("act", lambda e: e.activation(out=sq[:], in_=hT[:, c, cs], func=AF.Square), r=[dh[c][sc]], w=[dsq])
                    fw.op("pe", lambda e: e.matmul(pt[:], lhsT=ones_bf[:], rhs=sq[:], start=(c == 0), stop=(c == 7)),
                          r=[dsq, dcon], w=[dp])
                rt, drt = f32b.next()
                fw.op("act", lambda e: e.activation(out=rt[:], in_=pt[:], func=AF.Ln, bias=EPS, scale=1.0 / D), r=[dp], w=[drt])
                fw.op("act", lambda e: e.activation(out=rt[:], in_=rt[:], func=AF.Exp, scale=-0.5), r=[drt], w=[drt])
                for c in range(8):
                    fw.op("dve", lambda e: e.scalar_tensor_tensor(
                        out=xnT[:, c, cs], in0=hT[:, c, cs], scalar=gains[:, gidx * 8 + c:gidx * 8 + c + 1], in1=rt[:],
                        op0=ALU.mult, op1=ALU.mult), r=[dh[c][sc], drt, dcon], w=[dxn[sc]])

        def proj_fm(wt, dw, sc, M, c0=0):
            pt, dp = psA.next()
            for kc in range(8):
                fw.op("pe", lambda e: e.matmul(pt[0:M, :], lhsT=wt[:, kc, c0:c0 + M], rhs=xnT[:, kc, cs_(sc)],
                                               start=(kc == 0), stop=(kc == 7)), r=[dw, dxn[sc]], w=[dp])
            return pt, dp

        def headnorm(pt, dp, M, gcol, outs):
            sq, dsq = sqb.next()
            fw.op("act", lambda e: e.activation(out=sq[0:M, :], in_=pt[0:M, :], func=AF.Square), r=[dp], w=[dsq])
            ps2, dp2 = psB.next()
            fw.op("pe", lambda e: e.matmul(ps2[0:M, :], lhsT=blk_ones[0:M, 0:M], rhs=sq[0:M, :], start=True, stop=True),
                  r=[dsq, dcon], w=[dp2])
            rt, drt = f32b.next()
            fw.op("act", lambda e: e.activation(out=rt[0:M, :], in_=ps2[0:M, :], func=AF.Ln, bias=EPS, scale=1.0 / 64), r=[dp2], w=[drt])
            fw.op("act", lambda e: e.activation(out=rt[0:M, :], in_=rt[0:M, :], func=AF.Exp, scale=-0.5), r=[drt], w=[drt])
            for (dst, ddst, r0) in outs:
                fw.op("dve", lambda e: e.scalar_tensor_tensor(
                    out=dst, in0=pt[r0:r0 + 64, :], scalar=hg[r0:r0 + 64, gcol:gcol + 1], in1=rt[r0:r0 + 64, :],
                    op0=ALU.mult, op1=ALU.mult), r=[dp, drt, dcon], w=[ddst])

        def load_w3(q, wt, dw, src2d, c0, ncols, dst_c0=0):
            v = src2d.rearrange("(kc p) n -> p kc n", p=128)
            fw.dma(q, wt[:, :, dst_c0:dst_c0 + ncols], v[:, :, c0:c0 + ncols], w=[dw])

        def rev(t, n, rows=128):
            a = t[0:rows, n - 1:n]
            return bass.AP(tensor=a.tensor, offset=a.offset, ap=[list(a.ap[0]), [-1, n]])

        pbuf_items = []

        LOOK = 2
        CBDELAY = 2

        def make_items(qt, dq, K, ktile, dk, vt, dv, E, dE, tiles_fn, out_cb, chunks=range(4)):
            items = []
            for qc in chunks:
                c0 = qc * 512
                tiles = tiles_fn(qc)
                assert tiles[0][1] == 0 and tiles[0][2] == 512
                for idx, (kt, lo, hi) in enumerate(tiles):
                    n = hi - lo
                    u0 = c0 + lo - kt * 128
                    items.append(dict(
                        rows=128, n=n, lo=lo, hi=hi, K=K,
                        lhsT=ktile[0:K, kt * 128:(kt + 1) * 128], rhs=qt[0:K, c0 + lo:c0 + hi], sdeps=[dk[kt // 4], dq[qc]],
                        E=E[:, u0:u0 + n], dE=dE, v=vt[:, kt, :], dv=dv[kt // 4],
                        first=(idx == 0), last=(idx == len(tiles) - 1), cb=out_cb, qc=qc, post=None))
            return items

        def run_items(items):
            staged = {}
            cur = [None]
            pend = []
            n_it = len(items)

            def fire(force_to):
                while pend and (pend[0][0] <= 0 or len(pend) > force_to):
                    pend.pop(0)[1]()

            for j in range(n_it + LOOK):
                if j < n_it:
                    it = items[j]
                    pss, dps = psS.next()
                    fw.op("pe", lambda e: e.matmul(pss[0:it["rows"], 0:it["n"]], lhsT=it["lhsT"], rhs=it["rhs"], start=True, stop=True),
                          r=it["sdeps"], w=[dps])
                    staged[j] = (pss, dps)
                i = j - LOOK
                if i < 0:
                    continue
                it = items[i]
                pss, dps = staged.pop(i)
                R, n = it["rows"], it["n"]
                pb, dpb = pbuf.next()
                fw.op("act", lambda e: e.activation(out=pb[0:R, 0:n], in_=pss[0:R, 0:n], func=AF.Exp, scale=0.125), r=[dps], w=[dpb])
                fw.op("dve", lambda e: e.tensor_tensor(out=pb[0:R, 0:n], in0=pb[0:R, 0:n], in1=it["E"], op=ALU.mult),
                      r=[it["dE"], dpb], w=[dpb])
                if it["first"]:
                    fire(1)
                    cur[0] = psO.next()
                po, dpo = cur[0]
                fw.op("pe", lambda e: e.matmul(po[:, it["lo"]:it["hi"]], lhsT=it["v"], rhs=pb[0:R, 0:n], start=it["first"], stop=it["last"]),
                      r=[it["dv"], dpb], w=[dpo])
                for p_ in pend:
                    p_[0] -= 1
                if it["post"] is not None:
                    pend.append([CBDELAY, (lambda it=it, pb=pb, dpb=dpb: it["post"](pb, dpb))])
                if it["last"]:
                    pend.append([CBDELAY, (lambda it=it, po=po, dpo=dpo: it["cb"](it["qc"], po, dpo))])
                fire(99)
            fire(0)

        def attend(*args, **kw):
            run_items(make_items(*args, **kw))

        def recip_den(rd, drd, po, dpo):
            fw.op("act", lambda e: e.activation(out=rd[64:128, :], in_=po[64:128, :], func=AF.Ln, bias=1e-30, scale=1.0), r=[dpo], w=[drd])
            fw.op("act", lambda e: e.activation(out=rd[64:128, :], in_=rd[64:128, :], func=AF.Exp, scale=-1.0), r=[drd], w=[drd])

        def causal_tiles(qc):
            res = []
            for kt in range(4 * qc + 4):
                lo = max(0, kt * 128 - qc * 512)
                res.append((kt, lo, 512))
            return res

        def win_tiles(qc):
            res = []
            order = [4 * qc] + [k for k in range(max(0, 4 * qc - 4), 4 * qc + 4) if k != 4 * qc]
            for kt in order:
                off = kt * 128 - qc * 512
                lo = max(0, off)
                hi = min(512, ((off + 638) // 128 + 1) * 128)
                res.append((kt, lo, hi))
            return res

        def build_E(H, E, dE, M, dM, stage, dstage, width=2048, pstride=1, off=0, rows=128):
            fw.dma("sp", stage[0:rows, 0:width], _dap(whb, H * WHL + off + (2048 - width), [[pstride, rows], [1, width]]), w=[dstage])
            fw.op("act", lambda e: e.activation(out=E[0:rows, 0:width], in_=rev(stage, width, rows), func=AF.Exp), r=[dstage], w=[dE])
            if M is not None:
                fw.op("dve", lambda e: e.tensor_tensor(out=E[0:rows, 0:width], in0=E[0:rows, 0:width], in1=M[0:rows, 0:width], op=ALU.mult),
                      r=[dM, dE], w=[dE])

        def build_M(kind, M, dM, stage, dstage, width=2048, pstride=1, off=0, rows=128):
            fw.dma("sp", stage[0:rows, 0:width], _dap(IN["c_wm"], kind * WHL + off + (2048 - width), [[pstride, rows], [1, width]]), w=[dstage])
            fw.op("act", lambda e: e.activation(out=M[0:rows, 0:width], in_=rev(stage, width, rows), func=AF.Copy), r=[dstage], w=[dM])

        def out_proj(w_out, st):
            wo = [(sb("wo%d" % i, [128, 8, 128], BF16, st), Dep()) for i in range(2)]
            wv = w_out.rearrange("(kc p) n -> p kc n", p=128)

            def ld(fc):
                t, d_ = wo[fc % 2]
                fw.dma("pool", t[:], wv[:, :, fc * 128:(fc + 1) * 128], w=[d_])
            ld(0)
            for fc in range(8):
                if fc + 1 < 8:
                    ld(fc + 1)
                t, d_ = wo[fc % 2]
                for sc in range(4):
                    pt, dp = psA.next()
                    for pr in range(8):
                        fw.op("pe", lambda e: e.matmul(pt[:], lhsT=t[:, pr, :], rhs=oT[:, pr, cs_(sc)], start=(pr == 0), stop=(pr == 7)),
                              r=[d_, doT[pr][sc]], w=[dp])
                    fw.op("dve", lambda e: e.tensor_tensor(out=hT[:, fc, cs_(sc)], in0=pt[:], in1=hT[:, fc, cs_(sc)], op=ALU.add),
                          r=[dp, dh[fc][sc]], w=[dh[fc][sc]])

        def ffn(layer):
            rmsnorm(layer * 3 + 1)
            with contextlib.ExitStack() as st:
                act2 = sb("ffn_act", [128, NF - 16, 1024], BF16, st)
                dact = [[Dep() for _ in range(2)] for _ in range(NF)]

                def act_ap(f, q):
                    if f < 16:
                        return oT[:, f // 2, (f % 2) * 1024 + q * 512:(f % 2) * 1024 + (q + 1) * 512]
                    return act2[:, f - 16, q * 512:(q + 1) * 512]
                wg = [(sb("wg%d" % i, [128, 8, 128], BF16, st), Dep()) for i in range(2)]
                wu = [(sb("wu%d" % i, [128, 8, 128], BF16, st), Dep()) for i in range(2)]
                wd = [(sb("wd%d" % i, [128, NF, 128], BF16, st), Dep()) for i in range(2)]
                sg = Rot([(sb("sg%d" % i, [128, 512], F32, st), Dep()) for i in range(2)])
                wgv = w_g[layer].rearrange("(kc p) n -> p kc n", p=128)
                wuv = w_u[layer].rearrange("(kc p) n -> p kc n", p=128)
                wdv = w_d[layer].rearrange("(f p) n -> p f n", p=128)

                def ld1(f):
                    fw.dma("pool", wg[f % 2][0][:], wgv[:, :, f * 128:(f + 1) * 128], w=[wg[f % 2][1]])
                    fw.dma("pool", wu[f % 2][0][:], wuv[:, :, f * 128:(f + 1) * 128], w=[wu[f % 2][1]])

                def ld2(dc):
                    fw.dma("pool", wd[dc % 2][0][:], wdv[:, :, dc * 128:(dc + 1) * 128], w=[wd[dc % 2][1]])

                for half in range(2):
                    ld1(0)
                    for f in range(NF):
                        if f + 1 < NF:
                            ld1(f + 1)
                        else:
                            ld2(0)
                        tg, dg_ = wg[f % 2]
                        tu, du_ = wu[f % 2]
                        for q in range(2):
                            sc = half * 2 + q
                            pg, dpg = psA.next()
                            pu, dpu = psB.next()
                            for kc in range(8):
                                fw.op("pe", lambda e: e.matmul(pg[:], lhsT=tg[:, kc, :], rhs=xnT[:, kc, cs_(sc)], start=(kc == 0), stop=(kc == 7)),
                                      r=[dg_, dxn[sc]], w=[dpg])
                            for kc in range(8):
                                fw.op("pe", lambda e: e.matmul(pu[:], lhsT=tu[:, kc, :], rhs=xnT[:, kc, cs_(sc)], start=(kc == 0), stop=(kc == 7)),
                                      r=[du_, dxn[sc]], w=[dpu])
                            s_, ds_ = sg.next()
                            fw.op("act", lambda e: e.activation(out=s_[:], in_=pg[:], func=AF.Silu), r=[dpg], w=[ds_])
                            fw.op("dve", lambda e: e.tensor_tensor(out=act_ap(f, q), in0=s_[:], in1=pu[:], op=ALU.mult),
                                  r=[ds_, dpu], w=[dact[f][q]])
                    for dc in range(8):
                        if dc + 1 < 8:
                            ld2(dc + 1)
                        td, dd_ = wd[dc % 2]
                        for q in range(2):
                            sc = half * 2 + q
                            pt, dp = psS.next()
                            for f in range(NF):
                                fw.op("pe", lambda e: e.matmul(pt[:], lhsT=td[:, f, :], rhs=act_ap(f, q),
                                                               start=(f == 0), stop=(f == NF - 1)), r=[dd_, dact[f][q]], w=[dp])
                            fw.op("dve", lambda e: e.tensor_tensor(out=hT[:, dc, cs_(sc)], in0=pt[:], in1=hT[:, dc, cs_(sc)], op=ALU.add),
                                  r=[dp, dh[dc][sc]], w=[dh[dc][sc]])
                fw.barrier()

        def ple(layer):
            rmsnorm(layer * 3 + 2)
            with contextlib.ExitStack() as st:
                pTs = sb("pTs", [128, 2, S], BF16, st)
                dpT = Dep()
                wpg = [(sb("wpg%d" % i, [128, 8, 128], BF16, st), Dep()) for i in range(2)]
                wpp = [(sb("wpp%d" % i, [128, 2, 128], BF16, st), Dep()) for i in range(2)]
                sg = Rot([(sb("psg%d" % i, [128, 512], F32, st), Dep()) for i in range(2)])
                fw.dma("pool", pTs[:], pT[layer].rearrange("(kc p) s -> p kc s", p=128), w=[dpT])
                wgv = w_pg[layer].rearrange("(kc p) n -> p kc n", p=128)
                wpv = w_pp[layer].rearrange("(kc p) n -> p kc n", p=128)

                def ld(fc):
                    fw.dma("pool", wpg[fc % 2][0][:], wgv[:, :, fc * 128:(fc + 1) * 128], w=[wpg[fc % 2][1]])
                    fw.dma("pool", wpp[fc % 2][0][:], wpv[:, :, fc * 128:(fc + 1) * 128], w=[wpp[fc % 2][1]])
                ld(0)
                for fc in range(8):
                    if fc + 1 < 8:
                        ld(fc + 1)
                    tg, dg_ = wpg[fc % 2]
                    tp, dp_ = wpp[fc % 2]
                    for sc in range(4):
                        pg, dpg = psA.next()
                        pp_, dpp = psB.next()
                        for kc in range(8):
                            fw.op("pe", lambda e: e.matmul(pg[:], lhsT=tg[:, kc, :], rhs=xnT[:, kc, cs_(sc)], start=(kc == 0), stop=(kc == 7)),
                                  r=[dg_, dxn[sc]], w=[dpg])
                        for kc in range(2):
                            fw.op("pe", lambda e: e.matmul(pp_[:], lhsT=tp[:, kc, :], rhs=pTs[:, kc, cs_(sc)], start=(kc == 0), stop=(kc == 1)),
                                  r=[dp_, dpT], w=[dpp])
                        s_, ds_ = sg.next()
                        fw.op("act", lambda e: e.activation(out=s_[:], in_=pg[:], func=AF.Sigmoid), r=[dpg], w=[ds_])
                        fw.op("dve", lambda e: e.tensor_tensor(out=s_[:], in0=s_[:], in1=pp_[:], op=ALU.mult), r=[ds_, dpp], w=[ds_])
                        fw.op("dve", lambda e: e.tensor_tensor(out=hT[:, fc, cs_(sc)], in0=s_[:], in1=hT[:, fc, cs_(sc)], op=ALU.add),
                              r=[ds_, dh[fc][sc]], w=[dh[fc][sc]])
                fw.barrier()

        def layer0_mixer():
            with contextlib.ExitStack() as st:
                qh = [(sb("qh%d" % i, [72, S], BF16, st), [Dep() for _ in range(4)]) for i in range(2)]
                kh = [(sb("kh%d" % i, [72, S], BF16, st), [Dep() for _ in range(4)]) for i in range(2)]
                vh = [(sb("vh%d" % i, [128, 16, 128], BF16, st), [Dep() for _ in range(4)]) for i in range(2)]
                Eh = [(sb("Eh%d" % i, [128, S], BF16, st), Dep()) for i in range(2)]
                Mt = (sb("Mt", [128, S], BF16, st), Dep())
                stage = sb("stage", [128, S], F32, st)
                dstage = Dep()
                wq = [(sb("wq%d" % i, [128, 8, 384], BF16, st), Dep()) for i in range(2)]
                global_pbuf = [(sb("pbuf%d" % i, [128, 512], BF16, st), Dep()) for i in range(4)]
                nonlocal pbuf
                pbuf = Rot(global_pbuf)
                km = [(sb("km%d" % i, [64, 8], F32, st), Dep()) for i in range(2)]
                kmb = [(sb("kmb%d" % i, [72, 8], BF16, st), Dep()) for i in range(2)]
                gneg = sb("gneg", [128, 4, 8], F32, st)
                gm = sb("gm", [128, 8], F32, st)
                dgm = Dep()
                m8 = sb("m8", [128, 8], F32, st)
                dm8 = Dep()
                nmp = [(sb("nmp%d" % i, [128, 72], BF16, st), Dep()) for i in range(4)]
                dl0 = Dep()
                fw.dma("sp", gneg[:], IN["c_gneg"][:, :, :], w=[dl0])
                for i in range(4):
                    fw.op("dve", lambda e: e.memset(nmp[i][0][:], 0.0), w=[nmp[i][1]])
                for i in range(2):
                    fw.op("dve", lambda e: e.memset(kmb[i][0][:], 0.0), w=[kmb[i][1]])
                for i in range(2):
                    fw.op("dve", lambda e: e.memset(vh[i][0][:, :, 64:128], 1.0), w=vh[i][1])
                    fw.dma("pool", kh[i][0][64:72, :], IN["c_blk_moba"][:, :], w=kh[i][1])
                    fw.op("dve", lambda e: e.memset(qh[i][0][64:72, :], 0.0), w=qh[i][1])
                build_M(1, Mt[0], Mt[1], stage, dstage)

                def ldw(pair):
                    t, d_ = wq[pair % 2]
                    base = 0 if pair < 4 else 1536
                    pp = pair % 4
                    load_w3("pool", t, d_, w_in_ab, base + pp * 128, 128, 0)
                    load_w3("pool", t, d_, w_in_ab, base + 512 + pp * 128, 128, 128)
                    load_w3("pool", t, d_, w_in_ab, base + 1024 + pp * 128, 128, 256)

                ldw(0)
                for pair in range(8):
                    moba = pair < 4
                    if pair + 1 < 8:
                        ldw(pair + 1)
                    wt, dw = wq[pair % 2]
                    gq = 0 if moba else 2
                    gk = 1 if moba else 3
                    for sc in range(4):
                        pt, dp = proj_fm(wt, dw, sc, 128, 0)
                        headnorm(pt, dp, 128, gq, [(qh[0][0][0:64, cs_(sc)], qh[0][1][sc], 0), (qh[1][0][0:64, cs_(sc)], qh[1][1][sc], 64)])
                        pt, dp = proj_fm(wt, dw, sc, 128, 128)
                        headnorm(pt, dp, 128, gk, [(kh[0][0][0:64, cs_(sc)], kh[0][1][sc], 0), (kh[1][0][0:64, cs_(sc)], kh[1][1][sc], 64)])
                        if moba:
                            for hh in range(2):
                                kin = kh[hh][0][0:64, cs_(sc)]
                                kin3 = bass.AP(tensor=kin.tensor, offset=kin.offset, ap=[list(kin.ap[0]), [256, 2], [1, 256]])
                                fw.op("dve", lambda e: e.tensor_reduce(out=km[hh][0][0:64, 2 * sc:2 * sc + 2], in_=kin3, axis=AX.X, op=ALU.add),
                                      r=[kh[hh][1][sc]], w=[km[hh][1]])
                        pv, dpv = psA.next()
                        for j in range(4):
                            tt = sc * 4 + j
                            for kc in range(8):
                                fw.op("pe", lambda e: e.matmul(pv[:, j * 128:(j + 1) * 128], lhsT=xnT[:, kc, tt * 128:(tt + 1) * 128],
                                                               rhs=wt[:, kc, 256:384], start=(kc == 0), stop=(kc == 7)),
                                      r=[dw, dxn[sc]], w=[dpv])
                        for hh in range(2):
                            src = pv[:, hh * 64:hh * 64 + 1]
                            src3 = bass.AP(tensor=src.tensor, offset=src.offset, ap=[list(src.ap[0]), [128, 4], [1, 64]])
                            fw.op("act", lambda e: e.activation(out=vh[hh][0][:, sc * 4:sc * 4 + 4, 0:64], in_=src3, func=AF.Copy),
                                  r=[dpv], w=[vh[hh][1][sc]])
                    if pair == 0:
                        dump("q0", qh[0][0][0:64, :], [64, S], qh[0][1])
                        dump("k0", kh[0][0][0:64, :], [64, S], kh[0][1])
                        dump("v0", vh[0][0][:], [128, 16, 128], vh[0][1])
                    if moba:
                        for hh in range(2):
                            qt_, dq_ = qh[hh]
                            fw.op("dve", lambda e: e.tensor_copy(out=kmb[hh][0][0:64, :], in_=km[hh][0][:]), r=[km[hh][1]], w=[kmb[hh][1]])
                            fw.op("dve", lambda e: e.memset(qt_[64:72, 0:1024], 0.0), w=[dq_[0], dq_[1]])
                            for qtile in range(8, 16):
                                own = qtile // 2
                                sc = qtile // 4
                                pg, dpg = psB.next()
                                fw.op("pe", lambda e: e.matmul(pg[:, 0:8], lhsT=qt_[0:72, qtile * 128:(qtile + 1) * 128], rhs=kmb[hh][0][0:72, 0:8],
                                                               start=True, stop=True), r=[dq_[sc], kmb[hh][1]], w=[dpg])
                                fw.op("dve", lambda e: e.tensor_tensor(out=gm[:], in0=pg[:, 0:8], in1=gneg[:, own - 4, :], op=ALU.add),
                                      r=[dpg, dl0], w=[dgm])
                                fw.op("dve", lambda e: e.max(out=m8[:], in_=gm[:]), r=[dgm], w=[dm8])
                                nt, dnt = nmp[own - 4]
                                fw.op("dve", lambda e: e.tensor_scalar(out=nt[:, 64:64 + own], in0=gm[:, 0:own], scalar1=m8[:, 2:3], scalar2=NEG,
                                                                       op0=ALU.is_lt, op1=ALU.mult), r=[dgm, dm8], w=[dnt])
                                p2, dp2 = psB.next()
                                fw.op("pe", lambda e: e.matmul(p2[0:72, 0:128], lhsT=nt[:, 0:72], rhs=ident_bf[:], start=True, stop=True),
                                      r=[dnt, dcon], w=[dp2])
                                fw.op("act", lambda e: e.activation(out=qt_[64:72, qtile * 128:(qtile + 1) * 128], in_=p2[64:72, 0:128], func=AF.Copy),
                                      r=[dp2], w=[dq_[sc]])
                    if pair == 4:
                        for hh in range(2):
                            fw.op("dve", lambda e: e.memset(qh[hh][0][64:72, :], 0.0), w=qh[hh][1])
                    for hh in range(2):
                        H = pair * 2 + hh
                        E, dE = Eh[hh]
                        M, dM = (None, None) if moba else Mt
                        build_E(H, E, dE, M, dM, stage, dstage)

                        def cb(qc, po, dpo, hh=hh, pair=pair):
                            rd, drd = f32b.next()
                            recip_den(rd, drd, po, dpo)
                            fw.op("dve", lambda e: e.tensor_tensor(out=oT[hh * 64:hh * 64 + 64, pair, cs_(qc)], in0=po[0:64, :], in1=rd[64:128, :], op=ALU.mult),
                                  r=[dpo, drd], w=[doT[pair][qc]])
                        attend(qh[hh][0], qh[hh][1], 72, kh[hh][0], kh[hh][1], vh[hh][0], vh[hh][1], E, dE, causal_tiles, cb)
                pass
                fw.barrier()

        pbuf = None

        def layer1_mixer():
            with contextlib.ExitStack() as st:
                nonlocal pbuf
                pbuf = Rot([(sb("pbuf%d" % i, [128, 512], BF16, st), Dep()) for i in range(4)])
                qa = [(sb("qa%d" % i, [96, S], BF16, st), [Dep() for _ in range(4)]) for i in range(4)]
                ks = (sb("ksT", [96, S], BF16, st), [Dep() for _ in range(4)])
                kw = (sb("kwT", [96, S], BF16, st), [Dep() for _ in range(4)])
                vs = (sb("vsA", [128, 16, 128], BF16, st), [Dep() for _ in range(4)])
                vw = (sb("vwA", [128, 16, 128], BF16, st), [Dep() for _ in range(4)])
                ocmp = [(sb("ocmp%d" % i, [64, S], BF16, st), [Dep() for _ in range(4)]) for i in range(4)]
                tc_ = qa[2]
                tv_ = qa[3]
                kcT = (sb("kcT", [96, 128], BF16, st), Dep())
                vcA = (sb("vcA", [128, 128], BF16, st), Dep())
                Es = (sb("Es", [128, S], BF16, st), Dep())
                Ec = Es
                Ew = (sb("Ew", [128, 640], BF16, st), Dep())
                Mw = (sb("Mw", [128, 640], BF16, st), Dep())
                stage = sb("stage", [128, S], F32, st)
                dstage = Dep()
                gsig = (sb("gsig", [96, S], BF16, st), [Dep() for _ in range(4)])
                gsel = sb("gsel", [96, 48, 64], BF16, st)
                ovl = sb("ovl", [128, 33], BF16, st)
                addm = sb("addm", [128, 16, 32], F32, st)
                forced = sb("forced", [128, 16, 32], F32, st)
                imp = (sb("imp", [128, 16, 32], F32, st), [Dep() for _ in range(16)])
                w1s = (sb("w1s", [96, 32, 128], BF16, st), Dep())
                w1 = [w1s, w1s]
                w2 = [(sb("w2_%d" % i, [128, 64], BF16, st), Dep()) for i in range(2)]
                posT = sb("posT", [96, 64], BF16, st)
                b1c = sb("b1c", [128, 2], F32, st)
                b2k = sb("b2k", [64, 1], F32, st)
                b2v = sb("b2v", [128, 64], F32, st)
                cb1 = sb("cb1", [128, 2], F32, st)
                dcb1 = Dep()
                wA = [(sb("wA%d" % i, [128, 8, 128], BF16, st), Dep()) for i in range(3)]
                wQ = [wA[0], wA[1]]
                wG = (sb("wG", [128, 8, 48], BF16, st), Dep())
                sel_t = {}
                for nm_, shp in [("vals", [128, 32]), ("lt", [128, 32]), ("v2", [128, 32]), ("m8a", [128, 8]), ("m8b", [128, 8]), ("rdi", [128, 4])]:
                    sel_t[nm_] = (sb("sel_" + nm_, shp, F32, st), Dep())
                nmp = (sb("nmp1", [128, 96], BF16, st), Dep())
                gl = Dep()
                fw.op("dve", lambda e: e.memset(gsel[:], 0.0), w=[gl])
                fw.op("dve", lambda e: e.memset(gsig[0][:], 0.0), w=gsig[1])
                fw.op("dve", lambda e: e.memset(posT[:], 0.0), w=[gl])
                fw.op("dve", lambda e: e.memset(w1s[0][:], 0.0), w=[w1s[1]])
                fw.op("dve", lambda e: e.memset(kcT[0][:], 0.0), w=[kcT[1]])
                fw.op("dve", lambda e: e.memset(kw[0][64:96, :], 0.0), w=kw[1])
                for qd in qa:
                    fw.op("dve", lambda e: e.memset(qd[0][64:96, :], 0.0), w=qd[1])
                fw.dma("pool", gsel[0:48, :, :], IN["c_gsel"][:, :, :], w=[gl])
                fw.dma("pool", ovl[0:127, :], IN["c_ovl"][:, :], w=[gl])
                fw.dma("sp", addm[:], IN["c_addmask"][:, :, :], w=[gl])
                fw.dma("sp", forced[:], IN["c_forced"][:, :, :], w=[gl])
                fw.dma("pool", posT[0:64, :], posT_d[:, :], w=[gl])
                fw.dma("sp", b1c[:], b1_d[:, :], w=[gl])
                fw.dma("sp", b2k[:], b2k_d[:, :], w=[gl])
                fw.dma("sp", b2v[:], _dap(b2v_d, 0, [[0, 128], [1, 64]]), w=[gl])
                w1src = [ck_w1.rearrange("(l d) j -> d l j", d=64), cv_w1.rearrange("(l d) j -> d l j", d=64)]
                fw.dma("pool", w2[0][0][:], ck_w2[:, :], w=[w2[0][1]])
                fw.dma("pool", w2[1][0][:], cv_w2[:, :], w=[w2[1][1]])
                fw.dma("pool", ks[0][64:96, :], IN["c_blk_nsa"][:, :], w=ks[1])
                fw.op("dve", lambda e: e.memset(nmp[0][:], 0.0), w=[nmp[1]])
                fw.op("dve", lambda e: e.memset(vs[0][:, :, 64:128], 1.0), w=vs[1])
                fw.op("dve", lambda e: e.memset(vw[0][:, :, 64:128], 1.0), w=vw[1])
                fw.op("dve", lambda e: e.memset(vcA[0][:, 64:128], 1.0), w=[vcA[1]])
                build_M(2, Mw[0], Mw[1], stage, dstage, width=640)
                for i in range(2):
                    fw.dma("pool", w1s[0][0:64, :, :], w1src[i], w=[w1s[1]])
                    pt, dp = psB.next()
                    for l in range(32):
                        fw.op("pe", lambda e: e.matmul(pt[:, 0:1], lhsT=w1[i][0][0:96, l, :], rhs=posT[0:96, i * 32 + l:i * 32 + l + 1],
                                                       start=(l == 0), stop=(l == 31)), r=[w1[i][1], gl], w=[dp])
                    fw.op("dve", lambda e: e.tensor_tensor(out=cb1[:, i:i + 1], in0=pt[:, 0:1], in1=b1c[:, i:i + 1], op=ALU.add),
                          r=[dp, gl], w=[dcb1])
                load_w3("pool", wG[0], wG[1], w_in_nsa, 2560, 48, 0)
                for sc in range(4):
                    pt, dp = proj_fm(wG[0], wG[1], sc, 48, 0)
                    fw.op("act", lambda e: e.activation(out=gsig[0][0:48, cs_(sc)], in_=pt[0:48, :], func=AF.Sigmoid), r=[dp], w=[gsig[1][sc]])

                def gate_bc(h, j, qc):
                    pt, dp = psB.next()
                    fw.op("pe", lambda e: e.matmul(pt[0:64, :], lhsT=gsel[0:96, h * 3 + j, :], rhs=gsig[0][0:96, cs_(qc)], start=True, stop=True),
                          r=[gl, gsig[1][qc]], w=[dp])
                    return pt, dp

                for g in range(4):
                    load_w3("pool", wA[0][0], wA[0][1], w_in_nsa, 1536 + g * 64, 64, 0)
                    load_w3("pool", wA[0][0], wA[0][1], w_in_nsa, 2048 + g * 64, 64, 64)
                    load_w3("pool", wA[1][0], wA[1][1], w_in_nsa, 1024 + g * 64, 64, 0)
                    load_w3("pool", wA[1][0], wA[1][1], w_in_nsa, 1280 + g * 64, 64, 64)
                    load_w3("pool", wA[2][0], wA[2][1], w_in_nsa, 1792 + g * 64, 64, 0)
                    load_w3("pool", wA[2][0], wA[2][1], w_in_nsa, 2304 + g * 64, 64, 64)
                    for sc in range(4):
                        pt, dp = proj_fm(wA[0][0], wA[0][1], sc, 128, 0)
                        headnorm(pt, dp, 128, 5, [(ks[0][0:64, cs_(sc)], ks[1][sc], 0), (kw[0][0:64, cs_(sc)], kw[1][sc], 64)])
                        pt, dp = proj_fm(wA[1][0], wA[1][1], sc, 128, 0)
                        fw.op("act", lambda e: e.activation(out=tc_[0][0:64, cs_(sc)], in_=pt[0:64, :], func=AF.Copy), r=[dp], w=[tc_[1][sc]])
                        fw.op("act", lambda e: e.activation(out=tv_[0][0:64, cs_(sc)], in_=pt[64:128, :], func=AF.Copy), r=[dp], w=[tv_[1][sc]])
                        pv, dpv = psA.next()
                        for j in range(4):
                            tt = sc * 4 + j
                            for kc in range(8):
                                fw.op("pe", lambda e: e.matmul(pv[:, j * 128:(j + 1) * 128], lhsT=xnT[:, kc, tt * 128:(tt + 1) * 128],
                                                               rhs=wA[2][0][:, kc, :], start=(kc == 0), stop=(kc == 7)),
                                      r=[wA[2][1], dxn[sc]], w=[dpv])
                        for hh, vdst in enumerate((vs, vw)):
                            src = pv[:, hh * 64:hh * 64 + 1]
                            src3 = bass.AP(tensor=src.tensor, offset=src.offset, ap=[list(src.ap[0]), [128, 4], [1, 64]])
                            fw.op("act", lambda e: e.activation(out=vdst[0][:, sc * 4:sc * 4 + 4, 0:64], in_=src3, func=AF.Copy),
                                  r=[dpv], w=[vdst[1][sc]])
                    for i, tsrc in enumerate((tc_, tv_)):
                        fw.dma("pool", w1s[0][0:64, :, :], w1src[i], w=[w1s[1]])
                        ph, dph = psA.next()
                        for l in range(32):
                            a = tsrc[0][0:96, l:l + 1]
                            rhs = bass.AP(tensor=a.tensor, offset=a.offset, ap=[list(a.ap[0]), [16, 127]])
                            fw.op("pe", lambda e: e.matmul(ph[:, 0:127], lhsT=w1[i][0][0:96, l, :], rhs=rhs, start=(l == 0), stop=(l == 31)),
                                  r=[w1[i][1]] + tsrc[1], w=[dph])
                        xg, dxg = f32b.next()
                        tg_, dtg = f32b.next()
                        fw.op("act", lambda e: e.activation(out=xg[:, 0:127], in_=ph[:, 0:127], func=AF.Identity, bias=cb1[:, i:i + 1], scale=1.0),
                              r=[dph, dcb1], w=[dxg])
                        fw.op("dve", lambda e: e.tensor_tensor(out=tg_[:, 0:127], in0=xg[:, 0:127], in1=xg[:, 0:127], op=ALU.mult), r=[dxg], w=[dtg])
                        fw.op("dve", lambda e: e.tensor_scalar(out=tg_[:, 0:127], in0=tg_[:, 0:127], scalar1=0.044715, scalar2=1.0, op0=ALU.mult, op1=ALU.add),
                              r=[dtg], w=[dtg])
                        fw.op("dve", lambda e: e.tensor_tensor(out=tg_[:, 0:127], in0=tg_[:, 0:127], in1=xg[:, 0:127], op=ALU.mult), r=[dxg, dtg], w=[dtg])
                        fw.op("act", lambda e: e.activation(out=tg_[:, 0:127], in_=tg_[:, 0:127], func=AF.Sigmoid, scale=1.5957691216), r=[dtg], w=[dtg])
                        gb, dgb = sqb.next()
                        fw.op("dve", lambda e: e.tensor_tensor(out=gb[:, 0:127], in0=tg_[:, 0:127], in1=xg[:, 0:127], op=ALU.mult), r=[dxg, dtg], w=[dgb])
                        if i == 0:
                            pk, dpk = psA.next()
                            fw.op("pe", lambda e: e.matmul(pk[0:64, 0:127], lhsT=w2[0][0][:, :], rhs=gb[:, 0:127], start=True, stop=True),
                                  r=[w2[0][1], dgb], w=[dpk])
                            kf, dkf = f32b.next()
                            fw.op("act", lambda e: e.activation(out=kf[0:64, 0:127], in_=pk[0:64, 0:127], func=AF.Identity, bias=b2k[:, 0:1], scale=1.0),
                                  r=[dpk, gl], w=[dkf])
                            sq, dsq = sqb.next()
                            fw.op("act", lambda e: e.activation(out=sq[0:64, 0:127], in_=kf[0:64, 0:127], func=AF.Square), r=[dkf], w=[dsq])
                            ps2, dp2 = psB.next()
                            fw.op("pe", lambda e: e.matmul(ps2[0:64, 0:127], lhsT=blk_ones[0:128, 0:64], rhs=sq[0:128, 0:127], start=True, stop=True),
                                  r=[dsq, dcon], w=[dp2])
                            rt, drt = f32b.next()
                            fw.op("act", lambda e: e.activation(out=rt[0:64, 0:127], in_=ps2[0:64, 0:127], func=AF.Ln, bias=EPS, scale=1.0 / 64), r=[dp2], w=[drt])
                            fw.op("act", lambda e: e.activation(out=rt[0:64, 0:127], in_=rt[0:64, 0:127], func=AF.Exp, scale=-0.5), r=[drt], w=[drt])
                            fw.op("dve", lambda e: e.scalar_tensor_tensor(out=kcT[0][0:64, 0:127], in0=kf[0:64, 0:127], scalar=hg[0:64, 6:7], in1=rt[0:64, 0:127],
                                                                          op0=ALU.mult, op1=ALU.mult), r=[dkf, drt, dcon], w=[kcT[1]])
                        else:
                            pk, dpk = psA.next()
                            fw.op("pe", lambda e: e.matmul(pk[0:127, 0:64], lhsT=gb[:, 0:127], rhs=w2[1][0][:, :], start=True, stop=True),
                                  r=[w2[1][1], dgb], w=[dpk])
                            fw.op("dve", lambda e: e.tensor_tensor(out=vcA[0][0:127, 0:64], in0=pk[0:127, 0:64], in1=b2v[0:127, :], op=ALU.add),
                                  r=[dpk, gl], w=[vcA[1]])
                    if g == 0:
                        dump("kcT", kcT[0][:], [64, 128], [kcT[1]])
                        dump("vcA", vcA[0][:], [128, 128], [vcA[1]])
                        dump("ksT", ks[0][0:64, :], [64, S], ks[1])
                    for pr in range(2):
                        t, d_ = wQ[pr]
                        load_w3("pool", t, d_, w_in_nsa, (g * 4 + pr * 2) * 64, 128, 0)
                    for pr in range(2):
                        t, d_ = wQ[pr]
                        for sc in range(4):
                            pt, dp = proj_fm(t, d_, sc, 128, 0)
                            a, b_ = qa[pr * 2], qa[pr * 2 + 1]
                            headnorm(pt, dp, 128, 4, [(a[0][0:64, cs_(sc)], a[1][sc], 0), (b_[0][0:64, cs_(sc)], b_[1][sc], 64)])
                    for qd in qa:
                        fw.op("dve", lambda e: e.memset(qd[0][64:96, 0:1024], 0.0), w=[qd[1][0], qd[1][1]])
                    for qt in range(8, 16):
                        fw.op("dve", lambda e: e.memset(imp[0][:, qt, :], 0.0), w=[imp[1][qt]])
                    for hl in range(4):
                        h = g * 4 + hl
                        pair, hh = h // 2, h % 2
                        qt_, dq_ = qa[hl]
                        build_E(h, Ec[0], Ec[1], None, None, stage, dstage, pstride=16, off=31, rows=127)
                        items = []
                        for qc in range(4):
                            def post(pb, dpb, qc=qc):
                                if qc < 2:
                                    return
                                pi, dpi = psB.next()
                                for j in range(4):
                                    fw.op("pe", lambda e: e.matmul(pi[:, j * 64:j * 64 + 33], lhsT=pb[0:127, j * 128:(j + 1) * 128], rhs=ovl[0:127, :],
                                                                   start=True, stop=True), r=[dpb, gl], w=[dpi])
                                rdi, drdi = sel_t["rdi"]
                                src = pi[:, 32:33]
                                src3 = bass.AP(tensor=src.tensor, offset=src.offset, ap=[list(src.ap[0]), [64, 4]])
                                fw.op("dve", lambda e: e.reciprocal(out=rdi[:, 0:4], in_=src3), r=[dpi], w=[drdi])
                                for j in range(4):
                                    qtile = qc * 4 + j
                                    fw.op("dve", lambda e: e.scalar_tensor_tensor(out=imp[0][:, qtile, :], in0=pi[:, j * 64:j * 64 + 32], scalar=rdi[:, j:j + 1],
                                                                                  in1=imp[0][:, qtile, :], op0=ALU.mult, op1=ALU.add),
                                          r=[dpi, drdi, imp[1][qtile]], w=[imp[1][qtile]])

                            def cb_cmp(qc, po, dpo, h=h, hl=hl):
                                rd, drd = f32b.next()
                                recip_den(rd, drd, po, dpo)
                                fw.op("dve", lambda e: e.tensor_tensor(out=rd[0:64, :], in0=po[0:64, :], in1=rd[64:128, :], op=ALU.mult), r=[dpo, drd], w=[drd])
                                pgb, dpgb = gate_bc(h, 0, qc)
                                fw.op("dve", lambda e: e.tensor_tensor(out=ocmp[hl][0][0:64, cs_(qc)], in0=rd[0:64, :], in1=pgb[0:64, :], op=ALU.mult),
                                      r=[drd, dpgb], w=[ocmp[hl][1][qc]])
                            items.append(dict(
                                rows=127, n=512, lo=0, hi=512, K=96,
                                lhsT=kcT[0][0:96, 0:127], rhs=qt_[0:96, cs_(qc)], sdeps=[kcT[1], dq_[qc]],
                                E=Ec[0][0:127, cs_(qc)], dE=Ec[1], v=vcA[0][0:127, :], dv=vcA[1],
                                first=True, last=True, cb=cb_cmp, qc=qc, post=post))
                        run_items(items)
                    vals, dvals = sel_t["vals"]
                    lt, dlt = sel_t["lt"]
                    v2, dv2 = sel_t["v2"]
                    m8a, dm8a = sel_t["m8a"]
                    m8b, dm8b = sel_t["m8b"]
                    for qtile in range(8, 16):
                        sc = qtile // 4
                        fw.op("dve", lambda e: e.tensor_tensor(out=vals[:], in0=imp[0][:, qtile, :], in1=addm[:, qtile, :], op=ALU.add),
                              r=[imp[1][qtile], gl], w=[dvals])
                        fw.op("dve", lambda e: e.max(out=m8a[:], in_=vals[:]), r=[dvals], w=[dm8a])
                        fw.op("dve", lambda e: e.tensor_scalar(out=lt[:], in0=vals[:], scalar1=m8a[:, 7:8], scalar2=None, op0=ALU.is_lt), r=[dvals, dm8a], w=[dlt])
                        fw.op("dve", lambda e: e.tensor_tensor(out=v2[:], in0=vals[:], in1=lt[:], op=ALU.mult), r=[dvals, dlt], w=[dv2])
                        fw.op("dve", lambda e: e.tensor_scalar(out=lt[:], in0=lt[:], scalar1=-1.0, scalar2=BIG, op0=ALU.add, op1=ALU.mult), r=[dlt, dv2], w=[dlt])
                        fw.op("dve", lambda e: e.tensor_tensor(out=v2[:], in0=v2[:], in1=lt[:], op=ALU.add), r=[dlt, dv2], w=[dv2])
                        fw.op("dve", lambda e: e.max(out=m8b[:], in_=v2[:]), r=[dv2], w=[dm8b])
                        fw.op("dve", lambda e: e.tensor_scalar(out=lt[:], in0=vals[:], scalar1=m8b[:, 4:5], scalar2=None, op0=ALU.is_ge), r=[dvals, dm8b, dlt], w=[dlt])
                        fw.op("dve", lambda e: e.tensor_tensor(out=lt[:], in0=lt[:], in1=forced[:, qtile, :], op=ALU.max), r=[dlt, gl], w=[dlt])
                        fw.op("dve", lambda e: e.tensor_scalar(out=nmp[0][:, 64:96], in0=lt[:], scalar1=-1.0, scalar2=-NEG, op0=ALU.add, op1=ALU.mult),
                              r=[dlt], w=[nmp[1]])
                        p2, dp2 = psB.next()
                        fw.op("pe", lambda e: e.matmul(p2[0:96, 0:128], lhsT=nmp[0][:, 0:96], rhs=ident_bf[:], start=True, stop=True),
                              r=[nmp[1], dcon], w=[dp2])
                        for qd in qa:
                            fw.op("act", lambda e: e.activation(out=qd[0][64:96, qtile * 128:(qtile + 1) * 128], in_=p2[64:96, 0:128], func=AF.Copy),
                                  r=[dp2], w=[qd[1][sc]])
                    if g == 0:
                        dump("imp", imp[0][:], [128, 16, 32], imp[1])
                        dump("qa0", qa[0][0][:], [96, S], qa[0][1])
                    for hl in range(4):
                        h = g * 4 + hl
                        pair, hh = h // 2, h % 2
                        build_E(h, Es[0], Es[1], None, None, stage, dstage)
                        fw.op("dve", lambda e: e.tensor_tensor(out=Ew[0][:, :], in0=Es[0][:, 0:640], in1=Mw[0][:, :], op=ALU.mult),
                              r=[Es[1], Mw[1]], w=[Ew[1]])
                        acc = {}

                        def cb_slc(qc, po, dpo, h=h):
                            rd, drd = f32b.next()
                            recip_den(rd, drd, po, dpo)
                            fw.op("dve", lambda e: e.tensor_tensor(out=rd[0:64, :], in0=po[0:64, :], in1=rd[64:128, :], op=ALU.mult), r=[dpo, drd], w=[drd])
                            pgb, dpgb = gate_bc(h, 1, qc)
                            fw.op("dve", lambda e: e.tensor_tensor(out=rd[0:64, :], in0=rd[0:64, :], in1=pgb[0:64, :], op=ALU.mult), r=[drd, dpgb], w=[drd])
                            acc[qc] = (rd, drd)

                        def cb_win(qc, po, dpo, h=h, pair=pair, hh=hh, hl=hl):
                            rd, drd = f32b.next()
                            a_, da_ = acc[qc]
                            recip_den(rd, drd, po, dpo)
                            fw.op("dve", lambda e: e.tensor_tensor(out=rd[0:64, :], in0=po[0:64, :], in1=rd[64:128, :], op=ALU.mult), r=[dpo, drd], w=[drd])
                            pgb, dpgb = gate_bc(h, 2, qc)
                            fw.op("dve", lambda e: e.tensor_tensor(out=rd[0:64, :], in0=rd[0:64, :], in1=pgb[0:64, :], op=ALU.mult), r=[drd, dpgb], w=[drd])
                            fw.op("dve", lambda e: e.tensor_tensor(out=rd[0:64, :], in0=rd[0:64, :], in1=a_[0:64, :], op=ALU.add), r=[drd, da_], w=[drd])
                            fw.op("dve", lambda e: e.tensor_tensor(out=rd[0:64, :], in0=rd[0:64, :], in1=ocmp[hl][0][0:64, cs_(qc)], op=ALU.add),
                                  r=[drd, ocmp[hl][1][qc]], w=[drd])
                            fw.op("act", lambda e: e.activation(out=oT[hh * 64:hh * 64 + 64, pair, cs_(qc)], in_=rd[0:64, :], func=AF.Copy),
                                  r=[drd], w=[doT[pair][qc]])

                        items = []
                        for qc in range(4):
                            items += make_items(qa[hl][0], qa[hl][1], 96, ks[0], ks[1], vs[0], vs[1], Es[0], Es[1], causal_tiles, cb_slc, chunks=[qc])
                            items += make_items(qa[hl][0], qa[hl][1], 96, kw[0], kw[1], vw[0], vw[1], Ew[0], Ew[1], win_tiles, cb_win, chunks=[qc])
                        run_items(items)
                pass
                fw.barrier()

        alldh = [d_ for row in dh for d_ in row]
        alldo = [d_ for row in doT for d_ in row]
        with contextlib.ExitStack() as st:
            hT = sb("hT", [128, 8, S], F32, st)
            load_h(xT, [])
            rmsnorm(0)
            dump("xn0", xnT[:], [128, 8, S], dxn)
            fw.barrier()
        if upto >= 1:
            layer0_mixer()
            dump("oT0", oT[:], [128, 8, S], alldo)
        with contextlib.ExitStack() as st:
            hT = sb("hT", [128, 8, S], F32, st)
            load_h(xT, [])
            if upto >= 1:
                out_proj(w_out_ab, st)
                dump("hmix0", hT[:], [128, 8, S], alldh)
                fw.barrier()
            if upto >= 2:
                ffn(0)
                dump("hffn0", hT[:], [128, 8, S], alldh)
            if upto >= 3:
                ple(0)
                dump("h0", hT[:], [128, 8, S], alldh)
            if upto >= 4:
                rmsnorm(3)
                store_h(hS, [dhS])
            fw.barrier()
            if upto < 4:
                store_h(outT, [])
        if upto >= 4:
            layer1_mixer()
            dump("oT1", oT[:], [128, 8, S], alldo)
            with contextlib.ExitStack() as st:
                hT = sb("hT", [128, 8, S], F32, st)
                load_h(hS, [dhS])
                out_proj(w_out_nsa, st)
                dump("hmix1", hT[:], [128, 8, S], alldh)
                fw.barrier()
                if upto >= 5:
                    ffn(1)
                if upto >= 6:
                    ple(1)
                store_h(outT, [])
                fw.barrier()
        fw.finish("sp")
        build_nc.stats = (fw.n_ins, fw.n_wait)
    return nc, consts, DBG


def host_inputs(inputs, b, consts):
    f = lambda a: np.ascontiguousarray(np.asarray(a, dtype=np.float32))
    m = {}
    m["xT"] = f(inputs["x"][b].T)
    m["pT"] = f(np.transpose(inputs["p"][:, b], (0, 2, 1)))
    rb = np.asarray(inputs["rel_bias"], np.float32)
    dd = np.maximum(2047 - np.arange(WHL), 0)
    whb = rb[_bucket(dd), :].T.copy()
    whb[:, 2048:] = NEG
    m["whb"] = f(whb)
    gl = []
    for layer in range(2):
        for nm in ("norm_mix", "norm_ffn", "norm_ple"):
            gl.append(np.asarray(inputs[nm][layer], np.float32).reshape(8, 128).T)
    m["gains"] = f(np.concatenate(gl, axis=1))
    t2 = lambda a: np.concatenate([np.asarray(a, np.float32)] * 2)
    hg = np.zeros((128, 8), np.float32)
    hg[:, 0] = t2(inputs["qn_moba"][0])
    hg[:, 1] = t2(inputs["kn_moba"][0])
    hg[:, 2] = t2(inputs["qn_dil"][0])
    hg[:, 3] = t2(inputs["kn_dil"][0])
    hg[:, 4] = t2(inputs["qn_nsa"][0])
    hg[:, 5] = np.concatenate([np.asarray(inputs["kn_slc"][0], np.float32), np.asarray(inputs["kn_win"][0], np.float32)])
    hg[:, 6] = t2(inputs["kn_cmp"][0])
    m["hg"] = hg
    m["posT"] = f(np.concatenate([np.asarray(inputs["cmp_k_pos"][0]).T, np.asarray(inputs["cmp_v_pos"][0]).T], axis=1))
    m["b1c"] = f(np.stack([inputs["cmp_k_b1"][0], inputs["cmp_v_b1"][0]], axis=1))
    m["b2k"] = f(np.asarray(inputs["cmp_k_b2"][0]).reshape(64, 1))
    m["b2v"] = f(np.asarray(inputs["cmp_v_b2"][0]).reshape(1, 64))
    m["w_in_ab"] = f(inputs["w_in_ab"][0])
    m["w_out_ab"] = f(inputs["w_out_ab"][0])
    m["w_in_nsa"] = f(inputs["w_in_nsa"][0])
    m["w_out_nsa"] = f(inputs["w_out_nsa"][0])
    for nm in ("w_ffn_gate", "w_ffn_up", "w_ffn_down", "w_ple_proj", "w_ple_gate"):
        m[nm] = f(inputs[nm])
    m["cmp_k_w1"] = f(inputs["cmp_k_w1"][0])
    m["cmp_k_w2"] = f(inputs["cmp_k_w2"][0])
    m["cmp_v_w1"] = f(inputs["cmp_v_w1"][0])
    m["cmp_v_w2"] = f(inputs["cmp_v_w2"][0])
    for k, v in consts.items():
        m[k] = v
    return m


def kernel(**inputs):
    nc, consts, _ = build_nc()
    in_maps = [host_inputs(inputs, b, consts) for b in range(8)]
    res = run_bass_kernel_spmd(nc, in_maps, core_ids=list(range(8)))
    out = np.stack([np.asarray(r["outT"], np.float32).T for r in res.results], axis=0)
    return np.ascontiguousarray(out.astype(np.float32))
```

```python
import math
import contextlib
import numpy as np
import concourse.bass as bass
import concourse.mybir as mybir
from concourse.bass_utils import run_bass_kernel_spmd

F32 = mybir.dt.float32
BF16 = mybir.dt.bfloat16
AF = mybir.ActivationFunctionType
ALU = mybir.AluOpType
AX = mybir.AxisListType

S = 2048
D = 1024
FH = 2816
NF = 22
WHL = 4352
EPS = 1e-6
NEG = -30000.0
BIG = 3.0e38


class Dep:
    __slots__ = ("w", "r")

    def __init__(self):
        self.w = None
        self.r = []


class FW:
    NDMA = 24

    def __init__(self, nc, es):
        self.nc = nc
        self.engs = {"pe": nc.tensor, "act": nc.scalar, "dve": nc.vector, "pool": nc.gpsimd, "sp": nc.sync}
        self.sems = {}
        self.cnt = {}
        for k in self.engs:
            self.sems[k] = es.enter_context(nc.semaphore("sem_" + k))
            self.cnt[k] = 0
        for i in range(self.NDMA):
            k = ("dma", i)
            self.sems[k] = es.enter_context(nc.semaphore("sem_dma%d" % i))
            self.cnt[k] = 0
        self.seen = {e: {} for e in self.engs}
        self.dma_rr = {"sp": 0, "pool": 0, "act": 0}
        self.n_ins = 0
        self.n_wait = 0

    def _wait(self, eng, deps):
        seen = self.seen[eng]
        need = {}
        for d in deps:
            if d is None:
                continue
            k, v = d
            if k == "pe" and eng == "pe":
                continue
            if seen.get(k, 0) >= v:
                continue
            if need.get(k, 0) < v:
                need[k] = v
        for k, v in need.items():
            self.engs[eng].wait_ge(self.sems[k], v)
            seen[k] = v
            self.n_wait += 1

    @staticmethod
    def _collect(r, w):
        deps = []
        for t in r:
            deps.append(t.w)
        for t in w:
            deps.append(t.w)
            deps.extend(t.r)
        return deps

    def _mark(self, tok, r, w):
        for t in w:
            t.w = tok
            t.r = []
        for t in r:
            t.r.append(tok)
            if len(t.r) > 64:
                best = {}
                for k, v in t.r:
                    if best.get(k, 0) < v:
                        best[k] = v
                t.r = list(best.items())

    def op(self, eng, fn, r=(), w=()):
        self._wait(eng, self._collect(r, w))
        ins = fn(self.engs[eng])
        self.cnt[eng] += 1
        ins.then_inc(self.sems[eng], 1)
        self._mark((eng, self.cnt[eng]), r, w)
        self.n_ins += 1

    def dma(self, q, out, in_, r=(), w=()):
        half = self.NDMA // 2
        i = self.dma_rr[q]
        self.dma_rr[q] = (i + 1) % half
        k = ("dma", i + (half if q == "pool" else 0))
        deps = self._collect(r, w)
        if self.cnt[k] > 0:
            deps.append((k, self.cnt[k]))
        self._wait(q, deps)
        ins = self.engs[q].dma_start(out=out, in_=in_)
        self.cnt[k] += 16
        ins.then_inc(self.sems[k], 16)
        self._mark((k, self.cnt[k]), r, w)
        self.n_ins += 1

    def barrier(self):
        allk = [(k, v) for k, v in self.cnt.items() if v > 0]
        for e in self.engs:
            self._wait(e, allk)

    def finish(self, eng="sp"):
        allk = [(k, v) for k, v in self.cnt.items() if v > 0]
        self._wait(eng, allk)


class Rot:
    def __init__(self, items):
        self.items = items
        self.i = 0

    def next(self):
        t = self.items[self.i]
        self.i = (self.i + 1) % len(self.items)
        return t


def _bucket(d):
    n = np.maximum(d, 0)
    nf = np.maximum(n, 1).astype(np.float32)
    large = 16 + (np.log(nf / np.float32(16)) / np.float32(math.log(128.0)) * np.float32(16)).astype(np.int32)
    return np.where(n < 16, n, np.minimum(large, 31))


def _static_consts():
    c = {}
    m = np.arange(WHL)
    d = 2047 - m
    wm = np.zeros((3, WHL), np.float32)
    wm[0] = (d >= 0)
    wm[1] = (d >= 0) * ((d <= 128).astype(np.float32) + ((d % 4 == 0) & (d <= 512)) + ((d % 16 == 0) & (d <= 2048)))
    wm[2] = (d >= 0) & (d < 512)
    c["c_wm"] = wm
    c["c_ident"] = np.eye(128, dtype=np.float32)
    k = np.arange(S)
    c["c_blk_moba"] = (k[None, :] // 256 == np.arange(8)[:, None]).astype(np.float32)
    c["c_blk_nsa"] = (k[None, :] // 64 == np.arange(32)[:, None]).astype(np.float32)
    g = np.zeros((128, 4, 8), np.float32)
    for i, own in enumerate(range(4, 8)):
        g[:, i, own:] = -BIG
    c["c_gneg"] = g
    add = np.full((128, 16, 32), -BIG, np.float32)
    forced = np.zeros((128, 16, 32), np.float32)
    for qt in range(16):
        for q in range(128):
            cur = (qt * 128 + q) // 64
            for n in (0, cur, cur - 1):
                if n >= 0:
                    forced[q, qt, n] = 1.0
            for n in range(1, cur - 1):
                add[q, qt, n] = 0.0
    c["c_addmask"] = add
    c["c_forced"] = forced
    cs = np.arange(127) * 16
    ss = np.arange(32) * 64
    ov = np.maximum(np.minimum(cs[:, None] + 32, ss[None, :] + 64) - np.maximum(cs[:, None], ss[None, :]), 0)
    ovl = np.ones((127, 33), np.float32)
    ovl[:, :32] = ov
    c["c_ovl"] = ovl
    gs = np.zeros((48, 48, 64), np.float32)
    for i in range(48):
        gs[i, i, :] = 1.0
    c["c_gsel"] = gs
    return c


_CONST_SHAPES = None


def _dap(t, offset, ap):
    return bass.AP(tensor=t.tensor, offset=offset, ap=[list(a) for a in ap])


def build_nc(upto=99, dbg=()):
    nc = bass.Bass("TRN2", target_bir_lowering=False)
    consts = _static_consts()
    IN = {}

    def din(name, shape):
        IN[name] = nc.dram_tensor(name, list(shape), F32, kind="ExternalInput").ap()
        return IN[name]

    xT = din("xT", [D, S])
    pT = din("pT", [2, 256, S])
    whb = din("whb", [16, WHL])
    gains_d = din("gains", [128, 48])
    hg_d = din("hg", [128, 8])
    posT_d = din("posT", [64, 64])
    b1_d = din("b1c", [128, 2])
    b2k_d = din("b2k", [64, 1])
    b2v_d = din("b2v", [1, 64])
    w_in_ab = din("w_in_ab", [D, 3072])
    w_out_ab = din("w_out_ab", [D, D])
    w_in_nsa = din("w_in_nsa", [D, 2608])
    w_out_nsa = din("w_out_nsa", [D, D])
    w_g = din("w_ffn_gate", [2, D, FH])
    w_u = din("w_ffn_up", [2, D, FH])
    w_d = din("w_ffn_down", [2, FH, D])
    w_pp = din("w_ple_proj", [2, 256, D])
    w_pg = din("w_ple_gate", [2, D, D])
    ck_w1 = din("cmp_k_w1", [2048, 128])
    ck_w2 = din("cmp_k_w2", [128, 64])
    cv_w1 = din("cmp_v_w1", [2048, 128])
    cv_w2 = din("cmp_v_w2", [128, 64])
    for k, v in consts.items():
        din(k, v.shape)
    outT = nc.dram_tensor("outT", [D, S], F32, kind="ExternalOutput").ap()
    DBG = {}

    with contextlib.ExitStack() as es:
        fw = FW(nc, es)

        uniq = [0]

        def sb(name, shape, dt=F32, stack=es):
            uniq[0] += 1
            return stack.enter_context(nc.sbuf_tensor("%s_%d" % (name, uniq[0]), list(shape), dt))

        def pst(name):
            return es.enter_context(nc.psum_tensor(name, [128, 512], F32))

        psS = Rot([(pst("psS%d" % i), Dep()) for i in range(3)])
        psO = Rot([(pst("psO%d" % i), Dep()) for i in range(2)])
        psA = Rot([(pst("psM%d" % i), Dep()) for i in range(3)])
        psB = psA

        def dump(name, ap, shape, deps):
            if name not in dbg:
                return
            t = nc.dram_tensor("dbg_" + name, list(shape), ap.dtype if hasattr(ap, "dtype") else F32, kind="ExternalOutput").ap()
            DBG[name] = t
            fw.dma("sp", t, ap, r=deps)

        hS = nc.dram_tensor("hS", [D, S], F32, kind="Internal").ap()
        dhS = Dep()
        dh = [[Dep() for _ in range(4)] for _ in range(8)]
        xnT = sb("xnT", [128, 8, S], BF16)
        dxn = [Dep() for _ in range(4)]
        oT = sb("oT", [128, 8, S], BF16)
        doT = [[Dep() for _ in range(4)] for _ in range(8)]
        hT = None
        gains = sb("gains_sb", [128, 48])
        hg = sb("hg_sb", [128, 8])
        dcon = Dep()
        ones_bf = sb("ones_bf", [128, 128], BF16)
        blk_ones = sb("blk_ones", [128, 128], BF16)
        ident_bf = sb("ident_bf", [128, 128], BF16)
        sqb = Rot([(sb("sqb%d" % i, [128, 512], BF16), Dep()) for i in range(2)])
        f32b = Rot([(sb("f32b%d" % i, [128, 512]), Dep()) for i in range(6)])

        fw.dma("sp", gains[:], gains_d[:, :], w=[dcon])
        fw.dma("sp", hg[:], hg_d[:, :], w=[dcon])
        fw.dma("pool", ident_bf[:], IN["c_ident"][:, :], w=[dcon])
        fw.op("dve", lambda e: e.memset(ones_bf[:], 1.0), w=[dcon])
        fw.op("dve", lambda e: e.memset(blk_ones[:], 0.0), w=[dcon])
        fw.op("dve", lambda e: e.memset(blk_ones[0:64, 0:64], 1.0), w=[dcon])
        fw.op("dve", lambda e: e.memset(blk_ones[64:128, 64:128], 1.0), w=[dcon])
        def load_h(src, dsrc):
            v = src.rearrange("(c p) s -> p c s", p=128)
            for c in range(8):
                for sc in range(4):
                    fw.dma("sp", hT[:, c, sc * 512:(sc + 1) * 512], v[:, c, sc * 512:(sc + 1) * 512], r=dsrc, w=[dh[c][sc]])

        def store_h(dst, ddst):
            v = dst.rearrange("(c p) s -> p c s", p=128)
            for c in range(8):
                for sc in range(4):
                    fw.dma("sp", v[:, c, sc * 512:(sc + 1) * 512], hT[:, c, sc * 512:(sc + 1) * 512], r=[dh[c][sc]], w=ddst)

        def cs_(sc):
            return slice(sc * 512, (sc + 1) * 512)

        def rmsnorm(gidx):
            for sc in range(4):
                cs = cs_(sc)
                pt, dp = psB.next()
                for c in range(8):
                    sq, dsq = sqb.next()
                    fw.op("act", lambda e: e.activation(out=sq[:], in_=hT[:, c, cs], func=AF.Square), r=[dh[c][sc]], w=[dsq])
                    fw.op("pe", lambda e: e.matmul(pt[:], lhsT=ones_bf[:], rhs=sq[:], start=(c == 0), stop=(c == 7)),
                          r=[dsq, dcon], w=[dp])
                rt, drt = f32b.next()
                fw.op("act", lambda e: e.activation(out=rt[:], in_=pt[:], func=AF.Ln, bias=EPS, scale=1.0 / D), r=[dp], w=[drt])
                fw.op("act", lambda e: e.activation(out=rt[:], in_=rt[:], func=AF.Exp, scale=-0.5), r=[drt], w=[drt])
                for c in range(8):
                    fw.op("dve", lambda e: e.scalar_tensor_tensor(
                        out=xnT[:, c, cs], in0=hT[:, c, cs], scalar=gains[:, gidx * 8 + c:gidx * 8 + c + 1], in1=rt[:],
                        op0=ALU.mult, op1=ALU.mult), r=[dh[c][sc], drt, dcon], w=[dxn[sc]])

        def proj_fm(wt, dw, sc, M, c0=0):
            pt, dp = psA.next()
            for kc in range(8):
                fw.op("pe", lambda e: e.matmul(pt[0:M, :], lhsT=wt[:, kc, c0:c0 + M], rhs=xnT[:, kc, cs_(sc)],
                                               start=(kc == 0), stop=(kc == 7)), r=[dw, dxn[sc]], w=[dp])
            return pt, dp

        def headnorm(pt, dp, M, gcol, outs):
            sq, dsq = sqb.next()
            fw.op("act", lambda e: e.activation(out=sq[0:M, :], in_=pt[0:M, :], func=AF.Square), r=[dp], w=[dsq])
            ps2, dp2 = psB.next()
            fw.op("pe", lambda e: e.matmul(ps2[0:M, :], lhsT=blk_ones[0:M, 0:M], rhs=sq[0:M, :], start=True, stop=True),
                  r=[dsq, dcon], w=[dp2])
            rt, drt = f32b.next()
            fw.op("act", lambda e: e.activation(out=rt[0:M, :], in_=ps2[0:M, :], func=AF.Ln, bias=EPS, scale=1.0 / 64), r=[dp2], w=[drt])
            fw.op("act", lambda e: e.activation(out=rt[0:M, :], in_=rt[0:M, :], func=AF.Exp, scale=-0.5), r=[drt], w=[drt])
            for (dst, ddst, r0) in outs:
                fw.op("dve", lambda e: e.scalar_tensor_tensor(
                    out=dst, in0=pt[r0:r0 + 64, :], scalar=hg[r0:r0 + 64, gcol:gcol + 1], in1=rt[r0:r0 + 64, :],
                    op0=ALU.mult, op1=ALU.mult), r=[dp, drt, dcon], w=[ddst])

        def load_w3(q, wt, dw, src2d, c0, ncols, dst_c0=0):
            v = src2d.rearrange("(kc p) n -> p kc n", p=128)
            fw.dma(q, wt[:, :, dst_c0:dst_c0 + ncols], v[:, :, c0:c0 + ncols], w=[dw])

        def rev(t, n, rows=128):
            a = t[0:rows, n - 1:n]
            return bass.AP(tensor=a.tensor, offset=a.offset, ap=[list(a.ap[0]), [-1, n]])

        pbuf_items = []

        LOOK = 2
        CBDELAY = 2

        def make_items(qt, dq, K, ktile, dk, vt, dv, E, dE, tiles_fn, out_cb, chunks=range(4)):
            items = []
            for qc in chunks:
                c0 = qc * 512
                tiles = tiles_fn(qc)
                assert tiles[0][1] == 0 and tiles[0][2] == 512
                for idx, (kt, lo, hi) in enumerate(tiles):
                    n = hi - lo
                    u0 = c0 + lo - kt * 128
                    items.append(dict(
                        rows=128, n=n, lo=lo, hi=hi, K=K,
                        lhsT=ktile[0:K, kt * 128:(kt + 1) * 128], rhs=qt[0:K, c0 + lo:c0 + hi], sdeps=[dk[kt // 4], dq[qc]],
                        E=E[:, u0:u0 + n], dE=dE, v=vt[:, kt, :], dv=dv[kt // 4],
                        first=(idx == 0), last=(idx == len(tiles) - 1), cb=out_cb, qc=qc, post=None, after=None))
            return items

        def run_items(items):
            staged = {}
            cur = [None]
            pend = []
            n_it = len(items)

            def fire(force_to):
                while pend and (pend[0][0] <= 0 or len(pend) > force_to):
                    pend.pop(0)[1]()

            for j in range(n_it + LOOK):
                if j < n_it:
                    it = items[j]
                    pss, dps = psS.next()
                    fw.op("pe", lambda e: e.matmul(pss[0:it["rows"], 0:it["n"]], lhsT=it["lhsT"], rhs=it["rhs"], start=True, stop=True),
                          r=it["sdeps"], w=[dps])
                    staged[j] = (pss, dps)
                i = j - LOOK
                if i < 0:
                    continue
                it = items[i]
                pss, dps = staged.pop(i)
                R, n = it["rows"], it["n"]
                pb, dpb = pbuf.next()
                fw.op("act", lambda e: e.activation(out=pb[0:R, 0:n], in_=pss[0:R, 0:n], func=AF.Exp, scale=0.125), r=[dps], w=[dpb])
                fw.op("dve", lambda e: e.tensor_tensor(out=pb[0:R, 0:n], in0=pb[0:R, 0:n], in1=it["E"], op=ALU.mult),
                      r=[it["dE"], dpb], w=[dpb])
                if it["first"]:
                    fire(1)
                    cur[0] = psO.next()
                po, dpo = cur[0]
                fw.op("pe", lambda e: e.matmul(po[:, it["lo"]:it["hi"]], lhsT=it["v"], rhs=pb[0:R, 0:n], start=it["first"], stop=it["last"]),
                      r=[it["dv"], dpb], w=[dpo])
                for p_ in pend:
                    p_[0] -= 1
                if it["post"] is not None:
                    pend.append([CBDELAY, (lambda it=it, pb=pb, dpb=dpb: it["post"](pb, dpb))])
                if it["last"]:
                    pend.append([CBDELAY, (lambda it=it, po=po, dpo=dpo: it["cb"](it["qc"], po, dpo))])
                fire(99)
                if it.get("after") is not None:
                    it["after"]()
            fire(0)

        def attend(*args, **kw):
            run_items(make_items(*args, **kw))

        def recip_den(rd, drd, po, dpo):
            fw.op("act", lambda e: e.activation(out=rd[64:128, :], in_=po[64:128, :], func=AF.Ln, bias=1e-30, scale=1.0), r=[dpo], w=[drd])
            fw.op("act", lambda e: e.activation(out=rd[64:128, :], in_=rd[64:128, :], func=AF.Exp, scale=-1.0), r=[drd], w=[drd])

        def causal_tiles(qc):
            res = []
            for kt in range(4 * qc + 4):
                lo = max(0, kt * 128 - qc * 512)
                res.append((kt, lo, 512))
            return res

        def win_tiles(qc):
            res = []
            order = [4 * qc] + [k for k in range(max(0, 4 * qc - 4), 4 * qc + 4) if k != 4 * qc]
            for kt in order:
                off = kt * 128 - qc * 512
                lo = max(0, off)
                hi = min(512, ((off + 638) // 128 + 1) * 128)
                res.append((kt, lo, hi))
            return res

        def build_E(H, E, dE, M, dM, stage, dstage, width=2048, pstride=1, off=0, rows=128):
            fw.dma("sp", stage[0:128, 0:width], _dap(whb, H * WHL + off + (2048 - width), [[pstride, 128], [1, width]]), w=[dstage])
            fw.op("act", lambda e: e.activation(out=E[0:rows, 0:width], in_=rev(stage, width, rows), func=AF.Exp), r=[dstage], w=[dE])
            if M is not None:
                fw.op("dve", lambda e: e.tensor_tensor(out=E[0:rows, 0:width], in0=E[0:rows, 0:width], in1=M[0:rows, 0:width], op=ALU.mult),
                      r=[dM, dE], w=[dE])

        def build_M(kind, M, dM, stage, dstage, width=2048, pstride=1, off=0, rows=128):
            fw.dma("sp", stage[0:128, 0:width], _dap(IN["c_wm"], kind * WHL + off + (2048 - width), [[pstride, 128], [1, width]]), w=[dstage])
            fw.op("act", lambda e: e.activation(out=M[0:rows, 0:width], in_=rev(stage, width, rows), func=AF.Copy), r=[dstage], w=[dM])

        def out_proj(w_out, st):
            wo = [(sb("wo%d" % i, [128, 8, 128], BF16, st), Dep()) for i in range(2)]
            wv = w_out.rearrange("(kc p) n -> p kc n", p=128)

            def ld(fc):
                t, d_ = wo[fc % 2]
                fw.dma("pool", t[:], wv[:, :, fc * 128:(fc + 1) * 128], w=[d_])
            ld(0)
            for fc in range(8):
                if fc + 1 < 8:
                    ld(fc + 1)
                t, d_ = wo[fc % 2]
                for sc in range(4):
                    pt, dp = psA.next()
                    for pr in range(8):
                        fw.op("pe", lambda e: e.matmul(pt[:], lhsT=t[:, pr, :], rhs=oT[:, pr, cs_(sc)], start=(pr == 0), stop=(pr == 7)),
                              r=[d_, doT[pr][sc]], w=[dp])
                    fw.op("dve", lambda e: e.tensor_tensor(out=hT[:, fc, cs_(sc)], in0=pt[:], in1=hT[:, fc, cs_(sc)], op=ALU.add),
                          r=[dp, dh[fc][sc]], w=[dh[fc][sc]])

        def ffn(layer):
            rmsnorm(layer * 3 + 1)
            with contextlib.ExitStack() as st:
                act2 = sb("ffn_act", [128, NF - 16, 1024], BF16, st)
                dact = [[Dep() for _ in range(2)] for _ in range(NF)]

                def act_ap(f, q):
                    if f < 16:
                        return oT[:, f // 2, (f % 2) * 1024 + q * 512:(f % 2) * 1024 + (q + 1) * 512]
                    return act2[:, f - 16, q * 512:(q + 1) * 512]
                wg = [(sb("wg%d" % i, [128, 8, 128], BF16, st), Dep()) for i in range(2)]
                wu = [(sb("wu%d" % i, [128, 8, 128], BF16, st), Dep()) for i in range(2)]
                wd = [(sb("wd%d" % i, [128, NF, 128], BF16, st), Dep()) for i in range(2)]
                sg = Rot([(sb("sg%d" % i, [128, 512], F32, st), Dep()) for i in range(2)])
                wgv = w_g[layer].rearrange("(kc p) n -> p kc n", p=128)
                wuv = w_u[layer].rearrange("(kc p) n -> p kc n", p=128)
                wdv = w_d[layer].rearrange("(f p) n -> p f n", p=128)

                def ld1(f):
                    fw.dma("pool", wg[f % 2][0][:], wgv[:, :, f * 128:(f + 1) * 128], w=[wg[f % 2][1]])
                    fw.dma("pool", wu[f % 2][0][:], wuv[:, :, f * 128:(f + 1) * 128], w=[wu[f % 2][1]])

                def ld2(dc):
                    fw.dma("pool", wd[dc % 2][0][:], wdv[:, :, dc * 128:(dc + 1) * 128], w=[wd[dc % 2][1]])

                for half in range(2):
                    ld1(0)
                    for f in range(NF):
                        if f + 1 < NF:
                            ld1(f + 1)
                        else:
                            ld2(0)
                        tg, dg_ = wg[f % 2]
                        tu, du_ = wu[f % 2]
                        for q in range(2):
                            sc = half * 2 + q
                            pg, dpg = psA.next()
                            pu, dpu = psB.next()
                            for kc in range(8):
                                fw.op("pe", lambda e: e.matmul(pg[:], lhsT=tg[:, kc, :], rhs=xnT[:, kc, cs_(sc)], start=(kc == 0), stop=(kc == 7)),
                                      r=[dg_, dxn[sc]], w=[dpg])
                            for kc in range(8):
                                fw.op("pe", lambda e: e.matmul(pu[:], lhsT=tu[:, kc, :], rhs=xnT[:, kc, cs_(sc)], start=(kc == 0), stop=(kc == 7)),
                                      r=[du_, dxn[sc]], w=[dpu])
                            s_, ds_ = sg.next()
                            fw.op("act", lambda e: e.activation(out=s_[:], in_=pg[:], func=AF.Silu), r=[dpg], w=[ds_])
                            fw.op("dve", lambda e: e.tensor_tensor(out=act_ap(f, q), in0=s_[:], in1=pu[:], op=ALU.mult),
                                  r=[ds_, dpu], w=[dact[f][q]])
                    for dc in range(8):
                        if dc + 1 < 8:
                            ld2(dc + 1)
                        td, dd_ = wd[dc % 2]
                        for q in range(2):
                            sc = half * 2 + q
                            pt, dp = psS.next()
                            for f in range(NF):
                                fw.op("pe", lambda e: e.matmul(pt[:], lhsT=td[:, f, :], rhs=act_ap(f, q),
                                                               start=(f == 0), stop=(f == NF - 1)), r=[dd_, dact[f][q]], w=[dp])
                            fw.op("dve", lambda e: e.tensor_tensor(out=hT[:, dc, cs_(sc)], in0=pt[:], in1=hT[:, dc, cs_(sc)], op=ALU.add),
                                  r=[dp, dh[dc][sc]], w=[dh[dc][sc]])
                fw.barrier()

        def ple(layer):
            rmsnorm(layer * 3 + 2)
            with contextlib.ExitStack() as st:
                pTs = sb("pTs", [128, 2, S], BF16, st)
                dpT = Dep()
                wpg = [(sb("wpg%d" % i, [128, 8, 128], BF16, st), Dep()) for i in range(2)]
                wpp = [(sb("wpp%d" % i, [128, 2, 128], BF16, st), Dep()) for i in range(2)]
                sg = Rot([(sb("psg%d" % i, [128, 512], F32, st), Dep()) for i in range(2)])
                fw.dma("pool", pTs[:], pT[layer].rearrange("(kc p) s -> p kc s", p=128), w=[dpT])
                wgv = w_pg[layer].rearrange("(kc p) n -> p kc n", p=128)
                wpv = w_pp[layer].rearrange("(kc p) n -> p kc n", p=128)

                def ld(fc):
                    fw.dma("pool", wpg[fc % 2][0][:], wgv[:, :, fc * 128:(fc + 1) * 128], w=[wpg[fc % 2][1]])
                    fw.dma("pool", wpp[fc % 2][0][:], wpv[:, :, fc * 128:(fc + 1) * 128], w=[wpp[fc % 2][1]])
                ld(0)
                for fc in range(8):
                    if fc + 1 < 8:
                        ld(fc + 1)
                    tg, dg_ = wpg[fc % 2]
                    tp, dp_ = wpp[fc % 2]
                    for sc in range(4):
                        pg, dpg = psA.next()
                        pp_, dpp = psB.next()
                        for kc in range(8):
                            fw.op("pe", lambda e: e.matmul(pg[:], lhsT=tg[:, kc, :], rhs=xnT[:, kc, cs_(sc)], start=(kc == 0), stop=(kc == 7)),
                                  r=[dg_, dxn[sc]], w=[dpg])
                        for kc in range(2):
                            fw.op("pe", lambda e: e.matmul(pp_[:], lhsT=tp[:, kc, :], rhs=pTs[:, kc, cs_(sc)], start=(kc == 0), stop=(kc == 1)),
                                  r=[dp_, dpT], w=[dpp])
                        s_, ds_ = sg.next()
                        fw.op("act", lambda e: e.activation(out=s_[:], in_=pg[:], func=AF.Sigmoid), r=[dpg], w=[ds_])
                        fw.op("dve", lambda e: e.tensor_tensor(out=s_[:], in0=s_[:], in1=pp_[:], op=ALU.mult), r=[ds_, dpp], w=[ds_])
                        fw.op("dve", lambda e: e.tensor_tensor(out=hT[:, fc, cs_(sc)], in0=s_[:], in1=hT[:, fc, cs_(sc)], op=ALU.add),
                              r=[ds_, dh[fc][sc]], w=[dh[fc][sc]])
                fw.barrier()

        def layer0_mixer():
            with contextlib.ExitStack() as st:
                qh = [(sb("qh%d" % i, [72, S], BF16, st), [Dep() for _ in range(4)]) for i in range(2)]
                kh = [(sb("kh%d" % i, [72, S], BF16, st), [Dep() for _ in range(4)]) for i in range(2)]
                vh = [(sb("vh%d" % i, [128, 16, 128], BF16, st), [Dep() for _ in range(4)]) for i in range(2)]
                Eh = [(sb("Eh%d" % i, [128, S], BF16, st), Dep()) for i in range(2)]
                Mt = (sb("Mt", [128, S], BF16, st), Dep())
                stage = sb("stage", [128, S], F32, st)
                dstage = Dep()
                wq = [(sb("wq%d" % i, [128, 8, 384], BF16, st), Dep()) for i in range(2)]
                global_pbuf = [(sb("pbuf%d" % i, [128, 512], BF16, st), Dep()) for i in range(4)]
                nonlocal pbuf
                pbuf = Rot(global_pbuf)
                km = [(sb("km%d" % i, [64, 8], F32, st), Dep()) for i in range(2)]
                kmb = [(sb("kmb%d" % i, [72, 8], BF16, st), Dep()) for i in range(2)]
                gneg = sb("gneg", [128, 4, 8], F32, st)
                gm = sb("gm", [128, 8], F32, st)
                dgm = Dep()
                m8 = sb("m8", [128, 8], F32, st)
                dm8 = Dep()
                nmp = [(sb("nmp%d" % i, [128, 72], BF16, st), Dep()) for i in range(4)]
                dl0 = Dep()
                fw.dma("sp", gneg[:], IN["c_gneg"][:, :, :], w=[dl0])
                for i in range(4):
                    fw.op("dve", lambda e: e.memset(nmp[i][0][:], 0.0), w=[nmp[i][1]])
                for i in range(2):
                    fw.op("dve", lambda e: e.memset(kmb[i][0][:], 0.0), w=[kmb[i][1]])
                for i in range(2):
                    fw.op("dve", lambda e: e.memset(vh[i][0][:, :, 64:128], 1.0), w=vh[i][1])
                    fw.dma("pool", kh[i][0][64:72, :], IN["c_blk_moba"][:, :], w=kh[i][1])
                    fw.op("dve", lambda e: e.memset(qh[i][0][64:72, :], 0.0), w=qh[i][1])
                build_M(1, Mt[0], Mt[1], stage, dstage)

                def ldw(pair):
                    t, d_ = wq[pair % 2]
                    base = 0 if pair < 4 else 1536
                    pp = pair % 4
                    load_w3("pool", t, d_, w_in_ab, base + pp * 128, 128, 0)
                    load_w3("pool", t, d_, w_in_ab, base + 512 + pp * 128, 128, 128)
                    load_w3("pool", t, d_, w_in_ab, base + 1024 + pp * 128, 128, 256)

                ldw(0)
                for pair in range(8):
                    moba = pair < 4
                    if pair + 1 < 8:
                        ldw(pair + 1)
                    wt, dw = wq[pair % 2]
                    gq = 0 if moba else 2
                    gk = 1 if moba else 3
                    for sc in range(4):
                        pt, dp = proj_fm(wt, dw, sc, 128, 0)
                        headnorm(pt, dp, 128, gq, [(qh[0][0][0:64, cs_(sc)], qh[0][1][sc], 0), (qh[1][0][0:64, cs_(sc)], qh[1][1][sc], 64)])
                        pt, dp = proj_fm(wt, dw, sc, 128, 128)
                        headnorm(pt, dp, 128, gk, [(kh[0][0][0:64, cs_(sc)], kh[0][1][sc], 0), (kh[1][0][0:64, cs_(sc)], kh[1][1][sc], 64)])
                        if moba:
                            for hh in range(2):
                                kin = kh[hh][0][0:64, cs_(sc)]
                                kin3 = bass.AP(tensor=kin.tensor, offset=kin.offset, ap=[list(kin.ap[0]), [256, 2], [1, 256]])
                                fw.op("dve", lambda e: e.tensor_reduce(out=km[hh][0][0:64, 2 * sc:2 * sc + 2], in_=kin3, axis=AX.X, op=ALU.add),
                                      r=[kh[hh][1][sc]], w=[km[hh][1]])
                        pv, dpv = psA.next()
                        for j in range(4):
                            tt = sc * 4 + j
                            for kc in range(8):
                                fw.op("pe", lambda e: e.matmul(pv[:, j * 128:(j + 1) * 128], lhsT=xnT[:, kc, tt * 128:(tt + 1) * 128],
                                                               rhs=wt[:, kc, 256:384], start=(kc == 0), stop=(kc == 7)),
                                      r=[dw, dxn[sc]], w=[dpv])
                        for hh in range(2):
                            src = pv[:, hh * 64:hh * 64 + 1]
                            src3 = bass.AP(tensor=src.tensor, offset=src.offset, ap=[list(src.ap[0]), [128, 4], [1, 64]])
                            fw.op("act", lambda e: e.activation(out=vh[hh][0][:, sc * 4:sc * 4 + 4, 0:64], in_=src3, func=AF.Copy),
                                  r=[dpv], w=[vh[hh][1][sc]])
                    if pair == 0:
                        dump("q0", qh[0][0][0:64, :], [64, S], qh[0][1])
                        dump("k0", kh[0][0][0:64, :], [64, S], kh[0][1])
                        dump("v0", vh[0][0][:], [128, 16, 128], vh[0][1])
                    if moba:
                        for hh in range(2):
                            qt_, dq_ = qh[hh]
                            fw.op("dve", lambda e: e.tensor_copy(out=kmb[hh][0][0:64, :], in_=km[hh][0][:]), r=[km[hh][1]], w=[kmb[hh][1]])
                            fw.op("dve", lambda e: e.memset(qt_[64:72, 0:1024], 0.0), w=[dq_[0], dq_[1]])
                            for qtile in range(8, 16):
                                own = qtile // 2
                                sc = qtile // 4
                                pg, dpg = psB.next()
                                fw.op("pe", lambda e: e.matmul(pg[:, 0:8], lhsT=qt_[0:72, qtile * 128:(qtile + 1) * 128], rhs=kmb[hh][0][0:72, 0:8],
                                                               start=True, stop=True), r=[dq_[sc], kmb[hh][1]], w=[dpg])
                                fw.op("dve", lambda e: e.tensor_tensor(out=gm[:], in0=pg[:, 0:8], in1=gneg[:, own - 4, :], op=ALU.add),
                                      r=[dpg, dl0], w=[dgm])
                                fw.op("dve", lambda e: e.max(out=m8[:], in_=gm[:]), r=[dgm], w=[dm8])
                                nt, dnt = nmp[own - 4]
                                fw.op("dve", lambda e: e.tensor_scalar(out=nt[:, 64:64 + own], in0=gm[:, 0:own], scalar1=m8[:, 2:3], scalar2=NEG,
                                                                       op0=ALU.is_lt, op1=ALU.mult), r=[dgm, dm8], w=[dnt])
                                p2, dp2 = psB.next()
                                fw.op("pe", lambda e: e.matmul(p2[0:72, 0:128], lhsT=nt[:, 0:72], rhs=ident_bf[:], start=True, stop=True),
                                      r=[dnt, dcon], w=[dp2])
                                fw.op("act", lambda e: e.activation(out=qt_[64:72, qtile * 128:(qtile + 1) * 128], in_=p2[64:72, 0:128], func=AF.Copy),
                                      r=[dp2], w=[dq_[sc]])
                    if pair == 4:
                        for hh in range(2):
                            fw.op("dve", lambda e: e.memset(qh[hh][0][64:72, :], 0.0), w=qh[hh][1])
                    items = []
                    for hh in range(2):
                        H = pair * 2 + hh
                        E, dE = Eh[hh]
                        M, dM = (None, None) if moba else Mt
                        build_E(H, E, dE, M, dM, stage, dstage)

                        def cb(qc, po, dpo, hh=hh, pair=pair):
                            rd, drd = f32b.next()
                            recip_den(rd, drd, po, dpo)
                            fw.op("dve", lambda e: e.tensor_tensor(out=oT[hh * 64:hh * 64 + 64, pair, cs_(qc)], in0=po[0:64, :], in1=rd[64:128, :], op=ALU.mult),
                                  r=[dpo, drd], w=[doT[pair][qc]])
                        items += make_items(qh[hh][0], qh[hh][1], 72, kh[hh][0], kh[hh][1], vh[hh][0], vh[hh][1], E, dE, causal_tiles, cb)
                    run_items(items)
                fw.barrier()

        pbuf = None

        def layer1_mixer():
            with contextlib.ExitStack() as st:
                nonlocal pbuf
                pbuf = Rot([(sb("pbuf%d" % i, [128, 512], BF16, st), Dep()) for i in range(4)])
                qa = [(sb("qa%d" % i, [96, S], BF16, st), [Dep() for _ in range(4)]) for i in range(4)]
                ks = (sb("ksT", [96, S], BF16, st), [Dep() for _ in range(4)])
                kw = (sb("kwT", [96, S], BF16, st), [Dep() for _ in range(4)])
                vs = (sb("vsA", [128, 16, 128], BF16, st), [Dep() for _ in range(4)])
                vw = (sb("vwA", [128, 16, 128], BF16, st), [Dep() for _ in range(4)])
                ocmp = [(sb("ocmp%d" % i, [64, S], BF16, st), [Dep() for _ in range(4)]) for i in range(4)]
                tc_ = qa[2]
                tv_ = qa[3]
                kcT = (sb("kcT", [96, 128], BF16, st), Dep())
                vcA = (sb("vcA", [128, 128], BF16, st), Dep())
                EE = [(sb("EE%d" % i, [128, S], BF16, st), Dep()) for i in range(2)]
                EW = [(sb("EW%d" % i, [128, 640], BF16, st), Dep()) for i in range(2)]
                Mw = (sb("Mw", [128, 640], BF16, st), Dep())
                stage = sb("stage", [128, S], F32, st)
                dstage = Dep()
                gsig = (sb("gsig", [96, S], BF16, st), [Dep() for _ in range(4)])
                gsel = sb("gsel", [96, 48, 64], BF16, st)
                ovl = sb("ovl", [128, 33], BF16, st)
                addm = sb("addm", [128, 16, 32], F32, st)
                forced = sb("forced", [128, 16, 32], F32, st)
                imp = (sb("imp", [128, 16, 32], F32, st), [Dep() for _ in range(16)])
                w1s = (sb("w1s", [96, 32, 128], BF16, st), Dep())
                w1 = [w1s, w1s]
                w2 = [(sb("w2_%d" % i, [128, 64], BF16, st), Dep()) for i in range(2)]
                posT = sb("posT", [96, 64], BF16, st)
                b1c = sb("b1c", [128, 2], F32, st)
                b2k = sb("b2k", [64, 1], F32, st)
                b2v = sb("b2v", [128, 64], F32, st)
                cb1 = sb("cb1", [128, 2], F32, st)
                dcb1 = Dep()
                wA = [(sb("wA%d" % i, [128, 8, 128], BF16, st), Dep()) for i in range(3)]
                wQ = [wA[0], wA[1]]
                wG = (sb("wG", [128, 8, 48], BF16, st), Dep())
                sel_t = {}
                for nm_, shp in [("vals", [128, 32]), ("lt", [128, 32]), ("v2", [128, 32]), ("m8a", [128, 8]), ("m8b", [128, 8]), ("rdi", [128, 4])]:
                    sel_t[nm_] = (sb("sel_" + nm_, shp, F32, st), Dep())
                nmp = (sb("nmp1", [128, 96], BF16, st), Dep())
                gl = Dep()
                fw.op("dve", lambda e: e.memset(gsel[:], 0.0), w=[gl])
                fw.op("dve", lambda e: e.memset(gsig[0][:], 0.0), w=gsig[1])
                fw.op("dve", lambda e: e.memset(posT[:], 0.0), w=[gl])
                fw.op("dve", lambda e: e.memset(w1s[0][:], 0.0), w=[w1s[1]])
                fw.op("dve", lambda e: e.memset(kcT[0][:], 0.0), w=[kcT[1]])
                fw.op("dve", lambda e: e.memset(kw[0][64:96, :], 0.0), w=kw[1])
                for qd in qa:
                    fw.op("dve", lambda e: e.memset(qd[0][64:96, :], 0.0), w=qd[1])
                fw.dma("pool", gsel[0:48, :, :], IN["c_gsel"][:, :, :], w=[gl])
                fw.dma("pool", ovl[0:127, :], IN["c_ovl"][:, :], w=[gl])
                fw.dma("sp", addm[:], IN["c_addmask"][:, :, :], w=[gl])
                fw.dma("sp", forced[:], IN["c_forced"][:, :, :], w=[gl])
                fw.dma("pool", posT[0:64, :], posT_d[:, :], w=[gl])
                fw.dma("sp", b1c[:], b1_d[:, :], w=[gl])
                fw.dma("sp", b2k[:], b2k_d[:, :], w=[gl])
                fw.dma("sp", b2v[:], _dap(b2v_d, 0, [[0, 128], [1, 64]]), w=[gl])
                w1src = [ck_w1.rearrange("(l d) j -> d l j", d=64), cv_w1.rearrange("(l d) j -> d l j", d=64)]
                fw.dma("pool", w2[0][0][:], ck_w2[:, :], w=[w2[0][1]])
                fw.dma("pool", w2[1][0][:], cv_w2[:, :], w=[w2[1][1]])
                fw.dma("pool", ks[0][64:96, :], IN["c_blk_nsa"][:, :], w=ks[1])
                fw.op("dve", lambda e: e.memset(nmp[0][:], 0.0), w=[nmp[1]])
                fw.op("dve", lambda e: e.memset(vs[0][:, :, 64:128], 1.0), w=vs[1])
                fw.op("dve", lambda e: e.memset(vw[0][:, :, 64:128], 1.0), w=vw[1])
                fw.op("dve", lambda e: e.memset(vcA[0][:, 64:128], 1.0), w=[vcA[1]])
                build_M(2, Mw[0], Mw[1], stage, dstage, width=640)
                for i in range(2):
                    fw.dma("pool", w1s[0][0:64, :, :], w1src[i], w=[w1s[1]])
                    pt, dp = psB.next()
                    for l in range(32):
                        fw.op("pe", lambda e: e.matmul(pt[:, 0:1], lhsT=w1[i][0][0:96, l, :], rhs=posT[0:96, i * 32 + l:i * 32 + l + 1],
                                                       start=(l == 0), stop=(l == 31)), r=[w1[i][1], gl], w=[dp])
                    fw.op("dve", lambda e: e.tensor_tensor(out=cb1[:, i:i + 1], in0=pt[:, 0:1], in1=b1c[:, i:i + 1], op=ALU.add),
                          r=[dp, gl], w=[dcb1])
                load_w3("pool", wG[0], wG[1], w_in_nsa, 2560, 48, 0)
                for sc in range(4):
                    pt, dp = proj_fm(wG[0], wG[1], sc, 48, 0)
                    fw.op("act", lambda e: e.activation(out=gsig[0][0:48, cs_(sc)], in_=pt[0:48, :], func=AF.Sigmoid), r=[dp], w=[gsig[1][sc]])

                def gate_bc(h, j, qc):
                    pt, dp = psB.next()
                    fw.op("pe", lambda e: e.matmul(pt[0:64, :], lhsT=gsel[0:96, h * 3 + j, :], rhs=gsig[0][0:96, cs_(qc)], start=True, stop=True),
                          r=[gl, gsig[1][qc]], w=[dp])
                    return pt, dp

                for g in range(4):
                    load_w3("pool", wA[0][0], wA[0][1], w_in_nsa, 1536 + g * 64, 64, 0)
                    load_w3("pool", wA[0][0], wA[0][1], w_in_nsa, 2048 + g * 64, 64, 64)
                    load_w3("pool", wA[1][0], wA[1][1], w_in_nsa, 1024 + g * 64, 64, 0)
                    load_w3("pool", wA[1][0], wA[1][1], w_in_nsa, 1280 + g * 64, 64, 64)
                    load_w3("pool", wA[2][0], wA[2][1], w_in_nsa, 1792 + g * 64, 64, 0)
                    load_w3("pool", wA[2][0], wA[2][1], w_in_nsa, 2304 + g * 64, 64, 64)
                    for sc in range(4):
                        pt, dp = proj_fm(wA[0][0], wA[0][1], sc, 128, 0)
                        headnorm(pt, dp, 128, 5, [(ks[0][0:64, cs_(sc)], ks[1][sc], 0), (kw[0][0:64, cs_(sc)], kw[1][sc], 64)])
                        pt, dp = proj_fm(wA[1][0], wA[1][1], sc, 128, 0)
                        fw.op("act", lambda e: e.activation(out=tc_[0][0:64, cs_(sc)], in_=pt[0:64, :], func=AF.Copy), r=[dp], w=[tc_[1][sc]])
                        fw.op("act", lambda e: e.activation(out=tv_[0][0:64, cs_(sc)], in_=pt[64:128, :], func=AF.Copy), r=[dp], w=[tv_[1][sc]])
                        pv, dpv = psA.next()
                        for j in range(4):
                            tt = sc * 4 + j
                            for kc in range(8):
                                fw.op("pe", lambda e: e.matmul(pv[:, j * 128:(j + 1) * 128], lhsT=xnT[:, kc, tt * 128:(tt + 1) * 128],
                                                               rhs=wA[2][0][:, kc, :], start=(kc == 0), stop=(kc == 7)),
                                      r=[wA[2][1], dxn[sc]], w=[dpv])
                        for hh, vdst in enumerate((vs, vw)):
                            src = pv[:, hh * 64:hh * 64 + 1]
                            src3 = bass.AP(tensor=src.tensor, offset=src.offset, ap=[list(src.ap[0]), [128, 4], [1, 64]])
                            fw.op("act", lambda e: e.activation(out=vdst[0][:, sc * 4:sc * 4 + 4, 0:64], in_=src3, func=AF.Copy),
                                  r=[dpv], w=[vdst[1][sc]])
                    for i, tsrc in enumerate((tc_, tv_)):
                        fw.dma("pool", w1s[0][0:64, :, :], w1src[i], w=[w1s[1]])
                        ph, dph = psA.next()
                        for l in range(32):
                            a = tsrc[0][0:96, l:l + 1]
                            rhs = bass.AP(tensor=a.tensor, offset=a.offset, ap=[list(a.ap[0]), [16, 127]])
                            fw.op("pe", lambda e: e.matmul(ph[:, 0:127], lhsT=w1[i][0][0:96, l, :], rhs=rhs, start=(l == 0), stop=(l == 31)),
                                  r=[w1[i][1]] + tsrc[1], w=[dph])
                        xg, dxg = f32b.next()
                        tg_, dtg = f32b.next()
                        fw.op("act", lambda e: e.activation(out=xg[:, 0:127], in_=ph[:, 0:127], func=AF.Identity, bias=cb1[:, i:i + 1], scale=1.0),
                              r=[dph, dcb1], w=[dxg])
                        fw.op("dve", lambda e: e.tensor_tensor(out=tg_[:, 0:127], in0=xg[:, 0:127], in1=xg[:, 0:127], op=ALU.mult), r=[dxg], w=[dtg])
                        fw.op("dve", lambda e: e.tensor_scalar(out=tg_[:, 0:127], in0=tg_[:, 0:127], scalar1=0.044715, scalar2=1.0, op0=ALU.mult, op1=ALU.add),
                              r=[dtg], w=[dtg])
                        fw.op("dve", lambda e: e.tensor_tensor(out=tg_[:, 0:127], in0=tg_[:, 0:127], in1=xg[:, 0:127], op=ALU.mult), r=[dxg, dtg], w=[dtg])
                        fw.op("act", lambda e: e.activation(out=tg_[:, 0:127], in_=tg_[:, 0:127], func=AF.Sigmoid, scale=1.5957691216), r=[dtg], w=[dtg])
                        gb, dgb = sqb.next()
                        fw.op("dve", lambda e: e.tensor_tensor(out=gb[:, 0:127], in0=tg_[:, 0:127], in1=xg[:, 0:127], op=ALU.mult), r=[dxg, dtg], w=[dgb])
                        if i == 0:
                            pk, dpk = psA.next()
                            fw.op("pe", lambda e: e.matmul(pk[0:64, 0:127], lhsT=w2[0][0][:, :], rhs=gb[:, 0:127], start=True, stop=True),
                                  r=[w2[0][1], dgb], w=[dpk])
                            kf, dkf = f32b.next()
                            fw.op("act", lambda e: e.activation(out=kf[0:64, 0:127], in_=pk[0:64, 0:127], func=AF.Identity, bias=b2k[:, 0:1], scale=1.0),
                                  r=[dpk, gl], w=[dkf])
                            sq, dsq = sqb.next()
                            fw.op("act", lambda e: e.activation(out=sq[0:64, 0:127], in_=kf[0:64, 0:127], func=AF.Square), r=[dkf], w=[dsq])
                            ps2, dp2 = psB.next()
                            fw.op("pe", lambda e: e.matmul(ps2[0:64, 0:127], lhsT=blk_ones[0:128, 0:64], rhs=sq[0:128, 0:127], start=True, stop=True),
                                  r=[dsq, dcon], w=[dp2])
                            rt, drt = f32b.next()
                            fw.op("act", lambda e: e.activation(out=rt[0:64, 0:127], in_=ps2[0:64, 0:127], func=AF.Ln, bias=EPS, scale=1.0 / 64), r=[dp2], w=[drt])
                            fw.op("act", lambda e: e.activation(out=rt[0:64, 0:127], in_=rt[0:64, 0:127], func=AF.Exp, scale=-0.5), r=[drt], w=[drt])
                            fw.op("dve", lambda e: e.scalar_tensor_tensor(out=kcT[0][0:64, 0:127], in0=kf[0:64, 0:127], scalar=hg[0:64, 6:7], in1=rt[0:64, 0:127],
                                                                          op0=ALU.mult, op1=ALU.mult), r=[dkf, drt, dcon], w=[kcT[1]])
                        else:
                            pk, dpk = psA.next()
                            fw.op("pe", lambda e: e.matmul(pk[0:127, 0:64], lhsT=gb[:, 0:127], rhs=w2[1][0][:, :], start=True, stop=True),
                                  r=[w2[1][1], dgb], w=[dpk])
                            fw.op("dve", lambda e: e.tensor_tensor(out=vcA[0][0:127, 0:64], in0=pk[0:127, 0:64], in1=b2v[0:127, :], op=ALU.add),
                                  r=[dpk, gl], w=[vcA[1]])
                    if g == 0:
                        dump("kcT", kcT[0][:], [64, 128], [kcT[1]])
                        dump("vcA", vcA[0][:], [128, 128], [vcA[1]])
                        dump("ksT", ks[0][0:64, :], [64, S], ks[1])
                    for pr in range(2):
                        t, d_ = wQ[pr]
                        load_w3("pool", t, d_, w_in_nsa, (g * 4 + pr * 2) * 64, 128, 0)
                    for pr in range(2):
                        t, d_ = wQ[pr]
                        for sc in range(4):
                            pt, dp = proj_fm(t, d_, sc, 128, 0)
                            a, b_ = qa[pr * 2], qa[pr * 2 + 1]
                            headnorm(pt, dp, 128, 4, [(a[0][0:64, cs_(sc)], a[1][sc], 0), (b_[0][0:64, cs_(sc)], b_[1][sc], 64)])
                    for qd in qa:
                        fw.op("dve", lambda e: e.memset(qd[0][64:96, 0:1024], 0.0), w=[qd[1][0], qd[1][1]])
                    for qt in range(8, 16):
                        fw.op("dve", lambda e: e.memset(imp[0][:, qt, :], 0.0), w=[imp[1][qt]])
                    def build_cmp_E(hl):
                        Ec = EE[hl % 2]
                        build_E(g * 4 + hl, Ec[0], Ec[1], None, None, stage, dstage, pstride=16, off=31, rows=127)

                    build_cmp_E(0)
                    build_cmp_E(1)
                    items = []
                    for hl in range(4):
                        h = g * 4 + hl
                        pair, hh = h // 2, h % 2
                        qt_, dq_ = qa[hl]
                        Ec = EE[hl % 2]
                        for qc in range(4):
                            def post(pb, dpb, qc=qc):
                                if qc < 2:
                                    return
                                pi, dpi = psB.next()
                                for j in range(4):
                                    fw.op("pe", lambda e: e.matmul(pi[:, j * 64:j * 64 + 33], lhsT=pb[0:127, j * 128:(j + 1) * 128], rhs=ovl[0:127, :],
                                                                   start=True, stop=True), r=[dpb, gl], w=[dpi])
                                rdi, drdi = sel_t["rdi"]
                                src = pi[:, 32:33]
                                src3 = bass.AP(tensor=src.tensor, offset=src.offset, ap=[list(src.ap[0]), [64, 4]])
                                fw.op("dve", lambda e: e.reciprocal(out=rdi[:, 0:4], in_=src3), r=[dpi], w=[drdi])
                                for j in range(4):
                                    qtile = qc * 4 + j
                                    fw.op("dve", lambda e: e.scalar_tensor_tensor(out=imp[0][:, qtile, :], in0=pi[:, j * 64:j * 64 + 32], scalar=rdi[:, j:j + 1],
                                                                                  in1=imp[0][:, qtile, :], op0=ALU.mult, op1=ALU.add),
                                          r=[dpi, drdi, imp[1][qtile]], w=[imp[1][qtile]])

                            def cb_cmp(qc, po, dpo, h=h, hl=hl):
                                rd, drd = f32b.next()
                                recip_den(rd, drd, po, dpo)
                                fw.op("dve", lambda e: e.tensor_tensor(out=rd[0:64, :], in0=po[0:64, :], in1=rd[64:128, :], op=ALU.mult), r=[dpo, drd], w=[drd])
                                pgb, dpgb = gate_bc(h, 0, qc)
                                fw.op("dve", lambda e: e.tensor_tensor(out=ocmp[hl][0][0:64, cs_(qc)], in0=rd[0:64, :], in1=pgb[0:64, :], op=ALU.mult),
                                      r=[drd, dpgb], w=[ocmp[hl][1][qc]])
                            items.append(dict(
                                rows=127, n=512, lo=0, hi=512, K=96,
                                lhsT=kcT[0][0:96, 0:127], rhs=qt_[0:96, cs_(qc)], sdeps=[kcT[1], dq_[qc]],
                                E=Ec[0][0:127, cs_(qc)], dE=Ec[1], v=vcA[0][0:127, :], dv=vcA[1],
                                first=True, last=True, cb=cb_cmp, qc=qc, post=post, after=None))
                        if hl + 2 < 4:
                            items[-1]["after"] = (lambda hl=hl: build_cmp_E(hl + 2))
                    run_items(items)
                    vals, dvals = sel_t["vals"]
                    lt, dlt = sel_t["lt"]
                    v2, dv2 = sel_t["v2"]
                    m8a, dm8a = sel_t["m8a"]
                    m8b, dm8b = sel_t["m8b"]
                    for qtile in range(8, 16):
                        sc = qtile // 4
                        fw.op("dve", lambda e: e.tensor_tensor(out=vals[:], in0=imp[0][:, qtile, :], in1=addm[:, qtile, :], op=ALU.add),
                              r=[imp[1][qtile], gl], w=[dvals])
                        fw.op("dve", lambda e: e.max(out=m8a[:], in_=vals[:]), r=[dvals], w=[dm8a])
                        fw.op("dve", lambda e: e.tensor_scalar(out=lt[:], in0=vals[:], scalar1=m8a[:, 7:8], scalar2=None, op0=ALU.is_lt), r=[dvals, dm8a], w=[dlt])
                        fw.op("dve", lambda e: e.tensor_tensor(out=v2[:], in0=vals[:], in1=lt[:], op=ALU.mult), r=[dvals, dlt], w=[dv2])
                        fw.op("dve", lambda e: e.tensor_scalar(out=lt[:], in0=lt[:], scalar1=-1.0, scalar2=BIG, op0=ALU.add, op1=ALU.mult), r=[dlt, dv2], w=[dlt])
                        fw.op("dve", lambda e: e.tensor_tensor(out=v2[:], in0=v2[:], in1=lt[:], op=ALU.add), r=[dlt, dv2], w=[dv2])
                        fw.op("dve", lambda e: e.max(out=m8b[:], in_=v2[:]), r=[dv2], w=[dm8b])
                        fw.op("dve", lambda e: e.tensor_scalar(out=lt[:], in0=vals[:], scalar1=m8b[:, 4:5], scalar2=None, op0=ALU.is_ge), r=[dvals, dm8b, dlt], w=[dlt])
                        fw.op("dve", lambda e: e.tensor_tensor(out=lt[:], in0=lt[:], in1=forced[:, qtile, :], op=ALU.max), r=[dlt, gl], w=[dlt])
                        fw.op("dve", lambda e: e.tensor_scalar(out=nmp[0][:, 64:96], in0=lt[:], scalar1=-1.0, scalar2=-NEG, op0=ALU.add, op1=ALU.mult),
                              r=[dlt], w=[nmp[1]])
                        p2, dp2 = psB.next()
                        fw.op("pe", lambda e: e.matmul(p2[0:96, 0:128], lhsT=nmp[0][:, 0:96], rhs=ident_bf[:], start=True, stop=True),
                              r=[nmp[1], dcon], w=[dp2])
                        for qd in qa:
                            fw.op("act", lambda e: e.activation(out=qd[0][64:96, qtile * 128:(qtile + 1) * 128], in_=p2[64:96, 0:128], func=AF.Copy),
                                  r=[dp2], w=[qd[1][sc]])
                    if g == 0:
                        dump("imp", imp[0][:], [128, 16, 32], imp[1])
                        dump("qa0", qa[0][0][:], [96, S], qa[0][1])
                    def build_sw_E(hl):
                        Es, Ew = EE[hl % 2], EW[hl % 2]
                        build_E(g * 4 + hl, Es[0], Es[1], None, None, stage, dstage)
                        fw.op("dve", lambda e: e.tensor_tensor(out=Ew[0][:, :], in0=Es[0][:, 0:640], in1=Mw[0][:, :], op=ALU.mult),
                              r=[Es[1], Mw[1]], w=[Ew[1]])

                    build_sw_E(0)
                    build_sw_E(1)
                    items = []
                    for hl in range(4):
                        h = g * 4 + hl
                        pair, hh = h // 2, h % 2
                        Es, Ew = EE[hl % 2], EW[hl % 2]
                        acc = {}

                        def cb_slc(qc, po, dpo, h=h, acc=acc):
                            rd, drd = f32b.next()
                            recip_den(rd, drd, po, dpo)
                            fw.op("dve", lambda e: e.tensor_tensor(out=rd[0:64, :], in0=po[0:64, :], in1=rd[64:128, :], op=ALU.mult), r=[dpo, drd], w=[drd])
                            pgb, dpgb = gate_bc(h, 1, qc)
                            fw.op("dve", lambda e: e.tensor_tensor(out=rd[0:64, :], in0=rd[0:64, :], in1=pgb[0:64, :], op=ALU.mult), r=[drd, dpgb], w=[drd])
                            acc[qc] = (rd, drd)

                        def cb_win(qc, po, dpo, h=h, pair=pair, hh=hh, hl=hl, acc=acc):
                            rd, drd = f32b.next()
                            a_, da_ = acc[qc]
                            recip_den(rd, drd, po, dpo)
                            fw.op("dve", lambda e: e.tensor_tensor(out=rd[0:64, :], in0=po[0:64, :], in1=rd[64:128, :], op=ALU.mult), r=[dpo, drd], w=[drd])
                            pgb, dpgb = gate_bc(h, 2, qc)
                            fw.op("dve", lambda e: e.tensor_tensor(out=rd[0:64, :], in0=rd[0:64, :], in1=pgb[0:64, :], op=ALU.mult), r=[drd, dpgb], w=[drd])
                            fw.op("pool", lambda e: e.tensor_tensor(out=rd[0:64, :], in0=rd[0:64, :], in1=a_[0:64, :], op=ALU.add), r=[drd, da_], w=[drd])
                            fw.op("pool", lambda e: e.tensor_tensor(out=rd[0:64, :], in0=rd[0:64, :], in1=ocmp[hl][0][0:64, cs_(qc)], op=ALU.add),
                                  r=[drd, ocmp[hl][1][qc]], w=[drd])
                            fw.op("pool", lambda e: e.tensor_copy(out=oT[hh * 64:hh * 64 + 64, pair, cs_(qc)], in_=rd[0:64, :]),
                                  r=[drd], w=[doT[pair][qc]])

                        for qc in range(4):
                            items += make_items(qa[hl][0], qa[hl][1], 96, ks[0], ks[1], vs[0], vs[1], Es[0], Es[1], causal_tiles, cb_slc, chunks=[qc])
                            items += make_items(qa[hl][0], qa[hl][1], 96, kw[0], kw[1], vw[0], vw[1], Ew[0], Ew[1], win_tiles, cb_win, chunks=[qc])
                        if hl + 2 < 4:
                            items[-1]["after"] = (lambda hl=hl: build_sw_E(hl + 2))
                    run_items(items)
                pass
                fw.barrier()

        alldh = [d_ for row in dh for d_ in row]
        alldo = [d_ for row in doT for d_ in row]
        with contextlib.ExitStack() as st:
            hT = sb("hT", [128, 8, S], F32, st)
            load_h(xT, [])
            rmsnorm(0)
            dump("xn0", xnT[:], [128, 8, S], dxn)
            fw.barrier()
        if upto >= 1:
            layer0_mixer()
            dump("oT0", oT[:], [128, 8, S], alldo)
        with contextlib.ExitStack() as st:
            hT = sb("hT", [128, 8, S], F32, st)
            load_h(xT, [])
            if upto >= 1:
                out_proj(w_out_ab, st)
                dump("hmix0", hT[:], [128, 8, S], alldh)
                fw.barrier()
            if upto >= 2:
                ffn(0)
                dump("hffn0", hT[:], [128, 8, S], alldh)
            if upto >= 3:
                ple(0)
                dump("h0", hT[:], [128, 8, S], alldh)
            if upto >= 4:
                rmsnorm(3)
                store_h(hS, [dhS])
            fw.barrier()
            if upto < 4:
                store_h(outT, [])
        if upto >= 4:
            layer1_mixer()
            dump("oT1", oT[:], [128, 8, S], alldo)
            with contextlib.ExitStack() as st:
                hT = sb("hT", [128, 8, S], F32, st)
                load_h(hS, [dhS])
                out_proj(w_out_nsa, st)
                dump("hmix1", hT[:], [128, 8, S], alldh)
                fw.barrier()
                if upto >= 5:
                    ffn(1)
                if upto >= 6:
                    ple(1)
                store_h(outT, [])
                fw.barrier()
        fw.finish("sp")
        build_nc.stats = (fw.n_ins, fw.n_wait)
    return nc, consts, DBG


def host_inputs(inputs, b, consts):
    f = lambda a: np.ascontiguousarray(np.asarray(a, dtype=np.float32))
    m = {}
    m["xT"] = f(inputs["x"][b].T)
    m["pT"] = f(np.transpose(inputs["p"][:, b], (0, 2, 1)))
    rb = np.asarray(inputs["rel_bias"], np.float32)
    dd = np.maximum(2047 - np.arange(WHL), 0)
    whb = rb[_bucket(dd), :].T.copy()
    whb[:, 2048:] = NEG
    m["whb"] = f(whb)
    gl = []
    for layer in range(2):
        for nm in ("norm_mix", "norm_ffn", "norm_ple"):
            gl.append(np.asarray(inputs[nm][layer], np.float32).reshape(8, 128).T)
    m["gains"] = f(np.concatenate(gl, axis=1))
    t2 = lambda a: np.concatenate([np.asarray(a, np.float32)] * 2)
    hg = np.zeros((128, 8), np.float32)
    hg[:, 0] = t2(inputs["qn_moba"][0])
    hg[:, 1] = t2(inputs["kn_moba"][0])
    hg[:, 2] = t2(inputs["qn_dil"][0])
    hg[:, 3] = t2(inputs["kn_dil"][0])
    hg[:, 4] = t2(inputs["qn_nsa"][0])
    hg[:, 5] = np.concatenate([np.asarray(inputs["kn_slc"][0], np.float32), np.asarray(inputs["kn_win"][0], np.float32)])
    hg[:, 6] = t2(inputs["kn_cmp"][0])
    m["hg"] = hg
    m["posT"] = f(np.concatenate([np.asarray(inputs["cmp_k_pos"][0]).T, np.asarray(inputs["cmp_v_pos"][0]).T], axis=1))
    m["b1c"] = f(np.stack([inputs["cmp_k_b1"][0], inputs["cmp_v_b1"][0]], axis=1))
    m["b2k"] = f(np.asarray(inputs["cmp_k_b2"][0]).reshape(64, 1))
    m["b2v"] = f(np.asarray(inputs["cmp_v_b2"][0]).reshape(1, 64))
    m["w_in_ab"] = f(inputs["w_in_ab"][0])
    m["w_out_ab"] = f(inputs["w_out_ab"][0])
    m["w_in_nsa"] = f(inputs["w_in_nsa"][0])
    m["w_out_nsa"] = f(inputs["w_out_nsa"][0])
    for nm in ("w_ffn_gate", "w_ffn_up", "w_ffn_down", "w_ple_proj", "w_ple_gate"):
        m[nm] = f(inputs[nm])
    m["cmp_k_w1"] = f(inputs["cmp_k_w1"][0])
    m["cmp_k_w2"] = f(inputs["cmp_k_w2"][0])
    m["cmp_v_w1"] = f(inputs["cmp_v_w1"][0])
    m["cmp_v_w2"] = f(inputs["cmp_v_w2"][0])
    for k, v in consts.items():
        m[k] = v
    return m


def kernel(**inputs):
    nc, consts, _ = build_nc()
    in_maps = [host_inputs(inputs, b, consts) for b in range(8)]
    res = run_bass_kernel_spmd(nc, in_maps, core_ids=list(range(8)))
    out = np.stack([np.asarray(r["outT"], np.float32).T for r in res.results], axis=0)
    return np.ascontiguousarray(out.astype(np.float32))
```

```python
import math
import contextlib
import numpy as np
import concourse.bass as bass
import concourse.mybir as mybir
from concourse.bass_utils import run_bass_kernel_spmd

F32 = mybir.dt.float32
BF16 = mybir.dt.bfloat16
AF = mybir.ActivationFunctionType
ALU = mybir.AluOpType
AX = mybir.AxisListType

S = 2048
D = 1024
FH = 2816
NF = 22
WHL = 4352
EPS = 1e-6
NEG = -30000.0
BIG = 3.0e38


class Dep:
    __slots__ = ("w", "r")

    def __init__(self):
        self.w = None
        self.r = []


class FW:
    NDMA = 24

    def __init__(self, nc, es):
        self.nc = nc
        self.engs = {"pe": nc.tensor, "act": nc.scalar, "dve": nc.vector, "pool": nc.gpsimd, "sp": nc.sync}
        self.sems = {}
        self.cnt = {}
        for k in self.engs:
            self.sems[k] = es.enter_context(nc.semaphore("sem_" + k))
            self.cnt[k] = 0
        for i in range(self.NDMA):
            k = ("dma", i)
            self.sems[k] = es.enter_context(nc.semaphore("sem_dma%d" % i))
            self.cnt[k] = 0
        self.seen = {e: {} for e in self.engs}
        self.dma_rr = {"sp": 0, "pool": 0, "act": 0}
        self.n_ins = 0
        self.n_wait = 0

    def _wait(self, eng, deps):
        seen = self.seen[eng]
        need = {}
        for d in deps:
            if d is None:
                continue
            k, v = d
            if k == "pe" and eng == "pe":
                continue
            if seen.get(k, 0) >= v:
                continue
            if need.get(k, 0) < v:
                need[k] = v
        for k, v in need.items():
            self.engs[eng].wait_ge(self.sems[k], v)
            seen[k] = v
            self.n_wait += 1

    @staticmethod
    def _collect(r, w):
        deps = []
        for t in r:
            deps.append(t.w)
        for t in w:
            deps.append(t.w)
            deps.extend(t.r)
        return deps

    def _mark(self, tok, r, w):
        for t in w:
            t.w = tok
            t.r = []
        for t in r:
            t.r.append(tok)
            if len(t.r) > 64:
                best = {}
                for k, v in t.r:
                    if best.get(k, 0) < v:
                        best[k] = v
                t.r = list(best.items())

    def op(self, eng, fn, r=(), w=()):
        self._wait(eng, self._collect(r, w))
        ins = fn(self.engs[eng])
        self.cnt[eng] += 1
        ins.then_inc(self.sems[eng], 1)
        self._mark((eng, self.cnt[eng]), r, w)
        self.n_ins += 1

    def dma(self, q, out, in_, r=(), w=()):
        half = self.NDMA // 2
        i = self.dma_rr[q]
        self.dma_rr[q] = (i + 1) % half
        k = ("dma", i + (half if q == "pool" else 0))
        deps = self._collect(r, w)
        if self.cnt[k] > 0:
            deps.append((k, self.cnt[k]))
        self._wait(q, deps)
        ins = self.engs[q].dma_start(out=out, in_=in_)
        self.cnt[k] += 16
        ins.then_inc(self.sems[k], 16)
        self._mark((k, self.cnt[k]), r, w)
        self.n_ins += 1

    def barrier(self):
        allk = [(k, v) for k, v in self.cnt.items() if v > 0]
        for e in self.engs:
            self._wait(e, allk)

    def finish(self, eng="sp"):
        allk = [(k, v) for k, v in self.cnt.items() if v > 0]
        self._wait(eng, allk)


class Rot:
    def __init__(self, items):
        self.items = items
        self.i = 0

    def next(self):
        t = self.items[self.i]
        self.i = (self.i + 1) % len(self.items)
        return t


def _bucket(d):
    n = np.maximum(d, 0)
    nf = np.maximum(n, 1).astype(np.float32)
    large = 16 + (np.log(nf / np.float32(16)) / np.float32(math.log(128.0)) * np.float32(16)).astype(np.int32)
    return np.where(n < 16, n, np.minimum(large, 31))


def _static_consts():
    c = {}
    m = np.arange(WHL)
    d = 2047 - m
    wm = np.zeros((3, WHL), np.float32)
    wm[0] = (d >= 0)
    wm[1] = (d >= 0) * ((d <= 128).astype(np.float32) + ((d % 4 == 0) & (d <= 512)) + ((d % 16 == 0) & (d <= 2048)))
    wm[2] = (d >= 0) & (d < 512)
    c["c_wm"] = wm
    c["c_ident"] = np.eye(128, dtype=np.float32)
    k = np.arange(S)
    c["c_blk_moba"] = (k[None, :] // 256 == np.arange(8)[:, None]).astype(np.float32)
    c["c_blk_nsa"] = (k[None, :] // 64 == np.arange(32)[:, None]).astype(np.float32)
    g = np.zeros((128, 4, 8), np.float32)
    for i, own in enumerate(range(4, 8)):
        g[:, i, own:] = -BIG
    c["c_gneg"] = g
    add = np.full((128, 16, 32), -BIG, np.float32)
    forced = np.zeros((128, 16, 32), np.float32)
    for qt in range(16):
        for q in range(128):
            cur = (qt * 128 + q) // 64
            for n in (0, cur, cur - 1):
                if n >= 0:
                    forced[q, qt, n] = 1.0
            for n in range(1, cur - 1):
                add[q, qt, n] = 0.0
    c["c_addmask"] = add
    c["c_forced"] = forced
    cs = np.arange(127) * 16
    ss = np.arange(32) * 64
    ov = np.maximum(np.minimum(cs[:, None] + 32, ss[None, :] + 64) - np.maximum(cs[:, None], ss[None, :]), 0)
    ovl = np.ones((127, 33), np.float32)
    ovl[:, :32] = ov
    c["c_ovl"] = ovl
    gs = np.zeros((48, 48, 64), np.float32)
    for i in range(48):
        gs[i, i, :] = 1.0
    c["c_gsel"] = gs
    return c


_CONST_SHAPES = None


def _dap(t, offset, ap):
    return bass.AP(tensor=t.tensor, offset=offset, ap=[list(a) for a in ap])


def build_nc(upto=99, dbg=()):
    nc = bass.Bass("TRN2", target_bir_lowering=False)
    consts = _static_consts()
    IN = {}

    def din(name, shape):
        IN[name] = nc.dram_tensor(name, list(shape), F32, kind="ExternalInput").ap()
        return IN[name]

    xT = din("xT", [D, S])
    pT = din("pT", [2, 256, S])
    whb = din("whb", [16, WHL])
    gains_d = din("gains", [128, 48])
    hg_d = din("hg", [128, 8])
    posT_d = din("posT", [64, 64])
    b1_d = din("b1c", [128, 2])
    b2k_d = din("b2k", [64, 1])
    b2v_d = din("b2v", [1, 64])
    w_in_ab = din("w_in_ab", [D, 3072])
    w_out_ab = din("w_out_ab", [D, D])
    w_in_nsa = din("w_in_nsa", [D, 2608])
    w_out_nsa = din("w_out_nsa", [D, D])
    w_g = din("w_ffn_gate", [2, D, FH])
    w_u = din("w_ffn_up", [2, D, FH])
    w_d = din("w_ffn_down", [2, FH, D])
    w_pp = din("w_ple_proj", [2, 256, D])
    w_pg = din("w_ple_gate", [2, D, D])
    ck_w1 = din("cmp_k_w1", [2048, 128])
    ck_w2 = din("cmp_k_w2", [128, 64])
    cv_w1 = din("cmp_v_w1", [2048, 128])
    cv_w2 = din("cmp_v_w2", [128, 64])
    for k, v in consts.items():
        din(k, v.shape)
    outT = nc.dram_tensor("outT", [D, S], F32, kind="ExternalOutput").ap()
    DBG = {}

    with contextlib.ExitStack() as es:
        fw = FW(nc, es)

        uniq = [0]

        def sb(name, shape, dt=F32, stack=es):
            uniq[0] += 1
            return stack.enter_context(nc.sbuf_tensor("%s_%d" % (name, uniq[0]), list(shape), dt))

        def pst(name):
            return es.enter_context(nc.psum_tensor(name, [128, 512], F32))

        psS = Rot([(pst("psS%d" % i), Dep()) for i in range(3)])
        psO = Rot([(pst("psO%d" % i), Dep()) for i in range(2)])
        psA = Rot([(pst("psM%d" % i), Dep()) for i in range(3)])
        psB = psA

        def dump(name, ap, shape, deps):
            if name not in dbg:
                return
            t = nc.dram_tensor("dbg_" + name, list(shape), ap.dtype if hasattr(ap, "dtype") else F32, kind="ExternalOutput").ap()
            DBG[name] = t
            fw.dma("sp", t, ap, r=deps)

        hS = nc.dram_tensor("hS", [D, S], F32, kind="Internal").ap()
        dhS = Dep()
        dh = [[Dep() for _ in range(4)] for _ in range(8)]
        xnT = sb("xnT", [128, 8, S], BF16)
        dxn = [Dep() for _ in range(4)]
        oT = sb("oT", [128, 8, S], BF16)
        doT = [[Dep() for _ in range(4)] for _ in range(8)]
        hT = None
        gains = sb("gains_sb", [128, 48])
        hg = sb("hg_sb", [128, 8])
        dcon = Dep()
        ones_bf = sb("ones_bf", [128, 128], BF16)
        blk_ones = sb("blk_ones", [128, 128], BF16)
        ident_bf = sb("ident_bf", [128, 128], BF16)
        sqb = Rot([(sb("sqb%d" % i, [128, 512], BF16), Dep()) for i in range(2)])
        f32b = Rot([(sb("f32b%d" % i, [128, 512]), Dep()) for i in range(6)])

        fw.dma("sp", gains[:], gains_d[:, :], w=[dcon])
        fw.dma("sp", hg[:], hg_d[:, :], w=[dcon])
        fw.dma("pool", ident_bf[:], IN["c_ident"][:, :], w=[dcon])
        fw.op("dve", lambda e: e.memset(ones_bf[:], 1.0), w=[dcon])
        fw.op("dve", lambda e: e.memset(blk_ones[:], 0.0), w=[dcon])
        fw.op("dve", lambda e: e.memset(blk_ones[0:64, 0:64], 1.0), w=[dcon])
        fw.op("dve", lambda e: e.memset(blk_ones[64:128, 64:128], 1.0), w=[dcon])
        def load_h(src, dsrc):
            v = src.rearrange("(c p) s -> p c s", p=128)
            for c in range(8):
                for sc in range(4):
                    fw.dma("sp", hT[:, c, sc * 512:(sc + 1) * 512], v[:, c, sc * 512:(sc + 1) * 512], r=dsrc, w=[dh[c][sc]])

        def store_h(dst, ddst):
            v = dst.rearrange("(c p) s -> p c s", p=128)
            for c in range(8):
                for sc in range(4):
                    fw.dma("sp", v[:, c, sc * 512:(sc + 1) * 512], hT[:, c, sc * 512:(sc + 1) * 512], r=[dh[c][sc]], w=ddst)

        def cs_(sc):
            return slice(sc * 512, (sc + 1) * 512)

        def rmsnorm(gidx):
            for sc in range(4):
                cs = cs_(sc)
                pt, dp = psB.next()
                for c in range(8):
                    sq, dsq = sqb.next()
                    fw.op("act", lambda e: e.activation(out=sq[:], in_=hT[:, c, cs], func=AF.Square), r=[dh[c][sc]], w=[dsq])
                    fw.op("pe", lambda e: e.matmul(pt[:], lhsT=ones_bf[:], rhs=sq[:], start=(c == 0), stop=(c == 7)),
                          r=[dsq, dcon], w=[dp])
                rt, drt = f32b.next()
                fw.op("act", lambda e: e.activation(out=rt[:], in_=pt[:], func=AF.Ln, bias=EPS, scale=1.0 / D), r=[dp], w=[drt])
                fw.op("act", lambda e: e.activation(out=rt[:], in_=rt[:], func=AF.Exp, scale=-0.5), r=[drt], w=[drt])
                for c in range(8):
                    fw.op("dve", lambda e: e.scalar_tensor_tensor(
                        out=xnT[:, c, cs], in0=hT[:, c, cs], scalar=gains[:, gidx * 8 + c:gidx * 8 + c + 1], in1=rt[:],
                        op0=ALU.mult, op1=ALU.mult), r=[dh[c][sc], drt, dcon], w=[dxn[sc]])

        def proj_fm(wt, dw, sc, M, c0=0):
            pt, dp = psA.next()
            for kc in range(8):
                fw.op("pe", lambda e: e.matmul(pt[0:M, :], lhsT=wt[:, kc, c0:c0 + M], rhs=xnT[:, kc, cs_(sc)],
                                               start=(kc == 0), stop=(kc == 7)), r=[dw, dxn[sc]], w=[dp])
            return pt, dp

        def headnorm(pt, dp, M, gcol, outs):
            sq, dsq = sqb.next()
            fw.op("act", lambda e: e.activation(out=sq[0:M, :], in_=pt[0:M, :], func=AF.Square), r=[dp], w=[dsq])
            ps2, dp2 = psB.next()
            fw.op("pe", lambda e: e.matmul(ps2[0:M, :], lhsT=blk_ones[0:M, 0:M], rhs=sq[0:M, :], start=True, stop=True),
                  r=[dsq, dcon], w=[dp2])
            rt, drt = f32b.next()
            fw.op("act", lambda e: e.activation(out=rt[0:M, :], in_=ps2[0:M, :], func=AF.Ln, bias=EPS, scale=1.0 / 64), r=[dp2], w=[drt])
            fw.op("act", lambda e: e.activation(out=rt[0:M, :], in_=rt[0:M, :], func=AF.Exp, scale=-0.5), r=[drt], w=[drt])
            for (dst, ddst, r0) in outs:
                fw.op("dve", lambda e: e.scalar_tensor_tensor(
                    out=dst, in0=pt[r0:r0 + 64, :], scalar=hg[r0:r0 + 64, gcol:gcol + 1], in1=rt[r0:r0 + 64, :],
                    op0=ALU.mult, op1=ALU.mult), r=[dp, drt, dcon], w=[ddst])

        def load_w3(q, wt, dw, src2d, c0, ncols, dst_c0=0):
            v = src2d.rearrange("(kc p) n -> p kc n", p=128)
            fw.dma(q, wt[:, :, dst_c0:dst_c0 + ncols], v[:, :, c0:c0 + ncols], w=[dw])

        def rev(t, n, rows=128):
            a = t[0:rows, n - 1:n]
            return bass.AP(tensor=a.tensor, offset=a.offset, ap=[list(a.ap[0]), [-1, n]])

        pbuf_items = []

        LOOK = 2
        CBDELAY = 2

        def make_items(qt, dq, K, ktile, dk, vt, dv, E, dE, tiles_fn, out_cb, chunks=range(4)):
            items = []
            for qc in chunks:
                c0 = qc * 512
                tiles = tiles_fn(qc)
                assert tiles[0][1] == 0 and tiles[0][2] == 512
                for idx, (kt, lo, hi) in enumerate(tiles):
                    n = hi - lo
                    u0 = c0 + lo - kt * 128
                    items.append(dict(
                        rows=128, n=n, lo=lo, hi=hi, K=K,
                        lhsT=ktile[0:K, kt * 128:(kt + 1) * 128], rhs=qt[0:K, c0 + lo:c0 + hi], sdeps=[dk[kt // 4], dq[qc]],
                        E=E[:, u0:u0 + n], dE=dE, v=vt[:, kt, :], dv=dv[kt // 4],
                        first=(idx == 0), last=(idx == len(tiles) - 1), cb=out_cb, qc=qc, post=None, after=None))
            return items

        def run_items(items):
            staged = {}
            cur = [None]
            pend = []
            n_it = len(items)

            def fire(force_to):
                while pend and (pend[0][0] <= 0 or len(pend) > force_to):
                    pend.pop(0)[1]()

            for j in range(n_it + LOOK):
                if j < n_it:
                    it = items[j]
                    pss, dps = psS.next()
                    fw.op("pe", lambda e: e.matmul(pss[0:it["rows"], 0:it["n"]], lhsT=it["lhsT"], rhs=it["rhs"], start=True, stop=True),
                          r=it["sdeps"], w=[dps])
                    staged[j] = (pss, dps)
                i = j - LOOK
                if i < 0:
                    continue
                it = items[i]
                pss, dps = staged.pop(i)
                R, n = it["rows"], it["n"]
                pb, dpb = pbuf.next()
                fw.op("act", lambda e: e.activation(out=pb[0:R, 0:n], in_=pss[0:R, 0:n], func=AF.Exp, scale=0.125), r=[dps], w=[dpb])
                fw.op("dve", lambda e: e.tensor_tensor(out=pb[0:R, 0:n], in0=pb[0:R, 0:n], in1=it["E"], op=ALU.mult),
                      r=[it["dE"], dpb], w=[dpb])
                if it["first"]:
                    fire(1)
                    cur[0] = psO.next()
                po, dpo = cur[0]
                fw.op("pe", lambda e: e.matmul(po[:, it["lo"]:it["hi"]], lhsT=it["v"], rhs=pb[0:R, 0:n], start=it["first"], stop=it["last"]),
                      r=[it["dv"], dpb], w=[dpo])
                for p_ in pend:
                    p_[0] -= 1
                if it["post"] is not None:
                    pend.append([CBDELAY, (lambda it=it, pb=pb, dpb=dpb: it["post"](pb, dpb))])
                if it["last"]:
                    pend.append([CBDELAY, (lambda it=it, po=po, dpo=dpo: it["cb"](it["qc"], po, dpo))])
                fire(99)
                if it.get("after") is not None:
                    it["after"]()
            fire(0)

        def attend(*args, **kw):
            run_items(make_items(*args, **kw))

        def recip_den(rd, drd, po, dpo):
            fw.op("act", lambda e: e.activation(out=rd[64:128, :], in_=po[64:128, :], func=AF.Ln, bias=1e-30, scale=1.0), r=[dpo], w=[drd])
            fw.op("act", lambda e: e.activation(out=rd[64:128, :], in_=rd[64:128, :], func=AF.Exp, scale=-1.0), r=[drd], w=[drd])

        def causal_tiles(qc):
            res = []
            for kt in range(4 * qc + 4):
                lo = max(0, kt * 128 - qc * 512)
                res.append((kt, lo, 512))
            return res

        def win_tiles(qc):
            res = []
            order = [4 * qc] + [k for k in range(max(0, 4 * qc - 4), 4 * qc + 4) if k != 4 * qc]
            for kt in order:
                off = kt * 128 - qc * 512
                lo = max(0, off)
                hi = min(512, ((off + 638) // 128 + 1) * 128)
                res.append((kt, lo, hi))
            return res

        def build_E(H, E, dE, M, dM, stage, dstage, width=2048, pstride=1, off=0, rows=128):
            fw.dma("sp", stage[0:128, 0:width], _dap(whb, H * WHL + off + (2048 - width), [[pstride, 128], [1, width]]), w=[dstage])
            fw.op("act", lambda e: e.activation(out=E[0:rows, 0:width], in_=rev(stage, width, rows), func=AF.Exp), r=[dstage], w=[dE])
            if M is not None:
                fw.op("dve", lambda e: e.tensor_tensor(out=E[0:rows, 0:width], in0=E[0:rows, 0:width], in1=M[0:rows, 0:width], op=ALU.mult),
                      r=[dM, dE], w=[dE])

        def build_M(kind, M, dM, stage, dstage, width=2048, pstride=1, off=0, rows=128):
            fw.dma("sp", stage[0:128, 0:width], _dap(IN["c_wm"], kind * WHL + off + (2048 - width), [[pstride, 128], [1, width]]), w=[dstage])
            fw.op("act", lambda e: e.activation(out=M[0:rows, 0:width], in_=rev(stage, width, rows), func=AF.Copy), r=[dstage], w=[dM])

        def out_proj(w_out, st):
            wo = [(sb("wo%d" % i, [128, 8, 128], BF16, st), Dep()) for i in range(2)]
            wv = w_out.rearrange("(kc p) n -> p kc n", p=128)

            def ld(fc):
                t, d_ = wo[fc % 2]
                fw.dma("pool", t[:], wv[:, :, fc * 128:(fc + 1) * 128], w=[d_])
            ld(0)
            for fc in range(8):
                if fc + 1 < 8:
                    ld(fc + 1)
                t, d_ = wo[fc % 2]
                for sc in range(4):
                    pt, dp = psA.next()
                    for pr in range(8):
                        fw.op("pe", lambda e: e.matmul(pt[:], lhsT=t[:, pr, :], rhs=oT[:, pr, cs_(sc)], start=(pr == 0), stop=(pr == 7)),
                              r=[d_, doT[pr][sc]], w=[dp])
                    fw.op("dve", lambda e: e.tensor_tensor(out=hT[:, fc, cs_(sc)], in0=pt[:], in1=hT[:, fc, cs_(sc)], op=ALU.add),
                          r=[dp, dh[fc][sc]], w=[dh[fc][sc]])

        def ffn(layer):
            rmsnorm(layer * 3 + 1)
            with contextlib.ExitStack() as st:
                act2 = sb("ffn_act", [128, NF - 16, 1024], BF16, st)
                dact = [[Dep() for _ in range(2)] for _ in range(NF)]

                def act_ap(f, q):
                    if f < 16:
                        return oT[:, f // 2, (f % 2) * 1024 + q * 512:(f % 2) * 1024 + (q + 1) * 512]
                    return act2[:, f - 16, q * 512:(q + 1) * 512]
                wg = [(sb("wg%d" % i, [128, 8, 128], BF16, st), Dep()) for i in range(2)]
                wu = [(sb("wu%d" % i, [128, 8, 128], BF16, st), Dep()) for i in range(2)]
                wd = [(sb("wd%d" % i, [128, NF, 128], BF16, st), Dep()) for i in range(2)]
                sg = Rot([(sb("sg%d" % i, [128, 512], F32, st), Dep()) for i in range(2)])
                wgv = w_g[layer].rearrange("(kc p) n -> p kc n", p=128)
                wuv = w_u[layer].rearrange("(kc p) n -> p kc n", p=128)
                wdv = w_d[layer].rearrange("(f p) n -> p f n", p=128)

                def ld1(f):
                    fw.dma("pool", wg[f % 2][0][:], wgv[:, :, f * 128:(f + 1) * 128], w=[wg[f % 2][1]])
                    fw.dma("pool", wu[f % 2][0][:], wuv[:, :, f * 128:(f + 1) * 128], w=[wu[f % 2][1]])

                def ld2(dc):
                    fw.dma("pool", wd[dc % 2][0][:], wdv[:, :, dc * 128:(dc + 1) * 128], w=[wd[dc % 2][1]])

                for half in range(2):
                    ld1(0)
                    for f in range(NF):
                        if f + 1 < NF:
                            ld1(f + 1)
                        else:
                            ld2(0)
                        tg, dg_ = wg[f % 2]
                        tu, du_ = wu[f % 2]
                        for q in range(2):
                            sc = half * 2 + q
                            pg, dpg = psA.next()
                            pu, dpu = psB.next()
                            for kc in range(8):
                                fw.op("pe", lambda e: e.matmul(pg[:], lhsT=tg[:, kc, :], rhs=xnT[:, kc, cs_(sc)], start=(kc == 0), stop=(kc == 7)),
                                      r=[dg_, dxn[sc]], w=[dpg])
                            for kc in range(8):
                                fw.op("pe", lambda e: e.matmul(pu[:], lhsT=tu[:, kc, :], rhs=xnT[:, kc, cs_(sc)], start=(kc == 0), stop=(kc == 7)),
                                      r=[du_, dxn[sc]], w=[dpu])
                            s_, ds_ = sg.next()
                            fw.op("act", lambda e: e.activation(out=s_[:], in_=pg[:], func=AF.Silu), r=[dpg], w=[ds_])
                            fw.op("dve", lambda e: e.tensor_tensor(out=act_ap(f, q), in0=s_[:], in1=pu[:], op=ALU.mult),
                                  r=[ds_, dpu], w=[dact[f][q]])
                    for dc in range(8):
                        if dc + 1 < 8:
                            ld2(dc + 1)
                        td, dd_ = wd[dc % 2]
                        for q in range(2):
                            sc = half * 2 + q
                            pt, dp = psS.next()
                            for f in range(NF):
                                fw.op("pe", lambda e: e.matmul(pt[:], lhsT=td[:, f, :], rhs=act_ap(f, q),
                                                               start=(f == 0), stop=(f == NF - 1)), r=[dd_, dact[f][q]], w=[dp])
                            fw.op("dve", lambda e: e.tensor_tensor(out=hT[:, dc, cs_(sc)], in0=pt[:], in1=hT[:, dc, cs_(sc)], op=ALU.add),
                                  r=[dp, dh[dc][sc]], w=[dh[dc][sc]])
                fw.barrier()

        def ple(layer):
            rmsnorm(layer * 3 + 2)
            with contextlib.ExitStack() as st:
                pTs = sb("pTs", [128, 2, S], BF16, st)
                dpT = Dep()
                wpg = [(sb("wpg%d" % i, [128, 8, 128], BF16, st), Dep()) for i in range(2)]
                wpp = [(sb("wpp%d" % i, [128, 2, 128], BF16, st), Dep()) for i in range(2)]
                sg = Rot([(sb("psg%d" % i, [128, 512], F32, st), Dep()) for i in range(2)])
                fw.dma("pool", pTs[:], pT[layer].rearrange("(kc p) s -> p kc s", p=128), w=[dpT])
                wgv = w_pg[layer].rearrange("(kc p) n -> p kc n", p=128)
                wpv = w_pp[layer].rearrange("(kc p) n -> p kc n", p=128)

                def ld(fc):
                    fw.dma("pool", wpg[fc % 2][0][:], wgv[:, :, fc * 128:(fc + 1) * 128], w=[wpg[fc % 2][1]])
                    fw.dma("pool", wpp[fc % 2][0][:], wpv[:, :, fc * 128:(fc + 1) * 128], w=[wpp[fc % 2][1]])
                ld(0)
                for fc in range(8):
                    if fc + 1 < 8:
                        ld(fc + 1)
                    tg, dg_ = wpg[fc % 2]
                    tp, dp_ = wpp[fc % 2]
                    for sc in range(4):
                        pg, dpg = psA.next()
                        pp_, dpp = psB.next()
                        for kc in range(8):
                            fw.op("pe", lambda e: e.matmul(pg[:], lhsT=tg[:, kc, :], rhs=xnT[:, kc, cs_(sc)], start=(kc == 0), stop=(kc == 7)),
                                  r=[dg_, dxn[sc]], w=[dpg])
                        for kc in range(2):
                            fw.op("pe", lambda e: e.matmul(pp_[:], lhsT=tp[:, kc, :], rhs=pTs[:, kc, cs_(sc)], start=(kc == 0), stop=(kc == 1)),
                                  r=[dp_, dpT], w=[dpp])
                        s_, ds_ = sg.next()
                        fw.op("act", lambda e: e.activation(out=s_[:], in_=pg[:], func=AF.Sigmoid), r=[dpg], w=[ds_])
                        fw.op("dve", lambda e: e.tensor_tensor(out=s_[:], in0=s_[:], in1=pp_[:], op=ALU.mult), r=[ds_, dpp], w=[ds_])
                        fw.op("dve", lambda e: e.tensor_tensor(out=hT[:, fc, cs_(sc)], in0=s_[:], in1=hT[:, fc, cs_(sc)], op=ALU.add),
                              r=[ds_, dh[fc][sc]], w=[dh[fc][sc]])
                fw.barrier()

        def layer0_mixer():
            with contextlib.ExitStack() as st:
                qh = [(sb("qh%d" % i, [128, S], BF16, st), [Dep() for _ in range(4)]) for i in range(2)]
                kh = [(sb("kh%d" % i, [128, S], BF16, st), [Dep() for _ in range(4)]) for i in range(2)]
                vh = [(sb("vh%d" % i, [128, 16, 128], BF16, st), [Dep() for _ in range(4)]) for i in range(2)]
                Eh = [(sb("Eh%d" % i, [128, S], BF16, st), Dep()) for i in range(2)]
                Mt = (sb("Mt", [128, S], BF16, st), Dep())
                stage = sb("stage", [128, S], F32, st)
                dstage = Dep()
                wq = [(sb("wq%d" % i, [128, 8, 384], BF16, st), Dep()) for i in range(2)]
                global_pbuf = [(sb("pbuf%d" % i, [128, 512], BF16, st), Dep()) for i in range(4)]
                nonlocal pbuf
                pbuf = Rot(global_pbuf)
                km = [(sb("km%d" % i, [64, 8], F32, st), Dep()) for i in range(2)]
                kmb = [(sb("kmb%d" % i, [72, 8], BF16, st), Dep()) for i in range(2)]
                gneg = sb("gneg", [128, 4, 8], F32, st)
                gm = sb("gm", [128, 8], F32, st)
                dgm = Dep()
                m8 = sb("m8", [128, 8], F32, st)
                dm8 = Dep()
                nmp = [(sb("nmp%d" % i, [128, 72], BF16, st), Dep()) for i in range(4)]
                dl0 = Dep()
                fw.dma("sp", gneg[:], IN["c_gneg"][:, :, :], w=[dl0])
                for i in range(4):
                    fw.op("dve", lambda e: e.memset(nmp[i][0][:], 0.0), w=[nmp[i][1]])
                for i in range(2):
                    fw.op("dve", lambda e: e.memset(kmb[i][0][:], 0.0), w=[kmb[i][1]])
                for i in range(2):
                    fw.op("dve", lambda e: e.memset(vh[i][0][:, :, 64:128], 1.0), w=vh[i][1])
                    fw.op("dve", lambda e: e.memset(kh[i][0][64:128, :], 0.0), w=kh[i][1])
                    fw.op("dve", lambda e: e.memset(qh[i][0][64:128, :], 0.0), w=qh[i][1])
                    fw.dma("pool", kh[i][0][64:72, :], IN["c_blk_moba"][:, :], w=kh[i][1])
                build_M(1, Mt[0], Mt[1], stage, dstage)

                def ldw(pair):
                    t, d_ = wq[pair % 2]
                    base = 0 if pair < 4 else 1536
                    pp = pair % 4
                    load_w3("pool", t, d_, w_in_ab, base + pp * 128, 128, 0)
                    load_w3("pool", t, d_, w_in_ab, base + 512 + pp * 128, 128, 128)
                    load_w3("pool", t, d_, w_in_ab, base + 1024 + pp * 128, 128, 256)

                ldw(0)
                for pair in range(8):
                    moba = pair < 4
                    if pair + 1 < 8:
                        ldw(pair + 1)
                    wt, dw = wq[pair % 2]
                    gq = 0 if moba else 2
                    gk = 1 if moba else 3
                    for sc in range(4):
                        pt, dp = proj_fm(wt, dw, sc, 128, 0)
                        headnorm(pt, dp, 128, gq, [(qh[0][0][0:64, cs_(sc)], qh[0][1][sc], 0), (qh[1][0][0:64, cs_(sc)], qh[1][1][sc], 64)])
                        pt, dp = proj_fm(wt, dw, sc, 128, 128)
                        headnorm(pt, dp, 128, gk, [(kh[0][0][0:64, cs_(sc)], kh[0][1][sc], 0), (kh[1][0][0:64, cs_(sc)], kh[1][1][sc], 64)])
                        if moba:
                            for hh in range(2):
                                kin = kh[hh][0][0:64, cs_(sc)]
                                kin3 = bass.AP(tensor=kin.tensor, offset=kin.offset, ap=[list(kin.ap[0]), [256, 2], [1, 256]])
                                fw.op("dve", lambda e: e.tensor_reduce(out=km[hh][0][0:64, 2 * sc:2 * sc + 2], in_=kin3, axis=AX.X, op=ALU.add),
                                      r=[kh[hh][1][sc]], w=[km[hh][1]])
                        pv, dpv = psA.next()
                        for j in range(4):
                            tt = sc * 4 + j
                            for kc in range(8):
                                fw.op("pe", lambda e: e.matmul(pv[:, j * 128:(j + 1) * 128], lhsT=xnT[:, kc, tt * 128:(tt + 1) * 128],
                                                               rhs=wt[:, kc, 256:384], start=(kc == 0), stop=(kc == 7)),
                                      r=[dw, dxn[sc]], w=[dpv])
                        for hh in range(2):
                            src = pv[:, hh * 64:hh * 64 + 1]
                            src3 = bass.AP(tensor=src.tensor, offset=src.offset, ap=[list(src.ap[0]), [128, 4], [1, 64]])
                            fw.op("act", lambda e: e.activation(out=vh[hh][0][:, sc * 4:sc * 4 + 4, 0:64], in_=src3, func=AF.Copy),
                                  r=[dpv], w=[vh[hh][1][sc]])
                    if pair == 0:
                        dump("q0", qh[0][0][0:64, :], [64, S], qh[0][1])
                        dump("k0", kh[0][0][0:64, :], [64, S], kh[0][1])
                        dump("v0", vh[0][0][:], [128, 16, 128], vh[0][1])
                    if moba:
                        for hh in range(2):
                            qt_, dq_ = qh[hh]
                            fw.op("dve", lambda e: e.tensor_copy(out=kmb[hh][0][0:64, :], in_=km[hh][0][:]), r=[km[hh][1]], w=[kmb[hh][1]])
                            fw.op("dve", lambda e: e.memset(qt_[64:72, 0:1024], 0.0), w=[dq_[0], dq_[1]])
                            for qtile in range(8, 16):
                                own = qtile // 2
                                sc = qtile // 4
                                pg, dpg = psB.next()
                                fw.op("pe", lambda e: e.matmul(pg[:, 0:8], lhsT=qt_[0:72, qtile * 128:(qtile + 1) * 128], rhs=kmb[hh][0][0:72, 0:8],
                                                               start=True, stop=True), r=[dq_[sc], kmb[hh][1]], w=[dpg])
                                fw.op("dve", lambda e: e.tensor_tensor(out=gm[:], in0=pg[:, 0:8], in1=gneg[:, own - 4, :], op=ALU.add),
                                      r=[dpg, dl0], w=[dgm])
                                fw.op("dve", lambda e: e.max(out=m8[:], in_=gm[:]), r=[dgm], w=[dm8])
                                nt, dnt = nmp[own - 4]
                                fw.op("dve", lambda e: e.tensor_scalar(out=nt[:, 64:64 + own], in0=gm[:, 0:own], scalar1=m8[:, 2:3], scalar2=NEG,
                                                                       op0=ALU.is_lt, op1=ALU.mult), r=[dgm, dm8], w=[dnt])
                                p2, dp2 = psB.next()
                                fw.op("pe", lambda e: e.matmul(p2[0:72, 0:128], lhsT=nt[:, 0:72], rhs=ident_bf[:], start=True, stop=True),
                                      r=[dnt, dcon], w=[dp2])
                                fw.op("act", lambda e: e.activation(out=qt_[64:72, qtile * 128:(qtile + 1) * 128], in_=p2[64:72, 0:128], func=AF.Copy),
                                      r=[dp2], w=[dq_[sc]])
                    if pair == 4:
                        for hh in range(2):
                            fw.op("dve", lambda e: e.memset(qh[hh][0][64:72, :], 0.0), w=qh[hh][1])
                    items = []
                    for hh in range(2):
                        H = pair * 2 + hh
                        E, dE = Eh[hh]
                        M, dM = (None, None) if moba else Mt
                        build_E(H, E, dE, M, dM, stage, dstage)

                        def cb(qc, po, dpo, hh=hh, pair=pair):
                            rd, drd = f32b.next()
                            recip_den(rd, drd, po, dpo)
                            fw.op("dve", lambda e: e.tensor_tensor(out=oT[hh * 64:hh * 64 + 64, pair, cs_(qc)], in0=po[0:64, :], in1=rd[64:128, :], op=ALU.mult),
                                  r=[dpo, drd], w=[doT[pair][qc]])
                        items += make_items(qh[hh][0], qh[hh][1], 128, kh[hh][0], kh[hh][1], vh[hh][0], vh[hh][1], E, dE, causal_tiles, cb)
                    run_items(items)
                fw.barrier()

        pbuf = None

        def layer1_mixer():
            with contextlib.ExitStack() as st:
                nonlocal pbuf
                pbuf = Rot([(sb("pbuf%d" % i, [128, 512], BF16, st), Dep()) for i in range(4)])
                qa = [(sb("qa%d" % i, [128, S], BF16, st), [Dep() for _ in range(4)]) for i in range(4)]
                ks = (sb("ksT", [128, S], BF16, st), [Dep() for _ in range(4)])
                kw = (sb("kwT", [128, S], BF16, st), [Dep() for _ in range(4)])
                vs = (sb("vsA", [128, 16, 128], BF16, st), [Dep() for _ in range(4)])
                vw = (sb("vwA", [128, 16, 128], BF16, st), [Dep() for _ in range(4)])
                ocmp = [(sb("ocmp%d" % i, [64, S], BF16, st), [Dep() for _ in range(4)]) for i in range(4)]
                tc_ = qa[2]
                tv_ = qa[3]
                kcT = (sb("kcT", [128, 128], BF16, st), Dep())
                vcA = (sb("vcA", [128, 128], BF16, st), Dep())
                EE = [(sb("EE%d" % i, [128, S], BF16, st), Dep()) for i in range(2)]
                EW = [(sb("EW%d" % i, [128, 640], BF16, st), Dep()) for i in range(2)]
                Mw = (sb("Mw", [128, 640], BF16, st), Dep())
                stage = sb("stage", [128, S], F32, st)
                dstage = Dep()
                gsig = (sb("gsig", [96, S], BF16, st), [Dep() for _ in range(4)])
                gsel = sb("gsel", [96, 48, 64], BF16, st)
                ovl = sb("ovl", [128, 33], BF16, st)
                addm = sb("addm", [128, 16, 32], F32, st)
                forced = sb("forced", [128, 16, 32], F32, st)
                imp = (sb("imp", [128, 16, 32], F32, st), [Dep() for _ in range(16)])
                w1s = (sb("w1s", [96, 32, 128], BF16, st), Dep())
                w1 = [w1s, w1s]
                w2 = [(sb("w2_%d" % i, [128, 64], BF16, st), Dep()) for i in range(2)]
                posT = sb("posT", [96, 64], BF16, st)
                b1c = sb("b1c", [128, 2], F32, st)
                b2k = sb("b2k", [64, 1], F32, st)
                b2v = sb("b2v", [128, 64], F32, st)
                cb1 = sb("cb1", [128, 2], F32, st)
                dcb1 = Dep()
                wA = [(sb("wA%d" % i, [128, 8, 128], BF16, st), Dep()) for i in range(3)]
                wQ = [wA[0], wA[1]]
                wG = (sb("wG", [128, 8, 48], BF16, st), Dep())
                tsb = Rot([(sb("tsb%d" % i, [64, 512], BF16, st), Dep()) for i in range(4)])
                sel_t = {}
                for nm_, shp in [("vals", [128, 32]), ("lt", [128, 32]), ("v2", [128, 32]), ("m8a", [128, 8]), ("m8b", [128, 8]), ("rdi", [128, 4])]:
                    sel_t[nm_] = (sb("sel_" + nm_, shp, F32, st), Dep())
                nmp = (sb("nmp1", [128, 96], BF16, st), Dep())
                gl = Dep()
                fw.op("dve", lambda e: e.memset(gsel[:], 0.0), w=[gl])
                fw.op("dve", lambda e: e.memset(gsig[0][:], 0.0), w=gsig[1])
                fw.op("dve", lambda e: e.memset(posT[:], 0.0), w=[gl])
                fw.op("dve", lambda e: e.memset(w1s[0][:], 0.0), w=[w1s[1]])
                fw.op("dve", lambda e: e.memset(kcT[0][:], 0.0), w=[kcT[1]])
                fw.op("dve", lambda e: e.memset(kw[0][64:128, :], 0.0), w=kw[1])
                fw.op("dve", lambda e: e.memset(ks[0][64:128, :], 0.0), w=ks[1])
                for qd in qa:
                    fw.op("dve", lambda e: e.memset(qd[0][64:128, :], 0.0), w=qd[1])
                fw.dma("pool", gsel[0:48, :, :], IN["c_gsel"][:, :, :], w=[gl])
                fw.dma("pool", ovl[0:127, :], IN["c_ovl"][:, :], w=[gl])
                fw.dma("sp", addm[:], IN["c_addmask"][:, :, :], w=[gl])
                fw.dma("sp", forced[:], IN["c_forced"][:, :, :], w=[gl])
                fw.dma("pool", posT[0:64, :], posT_d[:, :], w=[gl])
                fw.dma("sp", b1c[:], b1_d[:, :], w=[gl])
                fw.dma("sp", b2k[:], b2k_d[:, :], w=[gl])
                fw.dma("sp", b2v[:], _dap(b2v_d, 0, [[0, 128], [1, 64]]), w=[gl])
                w1src = [ck_w1.rearrange("(l d) j -> d l j", d=64), cv_w1.rearrange("(l d) j -> d l j", d=64)]
                fw.dma("pool", w2[0][0][:], ck_w2[:, :], w=[w2[0][1]])
                fw.dma("pool", w2[1][0][:], cv_w2[:, :], w=[w2[1][1]])
                fw.dma("pool", ks[0][64:96, :], IN["c_blk_nsa"][:, :], w=ks[1])
                fw.op("dve", lambda e: e.memset(nmp[0][:], 0.0), w=[nmp[1]])
                fw.op("dve", lambda e: e.memset(vs[0][:, :, 64:128], 1.0), w=vs[1])
                fw.op("dve", lambda e: e.memset(vw[0][:, :, 64:128], 1.0), w=vw[1])
                fw.op("dve", lambda e: e.memset(vcA[0][:, 64:128], 1.0), w=[vcA[1]])
                build_M(2, Mw[0], Mw[1], stage, dstage, width=640)
                for i in range(2):
                    fw.dma("pool", w1s[0][0:64, :, :], w1src[i], w=[w1s[1]])
                    pt, dp = psB.next()
                    for l in range(32):
                        fw.op("pe", lambda e: e.matmul(pt[:, 0:1], lhsT=w1[i][0][0:96, l, :], rhs=posT[0:96, i * 32 + l:i * 32 + l + 1],
                                                       start=(l == 0), stop=(l == 31)), r=[w1[i][1], gl], w=[dp])
                    fw.op("dve", lambda e: e.tensor_tensor(out=cb1[:, i:i + 1], in0=pt[:, 0:1], in1=b1c[:, i:i + 1], op=ALU.add),
                          r=[dp, gl], w=[dcb1])
                load_w3("pool", wG[0], wG[1], w_in_nsa, 2560, 48, 0)
                for sc in range(4):
                    pt, dp = proj_fm(wG[0], wG[1], sc, 48, 0)
                    fw.op("act", lambda e: e.activation(out=gsig[0][0:48, cs_(sc)], in_=pt[0:48, :], func=AF.Sigmoid), r=[dp], w=[gsig[1][sc]])

                def gate_bc(h, j, qc):
                    pt, dp = psB.next()
                    fw.op("pe", lambda e: e.matmul(pt[0:64, :], lhsT=gsel[0:96, h * 3 + j, :], rhs=gsig[0][0:96, cs_(qc)], start=True, stop=True),
                          r=[gl, gsig[1][qc]], w=[dp])
                    return pt, dp

                for g in range(4):
                    load_w3("pool", wA[0][0], wA[0][1], w_in_nsa, 1536 + g * 64, 64, 0)
                    load_w3("pool", wA[0][0], wA[0][1], w_in_nsa, 2048 + g * 64, 64, 64)
                    load_w3("pool", wA[1][0], wA[1][1], w_in_nsa, 1024 + g * 64, 64, 0)
                    load_w3("pool", wA[1][0], wA[1][1], w_in_nsa, 1280 + g * 64, 64, 64)
                    load_w3("pool", wA[2][0], wA[2][1], w_in_nsa, 1792 + g * 64, 64, 0)
                    load_w3("pool", wA[2][0], wA[2][1], w_in_nsa, 2304 + g * 64, 64, 64)
                    for sc in range(4):
                        pt, dp = proj_fm(wA[0][0], wA[0][1], sc, 128, 0)
                        headnorm(pt, dp, 128, 5, [(ks[0][0:64, cs_(sc)], ks[1][sc], 0), (kw[0][0:64, cs_(sc)], kw[1][sc], 64)])
                        pt, dp = proj_fm(wA[1][0], wA[1][1], sc, 128, 0)
                        fw.op("act", lambda e: e.activation(out=tc_[0][0:64, cs_(sc)], in_=pt[0:64, :], func=AF.Copy), r=[dp], w=[tc_[1][sc]])
                        fw.op("act", lambda e: e.activation(out=tv_[0][0:64, cs_(sc)], in_=pt[64:128, :], func=AF.Copy), r=[dp], w=[tv_[1][sc]])
                        pv, dpv = psA.next()
                        for j in range(4):
                            tt = sc * 4 + j
                            for kc in range(8):
                                fw.op("pe", lambda e: e.matmul(pv[:, j * 128:(j + 1) * 128], lhsT=xnT[:, kc, tt * 128:(tt + 1) * 128],
                                                               rhs=wA[2][0][:, kc, :], start=(kc == 0), stop=(kc == 7)),
                                      r=[wA[2][1], dxn[sc]], w=[dpv])
                        for hh, vdst in enumerate((vs, vw)):
                            src = pv[:, hh * 64:hh * 64 + 1]
                            src3 = bass.AP(tensor=src.tensor, offset=src.offset, ap=[list(src.ap[0]), [128, 4], [1, 64]])
                            fw.op("act", lambda e: e.activation(out=vdst[0][:, sc * 4:sc * 4 + 4, 0:64], in_=src3, func=AF.Copy),
                                  r=[dpv], w=[vdst[1][sc]])
                    for i, tsrc in enumerate((tc_, tv_)):
                        fw.dma("pool", w1s[0][0:64, :, :], w1src[i], w=[w1s[1]])
                        ph, dph = psA.next()
                        for l in range(32):
                            a = tsrc[0][0:96, l:l + 1]
                            rhs = bass.AP(tensor=a.tensor, offset=a.offset, ap=[list(a.ap[0]), [16, 127]])
                            fw.op("pe", lambda e: e.matmul(ph[:, 0:127], lhsT=w1[i][0][0:96, l, :], rhs=rhs, start=(l == 0), stop=(l == 31)),
                                  r=[w1[i][1]] + tsrc[1], w=[dph])
                        xg, dxg = f32b.next()
                        tg_, dtg = f32b.next()
                        fw.op("act", lambda e: e.activation(out=xg[:, 0:127], in_=ph[:, 0:127], func=AF.Identity, bias=cb1[:, i:i + 1], scale=1.0),
                              r=[dph, dcb1], w=[dxg])
                        fw.op("dve", lambda e: e.tensor_tensor(out=tg_[:, 0:127], in0=xg[:, 0:127], in1=xg[:, 0:127], op=ALU.mult), r=[dxg], w=[dtg])
                        fw.op("dve", lambda e: e.tensor_scalar(out=tg_[:, 0:127], in0=tg_[:, 0:127], scalar1=0.044715, scalar2=1.0, op0=ALU.mult, op1=ALU.add),
                              r=[dtg], w=[dtg])
                        fw.op("dve", lambda e: e.tensor_tensor(out=tg_[:, 0:127], in0=tg_[:, 0:127], in1=xg[:, 0:127], op=ALU.mult), r=[dxg, dtg], w=[dtg])
                        fw.op("act", lambda e: e.activation(out=tg_[:, 0:127], in_=tg_[:, 0:127], func=AF.Sigmoid, scale=1.5957691216), r=[dtg], w=[dtg])
                        gb, dgb = sqb.next()
                        fw.op("dve", lambda e: e.tensor_tensor(out=gb[:, 0:127], in0=tg_[:, 0:127], in1=xg[:, 0:127], op=ALU.mult), r=[dxg, dtg], w=[dgb])
                        if i == 0:
                            pk, dpk = psA.next()
                            fw.op("pe", lambda e: e.matmul(pk[0:64, 0:127], lhsT=w2[0][0][:, :], rhs=gb[:, 0:127], start=True, stop=True),
                                  r=[w2[0][1], dgb], w=[dpk])
                            kf, dkf = f32b.next()
                            fw.op("act", lambda e: e.activation(out=kf[0:64, 0:127], in_=pk[0:64, 0:127], func=AF.Identity, bias=b2k[:, 0:1], scale=1.0),
                                  r=[dpk, gl], w=[dkf])
                            sq, dsq = sqb.next()
                            fw.op("act", lambda e: e.activation(out=sq[0:64, 0:127], in_=kf[0:64, 0:127], func=AF.Square), r=[dkf], w=[dsq])
                            ps2, dp2 = psB.next()
                            fw.op("pe", lambda e: e.matmul(ps2[0:64, 0:127], lhsT=blk_ones[0:128, 0:64], rhs=sq[0:128, 0:127], start=True, stop=True),
                                  r=[dsq, dcon], w=[dp2])
                            rt, drt = f32b.next()
                            fw.op("act", lambda e: e.activation(out=rt[0:64, 0:127], in_=ps2[0:64, 0:127], func=AF.Ln, bias=EPS, scale=1.0 / 64), r=[dp2], w=[drt])
                            fw.op("act", lambda e: e.activation(out=rt[0:64, 0:127], in_=rt[0:64, 0:127], func=AF.Exp, scale=-0.5), r=[drt], w=[drt])
                            fw.op("dve", lambda e: e.scalar_tensor_tensor(out=kcT[0][0:64, 0:127], in0=kf[0:64, 0:127], scalar=hg[0:64, 6:7], in1=rt[0:64, 0:127],
                                                                          op0=ALU.mult, op1=ALU.mult), r=[dkf, drt, dcon], w=[kcT[1]])
                        else:
                            pk, dpk = psA.next()
                            fw.op("pe", lambda e: e.matmul(pk[0:127, 0:64], lhsT=gb[:, 0:127], rhs=w2[1][0][:, :], start=True, stop=True),
                                  r=[w2[1][1], dgb], w=[dpk])
                            fw.op("dve", lambda e: e.tensor_tensor(out=vcA[0][0:127, 0:64], in0=pk[0:127, 0:64], in1=b2v[0:127, :], op=ALU.add),
                                  r=[dpk, gl], w=[vcA[1]])
                    if g == 0:
                        dump("kcT", kcT[0][:], [64, 128], [kcT[1]])
                        dump("vcA", vcA[0][:], [128, 128], [vcA[1]])
                        dump("ksT", ks[0][0:64, :], [64, S], ks[1])
                    for pr in range(2):
                        t, d_ = wQ[pr]
                        load_w3("pool", t, d_, w_in_nsa, (g * 4 + pr * 2) * 64, 128, 0)
                    for pr in range(2):
                        t, d_ = wQ[pr]
                        for sc in range(4):
                            pt, dp = proj_fm(t, d_, sc, 128, 0)
                            a, b_ = qa[pr * 2], qa[pr * 2 + 1]
                            headnorm(pt, dp, 128, 4, [(a[0][0:64, cs_(sc)], a[1][sc], 0), (b_[0][0:64, cs_(sc)], b_[1][sc], 64)])
                    for qd in qa:
                        fw.op("dve", lambda e: e.memset(qd[0][64:96, 0:1024], 0.0), w=[qd[1][0], qd[1][1]])
                    for qt in range(8, 16):
                        fw.op("dve", lambda e: e.memset(imp[0][:, qt, :], 0.0), w=[imp[1][qt]])
                    def build_cmp_E(hl):
                        Ec = EE[hl % 2]
                        build_E(g * 4 + hl, Ec[0], Ec[1], None, None, stage, dstage, pstride=16, off=31, rows=127)

                    build_cmp_E(0)
                    build_cmp_E(1)
                    items = []
                    for hl in range(4):
                        h = g * 4 + hl
                        pair, hh = h // 2, h % 2
                        qt_, dq_ = qa[hl]
                        Ec = EE[hl % 2]
                        for qc in range(4):
                            def post(pb, dpb, qc=qc):
                                if qc < 2:
                                    return
                                pi, dpi = psB.next()
                                for j in range(4):
                                    fw.op("pe", lambda e: e.matmul(pi[:, j * 64:j * 64 + 33], lhsT=pb[0:127, j * 128:(j + 1) * 128], rhs=ovl[0:127, :],
                                                                   start=True, stop=True), r=[dpb, gl], w=[dpi])
                                rdi, drdi = sel_t["rdi"]
                                src = pi[:, 32:33]
                                src3 = bass.AP(tensor=src.tensor, offset=src.offset, ap=[list(src.ap[0]), [64, 4]])
                                fw.op("dve", lambda e: e.reciprocal(out=rdi[:, 0:4], in_=src3), r=[dpi], w=[drdi])
                                for j in range(4):
                                    qtile = qc * 4 + j
                                    fw.op("dve", lambda e: e.scalar_tensor_tensor(out=imp[0][:, qtile, :], in0=pi[:, j * 64:j * 64 + 32], scalar=rdi[:, j:j + 1],
                                                                                  in1=imp[0][:, qtile, :], op0=ALU.mult, op1=ALU.add),
                                          r=[dpi, drdi, imp[1][qtile]], w=[imp[1][qtile]])

                            def cb_cmp(qc, po, dpo, h=h, hl=hl):
                                rd, drd = f32b.next()
                                recip_den(rd, drd, po, dpo)
                                fw.op("dve", lambda e: e.tensor_tensor(out=rd[0:64, :], in0=po[0:64, :], in1=rd[64:128, :], op=ALU.mult), r=[dpo, drd], w=[drd])
                                pgb, dpgb = gate_bc(h, 0, qc)
                                fw.op("dve", lambda e: e.tensor_tensor(out=ocmp[hl][0][0:64, cs_(qc)], in0=rd[0:64, :], in1=pgb[0:64, :], op=ALU.mult),
                                      r=[drd, dpgb], w=[ocmp[hl][1][qc]])
                            items.append(dict(
                                rows=127, n=512, lo=0, hi=512, K=128,
                                lhsT=kcT[0][0:128, 0:127], rhs=qt_[0:128, cs_(qc)], sdeps=[kcT[1], dq_[qc]],
                                E=Ec[0][0:127, cs_(qc)], dE=Ec[1], v=vcA[0][0:127, :], dv=vcA[1],
                                first=True, last=True, cb=cb_cmp, qc=qc, post=post, after=None))
                        if hl + 2 < 4:
                            items[-1]["after"] = (lambda hl=hl: build_cmp_E(hl + 2))
                    run_items(items)
                    vals, dvals = sel_t["vals"]
                    lt, dlt = sel_t["lt"]
                    v2, dv2 = sel_t["v2"]
                    m8a, dm8a = sel_t["m8a"]
                    m8b, dm8b = sel_t["m8b"]
                    for qtile in range(8, 16):
                        sc = qtile // 4
                        fw.op("dve", lambda e: e.tensor_tensor(out=vals[:], in0=imp[0][:, qtile, :], in1=addm[:, qtile, :], op=ALU.add),
                              r=[imp[1][qtile], gl], w=[dvals])
                        fw.op("dve", lambda e: e.max(out=m8a[:], in_=vals[:]), r=[dvals], w=[dm8a])
                        fw.op("dve", lambda e: e.tensor_scalar(out=lt[:], in0=vals[:], scalar1=m8a[:, 7:8], scalar2=None, op0=ALU.is_lt), r=[dvals, dm8a], w=[dlt])
                        fw.op("dve", lambda e: e.tensor_tensor(out=v2[:], in0=vals[:], in1=lt[:], op=ALU.mult), r=[dvals, dlt], w=[dv2])
                        fw.op("dve", lambda e: e.tensor_scalar(out=lt[:], in0=lt[:], scalar1=-1.0, scalar2=BIG, op0=ALU.add, op1=ALU.mult), r=[dlt, dv2], w=[dlt])
                        fw.op("dve", lambda e: e.tensor_tensor(out=v2[:], in0=v2[:], in1=lt[:], op=ALU.add), r=[dlt, dv2], w=[dv2])
                        fw.op("dve", lambda e: e.max(out=m8b[:], in_=v2[:]), r=[dv2], w=[dm8b])
                        fw.op("dve", lambda e: e.tensor_scalar(out=lt[:], in0=vals[:], scalar1=m8b[:, 4:5], scalar2=None, op0=ALU.is_ge), r=[dvals, dm8b, dlt], w=[dlt])
                        fw.op("dve", lambda e: e.tensor_tensor(out=lt[:], in0=lt[:], in1=forced[:, qtile, :], op=ALU.max), r=[dlt, gl], w=[dlt])
                        fw.op("dve", lambda e: e.tensor_scalar(out=nmp[0][:, 64:96], in0=lt[:], scalar1=-1.0, scalar2=-NEG, op0=ALU.add, op1=ALU.mult),
                              r=[dlt], w=[nmp[1]])
                        p2, dp2 = psB.next()
                        fw.op("pe", lambda e: e.matmul(p2[0:96, 0:128], lhsT=nmp[0][:, 0:96], rhs=ident_bf[:], start=True, stop=True),
                              r=[nmp[1], dcon], w=[dp2])
                        for qd in qa:
                            fw.op("act", lambda e: e.activation(out=qd[0][64:96, qtile * 128:(qtile + 1) * 128], in_=p2[64:96, 0:128], func=AF.Copy),
                                  r=[dp2], w=[qd[1][sc]])
                    if g == 0:
                        dump("imp", imp[0][:], [128, 16, 32], imp[1])
                        dump("qa0", qa[0][0][:], [96, S], qa[0][1])
                    def build_sw_E(hl):
                        Es, Ew = EE[hl % 2], EW[hl % 2]
                        build_E(g * 4 + hl, Es[0], Es[1], None, None, stage, dstage)
                        fw.op("dve", lambda e: e.tensor_tensor(out=Ew[0][:, :], in0=Es[0][:, 0:640], in1=Mw[0][:, :], op=ALU.mult),
                              r=[Es[1], Mw[1]], w=[Ew[1]])

                    build_sw_E(0)
                    build_sw_E(1)
                    items = []
                    for hl in range(4):
                        h = g * 4 + hl
                        pair, hh = h // 2, h % 2
                        Es, Ew = EE[hl % 2], EW[hl % 2]
                        acc = {}

                        def cb_slc(qc, po, dpo, h=h, acc=acc):
                            rd, drd = f32b.next()
                            recip_den(rd, drd, po, dpo)
                            fw.op("dve", lambda e: e.tensor_tensor(out=rd[0:64, :], in0=po[0:64, :], in1=rd[64:128, :], op=ALU.mult), r=[dpo, drd], w=[drd])
                            pgb, dpgb = gate_bc(h, 1, qc)
                            tb, dtb = tsb.next()
                            fw.op("dve", lambda e: e.tensor_tensor(out=tb[:, :], in0=rd[0:64, :], in1=pgb[0:64, :], op=ALU.mult), r=[drd, dpgb], w=[dtb])
                            acc[qc] = (tb, dtb)

                        def cb_win(qc, po, dpo, h=h, pair=pair, hh=hh, hl=hl, acc=acc):
                            rd, drd = f32b.next()
                            a_, da_ = acc[qc]
                            recip_den(rd, drd, po, dpo)
                            fw.op("dve", lambda e: e.tensor_tensor(out=rd[0:64, :], in0=po[0:64, :], in1=rd[64:128, :], op=ALU.mult), r=[dpo, drd], w=[drd])
                            pgb, dpgb = gate_bc(h, 2, qc)
                            tb, dtb = tsb.next()
                            fw.op("dve", lambda e: e.tensor_tensor(out=tb[:, :], in0=rd[0:64, :], in1=pgb[0:64, :], op=ALU.mult), r=[drd, dpgb], w=[dtb])
                            fw.op("dve", lambda e: e.tensor_tensor(out=tb[:, :], in0=tb[:, :], in1=a_[:, :], op=ALU.add), r=[dtb, da_], w=[dtb])
                            fw.op("dve", lambda e: e.tensor_tensor(out=oT[hh * 64:hh * 64 + 64, pair, cs_(qc)], in0=tb[:, :], in1=ocmp[hl][0][0:64, cs_(qc)], op=ALU.add),
                                  r=[dtb, ocmp[hl][1][qc]], w=[doT[pair][qc]])

                        for qc in range(4):
                            items += make_items(qa[hl][0], qa[hl][1], 128, ks[0], ks[1], vs[0], vs[1], Es[0], Es[1], causal_tiles, cb_slc, chunks=[qc])
                            items += make_items(qa[hl][0], qa[hl][1], 128, kw[0], kw[1], vw[0], vw[1], Ew[0], Ew[1], win_tiles, cb_win, chunks=[qc])
                        if hl + 2 < 4:
                            items[-1]["after"] = (lambda hl=hl: build_sw_E(hl + 2))
                    run_items(items)
                pass
                fw.barrier()

        alldh = [d_ for row in dh for d_ in row]
        alldo = [d_ for row in doT for d_ in row]
        with contextlib.ExitStack() as st:
            hT = sb("hT", [128, 8, S], F32, st)
            load_h(xT, [])
            rmsnorm(0)
            dump("xn0", xnT[:], [128, 8, S], dxn)
            fw.barrier()
        if upto >= 1:
            layer0_mixer()
            dump("oT0", oT[:], [128, 8, S], alldo)
        with contextlib.ExitStack() as st:
            hT = sb("hT", [128, 8, S], F32, st)
            load_h(xT, [])
            if upto >= 1:
                out_proj(w_out_ab, st)
                dump("hmix0", hT[:], [128, 8, S], alldh)
                fw.barrier()
            if upto >= 2:
                ffn(0)
                dump("hffn0", hT[:], [128, 8, S], alldh)
            if upto >= 3:
                ple(0)
                dump("h0", hT[:], [128, 8, S], alldh)
            if upto >= 4:
                rmsnorm(3)
                store_h(hS, [dhS])
            fw.barrier()
            if upto < 4:
                store_h(outT, [])
        if upto >= 4:
            layer1_mixer()
            dump("oT1", oT[:], [128, 8, S], alldo)
            with contextlib.ExitStack() as st:
                hT = sb("hT", [128, 8, S], F32, st)
                load_h(hS, [dhS])
                out_proj(w_out_nsa, st)
                dump("hmix1", hT[:], [128, 8, S], alldh)
                fw.barrier()
                if upto >= 5:
                    ffn(1)
                if upto >= 6:
                    ple(1)
                store_h(outT, [])
                fw.barrier()
        fw.finish("sp")
        build_nc.stats = (fw.n_ins, fw.n_wait)
    return nc, consts, DBG


def host_inputs(inputs, b, consts):
    f = lambda a: np.ascontiguousarray(np.asarray(a, dtype=np.float32))
    m = {}
    m["xT"] = f(inputs["x"][b].T)
    m["pT"] = f(np.transpose(inputs["p"][:, b], (0, 2, 1)))
    rb = np.asarray(inputs["rel_bias"], np.float32)
    dd = np.maximum(2047 - np.arange(WHL), 0)
    whb = rb[_bucket(dd), :].T.copy()
    whb[:, 2048:] = NEG
    m["whb"] = f(whb)
    gl = []
    for layer in range(2):
        for nm in ("norm_mix", "norm_ffn", "norm_ple"):
            gl.append(np.asarray(inputs[nm][layer], np.float32).reshape(8, 128).T)
    m["gains"] = f(np.concatenate(gl, axis=1))
    t2 = lambda a: np.concatenate([np.asarray(a, np.float32)] * 2)
    hg = np.zeros((128, 8), np.float32)
    hg[:, 0] = t2(inputs["qn_moba"][0])
    hg[:, 1] = t2(inputs["kn_moba"][0])
    hg[:, 2] = t2(inputs["qn_dil"][0])
    hg[:, 3] = t2(inputs["kn_dil"][0])
    hg[:, 4] = t2(inputs["qn_nsa"][0])
    hg[:, 5] = np.concatenate([np.asarray(inputs["kn_slc"][0], np.float32), np.asarray(inputs["kn_win"][0], np.float32)])
    hg[:, 6] = t2(inputs["kn_cmp"][0])
    m["hg"] = hg
    m["posT"] = f(np.concatenate([np.asarray(inputs["cmp_k_pos"][0]).T, np.asarray(inputs["cmp_v_pos"][0]).T], axis=1))
    m["b1c"] = f(np.stack([inputs["cmp_k_b1"][0], inputs["cmp_v_b1"][0]], axis=1))
    m["b2k"] = f(np.asarray(inputs["cmp_k_b2"][0]).reshape(64, 1))
    m["b2v"] = f(np.asarray(inputs["cmp_v_b2"][0]).reshape(1, 64))
    m["w_in_ab"] = f(inputs["w_in_ab"][0])
    m["w_out_ab"] = f(inputs["w_out_ab"][0])
    m["w_in_nsa"] = f(inputs["w_in_nsa"][0])
    m["w_out_nsa"] = f(inputs["w_out_nsa"][0])
    for nm in ("w_ffn_gate", "w_ffn_up", "w_ffn_down", "w_ple_proj", "w_ple_gate"):
        m[nm] = f(inputs[nm])
    m["cmp_k_w1"] = f(inputs["cmp_k_w1"][0])
    m["cmp_k_w2"] = f(inputs["cmp_k_w2"][0])
    m["cmp_v_w1"] = f(inputs["cmp_v_w1"][0])
    m["cmp_v_w2"] = f(inputs["cmp_v_w2"][0])
    for k, v in consts.items():
        m[k] = v
    return m


def kernel(**inputs):
    nc, consts, _ = build_nc()
    in_maps = [host_inputs(inputs, b, consts) for b in range(8)]
    res = run_bass_kernel_spmd(nc, in_maps, core_ids=list(range(8)))
    out = np.stack([np.asarray(r["outT"], np.float32).T for r in res.results], axis=0)
    return np.ascontiguousarray(out.astype(np.float32))
```

```python
import math
import contextlib
import numpy as np
import concourse.bass as bass
import concourse.mybir as mybir
from concourse.bass_utils import run_bass_kernel_spmd

F32 = mybir.dt.float32
BF16 = mybir.dt.bfloat16
AF = mybir.ActivationFunctionType
ALU = mybir.AluOpType
AX = mybir.AxisListType

S = 2048
D = 1024
FH = 2816
NF = 22
WHL = 4352
EPS = 1e-6
NEG = -30000.0
BIG = 3.0e38


class Dep:
    __slots__ = ("w", "r")

    def __init__(self):
        self.w = None
        self.r = []


class FW:
    NDMA = 24

    def __init__(self, nc, es):
        self.nc = nc
        self.engs = {"pe": nc.tensor, "act": nc.scalar, "dve": nc.vector, "pool": nc.gpsimd, "sp": nc.sync}
        self.sems = {}
        self.cnt = {}
        for k in self.engs:
            self.sems[k] = es.enter_context(nc.semaphore("sem_" + k))
            self.cnt[k] = 0
        for i in range(self.NDMA):
            k = ("dma", i)
            self.sems[k] = es.enter_context(nc.semaphore("sem_dma%d" % i))
            self.cnt[k] = 0
        self.seen = {e: {} for e in self.engs}
        self.dma_rr = {"sp": 0, "pool": 0, "act": 0}
        self.n_ins = 0
        self.n_wait = 0

    def _wait(self, eng, deps):
        seen = self.seen[eng]
        need = {}
        for d in deps:
            if d is None:
                continue
            k, v = d
            if k == "pe" and eng == "pe":
                continue
            if seen.get(k, 0) >= v:
                continue
            if need.get(k, 0) < v:
                need[k] = v
        for k, v in need.items():
            self.engs[eng].wait_ge(self.sems[k], v)
            seen[k] = v
            self.n_wait += 1

    @staticmethod
    def _collect(r, w):
        deps = []
        for t in r:
            deps.append(t.w)
        for t in w:
            deps.append(t.w)
            deps.extend(t.r)
        return deps

    def _mark(self, tok, r, w):
        for t in w:
            t.w = tok
            t.r = []
        for t in r:
            t.r.append(tok)
            if len(t.r) > 64:
                best = {}
                for k, v in t.r:
                    if best.get(k, 0) < v:
                        best[k] = v
                t.r = list(best.items())

    def op(self, eng, fn, r=(), w=()):
        self._wait(eng, self._collect(r, w))
        ins = fn(self.engs[eng])
        self.cnt[eng] += 1
        ins.then_inc(self.sems[eng], 1)
        self._mark((eng, self.cnt[eng]), r, w)
        self.n_ins += 1

    def dma(self, q, out, in_, r=(), w=()):
        half = self.NDMA // 2
        i = self.dma_rr[q]
        self.dma_rr[q] = (i + 1) % half
        k = ("dma", i + (half if q == "pool" else 0))
        deps = self._collect(r, w)
        if self.cnt[k] > 0:
            deps.append((k, self.cnt[k]))
        self._wait(q, deps)
        ins = self.engs[q].dma_start(out=out, in_=in_)
        self.cnt[k] += 16
        ins.then_inc(self.sems[k], 16)
        self._mark((k, self.cnt[k]), r, w)
        self.n_ins += 1

    def barrier(self):
        allk = [(k, v) for k, v in self.cnt.items() if v > 0]
        for e in self.engs:
            self._wait(e, allk)

    def finish(self, eng="sp"):
        allk = [(k, v) for k, v in self.cnt.items() if v > 0]
        self._wait(eng, allk)


class Rot:
    def __init__(self, items):
        self.items = items
        self.i = 0

    def next(self):
        t = self.items[self.i]
        self.i = (self.i + 1) % len(self.items)
        return t


def _bucket(d):
    n = np.maximum(d, 0)
    nf = np.maximum(n, 1).astype(np.float32)
    large = 16 + (np.log(nf / np.float32(16)) / np.float32(math.log(128.0)) * np.float32(16)).astype(np.int32)
    return np.where(n < 16, n, np.minimum(large, 31))


def _static_consts():
    c = {}
    m = np.arange(WHL)
    d = 2047 - m
    wm = np.zeros((3, WHL), np.float32)
    mult = (d >= 0) * ((d <= 128).astype(np.float32) + ((d % 4 == 0) & (d <= 512)) + ((d % 16 == 0) & (d <= 2048)))
    wm[1] = np.where(mult > 0, 8.0 * np.log(np.maximum(mult, 1.0)), 8.0 * NEG)
    wm[2] = np.where((d >= 0) & (d < 512), 0.0, 8.0 * NEG)
    c["c_wm"] = wm
    c["c_ident"] = np.eye(128, dtype=np.float32)
    k = np.arange(S)
    c["c_blk_moba"] = (k[None, :] // 256 == np.arange(8)[:, None]).astype(np.float32)
    c["c_blk_nsa"] = (k[None, :] // 64 == np.arange(32)[:, None]).astype(np.float32)
    g = np.zeros((128, 4, 8), np.float32)
    for i, own in enumerate(range(4, 8)):
        g[:, i, own:] = -BIG
    c["c_gneg"] = g
    add = np.full((128, 16, 32), -BIG, np.float32)
    forced = np.zeros((128, 16, 32), np.float32)
    for qt in range(16):
        for q in range(128):
            cur = (qt * 128 + q) // 64
            for n in (0, cur, cur - 1):
                if n >= 0:
                    forced[q, qt, n] = 1.0
            for n in range(1, cur - 1):
                add[q, qt, n] = 0.0
    c["c_addmask"] = add
    c["c_forced"] = forced
    cs = np.arange(127) * 16
    ss = np.arange(32) * 64
    ov = np.maximum(np.minimum(cs[:, None] + 32, ss[None, :] + 64) - np.maximum(cs[:, None], ss[None, :]), 0)
    ovl = np.ones((127, 33), np.float32)
    ovl[:, :32] = ov
    c["c_ovl"] = ovl
    gs = np.zeros((48, 48, 64), np.float32)
    for i in range(48):
        gs[i, i, :] = 1.0
    c["c_gsel"] = gs
    return c


_CONST_SHAPES = None


def _dap(t, offset, ap):
    return bass.AP(tensor=t.tensor, offset=offset, ap=[list(a) for a in ap])


def build_nc(upto=99, dbg=()):
    nc = bass.Bass("TRN2", target_bir_lowering=False)
    consts = _static_consts()
    IN = {}

    def din(name, shape):
        IN[name] = nc.dram_tensor(name, list(shape), F32, kind="ExternalInput").ap()
        return IN[name]

    xT = din("xT", [D, S])
    pT = din("pT", [2, 256, S])
    whb = din("whb", [16, WHL])
    gains_d = din("gains", [128, 48])
    hg_d = din("hg", [128, 8])
    posT_d = din("posT", [64, 64])
    b1_d = din("b1c", [128, 2])
    b2k_d = din("b2k", [64, 1])
    b2v_d = din("b2v", [1, 64])
    w_in_ab = din("w_in_ab", [D, 3072])
    w_out_ab = din("w_out_ab", [D, D])
    w_in_nsa = din("w_in_nsa", [D, 2608])
    w_out_nsa = din("w_out_nsa", [D, D])
    w_g = din("w_ffn_gate", [2, D, FH])
    w_u = din("w_ffn_up", [2, D, FH])
    w_d = din("w_ffn_down", [2, FH, D])
    w_pp = din("w_ple_proj", [2, 256, D])
    w_pg = din("w_ple_gate", [2, D, D])
    ck_w1 = din("cmp_k_w1", [2048, 128])
    ck_w2 = din("cmp_k_w2", [128, 64])
    cv_w1 = din("cmp_v_w1", [2048, 128])
    cv_w2 = din("cmp_v_w2", [128, 64])
    for k, v in consts.items():
        din(k, v.shape)
    outT = nc.dram_tensor("outT", [D, S], F32, kind="ExternalOutput").ap()
    DBG = {}

    with contextlib.ExitStack() as es:
        fw = FW(nc, es)

        uniq = [0]

        def sb(name, shape, dt=F32, stack=es):
            uniq[0] += 1
            return stack.enter_context(nc.sbuf_tensor("%s_%d" % (name, uniq[0]), list(shape), dt))

        def pst(name):
            return es.enter_context(nc.psum_tensor(name, [128, 512], F32))

        psS = Rot([(pst("psS%d" % i), Dep()) for i in range(3)])
        psO = Rot([(pst("psO%d" % i), Dep()) for i in range(2)])
        psA = Rot([(pst("psM%d" % i), Dep()) for i in range(3)])
        psB = psA

        def dump(name, ap, shape, deps):
            if name not in dbg:
                return
            t = nc.dram_tensor("dbg_" + name, list(shape), ap.dtype if hasattr(ap, "dtype") else F32, kind="ExternalOutput").ap()
            DBG[name] = t
            fw.dma("sp", t, ap, r=deps)

        hS = nc.dram_tensor("hS", [D, S], F32, kind="Internal").ap()
        dhS = Dep()
        dh = [[Dep() for _ in range(4)] for _ in range(8)]
        xnT = sb("xnT", [128, 8, S], BF16)
        dxn = [Dep() for _ in range(4)]
        oT = sb("oT", [128, 8, S], BF16)
        doT = [[Dep() for _ in range(4)] for _ in range(8)]
        hT = None
        gains = sb("gains_sb", [128, 48])
        hg = sb("hg_sb", [128, 8])
        dcon = Dep()
        ones_bf = sb("ones_bf", [128, 128], BF16)
        blk_ones = sb("blk_ones", [128, 128], BF16)
        ident_bf = sb("ident_bf", [128, 128], BF16)
        sqb = Rot([(sb("sqb%d" % i, [128, 512], BF16), Dep()) for i in range(2)])
        f32b = Rot([(sb("f32b%d" % i, [128, 512]), Dep()) for i in range(6)])

        fw.dma("sp", gains[:], gains_d[:, :], w=[dcon])
        fw.dma("sp", hg[:], hg_d[:, :], w=[dcon])
        fw.dma("pool", ident_bf[:], IN["c_ident"][:, :], w=[dcon])
        fw.op("dve", lambda e: e.memset(ones_bf[:], 1.0), w=[dcon])
        fw.op("dve", lambda e: e.memset(blk_ones[:], 0.0), w=[dcon])
        fw.op("dve", lambda e: e.memset(blk_ones[0:64, 0:64], 1.0), w=[dcon])
        fw.op("dve", lambda e: e.memset(blk_ones[64:128, 64:128], 1.0), w=[dcon])
        def load_h(src, dsrc):
            v = src.rearrange("(c p) s -> p c s", p=128)
            for c in range(8):
                for sc in range(4):
                    fw.dma("sp", hT[:, c, sc * 512:(sc + 1) * 512], v[:, c, sc * 512:(sc + 1) * 512], r=dsrc, w=[dh[c][sc]])

        def store_h(dst, ddst):
            v = dst.rearrange("(c p) s -> p c s", p=128)
            for c in range(8):
                for sc in range(4):
                    fw.dma("sp", v[:, c, sc * 512:(sc + 1) * 512], hT[:, c, sc * 512:(sc + 1) * 512], r=[dh[c][sc]], w=ddst)

        def cs_(sc):
            return slice(sc * 512, (sc + 1) * 512)

        def rmsnorm(gidx):
            for sc in range(4):
                cs = cs_(sc)
                pt, dp = psB.next()
                for c in range(8):
                    sq, dsq = sqb.next()
                    fw.op("act", lambda e: e.activation(out=sq[:], in_=hT[:, c, cs], func=AF.Square), r=[dh[c][sc]], w=[dsq])
                    fw.op("pe", lambda e: e.matmul(pt[:], lhsT=ones_bf[:], rhs=sq[:], start=(c == 0), stop=(c == 7)),
                          r=[dsq, dcon], w=[dp])
                rt, drt = f32b.next()
                fw.op("act", lambda e: e.activation(out=rt[:], in_=pt[:], func=AF.Ln, bias=EPS, scale=1.0 / D), r=[dp], w=[drt])
                fw.op("act", lambda e: e.activation(out=rt[:], in_=rt[:], func=AF.Exp, scale=-0.5), r=[drt], w=[drt])
                for c in range(8):
                    fw.op("dve", lambda e: e.scalar_tensor_tensor(
                        out=xnT[:, c, cs], in0=hT[:, c, cs], scalar=gains[:, gidx * 8 + c:gidx * 8 + c + 1], in1=rt[:],
                        op0=ALU.mult, op1=ALU.mult), r=[dh[c][sc], drt, dcon], w=[dxn[sc]])

        def proj_fm(wt, dw, sc, M, c0=0):
            pt, dp = psA.next()
            for kc in range(8):
                fw.op("pe", lambda e: e.matmul(pt[0:M, :], lhsT=wt[:, kc, c0:c0 + M], rhs=xnT[:, kc, cs_(sc)],
                                               start=(kc == 0), stop=(kc == 7)), r=[dw, dxn[sc]], w=[dp])
            return pt, dp

        def headnorm(pt, dp, M, gcol, outs):
            sq, dsq = sqb.next()
            fw.op("act", lambda e: e.activation(out=sq[0:M, :], in_=pt[0:M, :], func=AF.Square), r=[dp], w=[dsq])
            ps2, dp2 = psB.next()
            fw.op("pe", lambda e: e.matmul(ps2[0:M, :], lhsT=blk_ones[0:M, 0:M], rhs=sq[0:M, :], start=True, stop=True),
                  r=[dsq, dcon], w=[dp2])
            rt, drt = f32b.next()
            fw.op("act", lambda e: e.activation(out=rt[0:M, :], in_=ps2[0:M, :], func=AF.Ln, bias=EPS, scale=1.0 / 64), r=[dp2], w=[drt])
            fw.op("act", lambda e: e.activation(out=rt[0:M, :], in_=rt[0:M, :], func=AF.Exp, scale=-0.5), r=[drt], w=[drt])
            for (dst, ddst, r0) in outs:
                fw.op("dve", lambda e: e.scalar_tensor_tensor(
                    out=dst, in0=pt[r0:r0 + 64, :], scalar=hg[r0:r0 + 64, gcol:gcol + 1], in1=rt[r0:r0 + 64, :],
                    op0=ALU.mult, op1=ALU.mult), r=[dp, drt, dcon], w=[ddst])

        def load_w3(q, wt, dw, src2d, c0, ncols, dst_c0=0):
            v = src2d.rearrange("(kc p) n -> p kc n", p=128)
            fw.dma(q, wt[:, :, dst_c0:dst_c0 + ncols], v[:, :, c0:c0 + ncols], w=[dw])

        def rev(t, n, rows=128):
            a = t[0:rows, n - 1:n]
            return bass.AP(tensor=a.tensor, offset=a.offset, ap=[list(a.ap[0]), [-1, n]])

        pbuf_items = []

        LOOK = 2
        CBDELAY = 2

        def make_items(qt, dq, K, ktile, dk, vt, dv, E, dE, tiles_fn, out_cb, chunks=range(4)):
            items = []
            for qc in chunks:
                c0 = qc * 512
                tiles = tiles_fn(qc)
                assert tiles[0][1] == 0 and tiles[0][2] == 512
                for idx, (kt, lo, hi) in enumerate(tiles):
                    n = hi - lo
                    u0 = c0 + lo - kt * 128
                    items.append(dict(
                        rows=128, n=n, lo=lo, hi=hi, K=K,
                        lhsT=ktile[0:K, kt * 128:(kt + 1) * 128], rhs=qt[0:K, c0 + lo:c0 + hi], sdeps=[dk[kt // 4], dq[qc]],
                        E=E[:, u0:u0 + n], dE=dE, v=vt[:, kt, :], dv=dv[kt // 4],
                        first=(idx == 0), last=(idx == len(tiles) - 1), cb=out_cb, qc=qc, post=None, after=None))
            return items

        def run_items(items):
            staged = {}
            cur = [None]
            pend = []
            n_it = len(items)

            def fire(force_to):
                while pend and (pend[0][0] <= 0 or len(pend) > force_to):
                    pend.pop(0)[1]()

            for j in range(n_it + LOOK):
                if j < n_it:
                    it = items[j]
                    pss, dps = psS.next()
                    fw.op("pe", lambda e: e.matmul(pss[0:it["rows"], 0:it["n"]], lhsT=it["lhsT"], rhs=it["rhs"], start=True, stop=False),
                          r=it["sdeps"], w=[dps])
                    fw.op("pe", lambda e: e.matmul(pss[0:it["rows"], 0:it["n"]], lhsT=ident_bf[0:it["rows"], 0:it["rows"]], rhs=it["E"], start=False, stop=True),
                          r=[it["dE"], dcon], w=[dps])
                    staged[j] = (pss, dps)
                i = j - LOOK
                if i < 0:
                    continue
                it = items[i]
                pss, dps = staged.pop(i)
                R, n = it["rows"], it["n"]
                pb, dpb = pbuf.next()
                fw.op("act", lambda e: e.activation(out=pb[0:R, 0:n], in_=pss[0:R, 0:n], func=AF.Exp, scale=0.125), r=[dps], w=[dpb])
                if it["first"]:
                    fire(1)
                    cur[0] = psO.next()
                po, dpo = cur[0]
                fw.op("pe", lambda e: e.matmul(po[:, it["lo"]:it["hi"]], lhsT=it["v"], rhs=pb[0:R, 0:n], start=it["first"], stop=it["last"]),
                      r=[it["dv"], dpb], w=[dpo])
                for p_ in pend:
                    p_[0] -= 1
                if it["post"] is not None:
                    pend.append([CBDELAY, (lambda it=it, pb=pb, dpb=dpb: it["post"](pb, dpb))])
                if it["last"]:
                    pend.append([CBDELAY, (lambda it=it, po=po, dpo=dpo: it["cb"](it["qc"], po, dpo))])
                fire(99)
                if it.get("after") is not None:
                    it["after"]()
            fire(0)

        def attend(*args, **kw):
            run_items(make_items(*args, **kw))

        def recip_den(rd, drd, po, dpo):
            fw.op("act", lambda e: e.activation(out=rd[64:128, :], in_=po[64:128, :], func=AF.Ln, bias=1e-30, scale=1.0), r=[dpo], w=[drd])
            fw.op("act", lambda e: e.activation(out=rd[64:128, :], in_=rd[64:128, :], func=AF.Exp, scale=-1.0), r=[drd], w=[drd])

        def causal_tiles(qc):
            res = []
            for kt in range(4 * qc + 4):
                lo = max(0, kt * 128 - qc * 512)
                res.append((kt, lo, 512))
            return res

        def win_tiles(qc):
            res = []
            order = [4 * qc] + [k for k in range(max(0, 4 * qc - 4), 4 * qc + 4) if k != 4 * qc]
            for kt in order:
                off = kt * 128 - qc * 512
                lo = max(0, off)
                hi = min(512, ((off + 638) // 128 + 1) * 128)
                res.append((kt, lo, hi))
            return res

        def build_E(H, E, dE, M, dM, stage, dstage, width=2048, pstride=1, off=0, rows=128):
            fw.dma("sp", stage[0:128, 0:width], _dap(whb, H * WHL + off + (2048 - width), [[pstride, 128], [1, width]]), w=[dstage])
            if M is None:
                fw.op("act", lambda e: e.activation(out=E[0:rows, 0:width], in_=rev(stage, width, rows), func=AF.Identity, scale=8.0), r=[dstage], w=[dE])
            else:
                fw.op("dve", lambda e: e.scalar_tensor_tensor(out=E[0:rows, 0:width], in0=rev(stage, width, rows), scalar=8.0, in1=M[0:rows, 0:width],
                                                              op0=ALU.mult, op1=ALU.add), r=[dstage, dM], w=[dE])

        def build_M(kind, M, dM, stage, dstage, width=2048, pstride=1, off=0, rows=128):
            fw.dma("sp", stage[0:128, 0:width], _dap(IN["c_wm"], kind * WHL + off + (2048 - width), [[pstride, 128], [1, width]]), w=[dstage])
            fw.op("act", lambda e: e.activation(out=M[0:rows, 0:width], in_=rev(stage, width, rows), func=AF.Copy), r=[dstage], w=[dM])

        def out_proj(w_out, st):
            wo = [(sb("wo%d" % i, [128, 8, 128], BF16, st), Dep()) for i in range(2)]
            wv = w_out.rearrange("(kc p) n -> p kc n", p=128)

            def ld(fc):
                t, d_ = wo[fc % 2]
                fw.dma("pool", t[:], wv[:, :, fc * 128:(fc + 1) * 128], w=[d_])
            ld(0)
            for fc in range(8):
                if fc + 1 < 8:
                    ld(fc + 1)
                t, d_ = wo[fc % 2]
                for sc in range(4):
                    pt, dp = psA.next()
                    for pr in range(8):
                        fw.op("pe", lambda e: e.matmul(pt[:], lhsT=t[:, pr, :], rhs=oT[:, pr, cs_(sc)], start=(pr == 0), stop=(pr == 7)),
                              r=[d_, doT[pr][sc]], w=[dp])
                    fw.op("dve", lambda e: e.tensor_tensor(out=hT[:, fc, cs_(sc)], in0=pt[:], in1=hT[:, fc, cs_(sc)], op=ALU.add),
                          r=[dp, dh[fc][sc]], w=[dh[fc][sc]])

        def ffn(layer):
            rmsnorm(layer * 3 + 1)
            with contextlib.ExitStack() as st:
                act2 = sb("ffn_act", [128, NF - 16, 1024], BF16, st)
                dact = [[Dep() for _ in range(2)] for _ in range(NF)]

                def act_ap(f, q):
                    if f < 16:
                        return oT[:, f // 2, (f % 2) * 1024 + q * 512:(f % 2) * 1024 + (q + 1) * 512]
                    return act2[:, f - 16, q * 512:(q + 1) * 512]
                wg = [(sb("wg%d" % i, [128, 8, 128], BF16, st), Dep()) for i in range(2)]
                wu = [(sb("wu%d" % i, [128, 8, 128], BF16, st), Dep()) for i in range(2)]
                wd = [(sb("wd%d" % i, [128, NF, 128], BF16, st), Dep()) for i in range(2)]
                sg = Rot([(sb("sg%d" % i, [128, 512], F32, st), Dep()) for i in range(2)])
                wgv = w_g[layer].rearrange("(kc p) n -> p kc n", p=128)
                wuv = w_u[layer].rearrange("(kc p) n -> p kc n", p=128)
                wdv = w_d[layer].rearrange("(f p) n -> p f n", p=128)

                def ld1(f):
                    fw.dma("pool", wg[f % 2][0][:], wgv[:, :, f * 128:(f + 1) * 128], w=[wg[f % 2][1]])
                    fw.dma("pool", wu[f % 2][0][:], wuv[:, :, f * 128:(f + 1) * 128], w=[wu[f % 2][1]])

                def ld2(dc):
                    fw.dma("pool", wd[dc % 2][0][:], wdv[:, :, dc * 128:(dc + 1) * 128], w=[wd[dc % 2][1]])

                for half in range(2):
                    ld1(0)
                    for f in range(NF):
                        if f + 1 < NF:
                            ld1(f + 1)
                        else:
                            ld2(0)
                        tg, dg_ = wg[f % 2]
                        tu, du_ = wu[f % 2]
                        for q in range(2):
                            sc = half * 2 + q
                            pg, dpg = psA.next()
                            pu, dpu = psB.next()
                            for kc in range(8):
                                fw.op("pe", lambda e: e.matmul(pg[:], lhsT=tg[:, kc, :], rhs=xnT[:, kc, cs_(sc)], start=(kc == 0), stop=(kc == 7)),
                                      r=[dg_, dxn[sc]], w=[dpg])
                            for kc in range(8):
                                fw.op("pe", lambda e: e.matmul(pu[:], lhsT=tu[:, kc, :], rhs=xnT[:, kc, cs_(sc)], start=(kc == 0), stop=(kc == 7)),
                                      r=[du_, dxn[sc]], w=[dpu])
                            s_, ds_ = sg.next()
                            fw.op("act", lambda e: e.activation(out=s_[:], in_=pg[:], func=AF.Silu), r=[dpg], w=[ds_])
                            fw.op("dve", lambda e: e.tensor_tensor(out=act_ap(f, q), in0=s_[:], in1=pu[:], op=ALU.mult),
                                  r=[ds_, dpu], w=[dact[f][q]])
                    for dc in range(8):
                        if dc + 1 < 8:
                            ld2(dc + 1)
                        td, dd_ = wd[dc % 2]
                        for q in range(2):
                            sc = half * 2 + q
                            pt, dp = psS.next()
                            for f in range(NF):
                                fw.op("pe", lambda e: e.matmul(pt[:], lhsT=td[:, f, :], rhs=act_ap(f, q),
                                                               start=(f == 0), stop=(f == NF - 1)), r=[dd_, dact[f][q]], w=[dp])
                            fw.op("dve", lambda e: e.tensor_tensor(out=hT[:, dc, cs_(sc)], in0=pt[:], in1=hT[:, dc, cs_(sc)], op=ALU.add),
                                  r=[dp, dh[dc][sc]], w=[dh[dc][sc]])
                fw.barrier()

        def ple(layer):
            rmsnorm(layer * 3 + 2)
            with contextlib.ExitStack() as st:
                pTs = sb("pTs", [128, 2, S], BF16, st)
                dpT = Dep()
                wpg = [(sb("wpg%d" % i, [128, 8, 128], BF16, st), Dep()) for i in range(2)]
                wpp = [(sb("wpp%d" % i, [128, 2, 128], BF16, st), Dep()) for i in range(2)]
                sg = Rot([(sb("psg%d" % i, [128, 512], F32, st), Dep()) for i in range(2)])
                fw.dma("pool", pTs[:], pT[layer].rearrange("(kc p) s -> p kc s", p=128), w=[dpT])
                wgv = w_pg[layer].rearrange("(kc p) n -> p kc n", p=128)
                wpv = w_pp[layer].rearrange("(kc p) n -> p kc n", p=128)

                def ld(fc):
                    fw.dma("pool", wpg[fc % 2][0][:], wgv[:, :, fc * 128:(fc + 1) * 128], w=[wpg[fc % 2][1]])
                    fw.dma("pool", wpp[fc % 2][0][:], wpv[:, :, fc * 128:(fc + 1) * 128], w=[wpp[fc % 2][1]])
                ld(0)
                for fc in range(8):
                    if fc + 1 < 8:
                        ld(fc + 1)
                    tg, dg_ = wpg[fc % 2]
                    tp, dp_ = wpp[fc % 2]
                    for sc in range(4):
                        pg, dpg = psA.next()
                        pp_, dpp = psB.next()
                        for kc in range(8):
                            fw.op("pe", lambda e: e.matmul(pg[:], lhsT=tg[:, kc, :], rhs=xnT[:, kc, cs_(sc)], start=(kc == 0), stop=(kc == 7)),
                                  r=[dg_, dxn[sc]], w=[dpg])
                        for kc in range(2):
                            fw.op("pe", lambda e: e.matmul(pp_[:], lhsT=tp[:, kc, :], rhs=pTs[:, kc, cs_(sc)], start=(kc == 0), stop=(kc == 1)),
                                  r=[dp_, dpT], w=[dpp])
                        s_, ds_ = sg.next()
                        fw.op("act", lambda e: e.activation(out=s_[:], in_=pg[:], func=AF.Sigmoid), r=[dpg], w=[ds_])
                        fw.op("dve", lambda e: e.tensor_tensor(out=s_[:], in0=s_[:], in1=pp_[:], op=ALU.mult), r=[ds_, dpp], w=[ds_])
                        fw.op("dve", lambda e: e.tensor_tensor(out=hT[:, fc, cs_(sc)], in0=s_[:], in1=hT[:, fc, cs_(sc)], op=ALU.add),
                              r=[ds_, dh[fc][sc]], w=[dh[fc][sc]])
                fw.barrier()

        def layer0_mixer():
            with contextlib.ExitStack() as st:
                qh = [(sb("qh%d" % i, [128, S], BF16, st), [Dep() for _ in range(4)]) for i in range(2)]
                kh = [(sb("kh%d" % i, [128, S], BF16, st), [Dep() for _ in range(4)]) for i in range(2)]
                vh = [(sb("vh%d" % i, [128, 16, 128], BF16, st), [Dep() for _ in range(4)]) for i in range(2)]
                Eh = [(sb("Eh%d" % i, [128, S], BF16, st), Dep()) for i in range(2)]
                Mt = (sb("Mt", [128, S], BF16, st), Dep())
                stage = sb("stage", [128, S], F32, st)
                dstage = Dep()
                wq = [(sb("wq%d" % i, [128, 8, 384], BF16, st), Dep()) for i in range(2)]
                global_pbuf = [(sb("pbuf%d" % i, [128, 512], BF16, st), Dep()) for i in range(4)]
                nonlocal pbuf
                pbuf = Rot(global_pbuf)
                km = [(sb("km%d" % i, [64, 8], F32, st), Dep()) for i in range(2)]
                kmb = [(sb("kmb%d" % i, [72, 8], BF16, st), Dep()) for i in range(2)]
                gneg = sb("gneg", [128, 4, 8], F32, st)
                gm = sb("gm", [128, 8], F32, st)
                dgm = Dep()
                m8 = sb("m8", [128, 8], F32, st)
                dm8 = Dep()
                nmp = [(sb("nmp%d" % i, [128, 72], BF16, st), Dep()) for i in range(4)]
                dl0 = Dep()
                fw.dma("sp", gneg[:], IN["c_gneg"][:, :, :], w=[dl0])
                for i in range(4):
                    fw.op("dve", lambda e: e.memset(nmp[i][0][:], 0.0), w=[nmp[i][1]])
                for i in range(2):
                    fw.op("dve", lambda e: e.memset(kmb[i][0][:], 0.0), w=[kmb[i][1]])
                for i in range(2):
                    fw.op("dve", lambda e: e.memset(vh[i][0][:, :, 64:128], 1.0), w=vh[i][1])
                    fw.op("dve", lambda e: e.memset(kh[i][0][64:128, :], 0.0), w=kh[i][1])
                    fw.op("dve", lambda e: e.memset(qh[i][0][64:128, :], 0.0), w=qh[i][1])
                    fw.dma("pool", kh[i][0][64:72, :], IN["c_blk_moba"][:, :], w=kh[i][1])
                build_M(1, Mt[0], Mt[1], stage, dstage)

                def ldw(pair):
                    t, d_ = wq[pair % 2]
                    base = 0 if pair < 4 else 1536
                    pp = pair % 4
                    load_w3("pool", t, d_, w_in_ab, base + pp * 128, 128, 0)
                    load_w3("pool", t, d_, w_in_ab, base + 512 + pp * 128, 128, 128)
                    load_w3("pool", t, d_, w_in_ab, base + 1024 + pp * 128, 128, 256)

                ldw(0)
                for pair in range(8):
                    moba = pair < 4
                    if pair + 1 < 8:
                        ldw(pair + 1)
                    wt, dw = wq[pair % 2]
                    gq = 0 if moba else 2
                    gk = 1 if moba else 3
                    for sc in range(4):
                        pt, dp = proj_fm(wt, dw, sc, 128, 0)
                        headnorm(pt, dp, 128, gq, [(qh[0][0][0:64, cs_(sc)], qh[0][1][sc], 0), (qh[1][0][0:64, cs_(sc)], qh[1][1][sc], 64)])
                        pt, dp = proj_fm(wt, dw, sc, 128, 128)
                        headnorm(pt, dp, 128, gk, [(kh[0][0][0:64, cs_(sc)], kh[0][1][sc], 0), (kh[1][0][0:64, cs_(sc)], kh[1][1][sc], 64)])
                        if moba:
                            for hh in range(2):
                                kin = kh[hh][0][0:64, cs_(sc)]
                                kin3 = bass.AP(tensor=kin.tensor, offset=kin.offset, ap=[list(kin.ap[0]), [256, 2], [1, 256]])
                                fw.op("dve", lambda e: e.tensor_reduce(out=km[hh][0][0:64, 2 * sc:2 * sc + 2], in_=kin3, axis=AX.X, op=ALU.add),
                                      r=[kh[hh][1][sc]], w=[km[hh][1]])
                        pv, dpv = psA.next()
                        for j in range(4):
                            tt = sc * 4 + j
                            for kc in range(8):
                                fw.op("pe", lambda e: e.matmul(pv[:, j * 128:(j + 1) * 128], lhsT=xnT[:, kc, tt * 128:(tt + 1) * 128],
                                                               rhs=wt[:, kc, 256:384], start=(kc == 0), stop=(kc == 7)),
                                      r=[dw, dxn[sc]], w=[dpv])
                        for hh in range(2):
                            src = pv[:, hh * 64:hh * 64 + 1]
                            src3 = bass.AP(tensor=src.tensor, offset=src.offset, ap=[list(src.ap[0]), [128, 4], [1, 64]])
                            fw.op("act", lambda e: e.activation(out=vh[hh][0][:, sc * 4:sc * 4 + 4, 0:64], in_=src3, func=AF.Copy),
                                  r=[dpv], w=[vh[hh][1][sc]])
                    if pair == 0:
                        dump("q0", qh[0][0][0:64, :], [64, S], qh[0][1])
                        dump("k0", kh[0][0][0:64, :], [64, S], kh[0][1])
                        dump("v0", vh[0][0][:], [128, 16, 128], vh[0][1])
                    if moba:
                        for hh in range(2):
                            qt_, dq_ = qh[hh]
                            fw.op("dve", lambda e: e.tensor_copy(out=kmb[hh][0][0:64, :], in_=km[hh][0][:]), r=[km[hh][1]], w=[kmb[hh][1]])
                            fw.op("dve", lambda e: e.memset(qt_[64:72, 0:1024], 0.0), w=[dq_[0], dq_[1]])
                            for qtile in range(8, 16):
                                own = qtile // 2
                                sc = qtile // 4
                                pg, dpg = psB.next()
                                fw.op("pe", lambda e: e.matmul(pg[:, 0:8], lhsT=qt_[0:72, qtile * 128:(qtile + 1) * 128], rhs=kmb[hh][0][0:72, 0:8],
                                                               start=True, stop=True), r=[dq_[sc], kmb[hh][1]], w=[dpg])
                                fw.op("dve", lambda e: e.tensor_tensor(out=gm[:], in0=pg[:, 0:8], in1=gneg[:, own - 4, :], op=ALU.add),
                                      r=[dpg, dl0], w=[dgm])
                                fw.op("dve", lambda e: e.max(out=m8[:], in_=gm[:]), r=[dgm], w=[dm8])
                                nt, dnt = nmp[own - 4]
                                fw.op("dve", lambda e: e.tensor_scalar(out=nt[:, 64:64 + own], in0=gm[:, 0:own], scalar1=m8[:, 2:3], scalar2=NEG,
                                                                       op0=ALU.is_lt, op1=ALU.mult), r=[dgm, dm8], w=[dnt])
                                p2, dp2 = psB.next()
                                fw.op("pe", lambda e: e.matmul(p2[0:72, 0:128], lhsT=nt[:, 0:72], rhs=ident_bf[:], start=True, stop=True),
                                      r=[dnt, dcon], w=[dp2])
                                fw.op("act", lambda e: e.activation(out=qt_[64:72, qtile * 128:(qtile + 1) * 128], in_=p2[64:72, 0:128], func=AF.Copy),
                                      r=[dp2], w=[dq_[sc]])
                    if pair == 4:
                        for hh in range(2):
                            fw.op("dve", lambda e: e.memset(qh[hh][0][64:72, :], 0.0), w=qh[hh][1])
                    items = []
                    for hh in range(2):
                        H = pair * 2 + hh
                        E, dE = Eh[hh]
                        M, dM = (None, None) if moba else Mt
                        build_E(H, E, dE, M, dM, stage, dstage)

                        def cb(qc, po, dpo, hh=hh, pair=pair):
                            rd, drd = f32b.next()
                            recip_den(rd, drd, po, dpo)
                            fw.op("dve", lambda e: e.tensor_tensor(out=oT[hh * 64:hh * 64 + 64, pair, cs_(qc)], in0=po[0:64, :], in1=rd[64:128, :], op=ALU.mult),
                                  r=[dpo, drd], w=[doT[pair][qc]])
                        items += make_items(qh[hh][0], qh[hh][1], 128, kh[hh][0], kh[hh][1], vh[hh][0], vh[hh][1], E, dE, causal_tiles, cb)
                    run_items(items)
                fw.barrier()

        pbuf = None

        def layer1_mixer():
            with contextlib.ExitStack() as st:
                nonlocal pbuf
                pbuf = Rot([(sb("pbuf%d" % i, [128, 512], BF16, st), Dep()) for i in range(4)])
                qa = [(sb("qa%d" % i, [128, S], BF16, st), [Dep() for _ in range(4)]) for i in range(4)]
                ks = (sb("ksT", [128, S], BF16, st), [Dep() for _ in range(4)])
                kw = (sb("kwT", [128, S], BF16, st), [Dep() for _ in range(4)])
                vs = (sb("vsA", [128, 16, 128], BF16, st), [Dep() for _ in range(4)])
                vw = (sb("vwA", [128, 16, 128], BF16, st), [Dep() for _ in range(4)])
                ocmp = [(sb("ocmp%d" % i, [64, S], BF16, st), [Dep() for _ in range(4)]) for i in range(4)]
                tc_ = qa[2]
                tv_ = qa[3]
                kcT = (sb("kcT", [128, 128], BF16, st), Dep())
                vcA = (sb("vcA", [128, 128], BF16, st), Dep())
                EE = [(sb("EE%d" % i, [128, S], BF16, st), Dep()) for i in range(2)]
                EW = [(sb("EW%d" % i, [128, 640], BF16, st), Dep()) for i in range(2)]
                Mw = (sb("Mw", [128, 640], BF16, st), Dep())
                stage = sb("stage", [128, S], F32, st)
                dstage = Dep()
                gsig = (sb("gsig", [96, S], BF16, st), [Dep() for _ in range(4)])
                gsel = sb("gsel", [96, 48, 64], BF16, st)
                ovl = sb("ovl", [128, 33], BF16, st)
                addm = sb("addm", [128, 16, 32], F32, st)
                forced = sb("forced", [128, 16, 32], F32, st)
                imp = (sb("imp", [128, 16, 32], F32, st), [Dep() for _ in range(16)])
                w1s = (sb("w1s", [96, 32, 128], BF16, st), Dep())
                w1 = [w1s, w1s]
                w2 = [(sb("w2_%d" % i, [128, 64], BF16, st), Dep()) for i in range(2)]
                posT = sb("posT", [96, 64], BF16, st)
                b1c = sb("b1c", [128, 2], F32, st)
                b2k = sb("b2k", [64, 1], F32, st)
                b2v = sb("b2v", [128, 64], F32, st)
                cb1 = sb("cb1", [128, 2], F32, st)
                dcb1 = Dep()
                wA = [(sb("wA%d" % i, [128, 8, 128], BF16, st), Dep()) for i in range(3)]
                wQ = [wA[0], wA[1]]
                wG = (sb("wG", [128, 8, 48], BF16, st), Dep())
                tsb = Rot([(sb("tsb%d" % i, [64, 512], BF16, st), Dep()) for i in range(4)])
                sel_t = {}
                for nm_, shp in [("vals", [128, 32]), ("lt", [128, 32]), ("v2", [128, 32]), ("m8a", [128, 8]), ("m8b", [128, 8]), ("rdi", [128, 4])]:
                    sel_t[nm_] = (sb("sel_" + nm_, shp, F32, st), Dep())
                nmp = (sb("nmp1", [128, 96], BF16, st), Dep())
                gl = Dep()
                fw.op("dve", lambda e: e.memset(gsel[:], 0.0), w=[gl])
                fw.op("dve", lambda e: e.memset(gsig[0][:], 0.0), w=gsig[1])
                fw.op("dve", lambda e: e.memset(posT[:], 0.0), w=[gl])
                fw.op("dve", lambda e: e.memset(w1s[0][:], 0.0), w=[w1s[1]])
                fw.op("dve", lambda e: e.memset(kcT[0][:], 0.0), w=[kcT[1]])
                fw.op("dve", lambda e: e.memset(kw[0][64:128, :], 0.0), w=kw[1])
                fw.op("dve", lambda e: e.memset(ks[0][64:128, :], 0.0), w=ks[1])
                for qd in qa:
                    fw.op("dve", lambda e: e.memset(qd[0][64:128, :], 0.0), w=qd[1])
                fw.dma("pool", gsel[0:48, :, :], IN["c_gsel"][:, :, :], w=[gl])
                fw.dma("pool", ovl[0:127, :], IN["c_ovl"][:, :], w=[gl])
                fw.dma("sp", addm[:], IN["c_addmask"][:, :, :], w=[gl])
                fw.dma("sp", forced[:], IN["c_forced"][:, :, :], w=[gl])
                fw.dma("pool", posT[0:64, :], posT_d[:, :], w=[gl])
                fw.dma("sp", b1c[:], b1_d[:, :], w=[gl])
                fw.dma("sp", b2k[:], b2k_d[:, :], w=[gl])
                fw.dma("sp", b2v[:], _dap(b2v_d, 0, [[0, 128], [1, 64]]), w=[gl])
                w1src = [ck_w1.rearrange("(l d) j -> d l j", d=64), cv_w1.rearrange("(l d) j -> d l j", d=64)]
                fw.dma("pool", w2[0][0][:], ck_w2[:, :], w=[w2[0][1]])
                fw.dma("pool", w2[1][0][:], cv_w2[:, :], w=[w2[1][1]])
                fw.dma("pool", ks[0][64:96, :], IN["c_blk_nsa"][:, :], w=ks[1])
                fw.op("dve", lambda e: e.memset(nmp[0][:], 0.0), w=[nmp[1]])
                fw.op("dve", lambda e: e.memset(vs[0][:, :, 64:128], 1.0), w=vs[1])
                fw.op("dve", lambda e: e.memset(vw[0][:, :, 64:128], 1.0), w=vw[1])
                fw.op("dve", lambda e: e.memset(vcA[0][:, 64:128], 1.0), w=[vcA[1]])
                build_M(2, Mw[0], Mw[1], stage, dstage, width=640)
                for i in range(2):
                    fw.dma("pool", w1s[0][0:64, :, :], w1src[i], w=[w1s[1]])
                    pt, dp = psB.next()
                    for l in range(32):
                        fw.op("pe", lambda e: e.matmul(pt[:, 0:1], lhsT=w1[i][0][0:96, l, :], rhs=posT[0:96, i * 32 + l:i * 32 + l + 1],
                                                       start=(l == 0), stop=(l == 31)), r=[w1[i][1], gl], w=[dp])
                    fw.op("dve", lambda e: e.tensor_tensor(out=cb1[:, i:i + 1], in0=pt[:, 0:1], in1=b1c[:, i:i + 1], op=ALU.add),
                          r=[dp, gl], w=[dcb1])
                load_w3("pool", wG[0], wG[1], w_in_nsa, 2560, 48, 0)
                for sc in range(4):
                    pt, dp = proj_fm(wG[0], wG[1], sc, 48, 0)
                    fw.op("act", lambda e: e.activation(out=gsig[0][0:48, cs_(sc)], in_=pt[0:48, :], func=AF.Sigmoid), r=[dp], w=[gsig[1][sc]])

                def gate_bc(h, j, qc):
                    pt, dp = psB.next()
                    fw.op("pe", lambda e: e.matmul(pt[0:64, :], lhsT=gsel[0:96, h * 3 + j, :], rhs=gsig[0][0:96, cs_(qc)], start=True, stop=True),
                          r=[gl, gsig[1][qc]], w=[dp])
                    return pt, dp

                for g in range(4):
                    load_w3("pool", wA[0][0], wA[0][1], w_in_nsa, 1536 + g * 64, 64, 0)
                    load_w3("pool", wA[0][0], wA[0][1], w_in_nsa, 2048 + g * 64, 64, 64)
                    load_w3("pool", wA[1][0], wA[1][1], w_in_nsa, 1024 + g * 64, 64, 0)
                    load_w3("pool", wA[1][0], wA[1][1], w_in_nsa, 1280 + g * 64, 64, 64)
                    load_w3("pool", wA[2][0], wA[2][1], w_in_nsa, 1792 + g * 64, 64, 0)
                    load_w3("pool", wA[2][0], wA[2][1], w_in_nsa, 2304 + g * 64, 64, 64)
                    for sc in range(4):
                        pt, dp = proj_fm(wA[0][0], wA[0][1], sc, 128, 0)
                        headnorm(pt, dp, 128, 5, [(ks[0][0:64, cs_(sc)], ks[1][sc], 0), (kw[0][0:64, cs_(sc)], kw[1][sc], 64)])
                        pt, dp = proj_fm(wA[1][0], wA[1][1], sc, 128, 0)
                        fw.op("act", lambda e: e.activation(out=tc_[0][0:64, cs_(sc)], in_=pt[0:64, :], func=AF.Copy), r=[dp], w=[tc_[1][sc]])
                        fw.op("act", lambda e: e.activation(out=tv_[0][0:64, cs_(sc)], in_=pt[64:128, :], func=AF.Copy), r=[dp], w=[tv_[1][sc]])
                        pv, dpv = psA.next()
                        for j in range(4):
                            tt = sc * 4 + j
                            for kc in range(8):
                                fw.op("pe", lambda e: e.matmul(pv[:, j * 128:(j + 1) * 128], lhsT=xnT[:, kc, tt * 128:(tt + 1) * 128],
                                                               rhs=wA[2][0][:, kc, :], start=(kc == 0), stop=(kc == 7)),
                                      r=[wA[2][1], dxn[sc]], w=[dpv])
                        for hh, vdst in enumerate((vs, vw)):
                            src = pv[:, hh * 64:hh * 64 + 1]
                            src3 = bass.AP(tensor=src.tensor, offset=src.offset, ap=[list(src.ap[0]), [128, 4], [1, 64]])
                            fw.op("act", lambda e: e.activation(out=vdst[0][:, sc * 4:sc * 4 + 4, 0:64], in_=src3, func=AF.Copy),
                                  r=[dpv], w=[vdst[1][sc]])
                    for i, tsrc in enumerate((tc_, tv_)):
                        fw.dma("pool", w1s[0][0:64, :, :], w1src[i], w=[w1s[1]])
                        ph, dph = psA.next()
                        for l in range(32):
                            a = tsrc[0][0:96, l:l + 1]
                            rhs = bass.AP(tensor=a.tensor, offset=a.offset, ap=[list(a.ap[0]), [16, 127]])
                            fw.op("pe", lambda e: e.matmul(ph[:, 0:127], lhsT=w1[i][0][0:96, l, :], rhs=rhs, start=(l == 0), stop=(l == 31)),
                                  r=[w1[i][1]] + tsrc[1], w=[dph])
                        xg, dxg = f32b.next()
                        tg_, dtg = f32b.next()
                        fw.op("act", lambda e: e.activation(out=xg[:, 0:127], in_=ph[:, 0:127], func=AF.Identity, bias=cb1[:, i:i + 1], scale=1.0),
                              r=[dph, dcb1], w=[dxg])
                        fw.op("dve", lambda e: e.tensor_tensor(out=tg_[:, 0:127], in0=xg[:, 0:127], in1=xg[:, 0:127], op=ALU.mult), r=[dxg], w=[dtg])
                        fw.op("dve", lambda e: e.tensor_scalar(out=tg_[:, 0:127], in0=tg_[:, 0:127], scalar1=0.044715, scalar2=1.0, op0=ALU.mult, op1=ALU.add),
                              r=[dtg], w=[dtg])
                        fw.op("dve", lambda e: e.tensor_tensor(out=tg_[:, 0:127], in0=tg_[:, 0:127], in1=xg[:, 0:127], op=ALU.mult), r=[dxg, dtg], w=[dtg])
                        fw.op("act", lambda e: e.activation(out=tg_[:, 0:127], in_=tg_[:, 0:127], func=AF.Sigmoid, scale=1.5957691216), r=[dtg], w=[dtg])
                        gb, dgb = sqb.next()
                        fw.op("dve", lambda e: e.tensor_tensor(out=gb[:, 0:127], in0=tg_[:, 0:127], in1=xg[:, 0:127], op=ALU.mult), r=[dxg, dtg], w=[dgb])
                        if i == 0:
                            pk, dpk = psA.next()
                            fw.op("pe", lambda e: e.matmul(pk[0:64, 0:127], lhsT=w2[0][0][:, :], rhs=gb[:, 0:127], start=True, stop=True),
                                  r=[w2[0][1], dgb], w=[dpk])
                            kf, dkf = f32b.next()
                            fw.op("act", lambda e: e.activation(out=kf[0:64, 0:127], in_=pk[0:64, 0:127], func=AF.Identity, bias=b2k[:, 0:1], scale=1.0),
                                  r=[dpk, gl], w=[dkf])
                            sq, dsq = sqb.next()
                            fw.op("act", lambda e: e.activation(out=sq[0:64, 0:127], in_=kf[0:64, 0:127], func=AF.Square), r=[dkf], w=[dsq])
                            ps2, dp2 = psB.next()
                            fw.op("pe", lambda e: e.matmul(ps2[0:64, 0:127], lhsT=blk_ones[0:128, 0:64], rhs=sq[0:128, 0:127], start=True, stop=True),
                                  r=[dsq, dcon], w=[dp2])
                            rt, drt = f32b.next()
                            fw.op("act", lambda e: e.activation(out=rt[0:64, 0:127], in_=ps2[0:64, 0:127], func=AF.Ln, bias=EPS, scale=1.0 / 64), r=[dp2], w=[drt])
                            fw.op("act", lambda e: e.activation(out=rt[0:64, 0:127], in_=rt[0:64, 0:127], func=AF.Exp, scale=-0.5), r=[drt], w=[drt])
                            fw.op("dve", lambda e: e.scalar_tensor_tensor(out=kcT[0][0:64, 0:127], in0=kf[0:64, 0:127], scalar=hg[0:64, 6:7], in1=rt[0:64, 0:127],
                                                                          op0=ALU.mult, op1=ALU.mult), r=[dkf, drt, dcon], w=[kcT[1]])
                        else:
                            pk, dpk = psA.next()
                            fw.op("pe", lambda e: e.matmul(pk[0:127, 0:64], lhsT=gb[:, 0:127], rhs=w2[1][0][:, :], start=True, stop=True),
                                  r=[w2[1][1], dgb], w=[dpk])
                            fw.op("dve", lambda e: e.tensor_tensor(out=vcA[0][0:127, 0:64], in0=pk[0:127, 0:64], in1=b2v[0:127, :], op=ALU.add),
                                  r=[dpk, gl], w=[vcA[1]])
                    if g == 0:
                        dump("kcT", kcT[0][:], [64, 128], [kcT[1]])
                        dump("vcA", vcA[0][:], [128, 128], [vcA[1]])
                        dump("ksT", ks[0][0:64, :], [64, S], ks[1])
                    for pr in range(2):
                        t, d_ = wQ[pr]
                        load_w3("pool", t, d_, w_in_nsa, (g * 4 + pr * 2) * 64, 128, 0)
                    for pr in range(2):
                        t, d_ = wQ[pr]
                        for sc in range(4):
                            pt, dp = proj_fm(t, d_, sc, 128, 0)
                            a, b_ = qa[pr * 2], qa[pr * 2 + 1]
                            headnorm(pt, dp, 128, 4, [(a[0][0:64, cs_(sc)], a[1][sc], 0), (b_[0][0:64, cs_(sc)], b_[1][sc], 64)])
                    for qd in qa:
                        fw.op("dve", lambda e: e.memset(qd[0][64:96, 0:1024], 0.0), w=[qd[1][0], qd[1][1]])
                    for qt in range(8, 16):
                        fw.op("dve", lambda e: e.memset(imp[0][:, qt, :], 0.0), w=[imp[1][qt]])
                    def build_cmp_E(hl):
                        Ec = EE[hl % 2]
                        build_E(g * 4 + hl, Ec[0], Ec[1], None, None, stage, dstage, pstride=16, off=31, rows=127)

                    build_cmp_E(0)
                    build_cmp_E(1)
                    items = []
                    for hl in range(4):
                        h = g * 4 + hl
                        pair, hh = h // 2, h % 2
                        qt_, dq_ = qa[hl]
                        Ec = EE[hl % 2]
                        for qc in range(4):
                            def post(pb, dpb, qc=qc):
                                if qc < 2:
                                    return
                                pi, dpi = psB.next()
                                for j in range(4):
                                    fw.op("pe", lambda e: e.matmul(pi[:, j * 64:j * 64 + 33], lhsT=pb[0:127, j * 128:(j + 1) * 128], rhs=ovl[0:127, :],
                                                                   start=True, stop=True), r=[dpb, gl], w=[dpi])
                                rdi, drdi = sel_t["rdi"]
                                src = pi[:, 32:33]
                                src3 = bass.AP(tensor=src.tensor, offset=src.offset, ap=[list(src.ap[0]), [64, 4]])
                                fw.op("dve", lambda e: e.reciprocal(out=rdi[:, 0:4], in_=src3), r=[dpi], w=[drdi])
                                for j in range(4):
                                    qtile = qc * 4 + j
                                    fw.op("dve", lambda e: e.scalar_tensor_tensor(out=imp[0][:, qtile, :], in0=pi[:, j * 64:j * 64 + 32], scalar=rdi[:, j:j + 1],
                                                                                  in1=imp[0][:, qtile, :], op0=ALU.mult, op1=ALU.add),
                                          r=[dpi, drdi, imp[1][qtile]], w=[imp[1][qtile]])

                            def cb_cmp(qc, po, dpo, h=h, hl=hl):
                                rd, drd = f32b.next()
                                recip_den(rd, drd, po, dpo)
                                fw.op("dve", lambda e: e.tensor_tensor(out=rd[0:64, :], in0=po[0:64, :], in1=rd[64:128, :], op=ALU.mult), r=[dpo, drd], w=[drd])
                                pgb, dpgb = gate_bc(h, 0, qc)
                                fw.op("dve", lambda e: e.tensor_tensor(out=ocmp[hl][0][0:64, cs_(qc)], in0=rd[0:64, :], in1=pgb[0:64, :], op=ALU.mult),
                                      r=[drd, dpgb], w=[ocmp[hl][1][qc]])
                            items.append(dict(
                                rows=127, n=512, lo=0, hi=512, K=128,
                                lhsT=kcT[0][0:128, 0:127], rhs=qt_[0:128, cs_(qc)], sdeps=[kcT[1], dq_[qc]],
                                E=Ec[0][0:127, cs_(qc)], dE=Ec[1], v=vcA[0][0:127, :], dv=vcA[1],
                                first=True, last=True, cb=cb_cmp, qc=qc, post=post, after=None))
                        if hl + 2 < 4:
                            items[-1]["after"] = (lambda hl=hl: build_cmp_E(hl + 2))
                    run_items(items)
                    vals, dvals = sel_t["vals"]
                    lt, dlt = sel_t["lt"]
                    v2, dv2 = sel_t["v2"]
                    m8a, dm8a = sel_t["m8a"]
                    m8b, dm8b = sel_t["m8b"]
                    for qtile in range(8, 16):
                        sc = qtile // 4
                        fw.op("dve", lambda e: e.tensor_tensor(out=vals[:], in0=imp[0][:, qtile, :], in1=addm[:, qtile, :], op=ALU.add),
                              r=[imp[1][qtile], gl], w=[dvals])
                        fw.op("dve", lambda e: e.max(out=m8a[:], in_=vals[:]), r=[dvals], w=[dm8a])
                        fw.op("dve", lambda e: e.tensor_scalar(out=lt[:], in0=vals[:], scalar1=m8a[:, 7:8], scalar2=None, op0=ALU.is_lt), r=[dvals, dm8a], w=[dlt])
                        fw.op("dve", lambda e: e.tensor_tensor(out=v2[:], in0=vals[:], in1=lt[:], op=ALU.mult), r=[dvals, dlt], w=[dv2])
                        fw.op("dve", lambda e: e.tensor_scalar(out=lt[:], in0=lt[:], scalar1=-1.0, scalar2=BIG, op0=ALU.add, op1=ALU.mult), r=[dlt, dv2], w=[dlt])
                        fw.op("dve", lambda e: e.tensor_tensor(out=v2[:], in0=v2[:], in1=lt[:], op=ALU.add), r=[dlt, dv2], w=[dv2])
                        fw.op("dve", lambda e: e.max(out=m8b[:], in_=v2[:]), r=[dv2], w=[dm8b])
                        fw.op("dve", lambda e: e.tensor_scalar(out=lt[:], in0=vals[:], scalar1=m8b[:, 4:5], scalar2=None, op0=ALU.is_ge), r=[dvals, dm8b, dlt], w=[dlt])
                        fw.op("dve", lambda e: e.tensor_tensor(out=lt[:], in0=lt[:], in1=forced[:, qtile, :], op=ALU.max), r=[dlt, gl], w=[dlt])
                        fw.op("dve", lambda e: e.tensor_scalar(out=nmp[0][:, 64:96], in0=lt[:], scalar1=-1.0, scalar2=-NEG, op0=ALU.add, op1=ALU.mult),
                              r=[dlt], w=[nmp[1]])
                        p2, dp2 = psB.next()
                        fw.op("pe", lambda e: e.matmul(p2[0:96, 0:128], lhsT=nmp[0][:, 0:96], rhs=ident_bf[:], start=True, stop=True),
                              r=[nmp[1], dcon], w=[dp2])
                        for qd in qa:
                            fw.op("act", lambda e: e.activation(out=qd[0][64:96, qtile * 128:(qtile + 1) * 128], in_=p2[64:96, 0:128], func=AF.Copy),
                                  r=[dp2], w=[qd[1][sc]])
                    if g == 0:
                        dump("imp", imp[0][:], [128, 16, 32], imp[1])
                        dump("qa0", qa[0][0][:], [96, S], qa[0][1])
                    def build_sw_E(hl):
                        Es, Ew = EE[hl % 2], EW[hl % 2]
                        build_E(g * 4 + hl, Es[0], Es[1], None, None, stage, dstage)
                        fw.op("dve", lambda e: e.tensor_tensor(out=Ew[0][:, :], in0=Es[0][:, 0:640], in1=Mw[0][:, :], op=ALU.add),
                              r=[Es[1], Mw[1]], w=[Ew[1]])

                    build_sw_E(0)
                    build_sw_E(1)
                    items = []
                    for hl in range(4):
                        h = g * 4 + hl
                        pair, hh = h // 2, h % 2
                        Es, Ew = EE[hl % 2], EW[hl % 2]
                        acc = {}

                        def cb_slc(qc, po, dpo, h=h, acc=acc):
                            rd, drd = f32b.next()
                            recip_den(rd, drd, po, dpo)
                            fw.op("dve", lambda e: e.tensor_tensor(out=rd[0:64, :], in0=po[0:64, :], in1=rd[64:128, :], op=ALU.mult), r=[dpo, drd], w=[drd])
                            pgb, dpgb = gate_bc(h, 1, qc)
                            tb, dtb = tsb.next()
                            fw.op("dve", lambda e: e.tensor_tensor(out=tb[:, :], in0=rd[0:64, :], in1=pgb[0:64, :], op=ALU.mult), r=[drd, dpgb], w=[dtb])
                            acc[qc] = (tb, dtb)

                        def cb_win(qc, po, dpo, h=h, pair=pair, hh=hh, hl=hl, acc=acc):
                            rd, drd = f32b.next()
                            a_, da_ = acc[qc]
                            recip_den(rd, drd, po, dpo)
                            fw.op("dve", lambda e: e.tensor_tensor(out=rd[0:64, :], in0=po[0:64, :], in1=rd[64:128, :], op=ALU.mult), r=[dpo, drd], w=[drd])
                            pgb, dpgb = gate_bc(h, 2, qc)
                            tb, dtb = tsb.next()
                            fw.op("dve", lambda e: e.tensor_tensor(out=tb[:, :], in0=rd[0:64, :], in1=pgb[0:64, :], op=ALU.mult), r=[drd, dpgb], w=[dtb])
                            fw.op("dve", lambda e: e.tensor_tensor(out=tb[:, :], in0=tb[:, :], in1=a_[:, :], op=ALU.add), r=[dtb, da_], w=[dtb])
                            fw.op("dve", lambda e: e.tensor_tensor(out=oT[hh * 64:hh * 64 + 64, pair, cs_(qc)], in0=tb[:, :], in1=ocmp[hl][0][0:64, cs_(qc)], op=ALU.add),
                                  r=[dtb, ocmp[hl][1][qc]], w=[doT[pair][qc]])

                        for qc in range(4):
                            items += make_items(qa[hl][0], qa[hl][1], 128, ks[0], ks[1], vs[0], vs[1], Es[0], Es[1], causal_tiles, cb_slc, chunks=[qc])
                            items += make_items(qa[hl][0], qa[hl][1], 128, kw[0], kw[1], vw[0], vw[1], Ew[0], Ew[1], win_tiles, cb_win, chunks=[qc])
                        if hl + 2 < 4:
                            items[-1]["after"] = (lambda hl=hl: build_sw_E(hl + 2))
                    run_items(items)
                pass
                fw.barrier()

        alldh = [d_ for row in dh for d_ in row]
        alldo = [d_ for row in doT for d_ in row]
        with contextlib.ExitStack() as st:
            hT = sb("hT", [128, 8, S], F32, st)
            load_h(xT, [])
            rmsnorm(0)
            dump("xn0", xnT[:], [128, 8, S], dxn)
            fw.barrier()
        if upto >= 1:
            layer0_mixer()
            dump("oT0", oT[:], [128, 8, S], alldo)
        with contextlib.ExitStack() as st:
            hT = sb("hT", [128, 8, S], F32, st)
            load_h(xT, [])
            if upto >= 1:
                out_proj(w_out_ab, st)
                dump("hmix0", hT[:], [128, 8, S], alldh)
                fw.barrier()
            if upto >= 2:
                ffn(0)
                dump("hffn0", hT[:], [128, 8, S], alldh)
            if upto >= 3:
                ple(0)
                dump("h0", hT[:], [128, 8, S], alldh)
            if upto >= 4:
                rmsnorm(3)
                store_h(hS, [dhS])
            fw.barrier()
            if upto < 4:
                store_h(outT, [])
        if upto >= 4:
            layer1_mixer()
            dump("oT1", oT[:], [128, 8, S], alldo)
            with contextlib.ExitStack() as st:
                hT = sb("hT", [128, 8, S], F32, st)
                load_h(hS, [dhS])
                out_proj(w_out_nsa, st)
                dump("hmix1", hT[:], [128, 8, S], alldh)
                fw.barrier()
                if upto >= 5:
                    ffn(1)
                if upto >= 6:
                    ple(1)
                store_h(outT, [])
                fw.barrier()
        fw.finish("sp")
        build_nc.stats = (fw.n_ins, fw.n_wait)
    return nc, consts, DBG


def host_inputs(inputs, b, consts):
    f = lambda a: np.ascontiguousarray(np.asarray(a, dtype=np.float32))
    m = {}
    m["xT"] = f(inputs["x"][b].T)
    m["pT"] = f(np.transpose(inputs["p"][:, b], (0, 2, 1)))
    rb = np.asarray(inputs["rel_bias"], np.float32)
    dd = np.maximum(2047 - np.arange(WHL), 0)
    whb = rb[_bucket(dd), :].T.copy()
    whb[:, 2048:] = NEG
    m["whb"] = f(whb)
    gl = []
    for layer in range(2):
        for nm in ("norm_mix", "norm_ffn", "norm_ple"):
            gl.append(np.asarray(inputs[nm][layer], np.float32).reshape(8, 128).T)
    m["gains"] = f(np.concatenate(gl, axis=1))
    t2 = lambda a: np.concatenate([np.asarray(a, np.float32)] * 2)
    hg = np.zeros((128, 8), np.float32)
    hg[:, 0] = t2(inputs["qn_moba"][0])
    hg[:, 1] = t2(inputs["kn_moba"][0])
    hg[:, 2] = t2(inputs["qn_dil"][0])
    hg[:, 3] = t2(inputs["kn_dil"][0])
    hg[:, 4] = t2(inputs["qn_nsa"][0])
    hg[:, 5] = np.concatenate([np.asarray(inputs["kn_slc"][0], np.float32), np.asarray(inputs["kn_win"][0], np.float32)])
    hg[:, 6] = t2(inputs["kn_cmp"][0])
    m["hg"] = hg
    m["posT"] = f(np.concatenate([np.asarray(inputs["cmp_k_pos"][0]).T, np.asarray(inputs["cmp_v_pos"][0]).T], axis=1))
    m["b1c"] = f(np.stack([inputs["cmp_k_b1"][0], inputs["cmp_v_b1"][0]], axis=1))
    m["b2k"] = f(np.asarray(inputs["cmp_k_b2"][0]).reshape(64, 1))
    m["b2v"] = f(np.asarray(inputs["cmp_v_b2"][0]).reshape(1, 64))
    m["w_in_ab"] = f(inputs["w_in_ab"][0])
    m["w_out_ab"] = f(inputs["w_out_ab"][0])
    m["w_in_nsa"] = f(inputs["w_in_nsa"][0])
    m["w_out_nsa"] = f(inputs["w_out_nsa"][0])
    for nm in ("w_ffn_gate", "w_ffn_up", "w_ffn_down", "w_ple_proj", "w_ple_gate"):
        m[nm] = f(inputs[nm])
    m["cmp_k_w1"] = f(inputs["cmp_k_w1"][0])
    m["cmp_k_w2"] = f(inputs["cmp_k_w2"][0])
    m["cmp_v_w1"] = f(inputs["cmp_v_w1"][0])
    m["cmp_v_w2"] = f(inputs["cmp_v_w2"][0])
    for k, v in consts.items():
        m[k] = v
    return m


def kernel(**inputs):
    nc, consts, _ = build_nc()
    in_maps = [host_inputs(inputs, b, consts) for b in range(8)]
    res = run_bass_kernel_spmd(nc, in_maps, core_ids=list(range(8)))
    out = np.stack([np.asarray(r["outT"], np.float32).T for r in res.results], axis=0)
    return np.ascontiguousarray(out.astype(np.float32))
```

```python
import math
import contextlib
import numpy as np
import concourse.bass as bass
import concourse.mybir as mybir
from concourse.bass_utils import run_bass_kernel_spmd

F32 = mybir.dt.float32
BF16 = mybir.dt.bfloat16
AF = mybir.ActivationFunctionType
ALU = mybir.AluOpType
AX = mybir.AxisListType

S = 2048
D = 1024
FH = 2816
NF = 22
WHL = 4352
EPS = 1e-6
NEG = -30000.0
BIG = 3.0e38


class Dep:
    __slots__ = ("w", "r")

    def __init__(self):
        self.w = None
        self.r = []


class FW:
    NDMA = 24

    def __init__(self, nc, es):
        self.nc = nc
        self.engs = {"pe": nc.tensor, "act": nc.scalar, "dve": nc.vector, "pool": nc.gpsimd, "sp": nc.sync}
        self.sems = {}
        self.cnt = {}
        for k in self.engs:
            self.sems[k] = es.enter_context(nc.semaphore("sem_" + k))
            self.cnt[k] = 0
        for i in range(self.NDMA):
            k = ("dma", i)
            self.sems[k] = es.enter_context(nc.semaphore("sem_dma%d" % i))
            self.cnt[k] = 0
        self.seen = {e: {} for e in self.engs}
        self.dma_rr = {"sp": 0, "pool": 0, "act": 0}
        self.n_ins = 0
        self.n_wait = 0

    def _wait(self, eng, deps):
        seen = self.seen[eng]
        need = {}
        for d in deps:
            if d is None:
                continue
            k, v = d
            if k == "pe" and eng == "pe":
                continue
            if seen.get(k, 0) >= v:
                continue
            if need.get(k, 0) < v:
                need[k] = v
        for k, v in need.items():
            self.engs[eng].wait_ge(self.sems[k], v)
            seen[k] = v
            self.n_wait += 1

    @staticmethod
    def _collect(r, w):
        deps = []
        for t in r:
            deps.append(t.w)
        for t in w:
            deps.append(t.w)
            deps.extend(t.r)
        return deps

    def _mark(self, tok, r, w):
        for t in w:
            t.w = tok
            t.r = []
        for t in r:
            t.r.append(tok)
            if len(t.r) > 64:
                best = {}
                for k, v in t.r:
                    if best.get(k, 0) < v:
                        best[k] = v
                t.r = list(best.items())

    def op(self, eng, fn, r=(), w=()):
        self._wait(eng, self._collect(r, w))
        ins = fn(self.engs[eng])
        self.cnt[eng] += 1
        ins.then_inc(self.sems[eng], 1)
        self._mark((eng, self.cnt[eng]), r, w)
        self.n_ins += 1

    def dma(self, q, out, in_, r=(), w=()):
        half = self.NDMA // 2
        i = self.dma_rr[q]
        self.dma_rr[q] = (i + 1) % half
        k = ("dma", i + (half if q == "pool" else 0))
        deps = self._collect(r, w)
        if self.cnt[k] > 0:
            deps.append((k, self.cnt[k]))
        self._wait(q, deps)
        ins = self.engs[q].dma_start(out=out, in_=in_)
        self.cnt[k] += 16
        ins.then_inc(self.sems[k], 16)
        self._mark((k, self.cnt[k]), r, w)
        self.n_ins += 1

    def barrier(self):
        allk = [(k, v) for k, v in self.cnt.items() if v > 0]
        for e in self.engs:
            self._wait(e, allk)

    def finish(self, eng="sp"):
        allk = [(k, v) for k, v in self.cnt.items() if v > 0]
        self._wait(eng, allk)


class Rot:
    def __init__(self, items):
        self.items = items
        self.i = 0

    def next(self):
        t = self.items[self.i]
        self.i = (self.i + 1) % len(self.items)
        return t


def _bucket(d):
    n = np.maximum(d, 0)
    nf = np.maximum(n, 1).astype(np.float32)
    large = 16 + (np.log(nf / np.float32(16)) / np.float32(math.log(128.0)) * np.float32(16)).astype(np.int32)
    return np.where(n < 16, n, np.minimum(large, 31))


def _static_consts():
    c = {}
    m = np.arange(WHL)
    d = 2047 - m
    wm = np.zeros((3, WHL), np.float32)
    mult = (d >= 0) * ((d <= 128).astype(np.float32) + ((d % 4 == 0) & (d <= 512)) + ((d % 16 == 0) & (d <= 2048)))
    wm[1] = np.where(mult > 0, 8.0 * np.log(np.maximum(mult, 1.0)), 8.0 * NEG)
    wm[2] = np.where((d >= 0) & (d < 512), 0.0, 8.0 * NEG)
    c["c_wm"] = wm
    c["c_ident"] = np.eye(128, dtype=np.float32)
    k = np.arange(S)
    c["c_blk_moba"] = (k[None, :] // 256 == np.arange(8)[:, None]).astype(np.float32)
    c["c_blk_nsa"] = (k[None, :] // 64 == np.arange(32)[:, None]).astype(np.float32)
    g = np.zeros((128, 4, 8), np.float32)
    for i, own in enumerate(range(4, 8)):
        g[:, i, own:] = -BIG
    c["c_gneg"] = g
    g16 = np.zeros((128, 16, 8), np.float32)
    for i in range(16):
        own = (8 + i % 8) // 2
        g16[:, i, own:] = -BIG
    c["c_gneg16"] = g16
    add = np.full((128, 16, 32), -BIG, np.float32)
    forced = np.zeros((128, 16, 32), np.float32)
    for qt in range(16):
        for q in range(128):
            cur = (qt * 128 + q) // 64
            for n in (0, cur, cur - 1):
                if n >= 0:
                    forced[q, qt, n] = 1.0
            for n in range(1, cur - 1):
                add[q, qt, n] = 0.0
    c["c_addmask"] = add
    c["c_forced"] = forced
    cs = np.arange(127) * 16
    ss = np.arange(32) * 64
    ov = np.maximum(np.minimum(cs[:, None] + 32, ss[None, :] + 64) - np.maximum(cs[:, None], ss[None, :]), 0)
    ovl = np.ones((127, 33), np.float32)
    ovl[:, :32] = ov
    c["c_ovl"] = ovl
    gs = np.zeros((48, 48, 64), np.float32)
    for i in range(48):
        gs[i, i, :] = 1.0
    c["c_gsel"] = gs
    return c


_CONST_SHAPES = None


def _dap(t, offset, ap):
    return bass.AP(tensor=t.tensor, offset=offset, ap=[list(a) for a in ap])


def build_nc(upto=99, dbg=()):
    nc = bass.Bass("TRN2", target_bir_lowering=False)
    consts = _static_consts()
    IN = {}

    def din(name, shape):
        IN[name] = nc.dram_tensor(name, list(shape), F32, kind="ExternalInput").ap()
        return IN[name]

    xT = din("xT", [D, S])
    pT = din("pT", [2, 256, S])
    whb = din("whb", [16, WHL])
    gains_d = din("gains", [128, 48])
    hg_d = din("hg", [128, 8])
    posT_d = din("posT", [64, 64])
    b1_d = din("b1c", [128, 2])
    b2k_d = din("b2k", [64, 1])
    b2v_d = din("b2v", [1, 64])
    w_in_ab = din("w_in_ab", [D, 3072])
    w_out_ab = din("w_out_ab", [D, D])
    w_in_nsa = din("w_in_nsa", [D, 2608])
    w_out_nsa = din("w_out_nsa", [D, D])
    w_g = din("w_ffn_gate", [2, D, FH])
    w_u = din("w_ffn_up", [2, D, FH])
    w_d = din("w_ffn_down", [2, FH, D])
    w_pp = din("w_ple_proj", [2, 256, D])
    w_pg = din("w_ple_gate", [2, D, D])
    ck_w1 = din("cmp_k_w1", [2048, 128])
    ck_w2 = din("cmp_k_w2", [128, 64])
    cv_w1 = din("cmp_v_w1", [2048, 128])
    cv_w2 = din("cmp_v_w2", [128, 64])
    for k, v in consts.items():
        din(k, v.shape)
    outT = nc.dram_tensor("outT", [D, S], F32, kind="ExternalOutput").ap()
    DBG = {}

    with contextlib.ExitStack() as es:
        fw = FW(nc, es)

        uniq = [0]

        def sb(name, shape, dt=F32, stack=es):
            uniq[0] += 1
            return stack.enter_context(nc.sbuf_tensor("%s_%d" % (name, uniq[0]), list(shape), dt))

        def pst(name):
            return es.enter_context(nc.psum_tensor(name, [128, 512], F32))

        psS = Rot([(pst("psS%d" % i), Dep()) for i in range(3)])
        psO = Rot([(pst("psO%d" % i), Dep()) for i in range(2)])
        psA = Rot([(pst("psM%d" % i), Dep()) for i in range(3)])
        psB = psA

        def dump(name, ap, shape, deps):
            if name not in dbg:
                return
            t = nc.dram_tensor("dbg_" + name, list(shape), ap.dtype if hasattr(ap, "dtype") else F32, kind="ExternalOutput").ap()
            DBG[name] = t
            fw.dma("sp", t, ap, r=deps)

        hS = nc.dram_tensor("hS", [D, S], F32, kind="Internal").ap()
        dhS = Dep()
        dh = [[Dep() for _ in range(4)] for _ in range(8)]
        xnT = sb("xnT", [128, 8, S], BF16)
        dxn = [Dep() for _ in range(4)]
        oT = sb("oT", [128, 8, S], BF16)
        doT = [[Dep() for _ in range(4)] for _ in range(8)]
        hT = None
        gains = sb("gains_sb", [128, 48])
        hg = sb("hg_sb", [128, 8])
        dcon = Dep()
        ones_bf = sb("ones_bf", [128, 128], BF16)
        blk_ones = sb("blk_ones", [128, 128], BF16)
        ident_bf = sb("ident_bf", [128, 128], BF16)
        sqb = Rot([(sb("sqb%d" % i, [128, 512], BF16), Dep()) for i in range(2)])
        f32b = Rot([(sb("f32b%d" % i, [128, 512]), Dep()) for i in range(6)])

        fw.dma("sp", gains[:], gains_d[:, :], w=[dcon])
        fw.dma("sp", hg[:], hg_d[:, :], w=[dcon])
        fw.dma("pool", ident_bf[:], IN["c_ident"][:, :], w=[dcon])
        fw.op("dve", lambda e: e.memset(ones_bf[:], 1.0), w=[dcon])
        fw.op("dve", lambda e: e.memset(blk_ones[:], 0.0), w=[dcon])
        fw.op("dve", lambda e: e.memset(blk_ones[0:64, 0:64], 1.0), w=[dcon])
        fw.op("dve", lambda e: e.memset(blk_ones[64:128, 64:128], 1.0), w=[dcon])
        def load_h(src, dsrc):
            v = src.rearrange("(c p) s -> p c s", p=128)
            for c in range(8):
                for sc in range(4):
                    fw.dma("sp", hT[:, c, sc * 512:(sc + 1) * 512], v[:, c, sc * 512:(sc + 1) * 512], r=dsrc, w=[dh[c][sc]])

        def store_h(dst, ddst):
            v = dst.rearrange("(c p) s -> p c s", p=128)
            for c in range(8):
                for sc in range(4):
                    fw.dma("sp", v[:, c, sc * 512:(sc + 1) * 512], hT[:, c, sc * 512:(sc + 1) * 512], r=[dh[c][sc]], w=ddst)

        def cs_(sc):
            return slice(sc * 512, (sc + 1) * 512)

        def rmsnorm(gidx):
            for sc in range(4):
                cs = cs_(sc)
                pt, dp = psB.next()
                for c in range(8):
                    sq, dsq = sqb.next()
                    fw.op("act", lambda e: e.activation(out=sq[:], in_=hT[:, c, cs], func=AF.Square), r=[dh[c][sc]], w=[dsq])
                    fw.op("pe", lambda e: e.matmul(pt[:], lhsT=ones_bf[:], rhs=sq[:], start=(c == 0), stop=(c == 7)),
                          r=[dsq, dcon], w=[dp])
                rt, drt = f32b.next()
                fw.op("act", lambda e: e.activation(out=rt[:], in_=pt[:], func=AF.Ln, bias=EPS, scale=1.0 / D), r=[dp], w=[drt])
                fw.op("act", lambda e: e.activation(out=rt[:], in_=rt[:], func=AF.Exp, scale=-0.5), r=[drt], w=[drt])
                for c in range(8):
                    fw.op("dve", lambda e: e.scalar_tensor_tensor(
                        out=xnT[:, c, cs], in0=hT[:, c, cs], scalar=gains[:, gidx * 8 + c:gidx * 8 + c + 1], in1=rt[:],
                        op0=ALU.mult, op1=ALU.mult), r=[dh[c][sc], drt, dcon], w=[dxn[sc]])

        def proj_fm(wt, dw, sc, M, c0=0):
            pt, dp = psA.next()
            for kc in range(8):
                fw.op("pe", lambda e: e.matmul(pt[0:M, :], lhsT=wt[:, kc, c0:c0 + M], rhs=xnT[:, kc, cs_(sc)],
                                               start=(kc == 0), stop=(kc == 7)), r=[dw, dxn[sc]], w=[dp])
            return pt, dp

        def headnorm(pt, dp, M, gcol, outs):
            sq, dsq = sqb.next()
            fw.op("act", lambda e: e.activation(out=sq[0:M, :], in_=pt[0:M, :], func=AF.Square), r=[dp], w=[dsq])
            ps2, dp2 = psB.next()
            fw.op("pe", lambda e: e.matmul(ps2[0:M, :], lhsT=blk_ones[0:M, 0:M], rhs=sq[0:M, :], start=True, stop=True),
                  r=[dsq, dcon], w=[dp2])
            rt, drt = f32b.next()
            fw.op("act", lambda e: e.activation(out=rt[0:M, :], in_=ps2[0:M, :], func=AF.Ln, bias=EPS, scale=1.0 / 64), r=[dp2], w=[drt])
            fw.op("act", lambda e: e.activation(out=rt[0:M, :], in_=rt[0:M, :], func=AF.Exp, scale=-0.5), r=[drt], w=[drt])
            for (dst, ddst, r0) in outs:
                fw.op("dve", lambda e: e.scalar_tensor_tensor(
                    out=dst, in0=pt[r0:r0 + 64, :], scalar=hg[r0:r0 + 64, gcol:gcol + 1], in1=rt[r0:r0 + 64, :],
                    op0=ALU.mult, op1=ALU.mult), r=[dp, drt, dcon], w=[ddst])

        def load_w3(q, wt, dw, src2d, c0, ncols, dst_c0=0):
            v = src2d.rearrange("(kc p) n -> p kc n", p=128)
            fw.dma(q, wt[:, :, dst_c0:dst_c0 + ncols], v[:, :, c0:c0 + ncols], w=[dw])

        def rev(t, n, rows=128):
            a = t[0:rows, n - 1:n]
            return bass.AP(tensor=a.tensor, offset=a.offset, ap=[list(a.ap[0]), [-1, n]])

        pbuf_items = []

        LOOK = 2
        CBDELAY = 2

        def make_items(qt, dq, K, ktile, dk, vt, dv, E, dE, tiles_fn, out_cb, chunks=range(4)):
            items = []
            for qc in chunks:
                c0 = qc * 512
                tiles = tiles_fn(qc)
                assert tiles[0][1] == 0 and tiles[0][2] == 512
                for idx, (kt, lo, hi) in enumerate(tiles):
                    n = hi - lo
                    u0 = c0 + lo - kt * 128
                    items.append(dict(
                        rows=128, n=n, lo=lo, hi=hi, K=K,
                        lhsT=ktile[0:K, kt * 128:(kt + 1) * 128], rhs=qt[0:K, c0 + lo:c0 + hi], sdeps=[dk[kt // 4], dq[qc]],
                        E=E[:, u0:u0 + n], dE=dE, v=vt[:, kt, :], dv=dv[kt // 4],
                        first=(idx == 0), last=(idx == len(tiles) - 1), cb=out_cb, qc=qc, post=None, after=None))
            return items

        def run_items(items):
            staged = {}
            cur = [None]
            pend = []
            n_it = len(items)

            def fire(force_to):
                while pend and (pend[0][0] <= 0 or len(pend) > force_to):
                    pend.pop(0)[1]()

            for j in range(n_it + LOOK):
                if j < n_it:
                    it = items[j]
                    pss, dps = psS.next()
                    fw.op("pe", lambda e: e.matmul(pss[0:it["rows"], 0:it["n"]], lhsT=it["lhsT"], rhs=it["rhs"], start=True, stop=False),
                          r=it["sdeps"], w=[dps])
                    fw.op("pe", lambda e: e.matmul(pss[0:it["rows"], 0:it["n"]], lhsT=ident_bf[0:it["rows"], 0:it["rows"]], rhs=it["E"], start=False, stop=True),
                          r=[it["dE"], dcon], w=[dps])
                    staged[j] = (pss, dps)
                i = j - LOOK
                if i < 0:
                    continue
                it = items[i]
                pss, dps = staged.pop(i)
                R, n = it["rows"], it["n"]
                pb, dpb = pbuf.next()
                fw.op("act", lambda e: e.activation(out=pb[0:R, 0:n], in_=pss[0:R, 0:n], func=AF.Exp, scale=0.125), r=[dps], w=[dpb])
                if it["first"]:
                    fire(1)
                    cur[0] = psO.next()
                po, dpo = cur[0]
                fw.op("pe", lambda e: e.matmul(po[:, it["lo"]:it["hi"]], lhsT=it["v"], rhs=pb[0:R, 0:n], start=it["first"], stop=it["last"]),
                      r=[it["dv"], dpb], w=[dpo])
                for p_ in pend:
                    p_[0] -= 1
                if it["post"] is not None:
                    pend.append([CBDELAY, (lambda it=it, pb=pb, dpb=dpb: it["post"](pb, dpb))])
                if it["last"]:
                    pend.append([CBDELAY, (lambda it=it, po=po, dpo=dpo: it["cb"](it["qc"], po, dpo))])
                fire(99)
                if it.get("after") is not None:
                    it["after"]()
            fire(0)

        def attend(*args, **kw):
            run_items(make_items(*args, **kw))

        def recip_den(rd, drd, po, dpo):
            fw.op("act", lambda e: e.activation(out=rd[64:128, :], in_=po[64:128, :], func=AF.Ln, bias=1e-30, scale=1.0), r=[dpo], w=[drd])
            fw.op("act", lambda e: e.activation(out=rd[64:128, :], in_=rd[64:128, :], func=AF.Exp, scale=-1.0), r=[drd], w=[drd])

        def causal_tiles(qc):
            res = []
            for kt in range(4 * qc + 4):
                lo = max(0, kt * 128 - qc * 512)
                res.append((kt, lo, 512))
            return res

        def win_tiles(qc):
            res = []
            order = [4 * qc] + [k for k in range(max(0, 4 * qc - 4), 4 * qc + 4) if k != 4 * qc]
            for kt in order:
                off = kt * 128 - qc * 512
                lo = max(0, off)
                hi = min(512, ((off + 638) // 128 + 1) * 128)
                res.append((kt, lo, hi))
            return res

        def build_E(H, E, dE, M, dM, stage, dstage, width=2048, pstride=1, off=0, rows=128):
            fw.dma("sp", stage[0:128, 0:width], _dap(whb, H * WHL + off + (2048 - width), [[pstride, 128], [1, width]]), w=[dstage])
            if M is None:
                fw.op("dve", lambda e: e.tensor_scalar(out=E[0:rows, 0:width], in0=rev(stage, width, rows), scalar1=8.0, scalar2=None, op0=ALU.mult),
                      r=[dstage], w=[dE])
            else:
                fw.op("dve", lambda e: e.scalar_tensor_tensor(out=E[0:rows, 0:width], in0=rev(stage, width, rows), scalar=8.0, in1=M[0:rows, 0:width],
                                                              op0=ALU.mult, op1=ALU.add), r=[dstage, dM], w=[dE])

        def build_M(kind, M, dM, stage, dstage, width=2048, pstride=1, off=0, rows=128):
            fw.dma("sp", stage[0:128, 0:width], _dap(IN["c_wm"], kind * WHL + off + (2048 - width), [[pstride, 128], [1, width]]), w=[dstage])
            fw.op("act", lambda e: e.activation(out=M[0:rows, 0:width], in_=rev(stage, width, rows), func=AF.Copy), r=[dstage], w=[dM])

        def out_proj(w_out, st):
            wo = [(sb("wo%d" % i, [128, 8, 128], BF16, st), Dep()) for i in range(2)]
            wv = w_out.rearrange("(kc p) n -> p kc n", p=128)

            def ld(fc):
                t, d_ = wo[fc % 2]
                fw.dma("pool", t[:], wv[:, :, fc * 128:(fc + 1) * 128], w=[d_])
            ld(0)
            for fc in range(8):
                if fc + 1 < 8:
                    ld(fc + 1)
                t, d_ = wo[fc % 2]
                for sc in range(4):
                    pt, dp = psA.next()
                    for pr in range(8):
                        fw.op("pe", lambda e: e.matmul(pt[:], lhsT=t[:, pr, :], rhs=oT[:, pr, cs_(sc)], start=(pr == 0), stop=(pr == 7)),
                              r=[d_, doT[pr][sc]], w=[dp])
                    fw.op("dve", lambda e: e.tensor_tensor(out=hT[:, fc, cs_(sc)], in0=pt[:], in1=hT[:, fc, cs_(sc)], op=ALU.add),
                          r=[dp, dh[fc][sc]], w=[dh[fc][sc]])

        def ffn(layer):
            rmsnorm(layer * 3 + 1)
            with contextlib.ExitStack() as st:
                act2 = sb("ffn_act", [128, NF - 16, 1024], BF16, st)
                dact = [[Dep() for _ in range(2)] for _ in range(NF)]

                def act_ap(f, q):
                    if f < 16:
                        return oT[:, f // 2, (f % 2) * 1024 + q * 512:(f % 2) * 1024 + (q + 1) * 512]
                    return act2[:, f - 16, q * 512:(q + 1) * 512]
                wg = [(sb("wg%d" % i, [128, 8, 128], BF16, st), Dep()) for i in range(2)]
                wu = [(sb("wu%d" % i, [128, 8, 128], BF16, st), Dep()) for i in range(2)]
                wd = [(sb("wd%d" % i, [128, NF, 128], BF16, st), Dep()) for i in range(2)]
                sg = Rot([(sb("sg%d" % i, [128, 512], F32, st), Dep()) for i in range(2)])
                wgv = w_g[layer].rearrange("(kc p) n -> p kc n", p=128)
                wuv = w_u[layer].rearrange("(kc p) n -> p kc n", p=128)
                wdv = w_d[layer].rearrange("(f p) n -> p f n", p=128)

                def ld1(f):
                    fw.dma("pool", wg[f % 2][0][:], wgv[:, :, f * 128:(f + 1) * 128], w=[wg[f % 2][1]])
                    fw.dma("pool", wu[f % 2][0][:], wuv[:, :, f * 128:(f + 1) * 128], w=[wu[f % 2][1]])

                def ld2(dc):
                    fw.dma("pool", wd[dc % 2][0][:], wdv[:, :, dc * 128:(dc + 1) * 128], w=[wd[dc % 2][1]])

                for half in range(2):
                    ld1(0)
                    for f in range(NF):
                        if f + 1 < NF:
                            ld1(f + 1)
                        else:
                            ld2(0)
                        tg, dg_ = wg[f % 2]
                        tu, du_ = wu[f % 2]
                        for q in range(2):
                            sc = half * 2 + q
                            pg, dpg = psA.next()
                            pu, dpu = psB.next()
                            for kc in range(8):
                                fw.op("pe", lambda e: e.matmul(pg[:], lhsT=tg[:, kc, :], rhs=xnT[:, kc, cs_(sc)], start=(kc == 0), stop=(kc == 7)),
                                      r=[dg_, dxn[sc]], w=[dpg])
                            for kc in range(8):
                                fw.op("pe", lambda e: e.matmul(pu[:], lhsT=tu[:, kc, :], rhs=xnT[:, kc, cs_(sc)], start=(kc == 0), stop=(kc == 7)),
                                      r=[du_, dxn[sc]], w=[dpu])
                            s_, ds_ = sg.next()
                            fw.op("act", lambda e: e.activation(out=s_[:], in_=pg[:], func=AF.Silu), r=[dpg], w=[ds_])
                            fw.op("dve", lambda e: e.tensor_tensor(out=act_ap(f, q), in0=s_[:], in1=pu[:], op=ALU.mult),
                                  r=[ds_, dpu], w=[dact[f][q]])
                    for dc in range(8):
                        if dc + 1 < 8:
                            ld2(dc + 1)
                        td, dd_ = wd[dc % 2]
                        for q in range(2):
                            sc = half * 2 + q
                            pt, dp = psS.next()
                            for f in range(NF):
                                fw.op("pe", lambda e: e.matmul(pt[:], lhsT=td[:, f, :], rhs=act_ap(f, q),
                                                               start=(f == 0), stop=(f == NF - 1)), r=[dd_, dact[f][q]], w=[dp])
                            fw.op("dve", lambda e: e.tensor_tensor(out=hT[:, dc, cs_(sc)], in0=pt[:], in1=hT[:, dc, cs_(sc)], op=ALU.add),
                                  r=[dp, dh[dc][sc]], w=[dh[dc][sc]])
                fw.barrier()

        def ple(layer):
            rmsnorm(layer * 3 + 2)
            with contextlib.ExitStack() as st:
                pTs = sb("pTs", [128, 2, S], BF16, st)
                dpT = Dep()
                wpg = [(sb("wpg%d" % i, [128, 8, 128], BF16, st), Dep()) for i in range(2)]
                wpp = [(sb("wpp%d" % i, [128, 2, 128], BF16, st), Dep()) for i in range(2)]
                sg = Rot([(sb("psg%d" % i, [128, 512], F32, st), Dep()) for i in range(2)])
                fw.dma("pool", pTs[:], pT[layer].rearrange("(kc p) s -> p kc s", p=128), w=[dpT])
                wgv = w_pg[layer].rearrange("(kc p) n -> p kc n", p=128)
                wpv = w_pp[layer].rearrange("(kc p) n -> p kc n", p=128)

                def ld(fc):
                    fw.dma("pool", wpg[fc % 2][0][:], wgv[:, :, fc * 128:(fc + 1) * 128], w=[wpg[fc % 2][1]])
                    fw.dma("pool", wpp[fc % 2][0][:], wpv[:, :, fc * 128:(fc + 1) * 128], w=[wpp[fc % 2][1]])
                ld(0)
                for fc in range(8):
                    if fc + 1 < 8:
                        ld(fc + 1)
                    tg, dg_ = wpg[fc % 2]
                    tp, dp_ = wpp[fc % 2]
                    for sc in range(4):
                        pg, dpg = psA.next()
                        pp_, dpp = psB.next()
                        for kc in range(8):
                            fw.op("pe", lambda e: e.matmul(pg[:], lhsT=tg[:, kc, :], rhs=xnT[:, kc, cs_(sc)], start=(kc == 0), stop=(kc == 7)),
                                  r=[dg_, dxn[sc]], w=[dpg])
                        for kc in range(2):
                            fw.op("pe", lambda e: e.matmul(pp_[:], lhsT=tp[:, kc, :], rhs=pTs[:, kc, cs_(sc)], start=(kc == 0), stop=(kc == 1)),
                                  r=[dp_, dpT], w=[dpp])
                        s_, ds_ = sg.next()
                        fw.op("act", lambda e: e.activation(out=s_[:], in_=pg[:], func=AF.Sigmoid), r=[dpg], w=[ds_])
                        fw.op("dve", lambda e: e.tensor_tensor(out=s_[:], in0=s_[:], in1=pp_[:], op=ALU.mult), r=[ds_, dpp], w=[ds_])
                        fw.op("dve", lambda e: e.tensor_tensor(out=hT[:, fc, cs_(sc)], in0=s_[:], in1=hT[:, fc, cs_(sc)], op=ALU.add),
                              r=[ds_, dh[fc][sc]], w=[dh[fc][sc]])
                fw.barrier()

        def layer0_mixer():
            with contextlib.ExitStack() as st:
                qh = [(sb("qh%d" % i, [128, S], BF16, st), [Dep() for _ in range(4)]) for i in range(2)]
                kh = [(sb("kh%d" % i, [128, S], BF16, st), [Dep() for _ in range(4)]) for i in range(2)]
                vh = [(sb("vh%d" % i, [128, 16, 128], BF16, st), [Dep() for _ in range(4)]) for i in range(2)]
                Eh = [(sb("Eh%d" % i, [128, S], BF16, st), Dep()) for i in range(2)]
                Mt = (sb("Mt", [128, S], BF16, st), Dep())
                stage = sb("stage", [128, S], F32, st)
                dstage = Dep()
                wq = [(sb("wq%d" % i, [128, 8, 384], BF16, st), Dep()) for i in range(2)]
                global_pbuf = [(sb("pbuf%d" % i, [128, 512], BF16, st), Dep()) for i in range(4)]
                nonlocal pbuf
                pbuf = Rot(global_pbuf)
                km = [(sb("km%d" % i, [64, 8], F32, st), Dep()) for i in range(2)]
                kmb = [(sb("kmb%d" % i, [72, 8], BF16, st), Dep()) for i in range(2)]
                gneg16 = sb("gneg16", [128, 128], F32, st)
                gm16 = sb("gm16", [128, 128], F32, st)
                dgm = Dep()
                m816 = sb("m816", [128, 128], F32, st)
                dm8 = Dep()
                nmp16 = sb("nmp16", [128, 16, 72], BF16, st)
                dnmp = Dep()
                dl0 = Dep()
                fw.dma("sp", gneg16[:], IN["c_gneg16"].rearrange("p a b -> p (a b)"), w=[dl0])
                fw.op("dve", lambda e: e.memset(nmp16[:], 0.0), w=[dnmp])
                for i in range(2):
                    fw.op("dve", lambda e: e.memset(kmb[i][0][:], 0.0), w=[kmb[i][1]])
                for i in range(2):
                    fw.op("dve", lambda e: e.memset(vh[i][0][:, :, 64:128], 1.0), w=vh[i][1])
                    fw.op("dve", lambda e: e.memset(kh[i][0][64:128, :], 0.0), w=kh[i][1])
                    fw.op("dve", lambda e: e.memset(qh[i][0][64:128, :], 0.0), w=qh[i][1])
                    fw.dma("pool", kh[i][0][64:72, :], IN["c_blk_moba"][:, :], w=kh[i][1])
                build_M(1, Mt[0], Mt[1], stage, dstage)

                def ldw(pair):
                    t, d_ = wq[pair % 2]
                    base = 0 if pair < 4 else 1536
                    pp = pair % 4
                    load_w3("pool", t, d_, w_in_ab, base + pp * 128, 128, 0)
                    load_w3("pool", t, d_, w_in_ab, base + 512 + pp * 128, 128, 128)
                    load_w3("pool", t, d_, w_in_ab, base + 1024 + pp * 128, 128, 256)

                ldw(0)
                for pair in range(8):
                    moba = pair < 4
                    if pair + 1 < 8:
                        ldw(pair + 1)
                    wt, dw = wq[pair % 2]
                    gq = 0 if moba else 2
                    gk = 1 if moba else 3
                    for sc in range(4):
                        pt, dp = proj_fm(wt, dw, sc, 128, 0)
                        headnorm(pt, dp, 128, gq, [(qh[0][0][0:64, cs_(sc)], qh[0][1][sc], 0), (qh[1][0][0:64, cs_(sc)], qh[1][1][sc], 64)])
                        pt, dp = proj_fm(wt, dw, sc, 128, 128)
                        headnorm(pt, dp, 128, gk, [(kh[0][0][0:64, cs_(sc)], kh[0][1][sc], 0), (kh[1][0][0:64, cs_(sc)], kh[1][1][sc], 64)])
                        if moba:
                            for hh in range(2):
                                kin = kh[hh][0][0:64, cs_(sc)]
                                kin3 = bass.AP(tensor=kin.tensor, offset=kin.offset, ap=[list(kin.ap[0]), [256, 2], [1, 256]])
                                fw.op("dve", lambda e: e.tensor_reduce(out=km[hh][0][0:64, 2 * sc:2 * sc + 2], in_=kin3, axis=AX.X, op=ALU.add),
                                      r=[kh[hh][1][sc]], w=[km[hh][1]])
                        pv, dpv = psA.next()
                        for j in range(4):
                            tt = sc * 4 + j
                            for kc in range(8):
                                fw.op("pe", lambda e: e.matmul(pv[:, j * 128:(j + 1) * 128], lhsT=xnT[:, kc, tt * 128:(tt + 1) * 128],
                                                               rhs=wt[:, kc, 256:384], start=(kc == 0), stop=(kc == 7)),
                                      r=[dw, dxn[sc]], w=[dpv])
                        for hh in range(2):
                            src = pv[:, hh * 64:hh * 64 + 1]
                            src3 = bass.AP(tensor=src.tensor, offset=src.offset, ap=[list(src.ap[0]), [128, 4], [1, 64]])
                            fw.op("act", lambda e: e.activation(out=vh[hh][0][:, sc * 4:sc * 4 + 4, 0:64], in_=src3, func=AF.Copy),
                                  r=[dpv], w=[vh[hh][1][sc]])
                    if pair == 0:
                        dump("q0", qh[0][0][0:64, :], [64, S], qh[0][1])
                        dump("k0", kh[0][0][0:64, :], [64, S], kh[0][1])
                        dump("v0", vh[0][0][:], [128, 16, 128], vh[0][1])
                    if moba:
                        for hh in range(2):
                            qt_, dq_ = qh[hh]
                            fw.op("dve", lambda e: e.tensor_copy(out=kmb[hh][0][0:64, :], in_=km[hh][0][:]), r=[km[hh][1]], w=[kmb[hh][1]])
                            fw.op("dve", lambda e: e.memset(qt_[64:72, 0:1024], 0.0), w=[dq_[0], dq_[1]])
                        pg, dpg = psB.next()
                        for hh in range(2):
                            qt_, dq_ = qh[hh]
                            for j in range(8):
                                qtile = 8 + j
                                i = hh * 8 + j
                                fw.op("pe", lambda e: e.matmul(pg[:, i * 8:i * 8 + 8], lhsT=qt_[0:72, qtile * 128:(qtile + 1) * 128], rhs=kmb[hh][0][0:72, 0:8],
                                                               start=True, stop=True), r=[dq_[qtile // 4], kmb[hh][1]], w=[dpg])
                        fw.op("dve", lambda e: e.tensor_tensor(out=gm16[:, :], in0=pg[:, 0:128], in1=gneg16[:, :], op=ALU.add), r=[dpg, dl0], w=[dgm])
                        for i in range(16):
                            fw.op("dve", lambda e: e.max(out=m816[:, i * 8:i * 8 + 8], in_=gm16[:, i * 8:i * 8 + 8]), r=[dgm], w=[dm8])
                        for i in range(16):
                            own = (8 + i % 8) // 2
                            fw.op("dve", lambda e: e.tensor_scalar(out=nmp16[:, i, 64:64 + own], in0=gm16[:, i * 8:i * 8 + own], scalar1=m816[:, i * 8 + 2:i * 8 + 3],
                                                                   scalar2=NEG, op0=ALU.is_lt, op1=ALU.mult), r=[dgm, dm8], w=[dnmp])
                        for hh in range(2):
                            qt_, dq_ = qh[hh]
                            for half in range(2):
                                p2, dp2 = psB.next()
                                for jj in range(4):
                                    i = hh * 8 + half * 4 + jj
                                    fw.op("pe", lambda e: e.matmul(p2[0:72, jj * 128:(jj + 1) * 128], lhsT=nmp16[:, i, 0:72], rhs=ident_bf[:], start=True, stop=True),
                                          r=[dnmp, dcon], w=[dp2])
                                c0 = (8 + half * 4) * 128
                                fw.op("act", lambda e: e.activation(out=qt_[64:72, c0:c0 + 512], in_=p2[64:72, 0:512], func=AF.Copy),
                                      r=[dp2], w=[dq_[2 + half]])
                    if pair == 4:
                        for hh in range(2):
                            fw.op("dve", lambda e: e.memset(qh[hh][0][64:72, :], 0.0), w=qh[hh][1])
                    items = []
                    for hh in range(2):
                        H = pair * 2 + hh
                        E, dE = Eh[hh]
                        M, dM = (None, None) if moba else Mt
                        build_E(H, E, dE, M, dM, stage, dstage)

                        def cb(qc, po, dpo, hh=hh, pair=pair):
                            rd, drd = f32b.next()
                            recip_den(rd, drd, po, dpo)
                            fw.op("dve", lambda e: e.tensor_tensor(out=oT[hh * 64:hh * 64 + 64, pair, cs_(qc)], in0=po[0:64, :], in1=rd[64:128, :], op=ALU.mult),
                                  r=[dpo, drd], w=[doT[pair][qc]])
                        items += make_items(qh[hh][0], qh[hh][1], 128, kh[hh][0], kh[hh][1], vh[hh][0], vh[hh][1], E, dE, causal_tiles, cb)
                    run_items(items)
                fw.barrier()

        pbuf = None

        def layer1_mixer():
            with contextlib.ExitStack() as st:
                nonlocal pbuf
                pbuf = Rot([(sb("pbuf%d" % i, [128, 512], BF16, st), Dep()) for i in range(4)])
                qa = [(sb("qa%d" % i, [128, S], BF16, st), [Dep() for _ in range(4)]) for i in range(4)]
                ks = (sb("ksT", [128, S], BF16, st), [Dep() for _ in range(4)])
                kw = (sb("kwT", [128, S], BF16, st), [Dep() for _ in range(4)])
                vs = (sb("vsA", [128, 16, 128], BF16, st), [Dep() for _ in range(4)])
                vw = (sb("vwA", [128, 16, 128], BF16, st), [Dep() for _ in range(4)])
                ocmp = [(sb("ocmp%d" % i, [64, S], BF16, st), [Dep() for _ in range(4)]) for i in range(4)]
                tc_ = qa[2]
                tv_ = qa[3]
                kcT = (sb("kcT", [128, 128], BF16, st), Dep())
                vcA = (sb("vcA", [128, 128], BF16, st), Dep())
                EE = [(sb("EE%d" % i, [128, S], BF16, st), Dep()) for i in range(2)]
                EW = [(sb("EW%d" % i, [128, 640], BF16, st), Dep()) for i in range(2)]
                Mw = (sb("Mw", [128, 640], BF16, st), Dep())
                stage = sb("stage", [128, S], F32, st)
                dstage = Dep()
                gsig = (sb("gsig", [96, S], BF16, st), [Dep() for _ in range(4)])
                gsel = sb("gsel", [96, 48, 64], BF16, st)
                ovl = sb("ovl", [128, 33], BF16, st)
                addm = sb("addm", [128, 16, 32], F32, st)
                forced = sb("forced", [128, 16, 32], F32, st)
                imp = (sb("imp", [128, 16, 32], F32, st), [Dep() for _ in range(16)])
                w1s = (sb("w1s", [96, 32, 128], BF16, st), Dep())
                w1 = [w1s, w1s]
                w2 = [(sb("w2_%d" % i, [128, 64], BF16, st), Dep()) for i in range(2)]
                posT = sb("posT", [96, 64], BF16, st)
                b1c = sb("b1c", [128, 2], F32, st)
                b2k = sb("b2k", [64, 1], F32, st)
                b2v = sb("b2v", [128, 64], F32, st)
                cb1 = sb("cb1", [128, 2], F32, st)
                dcb1 = Dep()
                wA = [(sb("wA%d" % i, [128, 8, 128], BF16, st), Dep()) for i in range(3)]
                wQ = [wA[0], wA[1]]
                wG = (sb("wG", [128, 8, 48], BF16, st), Dep())
                tsb = Rot([(sb("tsb%d" % i, [64, 512], BF16, st), Dep()) for i in range(4)])
                sel_t = {}
                for nm_, shp in [("vals", [128, 8, 32]), ("lt", [128, 8, 32]), ("v2", [128, 8, 32]), ("m8a", [128, 8, 8]), ("m8b", [128, 8, 8]), ("rdi", [128, 4])]:
                    sel_t[nm_] = (sb("sel_" + nm_, shp, F32, st), Dep())
                nmp = (sb("nmp1", [128, 8, 96], BF16, st), Dep())
                gl = Dep()
                fw.op("dve", lambda e: e.memset(gsel[:], 0.0), w=[gl])
                fw.op("dve", lambda e: e.memset(gsig[0][:], 0.0), w=gsig[1])
                fw.op("dve", lambda e: e.memset(posT[:], 0.0), w=[gl])
                fw.op("dve", lambda e: e.memset(w1s[0][:], 0.0), w=[w1s[1]])
                fw.op("dve", lambda e: e.memset(kcT[0][:], 0.0), w=[kcT[1]])
                fw.op("dve", lambda e: e.memset(kw[0][64:128, :], 0.0), w=kw[1])
                fw.op("dve", lambda e: e.memset(ks[0][64:128, :], 0.0), w=ks[1])
                for qd in qa:
                    fw.op("dve", lambda e: e.memset(qd[0][64:128, :], 0.0), w=qd[1])
                fw.dma("pool", gsel[0:48, :, :], IN["c_gsel"][:, :, :], w=[gl])
                fw.dma("pool", ovl[0:127, :], IN["c_ovl"][:, :], w=[gl])
                fw.dma("sp", addm[:], IN["c_addmask"][:, :, :], w=[gl])
                fw.dma("sp", forced[:], IN["c_forced"][:, :, :], w=[gl])
                fw.dma("pool", posT[0:64, :], posT_d[:, :], w=[gl])
                fw.dma("sp", b1c[:], b1_d[:, :], w=[gl])
                fw.dma("sp", b2k[:], b2k_d[:, :], w=[gl])
                fw.dma("sp", b2v[:], _dap(b2v_d, 0, [[0, 128], [1, 64]]), w=[gl])
                w1src = [ck_w1.rearrange("(l d) j -> d l j", d=64), cv_w1.rearrange("(l d) j -> d l j", d=64)]
                fw.dma("pool", w2[0][0][:], ck_w2[:, :], w=[w2[0][1]])
                fw.dma("pool", w2[1][0][:], cv_w2[:, :], w=[w2[1][1]])
                fw.dma("pool", ks[0][64:96, :], IN["c_blk_nsa"][:, :], w=ks[1])
                fw.op("dve", lambda e: e.memset(nmp[0][:], 0.0), w=[nmp[1]])
                fw.op("dve", lambda e: e.memset(vs[0][:, :, 64:128], 1.0), w=vs[1])
                fw.op("dve", lambda e: e.memset(vw[0][:, :, 64:128], 1.0), w=vw[1])
                fw.op("dve", lambda e: e.memset(vcA[0][:, 64:128], 1.0), w=[vcA[1]])
                build_M(2, Mw[0], Mw[1], stage, dstage, width=640)
                for i in range(2):
                    fw.dma("pool", w1s[0][0:64, :, :], w1src[i], w=[w1s[1]])
                    pt, dp = psB.next()
                    for l in range(32):
                        fw.op("pe", lambda e: e.matmul(pt[:, 0:1], lhsT=w1[i][0][0:96, l, :], rhs=posT[0:96, i * 32 + l:i * 32 + l + 1],
                                                       start=(l == 0), stop=(l == 31)), r=[w1[i][1], gl], w=[dp])
                    fw.op("dve", lambda e: e.tensor_tensor(out=cb1[:, i:i + 1], in0=pt[:, 0:1], in1=b1c[:, i:i + 1], op=ALU.add),
                          r=[dp, gl], w=[dcb1])
                load_w3("pool", wG[0], wG[1], w_in_nsa, 2560, 48, 0)
                for sc in range(4):
                    pt, dp = proj_fm(wG[0], wG[1], sc, 48, 0)
                    fw.op("act", lambda e: e.activation(out=gsig[0][0:48, cs_(sc)], in_=pt[0:48, :], func=AF.Sigmoid), r=[dp], w=[gsig[1][sc]])

                def gate_bc(h, j, qc):
                    pt, dp = psB.next()
                    fw.op("pe", lambda e: e.matmul(pt[0:64, :], lhsT=gsel[0:96, h * 3 + j, :], rhs=gsig[0][0:96, cs_(qc)], start=True, stop=True),
                          r=[gl, gsig[1][qc]], w=[dp])
                    return pt, dp

                for g in range(4):
                    load_w3("pool", wA[0][0], wA[0][1], w_in_nsa, 1536 + g * 64, 64, 0)
                    load_w3("pool", wA[0][0], wA[0][1], w_in_nsa, 2048 + g * 64, 64, 64)
                    load_w3("pool", wA[1][0], wA[1][1], w_in_nsa, 1024 + g * 64, 64, 0)
                    load_w3("pool", wA[1][0], wA[1][1], w_in_nsa, 1280 + g * 64, 64, 64)
                    load_w3("pool", wA[2][0], wA[2][1], w_in_nsa, 1792 + g * 64, 64, 0)
                    load_w3("pool", wA[2][0], wA[2][1], w_in_nsa, 2304 + g * 64, 64, 64)
                    for sc in range(4):
                        pt, dp = proj_fm(wA[0][0], wA[0][1], sc, 128, 0)
                        headnorm(pt, dp, 128, 5, [(ks[0][0:64, cs_(sc)], ks[1][sc], 0), (kw[0][0:64, cs_(sc)], kw[1][sc], 64)])
                        pt, dp = proj_fm(wA[1][0], wA[1][1], sc, 128, 0)
                        fw.op("act", lambda e: e.activation(out=tc_[0][0:64, cs_(sc)], in_=pt[0:64, :], func=AF.Copy), r=[dp], w=[tc_[1][sc]])
                        fw.op("act", lambda e: e.activation(out=tv_[0][0:64, cs_(sc)], in_=pt[64:128, :], func=AF.Copy), r=[dp], w=[tv_[1][sc]])
                        pv, dpv = psA.next()
                        for j in range(4):
                            tt = sc * 4 + j
                            for kc in range(8):
                                fw.op("pe", lambda e: e.matmul(pv[:, j * 128:(j + 1) * 128], lhsT=xnT[:, kc, tt * 128:(tt + 1) * 128],
                                                               rhs=wA[2][0][:, kc, :], start=(kc == 0), stop=(kc == 7)),
                                      r=[wA[2][1], dxn[sc]], w=[dpv])
                        for hh, vdst in enumerate((vs, vw)):
                            src = pv[:, hh * 64:hh * 64 + 1]
                            src3 = bass.AP(tensor=src.tensor, offset=src.offset, ap=[list(src.ap[0]), [128, 4], [1, 64]])
                            fw.op("act", lambda e: e.activation(out=vdst[0][:, sc * 4:sc * 4 + 4, 0:64], in_=src3, func=AF.Copy),
                                  r=[dpv], w=[vdst[1][sc]])
                    for i, tsrc in enumerate((tc_, tv_)):
                        fw.dma("pool", w1s[0][0:64, :, :], w1src[i], w=[w1s[1]])
                        ph, dph = psA.next()
                        for l in range(32):
                            a = tsrc[0][0:96, l:l + 1]
                            rhs = bass.AP(tensor=a.tensor, offset=a.offset, ap=[list(a.ap[0]), [16, 127]])
                            fw.op("pe", lambda e: e.matmul(ph[:, 0:127], lhsT=w1[i][0][0:96, l, :], rhs=rhs, start=(l == 0), stop=(l == 31)),
                                  r=[w1[i][1]] + tsrc[1], w=[dph])
                        xg, dxg = f32b.next()
                        tg_, dtg = f32b.next()
                        fw.op("act", lambda e: e.activation(out=xg[:, 0:127], in_=ph[:, 0:127], func=AF.Identity, bias=cb1[:, i:i + 1], scale=1.0),
                              r=[dph, dcb1], w=[dxg])
                        fw.op("dve", lambda e: e.tensor_tensor(out=tg_[:, 0:127], in0=xg[:, 0:127], in1=xg[:, 0:127], op=ALU.mult), r=[dxg], w=[dtg])
                        fw.op("dve", lambda e: e.tensor_scalar(out=tg_[:, 0:127], in0=tg_[:, 0:127], scalar1=0.044715, scalar2=1.0, op0=ALU.mult, op1=ALU.add),
                              r=[dtg], w=[dtg])
                        fw.op("dve", lambda e: e.tensor_tensor(out=tg_[:, 0:127], in0=tg_[:, 0:127], in1=xg[:, 0:127], op=ALU.mult), r=[dxg, dtg], w=[dtg])
                        fw.op("act", lambda e: e.activation(out=tg_[:, 0:127], in_=tg_[:, 0:127], func=AF.Sigmoid, scale=1.5957691216), r=[dtg], w=[dtg])
                        gb, dgb = sqb.next()
                        fw.op("dve", lambda e: e.tensor_tensor(out=gb[:, 0:127], in0=tg_[:, 0:127], in1=xg[:, 0:127], op=ALU.mult), r=[dxg, dtg], w=[dgb])
                        if i == 0:
                            pk, dpk = psA.next()
                            fw.op("pe", lambda e: e.matmul(pk[0:64, 0:127], lhsT=w2[0][0][:, :], rhs=gb[:, 0:127], start=True, stop=True),
                                  r=[w2[0][1], dgb], w=[dpk])
                            kf, dkf = f32b.next()
                            fw.op("act", lambda e: e.activation(out=kf[0:64, 0:127], in_=pk[0:64, 0:127], func=AF.Identity, bias=b2k[:, 0:1], scale=1.0),
                                  r=[dpk, gl], w=[dkf])
                            sq, dsq = sqb.next()
                            fw.op("act", lambda e: e.activation(out=sq[0:64, 0:127], in_=kf[0:64, 0:127], func=AF.Square), r=[dkf], w=[dsq])
                            ps2, dp2 = psB.next()
                            fw.op("pe", lambda e: e.matmul(ps2[0:64, 0:127], lhsT=blk_ones[0:128, 0:64], rhs=sq[0:128, 0:127], start=True, stop=True),
                                  r=[dsq, dcon], w=[dp2])
                            rt, drt = f32b.next()
                            fw.op("act", lambda e: e.activation(out=rt[0:64, 0:127], in_=ps2[0:64, 0:127], func=AF.Ln, bias=EPS, scale=1.0 / 64), r=[dp2], w=[drt])
                            fw.op("act", lambda e: e.activation(out=rt[0:64, 0:127], in_=rt[0:64, 0:127], func=AF.Exp, scale=-0.5), r=[drt], w=[drt])
                            fw.op("dve", lambda e: e.scalar_tensor_tensor(out=kcT[0][0:64, 0:127], in0=kf[0:64, 0:127], scalar=hg[0:64, 6:7], in1=rt[0:64, 0:127],
                                                                          op0=ALU.mult, op1=ALU.mult), r=[dkf, drt, dcon], w=[kcT[1]])
                        else:
                            pk, dpk = psA.next()
                            fw.op("pe", lambda e: e.matmul(pk[0:127, 0:64], lhsT=gb[:, 0:127], rhs=w2[1][0][:, :], start=True, stop=True),
                                  r=[w2[1][1], dgb], w=[dpk])
                            fw.op("dve", lambda e: e.tensor_tensor(out=vcA[0][0:127, 0:64], in0=pk[0:127, 0:64], in1=b2v[0:127, :], op=ALU.add),
                                  r=[dpk, gl], w=[vcA[1]])
                    if g == 0:
                        dump("kcT", kcT[0][:], [64, 128], [kcT[1]])
                        dump("vcA", vcA[0][:], [128, 128], [vcA[1]])
                        dump("ksT", ks[0][0:64, :], [64, S], ks[1])
                    for pr in range(2):
                        t, d_ = wQ[pr]
                        load_w3("pool", t, d_, w_in_nsa, (g * 4 + pr * 2) * 64, 128, 0)
                    for pr in range(2):
                        t, d_ = wQ[pr]
                        for sc in range(4):
                            pt, dp = proj_fm(t, d_, sc, 128, 0)
                            a, b_ = qa[pr * 2], qa[pr * 2 + 1]
                            headnorm(pt, dp, 128, 4, [(a[0][0:64, cs_(sc)], a[1][sc], 0), (b_[0][0:64, cs_(sc)], b_[1][sc], 64)])
                    for qd in qa:
                        fw.op("dve", lambda e: e.memset(qd[0][64:96, 0:1024], 0.0), w=[qd[1][0], qd[1][1]])
                    for qt in range(8, 16):
                        fw.op("dve", lambda e: e.memset(imp[0][:, qt, :], 0.0), w=[imp[1][qt]])
                    def build_cmp_E(hl):
                        Ec = EE[hl % 2]
                        build_E(g * 4 + hl, Ec[0], Ec[1], None, None, stage, dstage, pstride=16, off=31, rows=127)

                    build_cmp_E(0)
                    build_cmp_E(1)
                    items = []
                    for hl in range(4):
                        h = g * 4 + hl
                        pair, hh = h // 2, h % 2
                        qt_, dq_ = qa[hl]
                        Ec = EE[hl % 2]
                        for qc in range(4):
                            def post(pb, dpb, qc=qc):
                                if qc < 2:
                                    return
                                pi, dpi = psB.next()
                                for j in range(4):
                                    fw.op("pe", lambda e: e.matmul(pi[:, j * 64:j * 64 + 33], lhsT=pb[0:127, j * 128:(j + 1) * 128], rhs=ovl[0:127, :],
                                                                   start=True, stop=True), r=[dpb, gl], w=[dpi])
                                rdi, drdi = sel_t["rdi"]
                                src = pi[:, 32:33]
                                src3 = bass.AP(tensor=src.tensor, offset=src.offset, ap=[list(src.ap[0]), [64, 4]])
                                fw.op("dve", lambda e: e.reciprocal(out=rdi[:, 0:4], in_=src3), r=[dpi], w=[drdi])
                                for j in range(4):
                                    qtile = qc * 4 + j
                                    fw.op("dve", lambda e: e.scalar_tensor_tensor(out=imp[0][:, qtile, :], in0=pi[:, j * 64:j * 64 + 32], scalar=rdi[:, j:j + 1],
                                                                                  in1=imp[0][:, qtile, :], op0=ALU.mult, op1=ALU.add),
                                          r=[dpi, drdi, imp[1][qtile]], w=[imp[1][qtile]])

                            def cb_cmp(qc, po, dpo, h=h, hl=hl):
                                rd, drd = f32b.next()
                                recip_den(rd, drd, po, dpo)
                                fw.op("dve", lambda e: e.tensor_tensor(out=rd[0:64, :], in0=po[0:64, :], in1=rd[64:128, :], op=ALU.mult), r=[dpo, drd], w=[drd])
                                pgb, dpgb = gate_bc(h, 0, qc)
                                fw.op("dve", lambda e: e.tensor_tensor(out=ocmp[hl][0][0:64, cs_(qc)], in0=rd[0:64, :], in1=pgb[0:64, :], op=ALU.mult),
                                      r=[drd, dpgb], w=[ocmp[hl][1][qc]])
                            items.append(dict(
                                rows=127, n=512, lo=0, hi=512, K=128,
                                lhsT=kcT[0][0:128, 0:127], rhs=qt_[0:128, cs_(qc)], sdeps=[kcT[1], dq_[qc]],
                                E=Ec[0][0:127, cs_(qc)], dE=Ec[1], v=vcA[0][0:127, :], dv=vcA[1],
                                first=True, last=True, cb=cb_cmp, qc=qc, post=post, after=None))
                        if hl + 2 < 4:
                            items[-1]["after"] = (lambda hl=hl: build_cmp_E(hl + 2))
                    run_items(items)
                    vals, dvals = sel_t["vals"]
                    lt, dlt = sel_t["lt"]
                    v2, dv2 = sel_t["v2"]
                    m8a, dm8a = sel_t["m8a"]
                    m8b, dm8b = sel_t["m8b"]
                    impd = [imp[1][q_] for q_ in range(8, 16)]
                    fw.op("dve", lambda e: e.tensor_tensor(out=vals[:, :, :], in0=imp[0][:, 8:16, :], in1=addm[:, 8:16, :], op=ALU.add), r=impd + [gl], w=[dvals])
                    for j in range(8):
                        fw.op("dve", lambda e: e.max(out=m8a[:, j, :], in_=vals[:, j, :]), r=[dvals], w=[dm8a])
                    for j in range(8):
                        fw.op("dve", lambda e: e.tensor_scalar(out=lt[:, j, :], in0=vals[:, j, :], scalar1=m8a[:, j, 7:8], scalar2=None, op0=ALU.is_lt), r=[dvals, dm8a], w=[dlt])
                    fw.op("dve", lambda e: e.tensor_tensor(out=v2[:, :, :], in0=vals[:, :, :], in1=lt[:, :, :], op=ALU.mult), r=[dvals, dlt], w=[dv2])
                    fw.op("dve", lambda e: e.tensor_scalar(out=lt[:, :, :], in0=lt[:, :, :], scalar1=-1.0, scalar2=BIG, op0=ALU.add, op1=ALU.mult), r=[dlt, dv2], w=[dlt])
                    fw.op("dve", lambda e: e.tensor_tensor(out=v2[:, :, :], in0=v2[:, :, :], in1=lt[:, :, :], op=ALU.add), r=[dlt, dv2], w=[dv2])
                    for j in range(8):
                        fw.op("dve", lambda e: e.max(out=m8b[:, j, :], in_=v2[:, j, :]), r=[dv2], w=[dm8b])
                    for j in range(8):
                        fw.op("dve", lambda e: e.tensor_scalar(out=lt[:, j, :], in0=vals[:, j, :], scalar1=m8b[:, j, 4:5], scalar2=None, op0=ALU.is_ge), r=[dvals, dm8b, dlt], w=[dlt])
                    fw.op("dve", lambda e: e.tensor_tensor(out=lt[:, :, :], in0=lt[:, :, :], in1=forced[:, 8:16, :], op=ALU.max), r=[dlt, gl], w=[dlt])
                    fw.op("dve", lambda e: e.tensor_scalar(out=nmp[0][:, :, 64:96], in0=lt[:, :, :], scalar1=-1.0, scalar2=-NEG, op0=ALU.add, op1=ALU.mult),
                          r=[dlt], w=[nmp[1]])
                    for half in range(2):
                        p2, dp2 = psB.next()
                        for jj in range(4):
                            j = half * 4 + jj
                            fw.op("pe", lambda e: e.matmul(p2[0:96, jj * 128:(jj + 1) * 128], lhsT=nmp[0][:, j, 0:96], rhs=ident_bf[:], start=True, stop=True),
                                  r=[nmp[1], dcon], w=[dp2])
                        c0 = (8 + half * 4) * 128
                        for qd in qa:
                            fw.op("act", lambda e: e.activation(out=qd[0][64:96, c0:c0 + 512], in_=p2[64:96, 0:512], func=AF.Copy),
                                  r=[dp2], w=[qd[1][2 + half]])
                    if g == 0:
                        dump("imp", imp[0][:], [128, 16, 32], imp[1])
                        dump("qa0", qa[0][0][:], [96, S], qa[0][1])
                    def build_sw_E(hl):
                        Es, Ew = EE[hl % 2], EW[hl % 2]
                        build_E(g * 4 + hl, Es[0], Es[1], None, None, stage, dstage)
                        fw.op("dve", lambda e: e.tensor_tensor(out=Ew[0][:, :], in0=Es[0][:, 0:640], in1=Mw[0][:, :], op=ALU.add),
                              r=[Es[1], Mw[1]], w=[Ew[1]])

                    build_sw_E(0)
                    build_sw_E(1)
                    items = []
                    for hl in range(4):
                        h = g * 4 + hl
                        pair, hh = h // 2, h % 2
                        Es, Ew = EE[hl % 2], EW[hl % 2]
                        acc = {}

                        def cb_slc(qc, po, dpo, h=h, acc=acc):
                            rd, drd = f32b.next()
                            recip_den(rd, drd, po, dpo)
                            fw.op("dve", lambda e: e.tensor_tensor(out=rd[0:64, :], in0=po[0:64, :], in1=rd[64:128, :], op=ALU.mult), r=[dpo, drd], w=[drd])
                            pgb, dpgb = gate_bc(h, 1, qc)
                            tb, dtb = tsb.next()
                            fw.op("dve", lambda e: e.tensor_tensor(out=tb[:, :], in0=rd[0:64, :], in1=pgb[0:64, :], op=ALU.mult), r=[drd, dpgb], w=[dtb])
                            acc[qc] = (tb, dtb)

                        def cb_win(qc, po, dpo, h=h, pair=pair, hh=hh, hl=hl, acc=acc):
                            rd, drd = f32b.next()
                            a_, da_ = acc[qc]
                            recip_den(rd, drd, po, dpo)
                            fw.op("dve", lambda e: e.tensor_tensor(out=rd[0:64, :], in0=po[0:64, :], in1=rd[64:128, :], op=ALU.mult), r=[dpo, drd], w=[drd])
                            pgb, dpgb = gate_bc(h, 2, qc)
                            tb, dtb = tsb.next()
                            fw.op("dve", lambda e: e.tensor_tensor(out=tb[:, :], in0=rd[0:64, :], in1=pgb[0:64, :], op=ALU.mult), r=[drd, dpgb], w=[dtb])
                            fw.op("dve", lambda e: e.tensor_tensor(out=tb[:, :], in0=tb[:, :], in1=a_[:, :], op=ALU.add), r=[dtb, da_], w=[dtb])
                            fw.op("dve", lambda e: e.tensor_tensor(out=oT[hh * 64:hh * 64 + 64, pair, cs_(qc)], in0=tb[:, :], in1=ocmp[hl][0][0:64, cs_(qc)], op=ALU.add),
                                  r=[dtb, ocmp[hl][1][qc]], w=[doT[pair][qc]])

                        for qc in range(4):
                            items += make_items(qa[hl][0], qa[hl][1], 128, ks[0], ks[1], vs[0], vs[1], Es[0], Es[1], causal_tiles, cb_slc, chunks=[qc])
                            items += make_items(qa[hl][0], qa[hl][1], 128, kw[0], kw[1], vw[0], vw[1], Ew[0], Ew[1], win_tiles, cb_win, chunks=[qc])
                        if hl + 2 < 4:
                            items[-1]["after"] = (lambda hl=hl: build_sw_E(hl + 2))
                    run_items(items)
                pass
                fw.barrier()

        alldh = [d_ for row in dh for d_ in row]
        alldo = [d_ for row in doT for d_ in row]
        with contextlib.ExitStack() as st:
            hT = sb("hT", [128, 8, S], F32, st)
            load_h(xT, [])
            rmsnorm(0)
            dump("xn0", xnT[:], [128, 8, S], dxn)
            fw.barrier()
        if upto >= 1:
            layer0_mixer()
            dump("oT0", oT[:], [128, 8, S], alldo)
        with contextlib.ExitStack() as st:
            hT = sb("hT", [128, 8, S], F32, st)
            load_h(xT, [])
            if upto >= 1:
                out_proj(w_out_ab, st)
                dump("hmix0", hT[:], [128, 8, S], alldh)
                fw.barrier()
            if upto >= 2:
                ffn(0)
                dump("hffn0", hT[:], [128, 8, S], alldh)
            if upto >= 3:
                ple(0)
                dump("h0", hT[:], [128, 8, S], alldh)
            if upto >= 4:
                rmsnorm(3)
                store_h(hS, [dhS])
            fw.barrier()
            if upto < 4:
                store_h(outT, [])
        if upto >= 4:
            layer1_mixer()
            dump("oT1", oT[:], [128, 8, S], alldo)
            with contextlib.ExitStack() as st:
                hT = sb("hT", [128, 8, S], F32, st)
                load_h(hS, [dhS])
                out_proj(w_out_nsa, st)
                dump("hmix1", hT[:], [128, 8, S], alldh)
                fw.barrier()
                if upto >= 5:
                    ffn(1)
                if upto >= 6:
                    ple(1)
                store_h(outT, [])
                fw.barrier()
        fw.finish("sp")
        build_nc.stats = (fw.n_ins, fw.n_wait)
    return nc, consts, DBG


def host_inputs(inputs, b, consts):
    f = lambda a: np.ascontiguousarray(np.asarray(a, dtype=np.float32))
    m = {}
    m["xT"] = f(inputs["x"][b].T)
    m["pT"] = f(np.transpose(inputs["p"][:, b], (0, 2, 1)))
    rb = np.asarray(inputs["rel_bias"], np.float32)
    dd = np.maximum(2047 - np.arange(WHL), 0)
    whb = rb[_bucket(dd), :].T.copy()
    whb[:, 2048:] = NEG
    m["whb"] = f(whb)
    gl = []
    for layer in range(2):
        for nm in ("norm_mix", "norm_ffn", "norm_ple"):
            gl.append(np.asarray(inputs[nm][layer], np.float32).reshape(8, 128).T)
    m["gains"] = f(np.concatenate(gl, axis=1))
    t2 = lambda a: np.concatenate([np.asarray(a, np.float32)] * 2)
    hg = np.zeros((128, 8), np.float32)
    hg[:, 0] = t2(inputs["qn_moba"][0])
    hg[:, 1] = t2(inputs["kn_moba"][0])
    hg[:, 2] = t2(inputs["qn_dil"][0])
    hg[:, 3] = t2(inputs["kn_dil"][0])
    hg[:, 4] = t2(inputs["qn_nsa"][0])
    hg[:, 5] = np.concatenate([np.asarray(inputs["kn_slc"][0], np.float32), np.asarray(inputs["kn_win"][0], np.float32)])
    hg[:, 6] = t2(inputs["kn_cmp"][0])
    m["hg"] = hg
    m["posT"] = f(np.concatenate([np.asarray(inputs["cmp_k_pos"][0]).T, np.asarray(inputs["cmp_v_pos"][0]).T], axis=1))
    m["b1c"] = f(np.stack([inputs["cmp_k_b1"][0], inputs["cmp_v_b1"][0]], axis=1))
    m["b2k"] = f(np.asarray(inputs["cmp_k_b2"][0]).reshape(64, 1))
    m["b2v"] = f(np.asarray(inputs["cmp_v_b2"][0]).reshape(1, 64))
    m["w_in_ab"] = f(inputs["w_in_ab"][0])
    m["w_out_ab"] = f(inputs["w_out_ab"][0])
    m["w_in_nsa"] = f(inputs["w_in_nsa"][0])
    m["w_out_nsa"] = f(inputs["w_out_nsa"][0])
    for nm in ("w_ffn_gate", "w_ffn_up", "w_ffn_down", "w_ple_proj", "w_ple_gate"):
        m[nm] = f(inputs[nm])
    m["cmp_k_w1"] = f(inputs["cmp_k_w1"][0])
    m["cmp_k_w2"] = f(inputs["cmp_k_w2"][0])
    m["cmp_v_w1"] = f(inputs["cmp_v_w1"][0])
    m["cmp_v_w2"] = f(inputs["cmp_v_w2"][0])
    for k, v in consts.items():
        m[k] = v
    return m


def kernel(**inputs):
    nc, consts, _ = build_nc()
    in_maps = [host_inputs(inputs, b, consts) for b in range(8)]
    res = run_bass_kernel_spmd(nc, in_maps, core_ids=list(range(8)))
    out = np.stack([np.asarray(r["outT"], np.float32).T for r in res.results], axis=0)
    return np.ascontiguousarray(out.astype(np.float32))
```

```python
import math
import contextlib
import numpy as np
import concourse.bass as bass
import concourse.mybir as mybir
from concourse.bass_utils import run_bass_kernel_spmd

F32 = mybir.dt.float32
BF16 = mybir.dt.bfloat16
AF = mybir.ActivationFunctionType
ALU = mybir.AluOpType
AX = mybir.AxisListType

S = 2048
D = 1024
FH = 2816
NF = 22
WHL = 4352
EPS = 1e-6
NEG = -30000.0
BIG = 3.0e38


class Dep:
    __slots__ = ("w", "r")

    def __init__(self):
        self.w = None
        self.r = []


class FW:
    NDMA = 24

    def __init__(self, nc, es):
        self.nc = nc
        self.engs = {"pe": nc.tensor, "act": nc.scalar, "dve": nc.vector, "pool": nc.gpsimd, "sp": nc.sync}
        self.sems = {}
        self.cnt = {}
        for k in self.engs:
            self.sems[k] = es.enter_context(nc.semaphore("sem_" + k))
            self.cnt[k] = 0
        for i in range(self.NDMA):
            k = ("dma", i)
            self.sems[k] = es.enter_context(nc.semaphore("sem_dma%d" % i))
            self.cnt[k] = 0
        self.seen = {e: {} for e in self.engs}
        self.dma_rr = {"sp": 0, "pool": 0, "act": 0}
        self.n_ins = 0
        self.n_wait = 0

    def _wait(self, eng, deps):
        seen = self.seen[eng]
        need = {}
        for d in deps:
            if d is None:
                continue
            k, v = d
            if k == "pe" and eng == "pe":
                continue
            if seen.get(k, 0) >= v:
                continue
            if need.get(k, 0) < v:
                need[k] = v
        for k, v in need.items():
            self.engs[eng].wait_ge(self.sems[k], v)
            seen[k] = v
            self.n_wait += 1

    @staticmethod
    def _collect(r, w):
        deps = []
        for t in r:
            deps.append(t.w)
        for t in w:
            deps.append(t.w)
            deps.extend(t.r)
        return deps

    def _mark(self, tok, r, w):
        for t in w:
            t.w = tok
            t.r = []
        for t in r:
            t.r.append(tok)
            if len(t.r) > 64:
                best = {}
                for k, v in t.r:
                    if best.get(k, 0) < v:
                        best[k] = v
                t.r = list(best.items())

    def op(self, eng, fn, r=(), w=()):
        self._wait(eng, self._collect(r, w))
        ins = fn(self.engs[eng])
        self.cnt[eng] += 1
        ins.then_inc(self.sems[eng], 1)
        self._mark((eng, self.cnt[eng]), r, w)
        self.n_ins += 1

    def dma(self, q, out, in_, r=(), w=()):
        half = self.NDMA // 2
        i = self.dma_rr[q]
        self.dma_rr[q] = (i + 1) % half
        k = ("dma", i + (half if q == "pool" else 0))
        deps = self._collect(r, w)
        if self.cnt[k] > 0:
            deps.append((k, self.cnt[k]))
        self._wait(q, deps)
        ins = self.engs[q].dma_start(out=out, in_=in_)
        self.cnt[k] += 16
        ins.then_inc(self.sems[k], 16)
        self._mark((k, self.cnt[k]), r, w)
        self.n_ins += 1

    def barrier(self):
        allk = [(k, v) for k, v in self.cnt.items() if v > 0]
        for e in self.engs:
            self._wait(e, allk)

    def finish(self, eng="sp"):
        allk = [(k, v) for k, v in self.cnt.items() if v > 0]
        self._wait(eng, allk)


class Rot:
    def __init__(self, items):
        self.items = items
        self.i = 0

    def next(self):
        t = self.items[self.i]
        self.i = (self.i + 1) % len(self.items)
        return t


def _bucket(d):
    n = np.maximum(d, 0)
    nf = np.maximum(n, 1).astype(np.float32)
    large = 16 + (np.log(nf / np.float32(16)) / np.float32(math.log(128.0)) * np.float32(16)).astype(np.int32)
    return np.where(n < 16, n, np.minimum(large, 31))


def _static_consts():
    c = {}
    m = np.arange(WHL)
    d = 2047 - m
    wm = np.zeros((3, WHL), np.float32)
    mult = (d >= 0) * ((d <= 128).astype(np.float32) + ((d % 4 == 0) & (d <= 512)) + ((d % 16 == 0) & (d <= 2048)))
    wm[1] = np.where(mult > 0, 8.0 * np.log(np.maximum(mult, 1.0)), 8.0 * NEG)
    wm[2] = np.where((d >= 0) & (d < 512), 0.0, 8.0 * NEG)
    c["c_wm"] = wm
    c["c_ident"] = np.eye(128, dtype=np.float32)
    k = np.arange(S)
    c["c_blk_moba"] = (k[None, :] // 256 == np.arange(8)[:, None]).astype(np.float32)
    c["c_blk_nsa"] = (k[None, :] // 64 == np.arange(32)[:, None]).astype(np.float32)
    g = np.zeros((128, 4, 8), np.float32)
    for i, own in enumerate(range(4, 8)):
        g[:, i, own:] = -BIG
    c["c_gneg"] = g
    g16 = np.zeros((128, 16, 8), np.float32)
    for i in range(16):
        own = (8 + i % 8) // 2
        g16[:, i, own:] = -BIG
    c["c_gneg16"] = g16
    add = np.full((128, 16, 32), -BIG, np.float32)
    forced = np.zeros((128, 16, 32), np.float32)
    for qt in range(16):
        for q in range(128):
            cur = (qt * 128 + q) // 64
            for n in (0, cur, cur - 1):
                if n >= 0:
                    forced[q, qt, n] = 1.0
            for n in range(1, cur - 1):
                add[q, qt, n] = 0.0
    c["c_addmask"] = add
    c["c_forced"] = forced
    cs = np.arange(127) * 16
    ss = np.arange(32) * 64
    ov = np.maximum(np.minimum(cs[:, None] + 32, ss[None, :] + 64) - np.maximum(cs[:, None], ss[None, :]), 0)
    ovl = np.ones((127, 33), np.float32)
    ovl[:, :32] = ov
    c["c_ovl"] = ovl
    gs = np.zeros((48, 48, 64), np.float32)
    for i in range(48):
        gs[i, i, :] = 1.0
    c["c_gsel"] = gs
    return c


_CONST_SHAPES = None


def _dap(t, offset, ap):
    return bass.AP(tensor=t.tensor, offset=offset, ap=[list(a) for a in ap])


def build_nc(upto=99, dbg=()):
    nc = bass.Bass("TRN2", target_bir_lowering=False)
    consts = _static_consts()
    IN = {}

    def din(name, shape):
        IN[name] = nc.dram_tensor(name, list(shape), F32, kind="ExternalInput").ap()
        return IN[name]

    xT = din("xT", [D, S])
    pT = din("pT", [2, 256, S])
    whb = din("whb", [16, WHL])
    gains_d = din("gains", [128, 48])
    hg_d = din("hg", [128, 8])
    posT_d = din("posT", [64, 64])
    b1_d = din("b1c", [128, 2])
    b2k_d = din("b2k", [64, 1])
    b2v_d = din("b2v", [1, 64])
    w_in_ab = din("w_in_ab", [D, 3072])
    w_out_ab = din("w_out_ab", [D, D])
    w_in_nsa = din("w_in_nsa", [D, 2608])
    w_out_nsa = din("w_out_nsa", [D, D])
    w_g = din("w_ffn_gate", [2, D, FH])
    w_u = din("w_ffn_up", [2, D, FH])
    w_d = din("w_ffn_down", [2, FH, D])
    w_pp = din("w_ple_proj", [2, 256, D])
    w_pg = din("w_ple_gate", [2, D, D])
    ck_w1 = din("cmp_k_w1", [2048, 128])
    ck_w2 = din("cmp_k_w2", [128, 64])
    cv_w1 = din("cmp_v_w1", [2048, 128])
    cv_w2 = din("cmp_v_w2", [128, 64])
    for k, v in consts.items():
        din(k, v.shape)
    outT = nc.dram_tensor("outT", [D, S], F32, kind="ExternalOutput").ap()
    DBG = {}

    with contextlib.ExitStack() as es:
        fw = FW(nc, es)

        uniq = [0]

        def sb(name, shape, dt=F32, stack=es):
            uniq[0] += 1
            return stack.enter_context(nc.sbuf_tensor("%s_%d" % (name, uniq[0]), list(shape), dt))

        def pst(name):
            return es.enter_context(nc.psum_tensor(name, [128, 512], F32))

        psS = Rot([(pst("psS%d" % i), Dep()) for i in range(3)])
        psO = Rot([(pst("psO%d" % i), Dep()) for i in range(2)])
        psA = Rot([(pst("psM%d" % i), Dep()) for i in range(3)])
        psB = psA

        def dump(name, ap, shape, deps):
            if name not in dbg:
                return
            t = nc.dram_tensor("dbg_" + name, list(shape), ap.dtype if hasattr(ap, "dtype") else F32, kind="ExternalOutput").ap()
            DBG[name] = t
            fw.dma("sp", t, ap, r=deps)

        hS = nc.dram_tensor("hS", [D, S], F32, kind="Internal").ap()
        dhS = Dep()
        dh = [[Dep() for _ in range(4)] for _ in range(8)]
        xnT = sb("xnT", [128, 8, S], BF16)
        dxn = [Dep() for _ in range(4)]
        oT = sb("oT", [128, 8, S], BF16)
        doT = [[Dep() for _ in range(4)] for _ in range(8)]
        hT = None
        gains = sb("gains_sb", [128, 48])
        hg = sb("hg_sb", [128, 8])
        dcon = Dep()
        ones_bf = sb("ones_bf", [128, 128], BF16)
        blk_ones = sb("blk_ones", [128, 128], BF16)
        ident_bf = sb("ident_bf", [128, 128], BF16)
        sqb = Rot([(sb("sqb%d" % i, [128, 512], BF16), Dep()) for i in range(2)])
        f32b = Rot([(sb("f32b%d" % i, [128, 512]), Dep()) for i in range(6)])

        fw.dma("sp", gains[:], gains_d[:, :], w=[dcon])
        fw.dma("sp", hg[:], hg_d[:, :], w=[dcon])
        fw.dma("pool", ident_bf[:], IN["c_ident"][:, :], w=[dcon])
        fw.op("dve", lambda e: e.memset(ones_bf[:], 1.0), w=[dcon])
        fw.op("dve", lambda e: e.memset(blk_ones[:], 0.0), w=[dcon])
        fw.op("dve", lambda e: e.memset(blk_ones[0:64, 0:64], 1.0), w=[dcon])
        fw.op("dve", lambda e: e.memset(blk_ones[64:128, 64:128], 1.0), w=[dcon])
        def load_h(src, dsrc):
            v = src.rearrange("(c p) s -> p c s", p=128)
            for c in range(8):
                for sc in range(4):
                    fw.dma("sp", hT[:, c, sc * 512:(sc + 1) * 512], v[:, c, sc * 512:(sc + 1) * 512], r=dsrc, w=[dh[c][sc]])

        def store_h(dst, ddst, scs=range(4)):
            v = dst.rearrange("(c p) s -> p c s", p=128)
            for c in range(8):
                for sc in scs:
                    fw.dma("sp", v[:, c, sc * 512:(sc + 1) * 512], hT[:, c, sc * 512:(sc + 1) * 512], r=[dh[c][sc]], w=ddst)

        def cs_(sc):
            return slice(sc * 512, (sc + 1) * 512)

        def rmsnorm(gidx, scs=range(4)):
            for sc in scs:
                cs = cs_(sc)
                pt, dp = psB.next()
                for c in range(8):
                    sq, dsq = sqb.next()
                    fw.op("act", lambda e: e.activation(out=sq[:], in_=hT[:, c, cs], func=AF.Square), r=[dh[c][sc]], w=[dsq])
                    fw.op("pe", lambda e: e.matmul(pt[:], lhsT=ones_bf[:], rhs=sq[:], start=(c == 0), stop=(c == 7)),
                          r=[dsq, dcon], w=[dp])
                rt, drt = f32b.next()
                fw.op("act", lambda e: e.activation(out=rt[:], in_=pt[:], func=AF.Ln, bias=EPS, scale=1.0 / D), r=[dp], w=[drt])
                fw.op("act", lambda e: e.activation(out=rt[:], in_=rt[:], func=AF.Exp, scale=-0.5), r=[drt], w=[drt])
                for c in range(8):
                    fw.op("dve", lambda e: e.scalar_tensor_tensor(
                        out=xnT[:, c, cs], in0=hT[:, c, cs], scalar=gains[:, gidx * 8 + c:gidx * 8 + c + 1], in1=rt[:],
                        op0=ALU.mult, op1=ALU.mult), r=[dh[c][sc], drt, dcon], w=[dxn[sc]])

        def proj_fm(wt, dw, sc, M, c0=0):
            pt, dp = psA.next()
            for kc in range(8):
                fw.op("pe", lambda e: e.matmul(pt[0:M, :], lhsT=wt[:, kc, c0:c0 + M], rhs=xnT[:, kc, cs_(sc)],
                                               start=(kc == 0), stop=(kc == 7)), r=[dw, dxn[sc]], w=[dp])
            return pt, dp

        def headnorm(pt, dp, M, gcol, outs):
            sq, dsq = sqb.next()
            fw.op("act", lambda e: e.activation(out=sq[0:M, :], in_=pt[0:M, :], func=AF.Square), r=[dp], w=[dsq])
            ps2, dp2 = psB.next()
            fw.op("pe", lambda e: e.matmul(ps2[0:M, :], lhsT=blk_ones[0:M, 0:M], rhs=sq[0:M, :], start=True, stop=True),
                  r=[dsq, dcon], w=[dp2])
            rt, drt = f32b.next()
            fw.op("act", lambda e: e.activation(out=rt[0:M, :], in_=ps2[0:M, :], func=AF.Ln, bias=EPS, scale=1.0 / 64), r=[dp2], w=[drt])
            fw.op("act", lambda e: e.activation(out=rt[0:M, :], in_=rt[0:M, :], func=AF.Exp, scale=-0.5), r=[drt], w=[drt])
            for (dst, ddst, r0) in outs:
                fw.op("dve", lambda e: e.scalar_tensor_tensor(
                    out=dst, in0=pt[r0:r0 + 64, :], scalar=hg[r0:r0 + 64, gcol:gcol + 1], in1=rt[r0:r0 + 64, :],
                    op0=ALU.mult, op1=ALU.mult), r=[dp, drt, dcon], w=[ddst])

        def load_w3(q, wt, dw, src2d, c0, ncols, dst_c0=0):
            v = src2d.rearrange("(kc p) n -> p kc n", p=128)
            fw.dma(q, wt[:, :, dst_c0:dst_c0 + ncols], v[:, :, c0:c0 + ncols], w=[dw])

        def rev(t, n, rows=128):
            a = t[0:rows, n - 1:n]
            return bass.AP(tensor=a.tensor, offset=a.offset, ap=[list(a.ap[0]), [-1, n]])

        pbuf_items = []

        LOOK = 2
        CBDELAY = 2

        def make_items(qt, dq, K, ktile, dk, vt, dv, E, dE, tiles_fn, out_cb, chunks=range(4)):
            items = []
            for qc in chunks:
                c0 = qc * 512
                tiles = tiles_fn(qc)
                assert tiles[0][1] == 0 and tiles[0][2] == 512
                for idx, (kt, lo, hi) in enumerate(tiles):
                    n = hi - lo
                    u0 = c0 + lo - kt * 128
                    items.append(dict(
                        rows=128, n=n, lo=lo, hi=hi, K=K,
                        lhsT=ktile[0:K, kt * 128:(kt + 1) * 128], rhs=qt[0:K, c0 + lo:c0 + hi], sdeps=[dk[kt // 4], dq[qc]],
                        E=E[:, u0:u0 + n], dE=dE, v=vt[:, kt, :], dv=dv[kt // 4],
                        first=(idx == 0), last=(idx == len(tiles) - 1), cb=out_cb, qc=qc, post=None, after=None))
            return items

        def run_items(items):
            staged = {}
            cur = [None]
            pend = []
            n_it = len(items)

            def fire(force_to):
                while pend and (pend[0][0] <= 0 or len(pend) > force_to):
                    pend.pop(0)[1]()

            for j in range(n_it + LOOK):
                if j < n_it:
                    it = items[j]
                    pss, dps = psS.next()
                    fw.op("pe", lambda e: e.matmul(pss[0:it["rows"], 0:it["n"]], lhsT=it["lhsT"], rhs=it["rhs"], start=True, stop=False),
                          r=it["sdeps"], w=[dps])
                    fw.op("pe", lambda e: e.matmul(pss[0:it["rows"], 0:it["n"]], lhsT=ident_bf[0:it["rows"], 0:it["rows"]], rhs=it["E"], start=False, stop=True),
                          r=[it["dE"], dcon], w=[dps])
                    staged[j] = (pss, dps)
                i = j - LOOK
                if i < 0:
                    continue
                it = items[i]
                pss, dps = staged.pop(i)
                R, n = it["rows"], it["n"]
                pb, dpb = pbuf.next()
                fw.op("act", lambda e: e.activation(out=pb[0:R, 0:n], in_=pss[0:R, 0:n], func=AF.Exp, scale=0.125), r=[dps], w=[dpb])
                if it["first"]:
                    fire(1)
                    cur[0] = psO.next()
                po, dpo = cur[0]
                fw.op("pe", lambda e: e.matmul(po[:, it["lo"]:it["hi"]], lhsT=it["v"], rhs=pb[0:R, 0:n], start=it["first"], stop=it["last"]),
                      r=[it["dv"], dpb], w=[dpo])
                for p_ in pend:
                    p_[0] -= 1
                if it["post"] is not None:
                    pend.append([CBDELAY, (lambda it=it, pb=pb, dpb=dpb: it["post"](pb, dpb))])
                if it["last"]:
                    pend.append([CBDELAY, (lambda it=it, po=po, dpo=dpo: it["cb"](it["qc"], po, dpo))])
                fire(99)
                if it.get("after") is not None:
                    it["after"]()
            fire(0)

        def attend(*args, **kw):
            run_items(make_items(*args, **kw))

        def recip_den(rd, drd, po, dpo):
            fw.op("act", lambda e: e.activation(out=rd[64:128, :], in_=po[64:128, :], func=AF.Ln, bias=1e-30, scale=1.0), r=[dpo], w=[drd])
            fw.op("act", lambda e: e.activation(out=rd[64:128, :], in_=rd[64:128, :], func=AF.Exp, scale=-1.0), r=[drd], w=[drd])

        def causal_tiles(qc):
            res = []
            for kt in range(4 * qc + 4):
                lo = max(0, kt * 128 - qc * 512)
                res.append((kt, lo, 512))
            return res

        def win_tiles(qc):
            res = []
            order = [4 * qc] + [k for k in range(max(0, 4 * qc - 4), 4 * qc + 4) if k != 4 * qc]
            for kt in order:
                off = kt * 128 - qc * 512
                lo = max(0, off)
                hi = min(512, ((off + 638) // 128 + 1) * 128)
                res.append((kt, lo, hi))
            return res

        def build_E(H, E, dE, M, dM, stage, dstage, width=2048, pstride=1, off=0, rows=128):
            fw.dma("sp", stage[0:128, 0:width], _dap(whb, H * WHL + off + (2048 - width), [[pstride, 128], [1, width]]), w=[dstage])
            if M is None:
                fw.op("dve", lambda e: e.tensor_scalar(out=E[0:rows, 0:width], in0=rev(stage, width, rows), scalar1=8.0, scalar2=None, op0=ALU.mult),
                      r=[dstage], w=[dE])
            else:
                fw.op("dve", lambda e: e.scalar_tensor_tensor(out=E[0:rows, 0:width], in0=rev(stage, width, rows), scalar=8.0, in1=M[0:rows, 0:width],
                                                              op0=ALU.mult, op1=ALU.add), r=[dstage, dM], w=[dE])

        def build_M(kind, M, dM, stage, dstage, width=2048, pstride=1, off=0, rows=128):
            fw.dma("sp", stage[0:128, 0:width], _dap(IN["c_wm"], kind * WHL + off + (2048 - width), [[pstride, 128], [1, width]]), w=[dstage])
            fw.op("act", lambda e: e.activation(out=M[0:rows, 0:width], in_=rev(stage, width, rows), func=AF.Copy), r=[dstage], w=[dM])

        def out_proj(w_out, st):
            wo = [(sb("wo%d" % i, [128, 8, 128], BF16, st), Dep()) for i in range(2)]
            wv = w_out.rearrange("(kc p) n -> p kc n", p=128)

            def ld(fc):
                t, d_ = wo[fc % 2]
                fw.dma("pool", t[:], wv[:, :, fc * 128:(fc + 1) * 128], w=[d_])
            ld(0)
            for fc in range(8):
                if fc + 1 < 8:
                    ld(fc + 1)
                t, d_ = wo[fc % 2]
                for sc in range(4):
                    pt, dp = psA.next()
                    for pr in range(8):
                        fw.op("pe", lambda e: e.matmul(pt[:], lhsT=t[:, pr, :], rhs=oT[:, pr, cs_(sc)], start=(pr == 0), stop=(pr == 7)),
                              r=[d_, doT[pr][sc]], w=[dp])
                    fw.op("dve", lambda e: e.tensor_tensor(out=hT[:, fc, cs_(sc)], in0=pt[:], in1=hT[:, fc, cs_(sc)], op=ALU.add),
                          r=[dp, dh[fc][sc]], w=[dh[fc][sc]])

        def ffn(layer):
            rmsnorm(layer * 3 + 1)
            with contextlib.ExitStack() as st:
                act2 = sb("ffn_act", [128, NF - 16, 1024], BF16, st)
                dact = [[Dep() for _ in range(2)] for _ in range(NF)]

                def act_ap(f, q):
                    if f < 16:
                        return oT[:, f // 2, (f % 2) * 1024 + q * 512:(f % 2) * 1024 + (q + 1) * 512]
                    return act2[:, f - 16, q * 512:(q + 1) * 512]
                wg = [(sb("wg%d" % i, [128, 8, 128], BF16, st), Dep()) for i in range(2)]
                wu = [(sb("wu%d" % i, [128, 8, 128], BF16, st), Dep()) for i in range(2)]
                wd = [(sb("wd%d" % i, [128, NF, 128], BF16, st), Dep()) for i in range(2)]
                sg = Rot([(sb("sg%d" % i, [128, 512], F32, st), Dep()) for i in range(2)])
                wgv = w_g[layer].rearrange("(kc p) n -> p kc n", p=128)
                wuv = w_u[layer].rearrange("(kc p) n -> p kc n", p=128)
                wdv = w_d[layer].rearrange("(f p) n -> p f n", p=128)

                def ld1(f):
                    fw.dma("pool", wg[f % 2][0][:], wgv[:, :, f * 128:(f + 1) * 128], w=[wg[f % 2][1]])
                    fw.dma("pool", wu[f % 2][0][:], wuv[:, :, f * 128:(f + 1) * 128], w=[wu[f % 2][1]])

                def ld2(dc):
                    fw.dma("pool", wd[dc % 2][0][:], wdv[:, :, dc * 128:(dc + 1) * 128], w=[wd[dc % 2][1]])

                for half in range(2):
                    ld1(0)
                    for f in range(NF):
                        if f + 1 < NF:
                            ld1(f + 1)
                        else:
                            ld2(0)
                        tg, dg_ = wg[f % 2]
                        tu, du_ = wu[f % 2]
                        for q in range(2):
                            sc = half * 2 + q
                            pg, dpg = psA.next()
                            pu, dpu = psB.next()
                            for kc in range(8):
                                fw.op("pe", lambda e: e.matmul(pg[:], lhsT=tg[:, kc, :], rhs=xnT[:, kc, cs_(sc)], start=(kc == 0), stop=(kc == 7)),
                                      r=[dg_, dxn[sc]], w=[dpg])
                            for kc in range(8):
                                fw.op("pe", lambda e: e.matmul(pu[:], lhsT=tu[:, kc, :], rhs=xnT[:, kc, cs_(sc)], start=(kc == 0), stop=(kc == 7)),
                                      r=[du_, dxn[sc]], w=[dpu])
                            s_, ds_ = sg.next()
                            fw.op("act", lambda e: e.activation(out=s_[:], in_=pg[:], func=AF.Silu), r=[dpg], w=[ds_])
                            fw.op("dve", lambda e: e.tensor_tensor(out=act_ap(f, q), in0=s_[:], in1=pu[:], op=ALU.mult),
                                  r=[ds_, dpu], w=[dact[f][q]])
                    for dc in range(8):
                        if dc + 1 < 8:
                            ld2(dc + 1)
                        td, dd_ = wd[dc % 2]
                        for q in range(2):
                            sc = half * 2 + q
                            pt, dp = psS.next()
                            for f in range(NF):
                                fw.op("pe", lambda e: e.matmul(pt[:], lhsT=td[:, f, :], rhs=act_ap(f, q),
                                                               start=(f == 0), stop=(f == NF - 1)), r=[dd_, dact[f][q]], w=[dp])
                            fw.op("dve", lambda e: e.tensor_tensor(out=hT[:, dc, cs_(sc)], in0=pt[:], in1=hT[:, dc, cs_(sc)], op=ALU.add),
                                  r=[dp, dh[dc][sc]], w=[dh[dc][sc]])
                    rmsnorm(layer * 3 + 2, [half * 2, half * 2 + 1])
                fw.barrier()

        def ple(layer, after_sc=None):
            with contextlib.ExitStack() as st:
                pTs = sb("pTs", [128, 2, S], BF16, st)
                dpT = [Dep() for _ in range(4)]
                wpg = [(sb("wpg%d" % i, [128, 8, 128], BF16, st), Dep()) for i in range(8)]
                wpp = [(sb("wpp%d" % i, [128, 2, 128], BF16, st), Dep()) for i in range(8)]
                sg = Rot([(sb("psg%d" % i, [128, 512], F32, st), Dep()) for i in range(2)])
                pv_ = pT[layer].rearrange("(kc p) s -> p kc s", p=128)
                for sc in range(4):
                    fw.dma("pool", pTs[:, :, cs_(sc)], pv_[:, :, cs_(sc)], w=[dpT[sc]])
                wgv = w_pg[layer].rearrange("(kc p) n -> p kc n", p=128)
                wpv = w_pp[layer].rearrange("(kc p) n -> p kc n", p=128)
                for fc in range(8):
                    fw.dma("pool", wpg[fc][0][:], wgv[:, :, fc * 128:(fc + 1) * 128], w=[wpg[fc][1]])
                    fw.dma("pool", wpp[fc][0][:], wpv[:, :, fc * 128:(fc + 1) * 128], w=[wpp[fc][1]])
                for sc in range(4):
                    for fc in range(8):
                        tg, dg_ = wpg[fc]
                        tp, dp_ = wpp[fc]
                        pg, dpg = psA.next()
                        pp_, dpp = psS.next()
                        for kc in range(8):
                            fw.op("pe", lambda e: e.matmul(pg[:], lhsT=tg[:, kc, :], rhs=xnT[:, kc, cs_(sc)], start=(kc == 0), stop=(kc == 7)),
                                  r=[dg_, dxn[sc]], w=[dpg])
                        for kc in range(2):
                            fw.op("pe", lambda e: e.matmul(pp_[:], lhsT=tp[:, kc, :], rhs=pTs[:, kc, cs_(sc)], start=(kc == 0), stop=(kc == 1)),
                                  r=[dp_, dpT[sc]], w=[dpp])
                        s_, ds_ = sg.next()
                        fw.op("act", lambda e: e.activation(out=s_[:], in_=pg[:], func=AF.Sigmoid), r=[dpg], w=[ds_])
                        fw.op("dve", lambda e: e.tensor_tensor(out=s_[:], in0=s_[:], in1=pp_[:], op=ALU.mult), r=[ds_, dpp], w=[ds_])
                        fw.op("dve", lambda e: e.tensor_tensor(out=hT[:, fc, cs_(sc)], in0=s_[:], in1=hT[:, fc, cs_(sc)], op=ALU.add),
                              r=[ds_, dh[fc][sc]], w=[dh[fc][sc]])
                    if after_sc is not None:
                        after_sc(sc)
                fw.barrier()

        def layer0_mixer():
            with contextlib.ExitStack() as st:
                qh = [(sb("qh%d" % i, [128, S], BF16, st), [Dep() for _ in range(4)]) for i in range(2)]
                kh = [(sb("kh%d" % i, [128, S], BF16, st), [Dep() for _ in range(4)]) for i in range(2)]
                vh = [(sb("vh%d" % i, [128, 16, 128], BF16, st), [Dep() for _ in range(4)]) for i in range(2)]
                Eh = [(sb("Eh%d" % i, [128, S], BF16, st), Dep()) for i in range(2)]
                Mt = (sb("Mt", [128, S], BF16, st), Dep())
                stage = sb("stage", [128, S], F32, st)
                dstage = Dep()
                wq = [(sb("wq%d" % i, [128, 8, 384], BF16, st), Dep()) for i in range(2)]
                global_pbuf = [(sb("pbuf%d" % i, [128, 512], BF16, st), Dep()) for i in range(4)]
                nonlocal pbuf
                pbuf = Rot(global_pbuf)
                km = [(sb("km%d" % i, [64, 8], F32, st), Dep()) for i in range(2)]
                kmb = [(sb("kmb%d" % i, [72, 8], BF16, st), Dep()) for i in range(2)]
                gneg16 = sb("gneg16", [128, 128], F32, st)
                gm16 = sb("gm16", [128, 128], F32, st)
                dgm = Dep()
                m816 = sb("m816", [128, 128], F32, st)
                dm8 = Dep()
                nmp16 = sb("nmp16", [128, 16, 72], BF16, st)
                dnmp = Dep()
                dl0 = Dep()
                fw.dma("sp", gneg16[:], IN["c_gneg16"].rearrange("p a b -> p (a b)"), w=[dl0])
                fw.op("dve", lambda e: e.memset(nmp16[:], 0.0), w=[dnmp])
                for i in range(2):
                    fw.op("dve", lambda e: e.memset(kmb[i][0][:], 0.0), w=[kmb[i][1]])
                for i in range(2):
                    fw.op("dve", lambda e: e.memset(vh[i][0][:, :, 64:128], 1.0), w=vh[i][1])
                    fw.op("dve", lambda e: e.memset(kh[i][0][64:128, :], 0.0), w=kh[i][1])
                    fw.op("dve", lambda e: e.memset(qh[i][0][64:128, :], 0.0), w=qh[i][1])
                    fw.dma("pool", kh[i][0][64:72, :], IN["c_blk_moba"][:, :], w=kh[i][1])
                build_M(1, Mt[0], Mt[1], stage, dstage)

                def ldw(pair):
                    t, d_ = wq[pair % 2]
                    base = 0 if pair < 4 else 1536
                    pp = pair % 4
                    load_w3("pool", t, d_, w_in_ab, base + pp * 128, 128, 0)
                    load_w3("pool", t, d_, w_in_ab, base + 512 + pp * 128, 128, 128)
                    load_w3("pool", t, d_, w_in_ab, base + 1024 + pp * 128, 128, 256)

                ldw(0)
                for pair in range(8):
                    moba = pair < 4
                    if pair + 1 < 8:
                        ldw(pair + 1)
                    wt, dw = wq[pair % 2]
                    gq = 0 if moba else 2
                    gk = 1 if moba else 3
                    for sc in range(4):
                        pt, dp = proj_fm(wt, dw, sc, 128, 0)
                        headnorm(pt, dp, 128, gq, [(qh[0][0][0:64, cs_(sc)], qh[0][1][sc], 0), (qh[1][0][0:64, cs_(sc)], qh[1][1][sc], 64)])
                        pt, dp = proj_fm(wt, dw, sc, 128, 128)
                        headnorm(pt, dp, 128, gk, [(kh[0][0][0:64, cs_(sc)], kh[0][1][sc], 0), (kh[1][0][0:64, cs_(sc)], kh[1][1][sc], 64)])
                        if moba:
                            for hh in range(2):
                                kin = kh[hh][0][0:64, cs_(sc)]
                                kin3 = bass.AP(tensor=kin.tensor, offset=kin.offset, ap=[list(kin.ap[0]), [256, 2], [1, 256]])
                                fw.op("dve", lambda e: e.tensor_reduce(out=km[hh][0][0:64, 2 * sc:2 * sc + 2], in_=kin3, axis=AX.X, op=ALU.add),
                                      r=[kh[hh][1][sc]], w=[km[hh][1]])
                        pv, dpv = psA.next()
                        for j in range(4):
                            tt = sc * 4 + j
                            for kc in range(8):
                                fw.op("pe", lambda e: e.matmul(pv[:, j * 128:(j + 1) * 128], lhsT=xnT[:, kc, tt * 128:(tt + 1) * 128],
                                                               rhs=wt[:, kc, 256:384], start=(kc == 0), stop=(kc == 7)),
                                      r=[dw, dxn[sc]], w=[dpv])
                        for hh in range(2):
                            src = pv[:, hh * 64:hh * 64 + 1]
                            src3 = bass.AP(tensor=src.tensor, offset=src.offset, ap=[list(src.ap[0]), [128, 4], [1, 64]])
                            fw.op("act", lambda e: e.activation(out=vh[hh][0][:, sc * 4:sc * 4 + 4, 0:64], in_=src3, func=AF.Copy),
                                  r=[dpv], w=[vh[hh][1][sc]])
                    if pair == 0:
                        dump("q0", qh[0][0][0:64, :], [64, S], qh[0][1])
                        dump("k0", kh[0][0][0:64, :], [64, S], kh[0][1])
                        dump("v0", vh[0][0][:], [128, 16, 128], vh[0][1])
                    if moba:
                        for hh in range(2):
                            qt_, dq_ = qh[hh]
                            fw.op("dve", lambda e: e.tensor_copy(out=kmb[hh][0][0:64, :], in_=km[hh][0][:]), r=[km[hh][1]], w=[kmb[hh][1]])
                            fw.op("dve", lambda e: e.memset(qt_[64:72, 0:1024], 0.0), w=[dq_[0], dq_[1]])
                        pg, dpg = psB.next()
                        for hh in range(2):
                            qt_, dq_ = qh[hh]
                            for j in range(8):
                                qtile = 8 + j
                                i = hh * 8 + j
                                fw.op("pe", lambda e: e.matmul(pg[:, i * 8:i * 8 + 8], lhsT=qt_[0:72, qtile * 128:(qtile + 1) * 128], rhs=kmb[hh][0][0:72, 0:8],
                                                               start=True, stop=True), r=[dq_[qtile // 4], kmb[hh][1]], w=[dpg])
                        fw.op("dve", lambda e: e.tensor_tensor(out=gm16[:, :], in0=pg[:, 0:128], in1=gneg16[:, :], op=ALU.add), r=[dpg, dl0], w=[dgm])
                        for i in range(16):
                            fw.op("dve", lambda e: e.max(out=m816[:, i * 8:i * 8 + 8], in_=gm16[:, i * 8:i * 8 + 8]), r=[dgm], w=[dm8])
                        for i in range(16):
                            own = (8 + i % 8) // 2
                            fw.op("dve", lambda e: e.tensor_scalar(out=nmp16[:, i, 64:64 + own], in0=gm16[:, i * 8:i * 8 + own], scalar1=m816[:, i * 8 + 2:i * 8 + 3],
                                                                   scalar2=NEG, op0=ALU.is_lt, op1=ALU.mult), r=[dgm, dm8], w=[dnmp])
                        for hh in range(2):
                            qt_, dq_ = qh[hh]
                            for half in range(2):
                                p2, dp2 = psB.next()
                                for jj in range(4):
                                    i = hh * 8 + half * 4 + jj
                                    fw.op("pe", lambda e: e.matmul(p2[0:72, jj * 128:(jj + 1) * 128], lhsT=nmp16[:, i, 0:72], rhs=ident_bf[:], start=True, stop=True),
                                          r=[dnmp, dcon], w=[dp2])
                                c0 = (8 + half * 4) * 128
                                fw.op("act", lambda e: e.activation(out=qt_[64:72, c0:c0 + 512], in_=p2[64:72, 0:512], func=AF.Copy),
                                      r=[dp2], w=[dq_[2 + half]])
                    if pair == 4:
                        for hh in range(2):
                            fw.op("dve", lambda e: e.memset(qh[hh][0][64:72, :], 0.0), w=qh[hh][1])
                    items = []
                    for hh in range(2):
                        H = pair * 2 + hh
                        E, dE = Eh[hh]
                        M, dM = (None, None) if moba else Mt
                        build_E(H, E, dE, M, dM, stage, dstage)

                        def cb(qc, po, dpo, hh=hh, pair=pair):
                            rd, drd = f32b.next()
                            recip_den(rd, drd, po, dpo)
                            fw.op("dve", lambda e: e.tensor_tensor(out=oT[hh * 64:hh * 64 + 64, pair, cs_(qc)], in0=po[0:64, :], in1=rd[64:128, :], op=ALU.mult),
                                  r=[dpo, drd], w=[doT[pair][qc]])
                        items += make_items(qh[hh][0], qh[hh][1], 128, kh[hh][0], kh[hh][1], vh[hh][0], vh[hh][1], E, dE, causal_tiles, cb)
                    run_items(items)
                fw.barrier()

        pbuf = None

        def layer1_mixer():
            with contextlib.ExitStack() as st:
                nonlocal pbuf
                pbuf = Rot([(sb("pbuf%d" % i, [128, 512], BF16, st), Dep()) for i in range(4)])
                qa = [(sb("qa%d" % i, [128, S], BF16, st), [Dep() for _ in range(4)]) for i in range(4)]
                ks = (sb("ksT", [128, S], BF16, st), [Dep() for _ in range(4)])
                kw = (sb("kwT", [128, S], BF16, st), [Dep() for _ in range(4)])
                vs = (sb("vsA", [128, 16, 128], BF16, st), [Dep() for _ in range(4)])
                vw = (sb("vwA", [128, 16, 128], BF16, st), [Dep() for _ in range(4)])
                ocmp = [(sb("ocmp%d" % i, [64, S], BF16, st), [Dep() for _ in range(4)]) for i in range(4)]
                tc_ = qa[2]
                tv_ = qa[3]
                kcT = (sb("kcT", [128, 128], BF16, st), Dep())
                vcA = (sb("vcA", [128, 128], BF16, st), Dep())
                EE = [(sb("EE%d" % i, [128, S], BF16, st), Dep()) for i in range(2)]
                EW = [(sb("EW%d" % i, [128, 640], BF16, st), Dep()) for i in range(2)]
                Mw = (sb("Mw", [128, 640], BF16, st), Dep())
                stage = sb("stage", [128, S], F32, st)
                dstage = Dep()
                gsig = (sb("gsig", [96, S], BF16, st), [Dep() for _ in range(4)])
                gsel = sb("gsel", [96, 48, 64], BF16, st)
                ovl = sb("ovl", [128, 33], BF16, st)
                addm = sb("addm", [128, 16, 32], F32, st)
                forced = sb("forced", [128, 16, 32], F32, st)
                imp = (sb("imp", [128, 16, 32], F32, st), [Dep() for _ in range(16)])
                w1s = (sb("w1s", [96, 32, 128], BF16, st), Dep())
                w1 = [w1s, w1s]
                w2 = [(sb("w2_%d" % i, [128, 64], BF16, st), Dep()) for i in range(2)]
                posT = sb("posT", [96, 64], BF16, st)
                b1c = sb("b1c", [128, 2], F32, st)
                b2k = sb("b2k", [64, 1], F32, st)
                b2v = sb("b2v", [128, 64], F32, st)
                cb1 = sb("cb1", [128, 2], F32, st)
                dcb1 = Dep()
                wA = [(sb("wA%d" % i, [128, 8, 128], BF16, st), Dep()) for i in range(3)]
                wQ = [wA[0], wA[1]]
                wG = (sb("wG", [128, 8, 48], BF16, st), Dep())
                tsb = Rot([(sb("tsb%d" % i, [64, 512], BF16, st), Dep()) for i in range(4)])
                sel_t = {}
                for nm_, shp in [("vals", [128, 8, 32]), ("lt", [128, 8, 32]), ("v2", [128, 8, 32]), ("m8a", [128, 8, 8]), ("m8b", [128, 8, 8]), ("rdi", [128, 4])]:
                    sel_t[nm_] = (sb("sel_" + nm_, shp, F32, st), Dep())
                nmp = (sb("nmp1", [128, 8, 96], BF16, st), Dep())
                gl = Dep()
                fw.op("dve", lambda e: e.memset(gsel[:], 0.0), w=[gl])
                fw.op("dve", lambda e: e.memset(gsig[0][:], 0.0), w=gsig[1])
                fw.op("dve", lambda e: e.memset(posT[:], 0.0), w=[gl])
                fw.op("dve", lambda e: e.memset(w1s[0][:], 0.0), w=[w1s[1]])
                fw.op("dve", lambda e: e.memset(kcT[0][:], 0.0), w=[kcT[1]])
                fw.op("dve", lambda e: e.memset(kw[0][64:128, :], 0.0), w=kw[1])
                fw.op("dve", lambda e: e.memset(ks[0][64:128, :], 0.0), w=ks[1])
                for qd in qa:
                    fw.op("dve", lambda e: e.memset(qd[0][64:128, :], 0.0), w=qd[1])
                fw.dma("pool", gsel[0:48, :, :], IN["c_gsel"][:, :, :], w=[gl])
                fw.dma("pool", ovl[0:127, :], IN["c_ovl"][:, :], w=[gl])
                fw.dma("sp", addm[:], IN["c_addmask"][:, :, :], w=[gl])
                fw.dma("sp", forced[:], IN["c_forced"][:, :, :], w=[gl])
                fw.dma("pool", posT[0:64, :], posT_d[:, :], w=[gl])
                fw.dma("sp", b1c[:], b1_d[:, :], w=[gl])
                fw.dma("sp", b2k[:], b2k_d[:, :], w=[gl])
                fw.dma("sp", b2v[:], _dap(b2v_d, 0, [[0, 128], [1, 64]]), w=[gl])
                w1src = [ck_w1.rearrange("(l d) j -> d l j", d=64), cv_w1.rearrange("(l d) j -> d l j", d=64)]
                fw.dma("pool", w2[0][0][:], ck_w2[:, :], w=[w2[0][1]])
                fw.dma("pool", w2[1][0][:], cv_w2[:, :], w=[w2[1][1]])
                fw.dma("pool", ks[0][64:96, :], IN["c_blk_nsa"][:, :], w=ks[1])
                fw.op("dve", lambda e: e.memset(nmp[0][:], 0.0), w=[nmp[1]])
                fw.op("dve", lambda e: e.memset(vs[0][:, :, 64:128], 1.0), w=vs[1])
                fw.op("dve", lambda e: e.memset(vw[0][:, :, 64:128], 1.0), w=vw[1])
                fw.op("dve", lambda e: e.memset(vcA[0][:, 64:128], 1.0), w=[vcA[1]])
                build_M(2, Mw[0], Mw[1], stage, dstage, width=640)
                for i in range(2):
                    fw.dma("pool", w1s[0][0:64, :, :], w1src[i], w=[w1s[1]])
                    pt, dp = psB.next()
                    for l in range(32):
                        fw.op("pe", lambda e: e.matmul(pt[:, 0:1], lhsT=w1[i][0][0:96, l, :], rhs=posT[0:96, i * 32 + l:i * 32 + l + 1],
                                                       start=(l == 0), stop=(l == 31)), r=[w1[i][1], gl], w=[dp])
                    fw.op("dve", lambda e: e.tensor_tensor(out=cb1[:, i:i + 1], in0=pt[:, 0:1], in1=b1c[:, i:i + 1], op=ALU.add),
                          r=[dp, gl], w=[dcb1])
                load_w3("pool", wG[0], wG[1], w_in_nsa, 2560, 48, 0)
                for sc in range(4):
                    pt, dp = proj_fm(wG[0], wG[1], sc, 48, 0)
                    fw.op("act", lambda e: e.activation(out=gsig[0][0:48, cs_(sc)], in_=pt[0:48, :], func=AF.Sigmoid), r=[dp], w=[gsig[1][sc]])

                def gate_bc(h, j, qc):
                    pt, dp = psB.next()
                    fw.op("pe", lambda e: e.matmul(pt[0:64, :], lhsT=gsel[0:96, h * 3 + j, :], rhs=gsig[0][0:96, cs_(qc)], start=True, stop=True),
                          r=[gl, gsig[1][qc]], w=[dp])
                    return pt, dp

                for g in range(4):
                    load_w3("pool", wA[0][0], wA[0][1], w_in_nsa, 1536 + g * 64, 64, 0)
                    load_w3("pool", wA[0][0], wA[0][1], w_in_nsa, 2048 + g * 64, 64, 64)
                    load_w3("pool", wA[1][0], wA[1][1], w_in_nsa, 1024 + g * 64, 64, 0)
                    load_w3("pool", wA[1][0], wA[1][1], w_in_nsa, 1280 + g * 64, 64, 64)
                    load_w3("pool", wA[2][0], wA[2][1], w_in_nsa, 1792 + g * 64, 64, 0)
                    load_w3("pool", wA[2][0], wA[2][1], w_in_nsa, 2304 + g * 64, 64, 64)
                    for sc in range(4):
                        pt, dp = proj_fm(wA[0][0], wA[0][1], sc, 128, 0)
                        headnorm(pt, dp, 128, 5, [(ks[0][0:64, cs_(sc)], ks[1][sc], 0), (kw[0][0:64, cs_(sc)], kw[1][sc], 64)])
                        pt, dp = proj_fm(wA[1][0], wA[1][1], sc, 128, 0)
                        fw.op("act", lambda e: e.activation(out=tc_[0][0:64, cs_(sc)], in_=pt[0:64, :], func=AF.Copy), r=[dp], w=[tc_[1][sc]])
                        fw.op("act", lambda e: e.activation(out=tv_[0][0:64, cs_(sc)], in_=pt[64:128, :], func=AF.Copy), r=[dp], w=[tv_[1][sc]])
                        pv, dpv = psA.next()
                        for j in range(4):
                            tt = sc * 4 + j
                            for kc in range(8):
                                fw.op("pe", lambda e: e.matmul(pv[:, j * 128:(j + 1) * 128], lhsT=xnT[:, kc, tt * 128:(tt + 1) * 128],
                                                               rhs=wA[2][0][:, kc, :], start=(kc == 0), stop=(kc == 7)),
                                      r=[wA[2][1], dxn[sc]], w=[dpv])
                        for hh, vdst in enumerate((vs, vw)):
                            src = pv[:, hh * 64:hh * 64 + 1]
                            src3 = bass.AP(tensor=src.tensor, offset=src.offset, ap=[list(src.ap[0]), [128, 4], [1, 64]])
                            fw.op("act", lambda e: e.activation(out=vdst[0][:, sc * 4:sc * 4 + 4, 0:64], in_=src3, func=AF.Copy),
                                  r=[dpv], w=[vdst[1][sc]])
                    for i, tsrc in enumerate((tc_, tv_)):
                        fw.dma("pool", w1s[0][0:64, :, :], w1src[i], w=[w1s[1]])
                        ph, dph = psA.next()
                        for l in range(32):
                            a = tsrc[0][0:96, l:l + 1]
                            rhs = bass.AP(tensor=a.tensor, offset=a.offset, ap=[list(a.ap[0]), [16, 127]])
                            fw.op("pe", lambda e: e.matmul(ph[:, 0:127], lhsT=w1[i][0][0:96, l, :], rhs=rhs, start=(l == 0), stop=(l == 31)),
                                  r=[w1[i][1]] + tsrc[1], w=[dph])
                        xg, dxg = f32b.next()
                        tg_, dtg = f32b.next()
                        fw.op("act", lambda e: e.activation(out=xg[:, 0:127], in_=ph[:, 0:127], func=AF.Identity, bias=cb1[:, i:i + 1], scale=1.0),
                              r=[dph, dcb1], w=[dxg])
                        fw.op("dve", lambda e: e.tensor_tensor(out=tg_[:, 0:127], in0=xg[:, 0:127], in1=xg[:, 0:127], op=ALU.mult), r=[dxg], w=[dtg])
                        fw.op("dve", lambda e: e.tensor_scalar(out=tg_[:, 0:127], in0=tg_[:, 0:127], scalar1=0.044715, scalar2=1.0, op0=ALU.mult, op1=ALU.add),
                              r=[dtg], w=[dtg])
                        fw.op("dve", lambda e: e.tensor_tensor(out=tg_[:, 0:127], in0=tg_[:, 0:127], in1=xg[:, 0:127], op=ALU.mult), r=[dxg, dtg], w=[dtg])
                        fw.op("act", lambda e: e.activation(out=tg_[:, 0:127], in_=tg_[:, 0:127], func=AF.Sigmoid, scale=1.5957691216), r=[dtg], w=[dtg])
                        gb, dgb = sqb.next()
                        fw.op("dve", lambda e: e.tensor_tensor(out=gb[:, 0:127], in0=tg_[:, 0:127], in1=xg[:, 0:127], op=ALU.mult), r=[dxg, dtg], w=[dgb])
                        if i == 0:
                            pk, dpk = psA.next()
                            fw.op("pe", lambda e: e.matmul(pk[0:64, 0:127], lhsT=w2[0][0][:, :], rhs=gb[:, 0:127], start=True, stop=True),
                                  r=[w2[0][1], dgb], w=[dpk])
                            kf, dkf = f32b.next()
                            fw.op("act", lambda e: e.activation(out=kf[0:64, 0:127], in_=pk[0:64, 0:127], func=AF.Identity, bias=b2k[:, 0:1], scale=1.0),
                                  r=[dpk, gl], w=[dkf])
                            sq, dsq = sqb.next()
                            fw.op("act", lambda e: e.activation(out=sq[0:64, 0:127], in_=kf[0:64, 0:127], func=AF.Square), r=[dkf], w=[dsq])
                            ps2, dp2 = psB.next()
                            fw.op("pe", lambda e: e.matmul(ps2[0:64, 0:127], lhsT=blk_ones[0:128, 0:64], rhs=sq[0:128, 0:127], start=True, stop=True),
                                  r=[dsq, dcon], w=[dp2])
                            rt, drt = f32b.next()
                            fw.op("act", lambda e: e.activation(out=rt[0:64, 0:127], in_=ps2[0:64, 0:127], func=AF.Ln, bias=EPS, scale=1.0 / 64), r=[dp2], w=[drt])
                            fw.op("act", lambda e: e.activation(out=rt[0:64, 0:127], in_=rt[0:64, 0:127], func=AF.Exp, scale=-0.5), r=[drt], w=[drt])
                            fw.op("dve", lambda e: e.scalar_tensor_tensor(out=kcT[0][0:64, 0:127], in0=kf[0:64, 0:127], scalar=hg[0:64, 6:7], in1=rt[0:64, 0:127],
                                                                          op0=ALU.mult, op1=ALU.mult), r=[dkf, drt, dcon], w=[kcT[1]])
                        else:
                            pk, dpk = psA.next()
                            fw.op("pe", lambda e: e.matmul(pk[0:127, 0:64], lhsT=gb[:, 0:127], rhs=w2[1][0][:, :], start=True, stop=True),
                                  r=[w2[1][1], dgb], w=[dpk])
                            fw.op("dve", lambda e: e.tensor_tensor(out=vcA[0][0:127, 0:64], in0=pk[0:127, 0:64], in1=b2v[0:127, :], op=ALU.add),
                                  r=[dpk, gl], w=[vcA[1]])
                    if g == 0:
                        dump("kcT", kcT[0][:], [64, 128], [kcT[1]])
                        dump("vcA", vcA[0][:], [128, 128], [vcA[1]])
                        dump("ksT", ks[0][0:64, :], [64, S], ks[1])
                    for pr in range(2):
                        t, d_ = wQ[pr]
                        load_w3("pool", t, d_, w_in_nsa, (g * 4 + pr * 2) * 64, 128, 0)
                    for pr in range(2):
                        t, d_ = wQ[pr]
                        for sc in range(4):
                            pt, dp = proj_fm(t, d_, sc, 128, 0)
                            a, b_ = qa[pr * 2], qa[pr * 2 + 1]
                            headnorm(pt, dp, 128, 4, [(a[0][0:64, cs_(sc)], a[1][sc], 0), (b_[0][0:64, cs_(sc)], b_[1][sc], 64)])
                    for qd in qa:
                        fw.op("dve", lambda e: e.memset(qd[0][64:96, 0:1024], 0.0), w=[qd[1][0], qd[1][1]])
                    for qt in range(8, 16):
                        fw.op("dve", lambda e: e.memset(imp[0][:, qt, :], 0.0), w=[imp[1][qt]])
                    def build_cmp_E(hl):
                        Ec = EE[hl % 2]
                        build_E(g * 4 + hl, Ec[0], Ec[1], None, None, stage, dstage, pstride=16, off=31, rows=127)

                    build_cmp_E(0)
                    build_cmp_E(1)
                    items = []
                    for hl in range(4):
                        h = g * 4 + hl
                        pair, hh = h // 2, h % 2
                        qt_, dq_ = qa[hl]
                        Ec = EE[hl % 2]
                        for qc in range(4):
                            def post(pb, dpb, qc=qc):
                                if qc < 2:
                                    return
                                pi, dpi = psB.next()
                                for j in range(4):
                                    fw.op("pe", lambda e: e.matmul(pi[:, j * 64:j * 64 + 33], lhsT=pb[0:127, j * 128:(j + 1) * 128], rhs=ovl[0:127, :],
                                                                   start=True, stop=True), r=[dpb, gl], w=[dpi])
                                rdi, drdi = sel_t["rdi"]
                                src = pi[:, 32:33]
                                src3 = bass.AP(tensor=src.tensor, offset=src.offset, ap=[list(src.ap[0]), [64, 4]])
                                fw.op("dve", lambda e: e.reciprocal(out=rdi[:, 0:4], in_=src3), r=[dpi], w=[drdi])
                                for j in range(4):
                                    qtile = qc * 4 + j
                                    fw.op("dve", lambda e: e.scalar_tensor_tensor(out=imp[0][:, qtile, :], in0=pi[:, j * 64:j * 64 + 32], scalar=rdi[:, j:j + 1],
                                                                                  in1=imp[0][:, qtile, :], op0=ALU.mult, op1=ALU.add),
                                          r=[dpi, drdi, imp[1][qtile]], w=[imp[1][qtile]])

                            def cb_cmp(qc, po, dpo, h=h, hl=hl):
                                rd, drd = f32b.next()
                                recip_den(rd, drd, po, dpo)
                                fw.op("dve", lambda e: e.tensor_tensor(out=rd[0:64, :], in0=po[0:64, :], in1=rd[64:128, :], op=ALU.mult), r=[dpo, drd], w=[drd])
                                pgb, dpgb = gate_bc(h, 0, qc)
                                fw.op("dve", lambda e: e.tensor_tensor(out=ocmp[hl][0][0:64, cs_(qc)], in0=rd[0:64, :], in1=pgb[0:64, :], op=ALU.mult),
                                      r=[drd, dpgb], w=[ocmp[hl][1][qc]])
                            items.append(dict(
                                rows=127, n=512, lo=0, hi=512, K=128,
                                lhsT=kcT[0][0:128, 0:127], rhs=qt_[0:128, cs_(qc)], sdeps=[kcT[1], dq_[qc]],
                                E=Ec[0][0:127, cs_(qc)], dE=Ec[1], v=vcA[0][0:127, :], dv=vcA[1],
                                first=True, last=True, cb=cb_cmp, qc=qc, post=post, after=None))
                        if hl + 2 < 4:
                            items[-1]["after"] = (lambda hl=hl: build_cmp_E(hl + 2))
                    run_items(items)
                    vals, dvals = sel_t["vals"]
                    lt, dlt = sel_t["lt"]
                    v2, dv2 = sel_t["v2"]
                    m8a, dm8a = sel_t["m8a"]
                    m8b, dm8b = sel_t["m8b"]
                    impd = [imp[1][q_] for q_ in range(8, 16)]
                    fw.op("dve", lambda e: e.tensor_tensor(out=vals[:, :, :], in0=imp[0][:, 8:16, :], in1=addm[:, 8:16, :], op=ALU.add), r=impd + [gl], w=[dvals])
                    for j in range(8):
                        fw.op("dve", lambda e: e.max(out=m8a[:, j, :], in_=vals[:, j, :]), r=[dvals], w=[dm8a])
                    for j in range(8):
                        fw.op("dve", lambda e: e.tensor_scalar(out=lt[:, j, :], in0=vals[:, j, :], scalar1=m8a[:, j, 7:8], scalar2=None, op0=ALU.is_lt), r=[dvals, dm8a], w=[dlt])
                    fw.op("dve", lambda e: e.tensor_tensor(out=v2[:, :, :], in0=vals[:, :, :], in1=lt[:, :, :], op=ALU.mult), r=[dvals, dlt], w=[dv2])
                    fw.op("dve", lambda e: e.tensor_scalar(out=lt[:, :, :], in0=lt[:, :, :], scalar1=-1.0, scalar2=BIG, op0=ALU.add, op1=ALU.mult), r=[dlt, dv2], w=[dlt])
                    fw.op("dve", lambda e: e.tensor_tensor(out=v2[:, :, :], in0=v2[:, :, :], in1=lt[:, :, :], op=ALU.add), r=[dlt, dv2], w=[dv2])
                    for j in range(8):
                        fw.op("dve", lambda e: e.max(out=m8b[:, j, :], in_=v2[:, j, :]), r=[dv2], w=[dm8b])
                    for j in range(8):
                        fw.op("dve", lambda e: e.tensor_scalar(out=lt[:, j, :], in0=vals[:, j, :], scalar1=m8b[:, j, 4:5], scalar2=None, op0=ALU.is_ge), r=[dvals, dm8b, dlt], w=[dlt])
                    fw.op("dve", lambda e: e.tensor_tensor(out=lt[:, :, :], in0=lt[:, :, :], in1=forced[:, 8:16, :], op=ALU.max), r=[dlt, gl], w=[dlt])
                    fw.op("dve", lambda e: e.tensor_scalar(out=nmp[0][:, :, 64:96], in0=lt[:, :, :], scalar1=-1.0, scalar2=-NEG, op0=ALU.add, op1=ALU.mult),
                          r=[dlt], w=[nmp[1]])
                    for half in range(2):
                        p2, dp2 = psB.next()
                        for jj in range(4):
                            j = half * 4 + jj
                            fw.op("pe", lambda e: e.matmul(p2[0:96, jj * 128:(jj + 1) * 128], lhsT=nmp[0][:, j, 0:96], rhs=ident_bf[:], start=True, stop=True),
                                  r=[nmp[1], dcon], w=[dp2])
                        c0 = (8 + half * 4) * 128
                        for qd in qa:
                            fw.op("act", lambda e: e.activation(out=qd[0][64:96, c0:c0 + 512], in_=p2[64:96, 0:512], func=AF.Copy),
                                  r=[dp2], w=[qd[1][2 + half]])
                    if g == 0:
                        dump("imp", imp[0][:], [128, 16, 32], imp[1])
                        dump("qa0", qa[0][0][:], [96, S], qa[0][1])
                    def build_sw_E(hl):
                        Es, Ew = EE[hl % 2], EW[hl % 2]
                        build_E(g * 4 + hl, Es[0], Es[1], None, None, stage, dstage)
                        fw.op("dve", lambda e: e.tensor_tensor(out=Ew[0][:, :], in0=Es[0][:, 0:640], in1=Mw[0][:, :], op=ALU.add),
                              r=[Es[1], Mw[1]], w=[Ew[1]])

                    build_sw_E(0)
                    build_sw_E(1)
                    items = []
                    for hl in range(4):
                        h = g * 4 + hl
                        pair, hh = h // 2, h % 2
                        Es, Ew = EE[hl % 2], EW[hl % 2]
                        acc = {}

                        def cb_slc(qc, po, dpo, h=h, acc=acc):
                            rd, drd = f32b.next()
                            recip_den(rd, drd, po, dpo)
                            fw.op("dve", lambda e: e.tensor_tensor(out=rd[0:64, :], in0=po[0:64, :], in1=rd[64:128, :], op=ALU.mult), r=[dpo, drd], w=[drd])
                            pgb, dpgb = gate_bc(h, 1, qc)
                            tb, dtb = tsb.next()
                            fw.op("dve", lambda e: e.tensor_tensor(out=tb[:, :], in0=rd[0:64, :], in1=pgb[0:64, :], op=ALU.mult), r=[drd, dpgb], w=[dtb])
                            acc[qc] = (tb, dtb)

                        def cb_win(qc, po, dpo, h=h, pair=pair, hh=hh, hl=hl, acc=acc):
                            rd, drd = f32b.next()
                            a_, da_ = acc[qc]
                            recip_den(rd, drd, po, dpo)
                            fw.op("dve", lambda e: e.tensor_tensor(out=rd[0:64, :], in0=po[0:64, :], in1=rd[64:128, :], op=ALU.mult), r=[dpo, drd], w=[drd])
                            pgb, dpgb = gate_bc(h, 2, qc)
                            tb, dtb = tsb.next()
                            fw.op("dve", lambda e: e.tensor_tensor(out=tb[:, :], in0=rd[0:64, :], in1=pgb[0:64, :], op=ALU.mult), r=[drd, dpgb], w=[dtb])
                            fw.op("dve", lambda e: e.tensor_tensor(out=tb[:, :], in0=tb[:, :], in1=a_[:, :], op=ALU.add), r=[dtb, da_], w=[dtb])
                            fw.op("dve", lambda e: e.tensor_tensor(out=oT[hh * 64:hh * 64 + 64, pair, cs_(qc)], in0=tb[:, :], in1=ocmp[hl][0][0:64, cs_(qc)], op=ALU.add),
                                  r=[dtb, ocmp[hl][1][qc]], w=[doT[pair][qc]])

                        for qc in range(4):
                            items += make_items(qa[hl][0], qa[hl][1], 128, ks[0], ks[1], vs[0], vs[1], Es[0], Es[1], causal_tiles, cb_slc, chunks=[qc])
                            items += make_items(qa[hl][0], qa[hl][1], 128, kw[0], kw[1], vw[0], vw[1], Ew[0], Ew[1], win_tiles, cb_win, chunks=[qc])
                        if hl + 2 < 4:
                            items[-1]["after"] = (lambda hl=hl: build_sw_E(hl + 2))
                    run_items(items)
                pass
                fw.barrier()

        alldh = [d_ for row in dh for d_ in row]
        alldo = [d_ for row in doT for d_ in row]
        with contextlib.ExitStack() as st:
            hT = sb("hT", [128, 8, S], F32, st)
            load_h(xT, [])
            rmsnorm(0)
            dump("xn0", xnT[:], [128, 8, S], dxn)
            fw.barrier()
        if upto >= 1:
            layer0_mixer()
            dump("oT0", oT[:], [128, 8, S], alldo)
        with contextlib.ExitStack() as st:
            hT = sb("hT", [128, 8, S], F32, st)
            load_h(xT, [])
            if upto >= 1:
                out_proj(w_out_ab, st)
                dump("hmix0", hT[:], [128, 8, S], alldh)
                fw.barrier()
            if upto >= 2:
                ffn(0)
                dump("hffn0", hT[:], [128, 8, S], alldh)
            if upto >= 3:
                def after_ple0(sc):
                    if upto >= 4:
                        rmsnorm(3, [sc])
                        store_h(hS, [dhS], [sc])
                ple(0, after_sc=after_ple0)
                dump("h0", hT[:], [128, 8, S], alldh)
            fw.barrier()
            if upto < 4:
                store_h(outT, [])
        if upto >= 4:
            layer1_mixer()
            dump("oT1", oT[:], [128, 8, S], alldo)
            with contextlib.ExitStack() as st:
                hT = sb("hT", [128, 8, S], F32, st)
                load_h(hS, [dhS])
                out_proj(w_out_nsa, st)
                dump("hmix1", hT[:], [128, 8, S], alldh)
                fw.barrier()
                if upto >= 5:
                    ffn(1)
                if upto >= 6:
                    ple(1, after_sc=(lambda sc: store_h(outT, [], [sc])))
                else:
                    store_h(outT, [])
                fw.barrier()
        fw.finish("sp")
        build_nc.stats = (fw.n_ins, fw.n_wait)
    return nc, consts, DBG


def host_inputs(inputs, b, consts):
    f = lambda a: np.ascontiguousarray(np.asarray(a, dtype=np.float32))
    m = {}
    m["xT"] = f(inputs["x"][b].T)
    m["pT"] = f(np.transpose(inputs["p"][:, b], (0, 2, 1)))
    rb = np.asarray(inputs["rel_bias"], np.float32)
    dd = np.maximum(2047 - np.arange(WHL), 0)
    whb = rb[_bucket(dd), :].T.copy()
    whb[:, 2048:] = NEG
    m["whb"] = f(whb)
    gl = []
    for layer in range(2):
        for nm in ("norm_mix", "norm_ffn", "norm_ple"):
            gl.append(np.asarray(inputs[nm][layer], np.float32).reshape(8, 128).T)
    m["gains"] = f(np.concatenate(gl, axis=1))
    t2 = lambda a: np.concatenate([np.asarray(a, np.float32)] * 2)
    hg = np.zeros((128, 8), np.float32)
    hg[:, 0] = t2(inputs["qn_moba"][0])
    hg[:, 1] = t2(inputs["kn_moba"][0])
    hg[:, 2] = t2(inputs["qn_dil"][0])
    hg[:, 3] = t2(inputs["kn_dil"][0])
    hg[:, 4] = t2(inputs["qn_nsa"][0])
    hg[:, 5] = np.concatenate([np.asarray(inputs["kn_slc"][0], np.float32), np.asarray(inputs["kn_win"][0], np.float32)])
    hg[:, 6] = t2(inputs["kn_cmp"][0])
    m["hg"] = hg
    m["posT"] = f(np.concatenate([np.asarray(inputs["cmp_k_pos"][0]).T, np.asarray(inputs["cmp_v_pos"][0]).T], axis=1))
    m["b1c"] = f(np.stack([inputs["cmp_k_b1"][0], inputs["cmp_v_b1"][0]], axis=1))
    m["b2k"] = f(np.asarray(inputs["cmp_k_b2"][0]).reshape(64, 1))
    m["b2v"] = f(np.asarray(inputs["cmp_v_b2"][0]).reshape(1, 64))
    m["w_in_ab"] = f(inputs["w_in_ab"][0])
    m["w_out_ab"] = f(inputs["w_out_ab"][0])
    m["w_in_nsa"] = f(inputs["w_in_nsa"][0])
    m["w_out_nsa"] = f(inputs["w_out_nsa"][0])
    for nm in ("w_ffn_gate", "w_ffn_up", "w_ffn_down", "w_ple_proj", "w_ple_gate"):
        m[nm] = f(inputs[nm])
    m["cmp_k_w1"] = f(inputs["cmp_k_w1"][0])
    m["cmp_k_w2"] = f(inputs["cmp_k_w2"][0])
    m["cmp_v_w1"] = f(inputs["cmp_v_w1"][0])
    m["cmp_v_w2"] = f(inputs["cmp_v_w2"][0])
    for k, v in consts.items():
        m[k] = v
    return m


def kernel(**inputs):
    nc, consts, _ = build_nc()
    in_maps = [host_inputs(inputs, b, consts) for b in range(8)]
    res = run_bass_kernel_spmd(nc, in_maps, core_ids=list(range(8)))
    out = np.stack([np.asarray(r["outT"], np.float32).T for r in res.results], axis=0)
    return np.ascontiguousarray(out.astype(np.float32))
```

```python
import math
import contextlib
import numpy as np
import concourse.bass as bass
import concourse.mybir as mybir
from concourse.bass_utils import run_bass_kernel_spmd

F32 = mybir.dt.float32
BF16 = mybir.dt.bfloat16
AF = mybir.ActivationFunctionType
ALU = mybir.AluOpType
AX = mybir.AxisListType

S = 2048
D = 1024
FH = 2816
NF = 22
WHL = 4352
EPS = 1e-6
NEG = -30000.0
BIG = 3.0e38


class Dep:
    __slots__ = ("w", "r")

    def __init__(self):
        self.w = None
        self.r = []


class FW:
    NDMA = 24

    def __init__(self, nc, es):
        self.nc = nc
        self.engs = {"pe": nc.tensor, "act": nc.scalar, "dve": nc.vector, "pool": nc.gpsimd, "sp": nc.sync}
        self.sems = {}
        self.cnt = {}
        for k in self.engs:
            self.sems[k] = es.enter_context(nc.semaphore("sem_" + k))
            self.cnt[k] = 0
        for i in range(self.NDMA):
            k = ("dma", i)
            self.sems[k] = es.enter_context(nc.semaphore("sem_dma%d" % i))
            self.cnt[k] = 0
        self.seen = {e: {} for e in self.engs}
        self.dma_rr = {"sp": 0, "pool": 0, "act": 0}
        self.n_ins = 0
        self.n_wait = 0

    def _wait(self, eng, deps):
        seen = self.seen[eng]
        need = {}
        for d in deps:
            if d is None:
                continue
            k, v = d
            if k == "pe" and eng == "pe":
                continue
            if seen.get(k, 0) >= v:
                continue
            if need.get(k, 0) < v:
                need[k] = v
        for k, v in need.items():
            self.engs[eng].wait_ge(self.sems[k], v)
            seen[k] = v
            self.n_wait += 1

    @staticmethod
    def _collect(r, w):
        deps = []
        for t in r:
            deps.append(t.w)
        for t in w:
            deps.append(t.w)
            deps.extend(t.r)
        return deps

    def _mark(self, tok, r, w):
        for t in w:
            t.w = tok
            t.r = []
        for t in r:
            t.r.append(tok)
            if len(t.r) > 64:
                best = {}
                for k, v in t.r:
                    if best.get(k, 0) < v:
                        best[k] = v
                t.r = list(best.items())

    def op(self, eng, fn, r=(), w=()):
        self._wait(eng, self._collect(r, w))
        ins = fn(self.engs[eng])
        self.cnt[eng] += 1
        ins.then_inc(self.sems[eng], 1)
        self._mark((eng, self.cnt[eng]), r, w)
        self.n_ins += 1

    def dma(self, q, out, in_, r=(), w=()):
        half = self.NDMA // 2
        i = self.dma_rr[q]
        self.dma_rr[q] = (i + 1) % half
        k = ("dma", i + (half if q == "pool" else 0))
        deps = self._collect(r, w)
        if self.cnt[k] > 0:
            deps.append((k, self.cnt[k]))
        self._wait(q, deps)
        ins = self.engs[q].dma_start(out=out, in_=in_)
        self.cnt[k] += 16
        ins.then_inc(self.sems[k], 16)
        self._mark((k, self.cnt[k]), r, w)
        self.n_ins += 1

    def barrier(self):
        allk = [(k, v) for k, v in self.cnt.items() if v > 0]
        for e in self.engs:
            self._wait(e, allk)

    def finish(self, eng="sp"):
        allk = [(k, v) for k, v in self.cnt.items() if v > 0]
        self._wait(eng, allk)


class Rot:
    def __init__(self, items):
        self.items = items
        self.i = 0

    def next(self):
        t = self.items[self.i]
        self.i = (self.i + 1) % len(self.items)
        return t


def _bucket(d):
    n = np.maximum(d, 0)
    nf = np.maximum(n, 1).astype(np.float32)
    large = 16 + (np.log(nf / np.float32(16)) / np.float32(math.log(128.0)) * np.float32(16)).astype(np.int32)
    return np.where(n < 16, n, np.minimum(large, 31))


def _static_consts():
    c = {}
    m = np.arange(WHL)
    d = 2047 - m
    wm = np.zeros((3, WHL), np.float32)
    mult = (d >= 0) * ((d <= 128).astype(np.float32) + ((d % 4 == 0) & (d <= 512)) + ((d % 16 == 0) & (d <= 2048)))
    wm[1] = np.where(mult > 0, 8.0 * np.log(np.maximum(mult, 1.0)), 8.0 * NEG)
    wm[2] = np.where((d >= 0) & (d < 512), 0.0, 8.0 * NEG)
    c["c_wm"] = wm
    c["c_ident"] = np.eye(128, dtype=np.float32)
    k = np.arange(S)
    c["c_blk_moba"] = (k[None, :] // 256 == np.arange(8)[:, None]).astype(np.float32)
    c["c_blk_nsa"] = (k[None, :] // 64 == np.arange(32)[:, None]).astype(np.float32)
    g = np.zeros((128, 4, 8), np.float32)
    for i, own in enumerate(range(4, 8)):
        g[:, i, own:] = -BIG
    c["c_gneg"] = g
    g16 = np.zeros((128, 16, 8), np.float32)
    for i in range(16):
        own = (8 + i % 8) // 2
        g16[:, i, own:] = -BIG
    c["c_gneg16"] = g16
    add = np.full((128, 16, 32), -BIG, np.float32)
    forced = np.zeros((128, 16, 32), np.float32)
    for qt in range(16):
        for q in range(128):
            cur = (qt * 128 + q) // 64
            for n in (0, cur, cur - 1):
                if n >= 0:
                    forced[q, qt, n] = 1.0
            for n in range(1, cur - 1):
                add[q, qt, n] = 0.0
    c["c_addmask"] = add
    c["c_forced"] = forced
    cs = np.arange(127) * 16
    ss = np.arange(32) * 64
    ov = np.maximum(np.minimum(cs[:, None] + 32, ss[None, :] + 64) - np.maximum(cs[:, None], ss[None, :]), 0)
    ovl = np.ones((127, 33), np.float32)
    ovl[:, :32] = ov
    c["c_ovl"] = ovl
    gs = np.zeros((48, 48, 64), np.float32)
    for i in range(48):
        gs[i, i, :] = 1.0
    c["c_gsel"] = gs
    return c


_CONST_SHAPES = None


def _dap(t, offset, ap):
    return bass.AP(tensor=t.tensor, offset=offset, ap=[list(a) for a in ap])


def build_nc(upto=99, dbg=()):
    nc = bass.Bass("TRN2", target_bir_lowering=False)
    consts = _static_consts()
    IN = {}

    def din(name, shape):
        IN[name] = nc.dram_tensor(name, list(shape), F32, kind="ExternalInput").ap()
        return IN[name]

    xT = din("xT", [D, S])
    pT = din("pT", [2, 256, S])
    whb = din("whb", [16, WHL])
    gains_d = din("gains", [128, 48])
    hg_d = din("hg", [128, 8])
    posT_d = din("posT", [64, 64])
    b1_d = din("b1c", [128, 2])
    b2k_d = din("b2k", [64, 1])
    b2v_d = din("b2v", [1, 64])
    w_in_ab = din("w_in_ab", [D, 3072])
    w_out_ab = din("w_out_ab", [D, D])
    w_in_nsa = din("w_in_nsa", [D, 2608])
    w_out_nsa = din("w_out_nsa", [D, D])
    w_g = din("w_ffn_gate", [2, D, FH])
    w_u = din("w_ffn_up", [2, D, FH])
    w_d = din("w_ffn_down", [2, FH, D])
    w_pp = din("w_ple_proj", [2, 256, D])
    w_pg = din("w_ple_gate", [2, D, D])
    ck_w1 = din("cmp_k_w1", [2048, 128])
    ck_w2 = din("cmp_k_w2", [128, 64])
    cv_w1 = din("cmp_v_w1", [2048, 128])
    cv_w2 = din("cmp_v_w2", [128, 64])
    for k, v in consts.items():
        din(k, v.shape)
    outT = nc.dram_tensor("outT", [D, S], F32, kind="ExternalOutput").ap()
    DBG = {}

    with contextlib.ExitStack() as es:
        fw = FW(nc, es)

        uniq = [0]

        def sb(name, shape, dt=F32, stack=es):
            uniq[0] += 1
            return stack.enter_context(nc.sbuf_tensor("%s_%d" % (name, uniq[0]), list(shape), dt))

        def pst(name):
            return es.enter_context(nc.psum_tensor(name, [128, 512], F32))

        psS = Rot([(pst("psS%d" % i), Dep()) for i in range(3)])
        psO = Rot([(pst("psO%d" % i), Dep()) for i in range(2)])
        psA = Rot([(pst("psM%d" % i), Dep()) for i in range(3)])
        psB = psA

        def dump(name, ap, shape, deps):
            if name not in dbg:
                return
            t = nc.dram_tensor("dbg_" + name, list(shape), ap.dtype if hasattr(ap, "dtype") else F32, kind="ExternalOutput").ap()
            DBG[name] = t
            fw.dma("sp", t, ap, r=deps)

        hS = nc.dram_tensor("hS", [D, S], F32, kind="Internal").ap()
        dhS = Dep()
        dh = [[Dep() for _ in range(4)] for _ in range(8)]
        xnT = sb("xnT", [128, 8, S], BF16)
        dxn = [Dep() for _ in range(4)]
        oT = sb("oT", [128, 8, S], BF16)
        doT = [[Dep() for _ in range(4)] for _ in range(8)]
        hT = None
        gains = sb("gains_sb", [128, 48])
        hg = sb("hg_sb", [128, 8])
        dcon = Dep()
        ones_bf = sb("ones_bf", [128, 128], BF16)
        blk_ones = sb("blk_ones", [128, 128], BF16)
        ident_bf = sb("ident_bf", [128, 128], BF16)
        sqb = Rot([(sb("sqb%d" % i, [128, 512], BF16), Dep()) for i in range(2)])
        f32b = Rot([(sb("f32b%d" % i, [128, 512]), Dep()) for i in range(6)])

        fw.dma("sp", gains[:], gains_d[:, :], w=[dcon])
        fw.dma("sp", hg[:], hg_d[:, :], w=[dcon])
        fw.dma("pool", ident_bf[:], IN["c_ident"][:, :], w=[dcon])
        fw.op("dve", lambda e: e.memset(ones_bf[:], 1.0), w=[dcon])
        fw.op("dve", lambda e: e.memset(blk_ones[:], 0.0), w=[dcon])
        fw.op("dve", lambda e: e.memset(blk_ones[0:64, 0:64], 1.0), w=[dcon])
        fw.op("dve", lambda e: e.memset(blk_ones[64:128, 64:128], 1.0), w=[dcon])
        def load_h(src, dsrc):
            v = src.rearrange("(c p) s -> p c s", p=128)
            for c in range(8):
                for sc in range(4):
                    fw.dma("sp", hT[:, c, sc * 512:(sc + 1) * 512], v[:, c, sc * 512:(sc + 1) * 512], r=dsrc, w=[dh[c][sc]])

        def store_h(dst, ddst, scs=range(4)):
            v = dst.rearrange("(c p) s -> p c s", p=128)
            for c in range(8):
                for sc in scs:
                    fw.dma("sp", v[:, c, sc * 512:(sc + 1) * 512], hT[:, c, sc * 512:(sc + 1) * 512], r=[dh[c][sc]], w=ddst)

        def cs_(sc):
            return slice(sc * 512, (sc + 1) * 512)

        def rmsnorm(gidx, scs=range(4)):
            for sc in scs:
                cs = cs_(sc)
                pt, dp = psB.next()
                for c in range(8):
                    sq, dsq = sqb.next()
                    fw.op("act", lambda e: e.activation(out=sq[:], in_=hT[:, c, cs], func=AF.Square), r=[dh[c][sc]], w=[dsq])
                    fw.op("pe", lambda e: e.matmul(pt[:], lhsT=ones_bf[:], rhs=sq[:], start=(c == 0), stop=(c == 7)),
                          r=[dsq, dcon], w=[dp])
                rt, drt = f32b.next()
                fw.op("act", lambda e: e.activation(out=rt[:], in_=pt[:], func=AF.Ln, bias=EPS, scale=1.0 / D), r=[dp], w=[drt])
                fw.op("act", lambda e: e.activation(out=rt[:], in_=rt[:], func=AF.Exp, scale=-0.5), r=[drt], w=[drt])
                for c in range(8):
                    fw.op("dve", lambda e: e.scalar_tensor_tensor(
                        out=xnT[:, c, cs], in0=hT[:, c, cs], scalar=gains[:, gidx * 8 + c:gidx * 8 + c + 1], in1=rt[:],
                        op0=ALU.mult, op1=ALU.mult), r=[dh[c][sc], drt, dcon], w=[dxn[sc]])

        def proj_fm(wt, dw, sc, M, c0=0):
            pt, dp = psA.next()
            for kc in range(8):
                fw.op("pe", lambda e: e.matmul(pt[0:M, :], lhsT=wt[:, kc, c0:c0 + M], rhs=xnT[:, kc, cs_(sc)],
                                               start=(kc == 0), stop=(kc == 7)), r=[dw, dxn[sc]], w=[dp])
            return pt, dp

        def headnorm(pt, dp, M, gcol, outs):
            sq, dsq = sqb.next()
            fw.op("act", lambda e: e.activation(out=sq[0:M, :], in_=pt[0:M, :], func=AF.Square), r=[dp], w=[dsq])
            ps2, dp2 = psB.next()
            fw.op("pe", lambda e: e.matmul(ps2[0:M, :], lhsT=blk_ones[0:M, 0:M], rhs=sq[0:M, :], start=True, stop=True),
                  r=[dsq, dcon], w=[dp2])
            rt, drt = f32b.next()
            fw.op("act", lambda e: e.activation(out=rt[0:M, :], in_=ps2[0:M, :], func=AF.Ln, bias=EPS, scale=1.0 / 64), r=[dp2], w=[drt])
            fw.op("act", lambda e: e.activation(out=rt[0:M, :], in_=rt[0:M, :], func=AF.Exp, scale=-0.5), r=[drt], w=[drt])
            for (dst, ddst, r0) in outs:
                fw.op("dve", lambda e: e.scalar_tensor_tensor(
                    out=dst, in0=pt[r0:r0 + 64, :], scalar=hg[r0:r0 + 64, gcol:gcol + 1], in1=rt[r0:r0 + 64, :],
                    op0=ALU.mult, op1=ALU.mult), r=[dp, drt, dcon], w=[ddst])

        def load_w3(q, wt, dw, src2d, c0, ncols, dst_c0=0):
            v = src2d.rearrange("(kc p) n -> p kc n", p=128)
            fw.dma(q, wt[:, :, dst_c0:dst_c0 + ncols], v[:, :, c0:c0 + ncols], w=[dw])

        def rev(t, n, rows=128):
            a = t[0:rows, n - 1:n]
            return bass.AP(tensor=a.tensor, offset=a.offset, ap=[list(a.ap[0]), [-1, n]])

        pbuf_items = []

        LOOK = 2
        CBDELAY = 3

        def make_items(qt, dq, K, ktile, dk, vt, dv, E, dE, tiles_fn, out_cb, chunks=range(4)):
            items = []
            for qc in chunks:
                c0 = qc * 512
                tiles = tiles_fn(qc)
                assert tiles[0][1] == 0 and tiles[0][2] == 512
                for idx, (kt, lo, hi) in enumerate(tiles):
                    n = hi - lo
                    u0 = c0 + lo - kt * 128
                    items.append(dict(
                        rows=128, n=n, lo=lo, hi=hi, K=K,
                        lhsT=ktile[0:K, kt * 128:(kt + 1) * 128], rhs=qt[0:K, c0 + lo:c0 + hi], sdeps=[dk[kt // 4], dq[qc]],
                        E=E[:, u0:u0 + n], dE=dE, v=vt[:, kt, :], dv=dv[kt // 4],
                        first=(idx == 0), last=(idx == len(tiles) - 1), cb=out_cb, qc=qc, post=None, after=None))
            return items

        def run_items(items):
            staged = {}
            cur = [None]
            pend = []
            n_it = len(items)

            def fire(force_to):
                while pend and (pend[0][0] <= 0 or len(pend) > force_to):
                    pend.pop(0)[1]()

            for j in range(n_it + LOOK):
                if j < n_it:
                    it = items[j]
                    pss, dps = psS.next()
                    fw.op("pe", lambda e: e.matmul(pss[0:it["rows"], 0:it["n"]], lhsT=it["lhsT"], rhs=it["rhs"], start=True, stop=False),
                          r=it["sdeps"], w=[dps])
                    fw.op("pe", lambda e: e.matmul(pss[0:it["rows"], 0:it["n"]], lhsT=ident_bf[0:it["rows"], 0:it["rows"]], rhs=it["E"], start=False, stop=True),
                          r=[it["dE"], dcon], w=[dps])
                    staged[j] = (pss, dps)
                i = j - LOOK
                if i < 0:
                    continue
                it = items[i]
                pss, dps = staged.pop(i)
                R, n = it["rows"], it["n"]
                pb, dpb = pbuf.next()
                fw.op("act", lambda e: e.activation(out=pb[0:R, 0:n], in_=pss[0:R, 0:n], func=AF.Exp, scale=0.125), r=[dps], w=[dpb])
                if it["first"]:
                    fire(1)
                    cur[0] = psO.next()
                po, dpo = cur[0]
                fw.op("pe", lambda e: e.matmul(po[:, it["lo"]:it["hi"]], lhsT=it["v"], rhs=pb[0:R, 0:n], start=it["first"], stop=it["last"]),
                      r=[it["dv"], dpb], w=[dpo])
                for p_ in pend:
                    p_[0] -= 1
                if it["post"] is not None:
                    pend.append([CBDELAY, (lambda it=it, pb=pb, dpb=dpb: it["post"](pb, dpb))])
                if it["last"]:
                    pend.append([CBDELAY, (lambda it=it, po=po, dpo=dpo: it["cb"](it["qc"], po, dpo))])
                fire(99)
                if it.get("after") is not None:
                    it["after"]()
            fire(0)

        def attend(*args, **kw):
            run_items(make_items(*args, **kw))

        def recip_den(rd, drd, po, dpo):
            fw.op("act", lambda e: e.activation(out=rd[64:128, :], in_=po[64:128, :], func=AF.Ln, bias=1e-30, scale=1.0), r=[dpo], w=[drd])
            fw.op("act", lambda e: e.activation(out=rd[64:128, :], in_=rd[64:128, :], func=AF.Exp, scale=-1.0), r=[drd], w=[drd])

        def causal_tiles(qc):
            res = []
            for kt in range(4 * qc + 4):
                lo = max(0, kt * 128 - qc * 512)
                res.append((kt, lo, 512))
            return res

        def win_tiles(qc):
            res = []
            order = [4 * qc] + [k for k in range(max(0, 4 * qc - 4), 4 * qc + 4) if k != 4 * qc]
            for kt in order:
                off = kt * 128 - qc * 512
                lo = max(0, off)
                hi = min(512, ((off + 638) // 128 + 1) * 128)
                res.append((kt, lo, hi))
            return res

        def build_E(H, E, dE, M, dM, stage, dstage, width=2048, pstride=1, off=0, rows=128):
            fw.dma("sp", stage[0:128, 0:width], _dap(whb, H * WHL + off + (2048 - width), [[pstride, 128], [1, width]]), w=[dstage])
            if M is None:
                fw.op("dve", lambda e: e.tensor_scalar(out=E[0:rows, 0:width], in0=rev(stage, width, rows), scalar1=8.0, scalar2=None, op0=ALU.mult),
                      r=[dstage], w=[dE])
            else:
                fw.op("dve", lambda e: e.scalar_tensor_tensor(out=E[0:rows, 0:width], in0=rev(stage, width, rows), scalar=8.0, in1=M[0:rows, 0:width],
                                                              op0=ALU.mult, op1=ALU.add), r=[dstage, dM], w=[dE])

        def build_M(kind, M, dM, stage, dstage, width=2048, pstride=1, off=0, rows=128):
            fw.dma("sp", stage[0:128, 0:width], _dap(IN["c_wm"], kind * WHL + off + (2048 - width), [[pstride, 128], [1, width]]), w=[dstage])
            fw.op("act", lambda e: e.activation(out=M[0:rows, 0:width], in_=rev(stage, width, rows), func=AF.Copy), r=[dstage], w=[dM])

        def out_proj(w_out, st):
            wo = [(sb("wo%d" % i, [128, 8, 128], BF16, st), Dep()) for i in range(2)]
            wv = w_out.rearrange("(kc p) n -> p kc n", p=128)

            def ld(fc):
                t, d_ = wo[fc % 2]
                fw.dma("pool", t[:], wv[:, :, fc * 128:(fc + 1) * 128], w=[d_])
            ld(0)
            for fc in range(8):
                if fc + 1 < 8:
                    ld(fc + 1)
                t, d_ = wo[fc % 2]
                for sc in range(4):
                    pt, dp = psA.next()
                    for pr in range(8):
                        fw.op("pe", lambda e: e.matmul(pt[:], lhsT=t[:, pr, :], rhs=oT[:, pr, cs_(sc)], start=(pr == 0), stop=(pr == 7)),
                              r=[d_, doT[pr][sc]], w=[dp])
                    fw.op("dve", lambda e: e.tensor_tensor(out=hT[:, fc, cs_(sc)], in0=pt[:], in1=hT[:, fc, cs_(sc)], op=ALU.add),
                          r=[dp, dh[fc][sc]], w=[dh[fc][sc]])

        def ffn(layer):
            rmsnorm(layer * 3 + 1)
            with contextlib.ExitStack() as st:
                act2 = sb("ffn_act", [128, NF - 16, 1024], BF16, st)
                dact = [[Dep() for _ in range(2)] for _ in range(NF)]

                def act_ap(f, q):
                    if f < 16:
                        return oT[:, f // 2, (f % 2) * 1024 + q * 512:(f % 2) * 1024 + (q + 1) * 512]
                    return act2[:, f - 16, q * 512:(q + 1) * 512]
                wg = [(sb("wg%d" % i, [128, 8, 128], BF16, st), Dep()) for i in range(2)]
                wu = [(sb("wu%d" % i, [128, 8, 128], BF16, st), Dep()) for i in range(2)]
                wd = [(sb("wd%d" % i, [128, NF, 128], BF16, st), Dep()) for i in range(2)]
                sg = Rot([(sb("sg%d" % i, [128, 512], F32, st), Dep()) for i in range(2)])
                wgv = w_g[layer].rearrange("(kc p) n -> p kc n", p=128)
                wuv = w_u[layer].rearrange("(kc p) n -> p kc n", p=128)
                wdv = w_d[layer].rearrange("(f p) n -> p f n", p=128)

                def ld1(f):
                    fw.dma("pool", wg[f % 2][0][:], wgv[:, :, f * 128:(f + 1) * 128], w=[wg[f % 2][1]])
                    fw.dma("pool", wu[f % 2][0][:], wuv[:, :, f * 128:(f + 1) * 128], w=[wu[f % 2][1]])

                def ld2(dc):
                    fw.dma("pool", wd[dc % 2][0][:], wdv[:, :, dc * 128:(dc + 1) * 128], w=[wd[dc % 2][1]])

                for half in range(2):
                    ld1(0)
                    for f in range(NF):
                        if f + 1 < NF:
                            ld1(f + 1)
                        else:
                            ld2(0)
                        tg, dg_ = wg[f % 2]
                        tu, du_ = wu[f % 2]
                        for q in range(2):
                            sc = half * 2 + q
                            pg, dpg = psA.next()
                            pu, dpu = psB.next()
                            for kc in range(8):
                                fw.op("pe", lambda e: e.matmul(pg[:], lhsT=tg[:, kc, :], rhs=xnT[:, kc, cs_(sc)], start=(kc == 0), stop=(kc == 7)),
                                      r=[dg_, dxn[sc]], w=[dpg])
                            for kc in range(8):
                                fw.op("pe", lambda e: e.matmul(pu[:], lhsT=tu[:, kc, :], rhs=xnT[:, kc, cs_(sc)], start=(kc == 0), stop=(kc == 7)),
                                      r=[du_, dxn[sc]], w=[dpu])
                            s_, ds_ = sg.next()
                            fw.op("act", lambda e: e.activation(out=s_[:], in_=pg[:], func=AF.Silu), r=[dpg], w=[ds_])
                            fw.op("dve", lambda e: e.tensor_tensor(out=act_ap(f, q), in0=s_[:], in1=pu[:], op=ALU.mult),
                                  r=[ds_, dpu], w=[dact[f][q]])
                    for dc in range(8):
                        if dc + 1 < 8:
                            ld2(dc + 1)
                        td, dd_ = wd[dc % 2]
                        for q in range(2):
                            sc = half * 2 + q
                            pt, dp = psS.next()
                            for f in range(NF):
                                fw.op("pe", lambda e: e.matmul(pt[:], lhsT=td[:, f, :], rhs=act_ap(f, q),
                                                               start=(f == 0), stop=(f == NF - 1)), r=[dd_, dact[f][q]], w=[dp])
                            fw.op("dve", lambda e: e.tensor_tensor(out=hT[:, dc, cs_(sc)], in0=pt[:], in1=hT[:, dc, cs_(sc)], op=ALU.add),
                                  r=[dp, dh[dc][sc]], w=[dh[dc][sc]])
                    rmsnorm(layer * 3 + 2, [half * 2, half * 2 + 1])
                fw.barrier()

        def ple(layer, after_sc=None):
            with contextlib.ExitStack() as st:
                pTs = sb("pTs", [128, 2, S], BF16, st)
                dpT = [Dep() for _ in range(4)]
                wpg = [(sb("wpg%d" % i, [128, 8, 128], BF16, st), Dep()) for i in range(8)]
                wpp = [(sb("wpp%d" % i, [128, 2, 128], BF16, st), Dep()) for i in range(8)]
                sg = Rot([(sb("psg%d" % i, [128, 512], F32, st), Dep()) for i in range(2)])
                pv_ = pT[layer].rearrange("(kc p) s -> p kc s", p=128)
                for sc in range(4):
                    fw.dma("pool", pTs[:, :, cs_(sc)], pv_[:, :, cs_(sc)], w=[dpT[sc]])
                wgv = w_pg[layer].rearrange("(kc p) n -> p kc n", p=128)
                wpv = w_pp[layer].rearrange("(kc p) n -> p kc n", p=128)
                for fc in range(8):
                    fw.dma("pool", wpg[fc][0][:], wgv[:, :, fc * 128:(fc + 1) * 128], w=[wpg[fc][1]])
                    fw.dma("pool", wpp[fc][0][:], wpv[:, :, fc * 128:(fc + 1) * 128], w=[wpp[fc][1]])
                for sc in range(4):
                    for fc in range(8):
                        tg, dg_ = wpg[fc]
                        tp, dp_ = wpp[fc]
                        pg, dpg = psA.next()
                        pp_, dpp = psS.next()
                        for kc in range(8):
                            fw.op("pe", lambda e: e.matmul(pg[:], lhsT=tg[:, kc, :], rhs=xnT[:, kc, cs_(sc)], start=(kc == 0), stop=(kc == 7)),
                                  r=[dg_, dxn[sc]], w=[dpg])
                        for kc in range(2):
                            fw.op("pe", lambda e: e.matmul(pp_[:], lhsT=tp[:, kc, :], rhs=pTs[:, kc, cs_(sc)], start=(kc == 0), stop=(kc == 1)),
                                  r=[dp_, dpT[sc]], w=[dpp])
                        s_, ds_ = sg.next()
                        fw.op("act", lambda e: e.activation(out=s_[:], in_=pg[:], func=AF.Sigmoid), r=[dpg], w=[ds_])
                        fw.op("dve", lambda e: e.tensor_tensor(out=s_[:], in0=s_[:], in1=pp_[:], op=ALU.mult), r=[ds_, dpp], w=[ds_])
                        fw.op("dve", lambda e: e.tensor_tensor(out=hT[:, fc, cs_(sc)], in0=s_[:], in1=hT[:, fc, cs_(sc)], op=ALU.add),
                              r=[ds_, dh[fc][sc]], w=[dh[fc][sc]])
                    if after_sc is not None:
                        after_sc(sc)
                fw.barrier()

        def layer0_mixer():
            with contextlib.ExitStack() as st:
                qh = [(sb("qh%d" % i, [128, S], BF16, st), [Dep() for _ in range(4)]) for i in range(2)]
                kh = [(sb("kh%d" % i, [128, S], BF16, st), [Dep() for _ in range(4)]) for i in range(2)]
                vh = [(sb("vh%d" % i, [128, 16, 128], BF16, st), [Dep() for _ in range(4)]) for i in range(2)]
                Eh = [(sb("Eh%d" % i, [128, S], BF16, st), Dep()) for i in range(2)]
                Mt = (sb("Mt", [128, S], BF16, st), Dep())
                stage = sb("stage", [128, S], F32, st)
                dstage = Dep()
                wq = [(sb("wq%d" % i, [128, 8, 384], BF16, st), Dep()) for i in range(2)]
                global_pbuf = [(sb("pbuf%d" % i, [128, 512], BF16, st), Dep()) for i in range(4)]
                nonlocal pbuf
                pbuf = Rot(global_pbuf)
                km = [(sb("km%d" % i, [64, 8], F32, st), Dep()) for i in range(2)]
                kmb = [(sb("kmb%d" % i, [72, 8], BF16, st), Dep()) for i in range(2)]
                gneg16 = sb("gneg16", [128, 128], F32, st)
                gm16 = sb("gm16", [128, 128], F32, st)
                dgm = Dep()
                m816 = sb("m816", [128, 128], F32, st)
                dm8 = Dep()
                nmp16 = sb("nmp16", [128, 16, 72], BF16, st)
                dnmp = Dep()
                dl0 = Dep()
                fw.dma("sp", gneg16[:], IN["c_gneg16"].rearrange("p a b -> p (a b)"), w=[dl0])
                fw.op("dve", lambda e: e.memset(nmp16[:], 0.0), w=[dnmp])
                for i in range(2):
                    fw.op("dve", lambda e: e.memset(kmb[i][0][:], 0.0), w=[kmb[i][1]])
                for i in range(2):
                    fw.op("dve", lambda e: e.memset(vh[i][0][:, :, 64:128], 1.0), w=vh[i][1])
                    fw.op("dve", lambda e: e.memset(kh[i][0][64:128, :], 0.0), w=kh[i][1])
                    fw.op("dve", lambda e: e.memset(qh[i][0][64:128, :], 0.0), w=qh[i][1])
                    fw.dma("pool", kh[i][0][64:72, :], IN["c_blk_moba"][:, :], w=kh[i][1])
                build_M(1, Mt[0], Mt[1], stage, dstage)

                def ldw(pair):
                    t, d_ = wq[pair % 2]
                    base = 0 if pair < 4 else 1536
                    pp = pair % 4
                    load_w3("pool", t, d_, w_in_ab, base + pp * 128, 128, 0)
                    load_w3("pool", t, d_, w_in_ab, base + 512 + pp * 128, 128, 128)
                    load_w3("pool", t, d_, w_in_ab, base + 1024 + pp * 128, 128, 256)

                ldw(0)
                for pair in range(8):
                    moba = pair < 4
                    if pair + 1 < 8:
                        ldw(pair + 1)
                    wt, dw = wq[pair % 2]
                    gq = 0 if moba else 2
                    gk = 1 if moba else 3
                    for sc in range(4):
                        pt, dp = proj_fm(wt, dw, sc, 128, 0)
                        headnorm(pt, dp, 128, gq, [(qh[0][0][0:64, cs_(sc)], qh[0][1][sc], 0), (qh[1][0][0:64, cs_(sc)], qh[1][1][sc], 64)])
                        pt, dp = proj_fm(wt, dw, sc, 128, 128)
                        headnorm(pt, dp, 128, gk, [(kh[0][0][0:64, cs_(sc)], kh[0][1][sc], 0), (kh[1][0][0:64, cs_(sc)], kh[1][1][sc], 64)])
                        if moba:
                            for hh in range(2):
                                kin = kh[hh][0][0:64, cs_(sc)]
                                kin3 = bass.AP(tensor=kin.tensor, offset=kin.offset, ap=[list(kin.ap[0]), [256, 2], [1, 256]])
                                fw.op("dve", lambda e: e.tensor_reduce(out=km[hh][0][0:64, 2 * sc:2 * sc + 2], in_=kin3, axis=AX.X, op=ALU.add),
                                      r=[kh[hh][1][sc]], w=[km[hh][1]])
                        pv, dpv = psA.next()
                        for j in range(4):
                            tt = sc * 4 + j
                            for kc in range(8):
                                fw.op("pe", lambda e: e.matmul(pv[:, j * 128:(j + 1) * 128], lhsT=xnT[:, kc, tt * 128:(tt + 1) * 128],
                                                               rhs=wt[:, kc, 256:384], start=(kc == 0), stop=(kc == 7)),
                                      r=[dw, dxn[sc]], w=[dpv])
                        for hh in range(2):
                            src = pv[:, hh * 64:hh * 64 + 1]
                            src3 = bass.AP(tensor=src.tensor, offset=src.offset, ap=[list(src.ap[0]), [128, 4], [1, 64]])
                            fw.op("act", lambda e: e.activation(out=vh[hh][0][:, sc * 4:sc * 4 + 4, 0:64], in_=src3, func=AF.Copy),
                                  r=[dpv], w=[vh[hh][1][sc]])
                    if pair == 0:
                        dump("q0", qh[0][0][0:64, :], [64, S], qh[0][1])
                        dump("k0", kh[0][0][0:64, :], [64, S], kh[0][1])
                        dump("v0", vh[0][0][:], [128, 16, 128], vh[0][1])
                    if moba:
                        for hh in range(2):
                            qt_, dq_ = qh[hh]
                            fw.op("dve", lambda e: e.tensor_copy(out=kmb[hh][0][0:64, :], in_=km[hh][0][:]), r=[km[hh][1]], w=[kmb[hh][1]])
                            fw.op("dve", lambda e: e.memset(qt_[64:72, 0:1024], 0.0), w=[dq_[0], dq_[1]])
                        pg, dpg = psB.next()
                        for hh in range(2):
                            qt_, dq_ = qh[hh]
                            for j in range(8):
                                qtile = 8 + j
                                i = hh * 8 + j
                                fw.op("pe", lambda e: e.matmul(pg[:, i * 8:i * 8 + 8], lhsT=qt_[0:72, qtile * 128:(qtile + 1) * 128], rhs=kmb[hh][0][0:72, 0:8],
                                                               start=True, stop=True), r=[dq_[qtile // 4], kmb[hh][1]], w=[dpg])
                        fw.op("dve", lambda e: e.tensor_tensor(out=gm16[:, :], in0=pg[:, 0:128], in1=gneg16[:, :], op=ALU.add), r=[dpg, dl0], w=[dgm])
                        for i in range(16):
                            fw.op("dve", lambda e: e.max(out=m816[:, i * 8:i * 8 + 8], in_=gm16[:, i * 8:i * 8 + 8]), r=[dgm], w=[dm8])
                        for i in range(16):
                            own = (8 + i % 8) // 2
                            fw.op("dve", lambda e: e.tensor_scalar(out=nmp16[:, i, 64:64 + own], in0=gm16[:, i * 8:i * 8 + own], scalar1=m816[:, i * 8 + 2:i * 8 + 3],
                                                                   scalar2=NEG, op0=ALU.is_lt, op1=ALU.mult), r=[dgm, dm8], w=[dnmp])
                        for hh in range(2):
                            qt_, dq_ = qh[hh]
                            for half in range(2):
                                p2, dp2 = psB.next()
                                for jj in range(4):
                                    i = hh * 8 + half * 4 + jj
                                    fw.op("pe", lambda e: e.matmul(p2[0:72, jj * 128:(jj + 1) * 128], lhsT=nmp16[:, i, 0:72], rhs=ident_bf[:], start=True, stop=True),
                                          r=[dnmp, dcon], w=[dp2])
                                c0 = (8 + half * 4) * 128
                                fw.op("act", lambda e: e.activation(out=qt_[64:72, c0:c0 + 512], in_=p2[64:72, 0:512], func=AF.Copy),
                                      r=[dp2], w=[dq_[2 + half]])
                    if pair == 4:
                        for hh in range(2):
                            fw.op("dve", lambda e: e.memset(qh[hh][0][64:72, :], 0.0), w=qh[hh][1])
                    items = []
                    for hh in range(2):
                        H = pair * 2 + hh
                        E, dE = Eh[hh]
                        M, dM = (None, None) if moba else Mt
                        build_E(H, E, dE, M, dM, stage, dstage)

                        def cb(qc, po, dpo, hh=hh, pair=pair):
                            rd, drd = f32b.next()
                            recip_den(rd, drd, po, dpo)
                            fw.op("dve", lambda e: e.tensor_tensor(out=oT[hh * 64:hh * 64 + 64, pair, cs_(qc)], in0=po[0:64, :], in1=rd[64:128, :], op=ALU.mult),
                                  r=[dpo, drd], w=[doT[pair][qc]])
                        items += make_items(qh[hh][0], qh[hh][1], 128, kh[hh][0], kh[hh][1], vh[hh][0], vh[hh][1], E, dE, causal_tiles, cb)
                    run_items(items)
                fw.barrier()

        pbuf = None

        def layer1_mixer():
            with contextlib.ExitStack() as st:
                nonlocal pbuf
                pbuf = Rot([(sb("pbuf%d" % i, [128, 512], BF16, st), Dep()) for i in range(4)])
                qa = [(sb("qa%d" % i, [128, S], BF16, st), [Dep() for _ in range(4)]) for i in range(4)]
                ks = (sb("ksT", [128, S], BF16, st), [Dep() for _ in range(4)])
                kw = (sb("kwT", [128, S], BF16, st), [Dep() for _ in range(4)])
                vs = (sb("vsA", [128, 16, 128], BF16, st), [Dep() for _ in range(4)])
                vw = (sb("vwA", [128, 16, 128], BF16, st), [Dep() for _ in range(4)])
                ocmp = [(sb("ocmp%d" % i, [64, S], BF16, st), [Dep() for _ in range(4)]) for i in range(4)]
                tc_ = qa[2]
                tv_ = qa[3]
                kcT = (sb("kcT", [128, 128], BF16, st), Dep())
                vcA = (sb("vcA", [128, 128], BF16, st), Dep())
                EE = [(sb("EE%d" % i, [128, S], BF16, st), Dep()) for i in range(2)]
                EW = [(sb("EW%d" % i, [128, 640], BF16, st), Dep()) for i in range(2)]
                Mw = (sb("Mw", [128, 640], BF16, st), Dep())
                stage = sb("stage", [128, S], F32, st)
                dstage = Dep()
                gsig = (sb("gsig", [96, S], BF16, st), [Dep() for _ in range(4)])
                gsel = sb("gsel", [96, 48, 64], BF16, st)
                ovl = sb("ovl", [128, 33], BF16, st)
                addm = sb("addm", [128, 16, 32], F32, st)
                forced = sb("forced", [128, 16, 32], F32, st)
                imp = (sb("imp", [128, 16, 32], F32, st), [Dep() for _ in range(16)])
                w1s = (sb("w1s", [96, 32, 128], BF16, st), Dep())
                w1 = [w1s, w1s]
                w2 = [(sb("w2_%d" % i, [128, 64], BF16, st), Dep()) for i in range(2)]
                posT = sb("posT", [96, 64], BF16, st)
                b1c = sb("b1c", [128, 2], F32, st)
                b2k = sb("b2k", [64, 1], F32, st)
                b2v = sb("b2v", [128, 64], F32, st)
                cb1 = sb("cb1", [128, 2], F32, st)
                dcb1 = Dep()
                wA = [(sb("wA%d" % i, [128, 8, 128], BF16, st), Dep()) for i in range(3)]
                wQ = [wA[0], wA[1]]
                wG = (sb("wG", [128, 8, 48], BF16, st), Dep())
                tsb = Rot([(sb("tsb%d" % i, [64, 512], BF16, st), Dep()) for i in range(4)])
                sel_t = {}
                for nm_, shp in [("vals", [128, 8, 32]), ("lt", [128, 8, 32]), ("v2", [128, 8, 32]), ("m8a", [128, 8, 8]), ("m8b", [128, 8, 8]), ("rdi", [128, 4])]:
                    sel_t[nm_] = (sb("sel_" + nm_, shp, F32, st), Dep())
                nmp = (sb("nmp1", [128, 8, 96], BF16, st), Dep())
                gl = Dep()
                fw.op("dve", lambda e: e.memset(gsel[:], 0.0), w=[gl])
                fw.op("dve", lambda e: e.memset(gsig[0][:], 0.0), w=gsig[1])
                fw.op("dve", lambda e: e.memset(posT[:], 0.0), w=[gl])
                fw.op("dve", lambda e: e.memset(w1s[0][:], 0.0), w=[w1s[1]])
                fw.op("dve", lambda e: e.memset(kcT[0][:], 0.0), w=[kcT[1]])
                fw.op("dve", lambda e: e.memset(kw[0][64:128, :], 0.0), w=kw[1])
                fw.op("dve", lambda e: e.memset(ks[0][64:128, :], 0.0), w=ks[1])
                for qd in qa:
                    fw.op("dve", lambda e: e.memset(qd[0][64:128, :], 0.0), w=qd[1])
                fw.dma("pool", gsel[0:48, :, :], IN["c_gsel"][:, :, :], w=[gl])
                fw.dma("pool", ovl[0:127, :], IN["c_ovl"][:, :], w=[gl])
                fw.dma("sp", addm[:], IN["c_addmask"][:, :, :], w=[gl])
                fw.dma("sp", forced[:], IN["c_forced"][:, :, :], w=[gl])
                fw.dma("pool", posT[0:64, :], posT_d[:, :], w=[gl])
                fw.dma("sp", b1c[:], b1_d[:, :], w=[gl])
                fw.dma("sp", b2k[:], b2k_d[:, :], w=[gl])
                fw.dma("sp", b2v[:], _dap(b2v_d, 0, [[0, 128], [1, 64]]), w=[gl])
                w1src = [ck_w1.rearrange("(l d) j -> d l j", d=64), cv_w1.rearrange("(l d) j -> d l j", d=64)]
                fw.dma("pool", w2[0][0][:], ck_w2[:, :], w=[w2[0][1]])
                fw.dma("pool", w2[1][0][:], cv_w2[:, :], w=[w2[1][1]])
                fw.dma("pool", ks[0][64:96, :], IN["c_blk_nsa"][:, :], w=ks[1])
                fw.op("dve", lambda e: e.memset(nmp[0][:], 0.0), w=[nmp[1]])
                fw.op("dve", lambda e: e.memset(vs[0][:, :, 64:128], 1.0), w=vs[1])
                fw.op("dve", lambda e: e.memset(vw[0][:, :, 64:128], 1.0), w=vw[1])
                fw.op("dve", lambda e: e.memset(vcA[0][:, 64:128], 1.0), w=[vcA[1]])
                build_M(2, Mw[0], Mw[1], stage, dstage, width=640)
                for i in range(2):
                    fw.dma("pool", w1s[0][0:64, :, :], w1src[i], w=[w1s[1]])
                    pt, dp = psB.next()
                    for l in range(32):
                        fw.op("pe", lambda e: e.matmul(pt[:, 0:1], lhsT=w1[i][0][0:96, l, :], rhs=posT[0:96, i * 32 + l:i * 32 + l + 1],
                                                       start=(l == 0), stop=(l == 31)), r=[w1[i][1], gl], w=[dp])
                    fw.op("dve", lambda e: e.tensor_tensor(out=cb1[:, i:i + 1], in0=pt[:, 0:1], in1=b1c[:, i:i + 1], op=ALU.add),
                          r=[dp, gl], w=[dcb1])
                load_w3("pool", wG[0], wG[1], w_in_nsa, 2560, 48, 0)
                for sc in range(4):
                    pt, dp = proj_fm(wG[0], wG[1], sc, 48, 0)
                    fw.op("act", lambda e: e.activation(out=gsig[0][0:48, cs_(sc)], in_=pt[0:48, :], func=AF.Sigmoid), r=[dp], w=[gsig[1][sc]])

                def gate_bc(h, j, qc):
                    pt, dp = psB.next()
                    fw.op("pe", lambda e: e.matmul(pt[0:64, :], lhsT=gsel[0:96, h * 3 + j, :], rhs=gsig[0][0:96, cs_(qc)], start=True, stop=True),
                          r=[gl, gsig[1][qc]], w=[dp])
                    return pt, dp

                for g in range(4):
                    load_w3("pool", wA[0][0], wA[0][1], w_in_nsa, 1536 + g * 64, 64, 0)
                    load_w3("pool", wA[0][0], wA[0][1], w_in_nsa, 2048 + g * 64, 64, 64)
                    load_w3("pool", wA[1][0], wA[1][1], w_in_nsa, 1024 + g * 64, 64, 0)
                    load_w3("pool", wA[1][0], wA[1][1], w_in_nsa, 1280 + g * 64, 64, 64)
                    load_w3("pool", wA[2][0], wA[2][1], w_in_nsa, 1792 + g * 64, 64, 0)
                    load_w3("pool", wA[2][0], wA[2][1], w_in_nsa, 2304 + g * 64, 64, 64)
                    for sc in range(4):
                        pt, dp = proj_fm(wA[0][0], wA[0][1], sc, 128, 0)
                        headnorm(pt, dp, 128, 5, [(ks[0][0:64, cs_(sc)], ks[1][sc], 0), (kw[0][0:64, cs_(sc)], kw[1][sc], 64)])
                        pt, dp = proj_fm(wA[1][0], wA[1][1], sc, 128, 0)
                        fw.op("act", lambda e: e.activation(out=tc_[0][0:64, cs_(sc)], in_=pt[0:64, :], func=AF.Copy), r=[dp], w=[tc_[1][sc]])
                        fw.op("act", lambda e: e.activation(out=tv_[0][0:64, cs_(sc)], in_=pt[64:128, :], func=AF.Copy), r=[dp], w=[tv_[1][sc]])
                        pv, dpv = psA.next()
                        for j in range(4):
                            tt = sc * 4 + j
                            for kc in range(8):
                                fw.op("pe", lambda e: e.matmul(pv[:, j * 128:(j + 1) * 128], lhsT=xnT[:, kc, tt * 128:(tt + 1) * 128],
                                                               rhs=wA[2][0][:, kc, :], start=(kc == 0), stop=(kc == 7)),
                                      r=[wA[2][1], dxn[sc]], w=[dpv])
                        for hh, vdst in enumerate((vs, vw)):
                            src = pv[:, hh * 64:hh * 64 + 1]
                            src3 = bass.AP(tensor=src.tensor, offset=src.offset, ap=[list(src.ap[0]), [128, 4], [1, 64]])
                            fw.op("act", lambda e: e.activation(out=vdst[0][:, sc * 4:sc * 4 + 4, 0:64], in_=src3, func=AF.Copy),
                                  r=[dpv], w=[vdst[1][sc]])
                    for i, tsrc in enumerate((tc_, tv_)):
                        fw.dma("pool", w1s[0][0:64, :, :], w1src[i], w=[w1s[1]])
                        ph, dph = psA.next()
                        for l in range(32):
                            a = tsrc[0][0:96, l:l + 1]
                            rhs = bass.AP(tensor=a.tensor, offset=a.offset, ap=[list(a.ap[0]), [16, 127]])
                            fw.op("pe", lambda e: e.matmul(ph[:, 0:127], lhsT=w1[i][0][0:96, l, :], rhs=rhs, start=(l == 0), stop=(l == 31)),
                                  r=[w1[i][1]] + tsrc[1], w=[dph])
                        xg, dxg = f32b.next()
                        tg_, dtg = f32b.next()
                        fw.op("act", lambda e: e.activation(out=xg[:, 0:127], in_=ph[:, 0:127], func=AF.Identity, bias=cb1[:, i:i + 1], scale=1.0),
                              r=[dph, dcb1], w=[dxg])
                        fw.op("dve", lambda e: e.tensor_tensor(out=tg_[:, 0:127], in0=xg[:, 0:127], in1=xg[:, 0:127], op=ALU.mult), r=[dxg], w=[dtg])
                        fw.op("dve", lambda e: e.tensor_scalar(out=tg_[:, 0:127], in0=tg_[:, 0:127], scalar1=0.044715, scalar2=1.0, op0=ALU.mult, op1=ALU.add),
                              r=[dtg], w=[dtg])
                        fw.op("dve", lambda e: e.tensor_tensor(out=tg_[:, 0:127], in0=tg_[:, 0:127], in1=xg[:, 0:127], op=ALU.mult), r=[dxg, dtg], w=[dtg])
                        fw.op("act", lambda e: e.activation(out=tg_[:, 0:127], in_=tg_[:, 0:127], func=AF.Sigmoid, scale=1.5957691216), r=[dtg], w=[dtg])
                        gb, dgb = sqb.next()
                        fw.op("dve", lambda e: e.tensor_tensor(out=gb[:, 0:127], in0=tg_[:, 0:127], in1=xg[:, 0:127], op=ALU.mult), r=[dxg, dtg], w=[dgb])
                        if i == 0:
                            pk, dpk = psA.next()
                            fw.op("pe", lambda e: e.matmul(pk[0:64, 0:127], lhsT=w2[0][0][:, :], rhs=gb[:, 0:127], start=True, stop=True),
                                  r=[w2[0][1], dgb], w=[dpk])
                            kf, dkf = f32b.next()
                            fw.op("act", lambda e: e.activation(out=kf[0:64, 0:127], in_=pk[0:64, 0:127], func=AF.Identity, bias=b2k[:, 0:1], scale=1.0),
                                  r=[dpk, gl], w=[dkf])
                            sq, dsq = sqb.next()
                            fw.op("act", lambda e: e.activation(out=sq[0:64, 0:127], in_=kf[0:64, 0:127], func=AF.Square), r=[dkf], w=[dsq])
                            ps2, dp2 = psB.next()
                            fw.op("pe", lambda e: e.matmul(ps2[0:64, 0:127], lhsT=blk_ones[0:128, 0:64], rhs=sq[0:128, 0:127], start=True, stop=True),
                                  r=[dsq, dcon], w=[dp2])
                            rt, drt = f32b.next()
                            fw.op("act", lambda e: e.activation(out=rt[0:64, 0:127], in_=ps2[0:64, 0:127], func=AF.Ln, bias=EPS, scale=1.0 / 64), r=[dp2], w=[drt])
                            fw.op("act", lambda e: e.activation(out=rt[0:64, 0:127], in_=rt[0:64, 0:127], func=AF.Exp, scale=-0.5), r=[drt], w=[drt])
                            fw.op("dve", lambda e: e.scalar_tensor_tensor(out=kcT[0][0:64, 0:127], in0=kf[0:64, 0:127], scalar=hg[0:64, 6:7], in1=rt[0:64, 0:127],
                                                                          op0=ALU.mult, op1=ALU.mult), r=[dkf, drt, dcon], w=[kcT[1]])
                        else:
                            pk, dpk = psA.next()
                            fw.op("pe", lambda e: e.matmul(pk[0:127, 0:64], lhsT=gb[:, 0:127], rhs=w2[1][0][:, :], start=True, stop=True),
                                  r=[w2[1][1], dgb], w=[dpk])
                            fw.op("dve", lambda e: e.tensor_tensor(out=vcA[0][0:127, 0:64], in0=pk[0:127, 0:64], in1=b2v[0:127, :], op=ALU.add),
                                  r=[dpk, gl], w=[vcA[1]])
                    if g == 0:
                        dump("kcT", kcT[0][:], [64, 128], [kcT[1]])
                        dump("vcA", vcA[0][:], [128, 128], [vcA[1]])
                        dump("ksT", ks[0][0:64, :], [64, S], ks[1])
                    for pr in range(2):
                        t, d_ = wQ[pr]
                        load_w3("pool", t, d_, w_in_nsa, (g * 4 + pr * 2) * 64, 128, 0)
                    for pr in range(2):
                        t, d_ = wQ[pr]
                        for sc in range(4):
                            pt, dp = proj_fm(t, d_, sc, 128, 0)
                            a, b_ = qa[pr * 2], qa[pr * 2 + 1]
                            headnorm(pt, dp, 128, 4, [(a[0][0:64, cs_(sc)], a[1][sc], 0), (b_[0][0:64, cs_(sc)], b_[1][sc], 64)])
                    for qd in qa:
                        fw.op("dve", lambda e: e.memset(qd[0][64:96, 0:1024], 0.0), w=[qd[1][0], qd[1][1]])
                    for qt in range(8, 16):
                        fw.op("dve", lambda e: e.memset(imp[0][:, qt, :], 0.0), w=[imp[1][qt]])
                    def build_cmp_E(hl):
                        Ec = EE[hl % 2]
                        build_E(g * 4 + hl, Ec[0], Ec[1], None, None, stage, dstage, pstride=16, off=31, rows=127)

                    build_cmp_E(0)
                    build_cmp_E(1)
                    items = []
                    for hl in range(4):
                        h = g * 4 + hl
                        pair, hh = h // 2, h % 2
                        qt_, dq_ = qa[hl]
                        Ec = EE[hl % 2]
                        for qc in range(4):
                            def post(pb, dpb, qc=qc):
                                if qc < 2:
                                    return
                                pi, dpi = psB.next()
                                for j in range(4):
                                    fw.op("pe", lambda e: e.matmul(pi[:, j * 64:j * 64 + 33], lhsT=pb[0:127, j * 128:(j + 1) * 128], rhs=ovl[0:127, :],
                                                                   start=True, stop=True), r=[dpb, gl], w=[dpi])
                                rdi, drdi = sel_t["rdi"]
                                src = pi[:, 32:33]
                                src3 = bass.AP(tensor=src.tensor, offset=src.offset, ap=[list(src.ap[0]), [64, 4]])
                                fw.op("dve", lambda e: e.reciprocal(out=rdi[:, 0:4], in_=src3), r=[dpi], w=[drdi])
                                for j in range(4):
                                    qtile = qc * 4 + j
                                    fw.op("dve", lambda e: e.scalar_tensor_tensor(out=imp[0][:, qtile, :], in0=pi[:, j * 64:j * 64 + 32], scalar=rdi[:, j:j + 1],
                                                                                  in1=imp[0][:, qtile, :], op0=ALU.mult, op1=ALU.add),
                                          r=[dpi, drdi, imp[1][qtile]], w=[imp[1][qtile]])

                            def cb_cmp(qc, po, dpo, h=h, hl=hl):
                                rd, drd = f32b.next()
                                recip_den(rd, drd, po, dpo)
                                fw.op("dve", lambda e: e.tensor_tensor(out=rd[0:64, :], in0=po[0:64, :], in1=rd[64:128, :], op=ALU.mult), r=[dpo, drd], w=[drd])
                                pgb, dpgb = gate_bc(h, 0, qc)
                                fw.op("dve", lambda e: e.tensor_tensor(out=ocmp[hl][0][0:64, cs_(qc)], in0=rd[0:64, :], in1=pgb[0:64, :], op=ALU.mult),
                                      r=[drd, dpgb], w=[ocmp[hl][1][qc]])
                            items.append(dict(
                                rows=127, n=512, lo=0, hi=512, K=128,
                                lhsT=kcT[0][0:128, 0:127], rhs=qt_[0:128, cs_(qc)], sdeps=[kcT[1], dq_[qc]],
                                E=Ec[0][0:127, cs_(qc)], dE=Ec[1], v=vcA[0][0:127, :], dv=vcA[1],
                                first=True, last=True, cb=cb_cmp, qc=qc, post=post, after=None))
                        if hl + 2 < 4:
                            items[-1]["after"] = (lambda hl=hl: build_cmp_E(hl + 2))
                    run_items(items)
                    vals, dvals = sel_t["vals"]
                    lt, dlt = sel_t["lt"]
                    v2, dv2 = sel_t["v2"]
                    m8a, dm8a = sel_t["m8a"]
                    m8b, dm8b = sel_t["m8b"]
                    impd = [imp[1][q_] for q_ in range(8, 16)]
                    fw.op("dve", lambda e: e.tensor_tensor(out=vals[:, :, :], in0=imp[0][:, 8:16, :], in1=addm[:, 8:16, :], op=ALU.add), r=impd + [gl], w=[dvals])
                    for j in range(8):
                        fw.op("dve", lambda e: e.max(out=m8a[:, j, :], in_=vals[:, j, :]), r=[dvals], w=[dm8a])
                    for j in range(8):
                        fw.op("dve", lambda e: e.tensor_scalar(out=lt[:, j, :], in0=vals[:, j, :], scalar1=m8a[:, j, 7:8], scalar2=None, op0=ALU.is_lt), r=[dvals, dm8a], w=[dlt])
                    fw.op("dve", lambda e: e.tensor_tensor(out=v2[:, :, :], in0=vals[:, :, :], in1=lt[:, :, :], op=ALU.mult), r=[dvals, dlt], w=[dv2])
                    fw.op("dve", lambda e: e.tensor_scalar(out=lt[:, :, :], in0=lt[:, :, :], scalar1=-1.0, scalar2=BIG, op0=ALU.add, op1=ALU.mult), r=[dlt, dv2], w=[dlt])
                    fw.op("dve", lambda e: e.tensor_tensor(out=v2[:, :, :], in0=v2[:, :, :], in1=lt[:, :, :], op=ALU.add), r=[dlt, dv2], w=[dv2])
                    for j in range(8):
                        fw.op("dve", lambda e: e.max(out=m8b[:, j, :], in_=v2[:, j, :]), r=[dv2], w=[dm8b])
                    for j in range(8):
                        fw.op("dve", lambda e: e.tensor_scalar(out=lt[:, j, :], in0=vals[:, j, :], scalar1=m8b[:, j, 4:5], scalar2=None, op0=ALU.is_ge), r=[dvals, dm8b, dlt], w=[dlt])
                    fw.op("dve", lambda e: e.tensor_tensor(out=lt[:, :, :], in0=lt[:, :, :], in1=forced[:, 8:16, :], op=ALU.max), r=[dlt, gl], w=[dlt])
                    fw.op("dve", lambda e: e.tensor_scalar(out=nmp[0][:, :, 64:96], in0=lt[:, :, :], scalar1=-1.0, scalar2=-NEG, op0=ALU.add, op1=ALU.mult),
                          r=[dlt], w=[nmp[1]])
                    for half in range(2):
                        p2, dp2 = psB.next()
                        for jj in range(4):
                            j = half * 4 + jj
                            fw.op("pe", lambda e: e.matmul(p2[0:96, jj * 128:(jj + 1) * 128], lhsT=nmp[0][:, j, 0:96], rhs=ident_bf[:], start=True, stop=True),
                                  r=[nmp[1], dcon], w=[dp2])
                        c0 = (8 + half * 4) * 128
                        for qd in qa:
                            fw.op("act", lambda e: e.activation(out=qd[0][64:96, c0:c0 + 512], in_=p2[64:96, 0:512], func=AF.Copy),
                                  r=[dp2], w=[qd[1][2 + half]])
                    if g == 0:
                        dump("imp", imp[0][:], [128, 16, 32], imp[1])
                        dump("qa0", qa[0][0][:], [96, S], qa[0][1])
                    def build_sw_E(hl):
                        Es, Ew = EE[hl % 2], EW[hl % 2]
                        build_E(g * 4 + hl, Es[0], Es[1], None, None, stage, dstage)
                        fw.op("dve", lambda e: e.tensor_tensor(out=Ew[0][:, :], in0=Es[0][:, 0:640], in1=Mw[0][:, :], op=ALU.add),
                              r=[Es[1], Mw[1]], w=[Ew[1]])

                    build_sw_E(0)
                    build_sw_E(1)
                    items = []
                    for hl in range(4):
                        h = g * 4 + hl
                        pair, hh = h // 2, h % 2
                        Es, Ew = EE[hl % 2], EW[hl % 2]
                        acc = {}

                        def cb_slc(qc, po, dpo, h=h, acc=acc):
                            rd, drd = f32b.next()
                            recip_den(rd, drd, po, dpo)
                            fw.op("dve", lambda e: e.tensor_tensor(out=rd[0:64, :], in0=po[0:64, :], in1=rd[64:128, :], op=ALU.mult), r=[dpo, drd], w=[drd])
                            pgb, dpgb = gate_bc(h, 1, qc)
                            tb, dtb = tsb.next()
                            fw.op("dve", lambda e: e.tensor_tensor(out=tb[:, :], in0=rd[0:64, :], in1=pgb[0:64, :], op=ALU.mult), r=[drd, dpgb], w=[dtb])
                            acc[qc] = (tb, dtb)

                        def cb_win(qc, po, dpo, h=h, pair=pair, hh=hh, hl=hl, acc=acc):
                            rd, drd = f32b.next()
                            a_, da_ = acc[qc]
                            recip_den(rd, drd, po, dpo)
                            fw.op("dve", lambda e: e.tensor_tensor(out=rd[0:64, :], in0=po[0:64, :], in1=rd[64:128, :], op=ALU.mult), r=[dpo, drd], w=[drd])
                            pgb, dpgb = gate_bc(h, 2, qc)
                            tb, dtb = tsb.next()
                            fw.op("dve", lambda e: e.tensor_tensor(out=tb[:, :], in0=rd[0:64, :], in1=pgb[0:64, :], op=ALU.mult), r=[drd, dpgb], w=[dtb])
                            fw.op("dve", lambda e: e.tensor_tensor(out=tb[:, :], in0=tb[:, :], in1=a_[:, :], op=ALU.add), r=[dtb, da_], w=[dtb])
                            fw.op("dve", lambda e: e.tensor_tensor(out=oT[hh * 64:hh * 64 + 64, pair, cs_(qc)], in0=tb[:, :], in1=ocmp[hl][0][0:64, cs_(qc)], op=ALU.add),
                                  r=[dtb, ocmp[hl][1][qc]], w=[doT[pair][qc]])

                        for qc in range(4):
                            items += make_items(qa[hl][0], qa[hl][1], 128, ks[0], ks[1], vs[0], vs[1], Es[0], Es[1], causal_tiles, cb_slc, chunks=[qc])
                            items += make_items(qa[hl][0], qa[hl][1], 128, kw[0], kw[1], vw[0], vw[1], Ew[0], Ew[1], win_tiles, cb_win, chunks=[qc])
                        if hl + 2 < 4:
                            items[-1]["after"] = (lambda hl=hl: build_sw_E(hl + 2))
                    run_items(items)
                pass
                fw.barrier()

        alldh = [d_ for row in dh for d_ in row]
        alldo = [d_ for row in doT for d_ in row]
        with contextlib.ExitStack() as st:
            hT = sb("hT", [128, 8, S], F32, st)
            load_h(xT, [])
            rmsnorm(0)
            dump("xn0", xnT[:], [128, 8, S], dxn)
            fw.barrier()
        if upto >= 1:
            layer0_mixer()
            dump("oT0", oT[:], [128, 8, S], alldo)
        with contextlib.ExitStack() as st:
            hT = sb("hT", [128, 8, S], F32, st)
            load_h(xT, [])
            if upto >= 1:
                out_proj(w_out_ab, st)
                dump("hmix0", hT[:], [128, 8, S], alldh)
                fw.barrier()
            if upto >= 2:
                ffn(0)
                dump("hffn0", hT[:], [128, 8, S], alldh)
            if upto >= 3:
                def after_ple0(sc):
                    if upto >= 4:
                        rmsnorm(3, [sc])
                        store_h(hS, [dhS], [sc])
                ple(0, after_sc=after_ple0)
                dump("h0", hT[:], [128, 8, S], alldh)
            fw.barrier()
            if upto < 4:
                store_h(outT, [])
        if upto >= 4:
            layer1_mixer()
            dump("oT1", oT[:], [128, 8, S], alldo)
            with contextlib.ExitStack() as st:
                hT = sb("hT", [128, 8, S], F32, st)
                load_h(hS, [dhS])
                out_proj(w_out_nsa, st)
                dump("hmix1", hT[:], [128, 8, S], alldh)
                fw.barrier()
                if upto >= 5:
                    ffn(1)
                if upto >= 6:
                    ple(1, after_sc=(lambda sc: store_h(outT, [], [sc])))
                else:
                    store_h(outT, [])
                fw.barrier()
        fw.finish("sp")
        build_nc.stats = (fw.n_ins, fw.n_wait)
    return nc, consts, DBG


def host_inputs(inputs, b, consts):
    f = lambda a: np.ascontiguousarray(np.asarray(a, dtype=np.float32))
    m = {}
    m["xT"] = f(inputs["x"][b].T)
    m["pT"] = f(np.transpose(inputs["p"][:, b], (0, 2, 1)))
    rb = np.asarray(inputs["rel_bias"], np.float32)
    dd = np.maximum(2047 - np.arange(WHL), 0)
    whb = rb[_bucket(dd), :].T.copy()
    whb[:, 2048:] = NEG
    m["whb"] = f(whb)
    gl = []
    for layer in range(2):
        for nm in ("norm_mix", "norm_ffn", "norm_ple"):
            gl.append(np.asarray(inputs[nm][layer], np.float32).reshape(8, 128).T)
    m["gains"] = f(np.concatenate(gl, axis=1))
    t2 = lambda a: np.concatenate([np.asarray(a, np.float32)] * 2)
    hg = np.zeros((128, 8), np.float32)
    hg[:, 0] = t2(inputs["qn_moba"][0])
    hg[:, 1] = t2(inputs["kn_moba"][0])
    hg[:, 2] = t2(inputs["qn_dil"][0])
    hg[:, 3] = t2(inputs["kn_dil"][0])
    hg[:, 4] = t2(inputs["qn_nsa"][0])
    hg[:, 5] = np.concatenate([np.asarray(inputs["kn_slc"][0], np.float32), np.asarray(inputs["kn_win"][0], np.float32)])
    hg[:, 6] = t2(inputs["kn_cmp"][0])
    m["hg"] = hg
    m["posT"] = f(np.concatenate([np.asarray(inputs["cmp_k_pos"][0]).T, np.asarray(inputs["cmp_v_pos"][0]).T], axis=1))
    m["b1c"] = f(np.stack([inputs["cmp_k_b1"][0], inputs["cmp_v_b1"][0]], axis=1))
    m["b2k"] = f(np.asarray(inputs["cmp_k_b2"][0]).reshape(64, 1))
    m["b2v"] = f(np.asarray(inputs["cmp_v_b2"][0]).reshape(1, 64))
    m["w_in_ab"] = f(inputs["w_in_ab"][0])
    m["w_out_ab"] = f(inputs["w_out_ab"][0])
    m["w_in_nsa"] = f(inputs["w_in_nsa"][0])
    m["w_out_nsa"] = f(inputs["w_out_nsa"][0])
    for nm in ("w_ffn_gate", "w_ffn_up", "w_ffn_down", "w_ple_proj", "w_ple_gate"):
        m[nm] = f(inputs[nm])
    m["cmp_k_w1"] = f(inputs["cmp_k_w1"][0])
    m["cmp_k_w2"] = f(inputs["cmp_k_w2"][0])
    m["cmp_v_w1"] = f(inputs["cmp_v_w1"][0])
    m["cmp_v_w2"] = f(inputs["cmp_v_w2"][0])
    for k, v in consts.items():
        m[k] = v
    return m


def kernel(**inputs):
    nc, consts, _ = build_nc()
    in_maps = [host_inputs(inputs, b, consts) for b in range(8)]
    res = run_bass_kernel_spmd(nc, in_maps, core_ids=list(range(8)))
    out = np.stack([np.asarray(r["outT"], np.float32).T for r in res.results], axis=0)
    return np.ascontiguousarray(out.astype(np.float32))
```

```python
import math
import contextlib
import numpy as np
import concourse.bass as bass
import concourse.mybir as mybir
from concourse.bass_utils import run_bass_kernel_spmd

F32 = mybir.dt.float32
BF16 = mybir.dt.bfloat16
AF = mybir.ActivationFunctionType
ALU = mybir.AluOpType
AX = mybir.AxisListType

S = 2048
D = 1024
FH = 2816
NF = 22
WHL = 4352
EPS = 1e-6
NEG = -30000.0
BIG = 3.0e38


class Dep:
    __slots__ = ("w", "r")

    def __init__(self):
        self.w = None
        self.r = []


class FW:
    NDMA = 24

    def __init__(self, nc, es):
        self.nc = nc
        self.engs = {"pe": nc.tensor, "act": nc.scalar, "dve": nc.vector, "pool": nc.gpsimd, "sp": nc.sync}
        self.sems = {}
        self.cnt = {}
        for k in self.engs:
            self.sems[k] = es.enter_context(nc.semaphore("sem_" + k))
            self.cnt[k] = 0
        for i in range(self.NDMA):
            k = ("dma", i)
            self.sems[k] = es.enter_context(nc.semaphore("sem_dma%d" % i))
            self.cnt[k] = 0
        self.seen = {e: {} for e in self.engs}
        self.dma_rr = {"sp": 0, "pool": 0, "act": 0}
        self.n_ins = 0
        self.n_wait = 0

    def _wait(self, eng, deps):
        seen = self.seen[eng]
        need = {}
        for d in deps:
            if d is None:
                continue
            k, v = d
            if k == "pe" and eng == "pe":
                continue
            if seen.get(k, 0) >= v:
                continue
            if need.get(k, 0) < v:
                need[k] = v
        for k, v in need.items():
            self.engs[eng].wait_ge(self.sems[k], v)
            seen[k] = v
            self.n_wait += 1

    @staticmethod
    def _collect(r, w):
        deps = []
        for t in r:
            deps.append(t.w)
        for t in w:
            deps.append(t.w)
            deps.extend(t.r)
        return deps

    def _mark(self, tok, r, w):
        for t in w:
            t.w = tok
            t.r = []
        for t in r:
            t.r.append(tok)
            if len(t.r) > 64:
                best = {}
                for k, v in t.r:
                    if best.get(k, 0) < v:
                        best[k] = v
                t.r = list(best.items())

    def op(self, eng, fn, r=(), w=()):
        self._wait(eng, self._collect(r, w))
        ins = fn(self.engs[eng])
        self.cnt[eng] += 1
        ins.then_inc(self.sems[eng], 1)
        self._mark((eng, self.cnt[eng]), r, w)
        self.n_ins += 1

    def dma(self, q, out, in_, r=(), w=()):
        half = self.NDMA // 2
        i = self.dma_rr[q]
        self.dma_rr[q] = (i + 1) % half
        k = ("dma", i + (half if q == "pool" else 0))
        deps = self._collect(r, w)
        if self.cnt[k] > 0:
            deps.append((k, self.cnt[k]))
        self._wait(q, deps)
        ins = self.engs[q].dma_start(out=out, in_=in_)
        self.cnt[k] += 16
        ins.then_inc(self.sems[k], 16)
        self._mark((k, self.cnt[k]), r, w)
        self.n_ins += 1

    def barrier(self):
        allk = [(k, v) for k, v in self.cnt.items() if v > 0]
        for e in self.engs:
            self._wait(e, allk)

    def finish(self, eng="sp"):
        allk = [(k, v) for k, v in self.cnt.items() if v > 0]
        self._wait(eng, allk)


class Rot:
    def __init__(self, items):
        self.items = items
        self.i = 0

    def next(self):
        t = self.items[self.i]
        self.i = (self.i + 1) % len(self.items)
        return t


def _bucket(d):
    n = np.maximum(d, 0)
    nf = np.maximum(n, 1).astype(np.float32)
    large = 16 + (np.log(nf / np.float32(16)) / np.float32(math.log(128.0)) * np.float32(16)).astype(np.int32)
    return np.where(n < 16, n, np.minimum(large, 31))


def _static_consts():
    c = {}
    m = np.arange(WHL)
    d = 2047 - m
    wm = np.zeros((3, WHL), np.float32)
    mult = (d >= 0) * ((d <= 128).astype(np.float32) + ((d % 4 == 0) & (d <= 512)) + ((d % 16 == 0) & (d <= 2048)))
    wm[1] = np.where(mult > 0, 8.0 * np.log(np.maximum(mult, 1.0)), 8.0 * NEG)
    wm[2] = np.where((d >= 0) & (d < 512), 0.0, 8.0 * NEG)
    c["c_wm"] = wm
    c["c_ident"] = np.eye(128, dtype=np.float32)
    k = np.arange(S)
    c["c_blk_moba"] = (k[None, :] // 256 == np.arange(8)[:, None]).astype(np.float32)
    c["c_blk_nsa"] = (k[None, :] // 64 == np.arange(32)[:, None]).astype(np.float32)
    g = np.zeros((128, 4, 8), np.float32)
    for i, own in enumerate(range(4, 8)):
        g[:, i, own:] = -BIG
    c["c_gneg"] = g
    g16 = np.zeros((128, 16, 8), np.float32)
    for i in range(16):
        own = (8 + i % 8) // 2
        g16[:, i, own:] = -BIG
    c["c_gneg16"] = g16
    add = np.full((128, 16, 32), -BIG, np.float32)
    forced = np.zeros((128, 16, 32), np.float32)
    for qt in range(16):
        for q in range(128):
            cur = (qt * 128 + q) // 64
            for n in (0, cur, cur - 1):
                if n >= 0:
                    forced[q, qt, n] = 1.0
            for n in range(1, cur - 1):
                add[q, qt, n] = 0.0
    c["c_addmask"] = add
    c["c_forced"] = forced
    cs = np.arange(127) * 16
    ss = np.arange(32) * 64
    ov = np.maximum(np.minimum(cs[:, None] + 32, ss[None, :] + 64) - np.maximum(cs[:, None], ss[None, :]), 0)
    ovl = np.ones((127, 33), np.float32)
    ovl[:, :32] = ov
    c["c_ovl"] = ovl
    gs = np.zeros((48, 48, 64), np.float32)
    for i in range(48):
        gs[i, i, :] = 1.0
    c["c_gsel"] = gs
    return c


_CONST_SHAPES = None


def _dap(t, offset, ap):
    return bass.AP(tensor=t.tensor, offset=offset, ap=[list(a) for a in ap])


def build_nc(upto=99, dbg=()):
    nc = bass.Bass("TRN2", target_bir_lowering=False)
    consts = _static_consts()
    IN = {}

    def din(name, shape):
        IN[name] = nc.dram_tensor(name, list(shape), F32, kind="ExternalInput").ap()
        return IN[name]

    xT = din("xT", [D, S])
    pT = din("pT", [2, 256, S])
    whb = din("whb", [16, WHL])
    gains_d = din("gains", [128, 48])
    hg_d = din("hg", [128, 8])
    posT_d = din("posT", [64, 64])
    b1_d = din("b1c", [128, 2])
    b2k_d = din("b2k", [64, 1])
    b2v_d = din("b2v", [1, 64])
    w_in_ab = din("w_in_ab", [D, 3072])
    w_out_ab = din("w_out_ab", [D, D])
    w_in_nsa = din("w_in_nsa", [D, 2608])
    w_out_nsa = din("w_out_nsa", [D, D])
    w_g = din("w_ffn_gate", [2, D, FH])
    w_u = din("w_ffn_up", [2, D, FH])
    w_d = din("w_ffn_down", [2, FH, D])
    w_pp = din("w_ple_proj", [2, 256, D])
    w_pg = din("w_ple_gate", [2, D, D])
    ck_w1 = din("cmp_k_w1", [2048, 128])
    ck_w2 = din("cmp_k_w2", [128, 64])
    cv_w1 = din("cmp_v_w1", [2048, 128])
    cv_w2 = din("cmp_v_w2", [128, 64])
    for k, v in consts.items():
        din(k, v.shape)
    outT = nc.dram_tensor("outT", [D, S], F32, kind="ExternalOutput").ap()
    DBG = {}

    with contextlib.ExitStack() as es:
        fw = FW(nc, es)

        uniq = [0]

        def sb(name, shape, dt=F32, stack=es):
            uniq[0] += 1
            return stack.enter_context(nc.sbuf_tensor("%s_%d" % (name, uniq[0]), list(shape), dt))

        def pst(name):
            return es.enter_context(nc.psum_tensor(name, [128, 512], F32))

        psS = Rot([(pst("psS%d" % i), Dep()) for i in range(3)])
        psO = Rot([(pst("psO%d" % i), Dep()) for i in range(2)])
        psA = Rot([(pst("psM%d" % i), Dep()) for i in range(3)])
        psB = psA

        def dump(name, ap, shape, deps):
            if name not in dbg:
                return
            t = nc.dram_tensor("dbg_" + name, list(shape), ap.dtype if hasattr(ap, "dtype") else F32, kind="ExternalOutput").ap()
            DBG[name] = t
            fw.dma("sp", t, ap, r=deps)

        hS = nc.dram_tensor("hS", [D, S], F32, kind="Internal").ap()
        dhS = Dep()
        dh = [[Dep() for _ in range(4)] for _ in range(8)]
        xnT = sb("xnT", [128, 8, S], BF16)
        dxn = [Dep() for _ in range(4)]
        oT = sb("oT", [128, 8, S], BF16)
        doT = [[Dep() for _ in range(4)] for _ in range(8)]
        hT = None
        gains = sb("gains_sb", [128, 48])
        hg = sb("hg_sb", [128, 8])
        dcon = Dep()
        ones_bf = sb("ones_bf", [128, 128], BF16)
        blk_ones = sb("blk_ones", [128, 128], BF16)
        ident_bf = sb("ident_bf", [128, 128], BF16)
        sqb = Rot([(sb("sqb%d" % i, [128, 512], BF16), Dep()) for i in range(2)])
        f32b = Rot([(sb("f32b%d" % i, [128, 512]), Dep()) for i in range(6)])

        fw.dma("sp", gains[:], gains_d[:, :], w=[dcon])
        fw.dma("sp", hg[:], hg_d[:, :], w=[dcon])
        fw.dma("pool", ident_bf[:], IN["c_ident"][:, :], w=[dcon])
        fw.op("dve", lambda e: e.memset(ones_bf[:], 1.0), w=[dcon])
        fw.op("dve", lambda e: e.memset(blk_ones[:], 0.0), w=[dcon])
        fw.op("dve", lambda e: e.memset(blk_ones[0:64, 0:64], 1.0), w=[dcon])
        fw.op("dve", lambda e: e.memset(blk_ones[64:128, 64:128], 1.0), w=[dcon])
        def load_h(src, dsrc):
            v = src.rearrange("(c p) s -> p c s", p=128)
            for c in range(8):
                for sc in range(4):
                    fw.dma("sp", hT[:, c, sc * 512:(sc + 1) * 512], v[:, c, sc * 512:(sc + 1) * 512], r=dsrc, w=[dh[c][sc]])

        def store_h(dst, ddst, scs=range(4)):
            v = dst.rearrange("(c p) s -> p c s", p=128)
            for c in range(8):
                for sc in scs:
                    fw.dma("sp", v[:, c, sc * 512:(sc + 1) * 512], hT[:, c, sc * 512:(sc + 1) * 512], r=[dh[c][sc]], w=ddst)

        def cs_(sc):
            return slice(sc * 512, (sc + 1) * 512)

        def rmsnorm(gidx, scs=range(4)):
            for sc in scs:
                cs = cs_(sc)
                pt, dp = psB.next()
                for c in range(8):
                    sq, dsq = sqb.next()
                    fw.op("act", lambda e: e.activation(out=sq[:], in_=hT[:, c, cs], func=AF.Square), r=[dh[c][sc]], w=[dsq])
                    fw.op("pe", lambda e: e.matmul(pt[:], lhsT=ones_bf[:], rhs=sq[:], start=(c == 0), stop=(c == 7)),
                          r=[dsq, dcon], w=[dp])
                rt, drt = f32b.next()
                fw.op("act", lambda e: e.activation(out=rt[:], in_=pt[:], func=AF.Ln, bias=EPS, scale=1.0 / D), r=[dp], w=[drt])
                fw.op("act", lambda e: e.activation(out=rt[:], in_=rt[:], func=AF.Exp, scale=-0.5), r=[drt], w=[drt])
                for c in range(8):
                    fw.op("dve", lambda e: e.scalar_tensor_tensor(
                        out=xnT[:, c, cs], in0=hT[:, c, cs], scalar=gains[:, gidx * 8 + c:gidx * 8 + c + 1], in1=rt[:],
                        op0=ALU.mult, op1=ALU.mult), r=[dh[c][sc], drt, dcon], w=[dxn[sc]])

        def proj_fm(wt, dw, sc, M, c0=0):
            pt, dp = psA.next()
            for kc in range(8):
                fw.op("pe", lambda e: e.matmul(pt[0:M, :], lhsT=wt[:, kc, c0:c0 + M], rhs=xnT[:, kc, cs_(sc)],
                                               start=(kc == 0), stop=(kc == 7)), r=[dw, dxn[sc]], w=[dp])
            return pt, dp

        def headnorm(pt, dp, M, gcol, outs):
            sq, dsq = sqb.next()
            fw.op("act", lambda e: e.activation(out=sq[0:M, :], in_=pt[0:M, :], func=AF.Square), r=[dp], w=[dsq])
            ps2, dp2 = psB.next()
            fw.op("pe", lambda e: e.matmul(ps2[0:M, :], lhsT=blk_ones[0:M, 0:M], rhs=sq[0:M, :], start=True, stop=True),
                  r=[dsq, dcon], w=[dp2])
            rt, drt = f32b.next()
            fw.op("act", lambda e: e.activation(out=rt[0:M, :], in_=ps2[0:M, :], func=AF.Ln, bias=EPS, scale=1.0 / 64), r=[dp2], w=[drt])
            fw.op("act", lambda e: e.activation(out=rt[0:M, :], in_=rt[0:M, :], func=AF.Exp, scale=-0.5), r=[drt], w=[drt])
            for (dst, ddst, r0) in outs:
                fw.op("dve", lambda e: e.scalar_tensor_tensor(
                    out=dst, in0=pt[r0:r0 + 64, :], scalar=hg[r0:r0 + 64, gcol:gcol + 1], in1=rt[r0:r0 + 64, :],
                    op0=ALU.mult, op1=ALU.mult), r=[dp, drt, dcon], w=[ddst])

        def load_w3(q, wt, dw, src2d, c0, ncols, dst_c0=0):
            v = src2d.rearrange("(kc p) n -> p kc n", p=128)
            fw.dma(q, wt[:, :, dst_c0:dst_c0 + ncols], v[:, :, c0:c0 + ncols], w=[dw])

        def rev(t, n, rows=128):
            a = t[0:rows, n - 1:n]
            return bass.AP(tensor=a.tensor, offset=a.offset, ap=[list(a.ap[0]), [-1, n]])

        pbuf_items = []

        LOOK = 2
        CBDELAY = 3

        def make_items(qt, dq, K, ktile, dk, vt, dv, E, dE, tiles_fn, out_cb, chunks=range(4)):
            items = []
            for qc in chunks:
                c0 = qc * 512
                tiles = tiles_fn(qc)
                assert tiles[0][1] == 0 and tiles[0][2] == 512
                for idx, (kt, lo, hi) in enumerate(tiles):
                    n = hi - lo
                    u0 = c0 + lo - kt * 128
                    items.append(dict(
                        rows=128, n=n, lo=lo, hi=hi, K=K,
                        lhsT=ktile[0:K, kt * 128:(kt + 1) * 128], rhs=qt[0:K, c0 + lo:c0 + hi], sdeps=[dk[kt // 4], dq[qc]],
                        E=E[:, u0:u0 + n], dE=dE, v=vt[:, kt, :], dv=dv[kt // 4],
                        first=(idx == 0), last=(idx == len(tiles) - 1), cb=out_cb, qc=qc, post=None, after=None))
            return items

        def run_items(items):
            staged = {}
            cur = [None]
            pend = []
            n_it = len(items)

            def fire(force_to):
                while pend and (pend[0][0] <= 0 or len(pend) > force_to):
                    pend.pop(0)[1]()

            for j in range(n_it + LOOK):
                if j < n_it:
                    it = items[j]
                    pss, dps = psS.next()
                    fw.op("pe", lambda e: e.matmul(pss[0:it["rows"], 0:it["n"]], lhsT=it["lhsT"], rhs=it["rhs"], start=True, stop=False),
                          r=it["sdeps"], w=[dps])
                    fw.op("pe", lambda e: e.matmul(pss[0:it["rows"], 0:it["n"]], lhsT=ident_bf[0:it["rows"], 0:it["rows"]], rhs=it["E"], start=False, stop=True),
                          r=[it["dE"], dcon], w=[dps])
                    staged[j] = (pss, dps)
                i = j - LOOK
                if i < 0:
                    continue
                it = items[i]
                pss, dps = staged.pop(i)
                R, n = it["rows"], it["n"]
                pb, dpb = pbuf.next()
                fw.op("act", lambda e: e.activation(out=pb[0:R, 0:n], in_=pss[0:R, 0:n], func=AF.Exp, scale=0.125), r=[dps], w=[dpb])
                if it["first"]:
                    fire(1)
                    cur[0] = psO.next()
                po, dpo = cur[0]
                fw.op("pe", lambda e: e.matmul(po[:, it["lo"]:it["hi"]], lhsT=it["v"], rhs=pb[0:R, 0:n], start=it["first"], stop=it["last"]),
                      r=[it["dv"], dpb], w=[dpo])
                for p_ in pend:
                    p_[0] -= 1
                if it["post"] is not None:
                    pend.append([CBDELAY, (lambda it=it, pb=pb, dpb=dpb: it["post"](pb, dpb))])
                if it["last"]:
                    pend.append([CBDELAY, (lambda it=it, po=po, dpo=dpo: it["cb"](it["qc"], po, dpo))])
                fire(99)
                if it.get("after") is not None:
                    it["after"]()
            fire(0)

        def attend(*args, **kw):
            run_items(make_items(*args, **kw))

        def recip_den_dve(rd, drd, po, dpo):
            fw.op("dve", lambda e: e.reciprocal(out=rd[64:128, :], in_=po[64:128, :]), r=[dpo], w=[drd])

        def recip_den(rd, drd, po, dpo):
            fw.op("act", lambda e: e.activation(out=rd[64:128, :], in_=po[64:128, :], func=AF.Ln, bias=1e-30, scale=1.0), r=[dpo], w=[drd])
            fw.op("act", lambda e: e.activation(out=rd[64:128, :], in_=rd[64:128, :], func=AF.Exp, scale=-1.0), r=[drd], w=[drd])

        def causal_tiles(qc):
            res = []
            for kt in range(4 * qc + 4):
                lo = max(0, kt * 128 - qc * 512)
                res.append((kt, lo, 512))
            return res

        def win_tiles(qc):
            res = []
            order = [4 * qc] + [k for k in range(max(0, 4 * qc - 4), 4 * qc + 4) if k != 4 * qc]
            for kt in order:
                off = kt * 128 - qc * 512
                lo = max(0, off)
                hi = min(512, ((off + 638) // 128 + 1) * 128)
                res.append((kt, lo, hi))
            return res

        def build_E(H, E, dE, M, dM, stage, dstage, width=2048, pstride=1, off=0, rows=128):
            fw.dma("sp", stage[0:128, 0:width], _dap(whb, H * WHL + off + (2048 - width), [[pstride, 128], [1, width]]), w=[dstage])
            if M is None:
                fw.op("dve", lambda e: e.tensor_scalar(out=E[0:rows, 0:width], in0=rev(stage, width, rows), scalar1=8.0, scalar2=None, op0=ALU.mult),
                      r=[dstage], w=[dE])
            else:
                fw.op("dve", lambda e: e.scalar_tensor_tensor(out=E[0:rows, 0:width], in0=rev(stage, width, rows), scalar=8.0, in1=M[0:rows, 0:width],
                                                              op0=ALU.mult, op1=ALU.add), r=[dstage, dM], w=[dE])

        def build_M(kind, M, dM, stage, dstage, width=2048, pstride=1, off=0, rows=128):
            fw.dma("sp", stage[0:128, 0:width], _dap(IN["c_wm"], kind * WHL + off + (2048 - width), [[pstride, 128], [1, width]]), w=[dstage])
            fw.op("act", lambda e: e.activation(out=M[0:rows, 0:width], in_=rev(stage, width, rows), func=AF.Copy), r=[dstage], w=[dM])

        def out_proj(w_out, st):
            wo = [(sb("wo%d" % i, [128, 8, 128], BF16, st), Dep()) for i in range(2)]
            wv = w_out.rearrange("(kc p) n -> p kc n", p=128)

            def ld(fc):
                t, d_ = wo[fc % 2]
                fw.dma("pool", t[:], wv[:, :, fc * 128:(fc + 1) * 128], w=[d_])
            ld(0)
            for fc in range(8):
                if fc + 1 < 8:
                    ld(fc + 1)
                t, d_ = wo[fc % 2]
                for sc in range(4):
                    pt, dp = psA.next()
                    for pr in range(8):
                        fw.op("pe", lambda e: e.matmul(pt[:], lhsT=t[:, pr, :], rhs=oT[:, pr, cs_(sc)], start=(pr == 0), stop=(pr == 7)),
                              r=[d_, doT[pr][sc]], w=[dp])
                    fw.op("dve", lambda e: e.tensor_tensor(out=hT[:, fc, cs_(sc)], in0=pt[:], in1=hT[:, fc, cs_(sc)], op=ALU.add),
                          r=[dp, dh[fc][sc]], w=[dh[fc][sc]])

        def ffn(layer):
            rmsnorm(layer * 3 + 1)
            with contextlib.ExitStack() as st:
                act2 = sb("ffn_act", [128, NF - 16, 1024], BF16, st)
                dact = [[Dep() for _ in range(2)] for _ in range(NF)]

                def act_ap(f, q):
                    if f < 16:
                        return oT[:, f // 2, (f % 2) * 1024 + q * 512:(f % 2) * 1024 + (q + 1) * 512]
                    return act2[:, f - 16, q * 512:(q + 1) * 512]
                wg = [(sb("wg%d" % i, [128, 8, 128], BF16, st), Dep()) for i in range(2)]
                wu = [(sb("wu%d" % i, [128, 8, 128], BF16, st), Dep()) for i in range(2)]
                wd = [(sb("wd%d" % i, [128, NF, 128], BF16, st), Dep()) for i in range(2)]
                sg = Rot([(sb("sg%d" % i, [128, 512], F32, st), Dep()) for i in range(2)])
                wgv = w_g[layer].rearrange("(kc p) n -> p kc n", p=128)
                wuv = w_u[layer].rearrange("(kc p) n -> p kc n", p=128)
                wdv = w_d[layer].rearrange("(f p) n -> p f n", p=128)

                def ld1(f):
                    fw.dma("pool", wg[f % 2][0][:], wgv[:, :, f * 128:(f + 1) * 128], w=[wg[f % 2][1]])
                    fw.dma("pool", wu[f % 2][0][:], wuv[:, :, f * 128:(f + 1) * 128], w=[wu[f % 2][1]])

                def ld2(dc):
                    fw.dma("pool", wd[dc % 2][0][:], wdv[:, :, dc * 128:(dc + 1) * 128], w=[wd[dc % 2][1]])

                for half in range(2):
                    ld1(0)
                    for f in range(NF):
                        if f + 1 < NF:
                            ld1(f + 1)
                        else:
                            ld2(0)
                        tg, dg_ = wg[f % 2]
                        tu, du_ = wu[f % 2]
                        for q in range(2):
                            sc = half * 2 + q
                            pg, dpg = psA.next()
                            pu, dpu = psB.next()
                            for kc in range(8):
                                fw.op("pe", lambda e: e.matmul(pg[:], lhsT=tg[:, kc, :], rhs=xnT[:, kc, cs_(sc)], start=(kc == 0), stop=(kc == 7)),
                                      r=[dg_, dxn[sc]], w=[dpg])
                            for kc in range(8):
                                fw.op("pe", lambda e: e.matmul(pu[:], lhsT=tu[:, kc, :], rhs=xnT[:, kc, cs_(sc)], start=(kc == 0), stop=(kc == 7)),
                                      r=[du_, dxn[sc]], w=[dpu])
                            s_, ds_ = sg.next()
                            fw.op("act", lambda e: e.activation(out=s_[:], in_=pg[:], func=AF.Silu), r=[dpg], w=[ds_])
                            fw.op("dve", lambda e: e.tensor_tensor(out=act_ap(f, q), in0=s_[:], in1=pu[:], op=ALU.mult),
                                  r=[ds_, dpu], w=[dact[f][q]])
                    for dc in range(8):
                        if dc + 1 < 8:
                            ld2(dc + 1)
                        td, dd_ = wd[dc % 2]
                        for q in range(2):
                            sc = half * 2 + q
                            pt, dp = psS.next()
                            for f in range(NF):
                                fw.op("pe", lambda e: e.matmul(pt[:], lhsT=td[:, f, :], rhs=act_ap(f, q),
                                                               start=(f == 0), stop=(f == NF - 1)), r=[dd_, dact[f][q]], w=[dp])
                            fw.op("dve", lambda e: e.tensor_tensor(out=hT[:, dc, cs_(sc)], in0=pt[:], in1=hT[:, dc, cs_(sc)], op=ALU.add),
                                  r=[dp, dh[dc][sc]], w=[dh[dc][sc]])
                    rmsnorm(layer * 3 + 2, [half * 2, half * 2 + 1])
                fw.barrier()

        def ple(layer, after_sc=None):
            with contextlib.ExitStack() as st:
                pTs = sb("pTs", [128, 2, S], BF16, st)
                dpT = [Dep() for _ in range(4)]
                wpg = [(sb("wpg%d" % i, [128, 8, 128], BF16, st), Dep()) for i in range(8)]
                wpp = [(sb("wpp%d" % i, [128, 2, 128], BF16, st), Dep()) for i in range(8)]
                sg = Rot([(sb("psg%d" % i, [128, 512], F32, st), Dep()) for i in range(2)])
                pv_ = pT[layer].rearrange("(kc p) s -> p kc s", p=128)
                for sc in range(4):
                    fw.dma("pool", pTs[:, :, cs_(sc)], pv_[:, :, cs_(sc)], w=[dpT[sc]])
                wgv = w_pg[layer].rearrange("(kc p) n -> p kc n", p=128)
                wpv = w_pp[layer].rearrange("(kc p) n -> p kc n", p=128)
                for fc in range(8):
                    fw.dma("pool", wpg[fc][0][:], wgv[:, :, fc * 128:(fc + 1) * 128], w=[wpg[fc][1]])
                    fw.dma("pool", wpp[fc][0][:], wpv[:, :, fc * 128:(fc + 1) * 128], w=[wpp[fc][1]])
                for sc in range(4):
                    for fc in range(8):
                        tg, dg_ = wpg[fc]
                        tp, dp_ = wpp[fc]
                        pg, dpg = psA.next()
                        pp_, dpp = psS.next()
                        for kc in range(8):
                            fw.op("pe", lambda e: e.matmul(pg[:], lhsT=tg[:, kc, :], rhs=xnT[:, kc, cs_(sc)], start=(kc == 0), stop=(kc == 7)),
                                  r=[dg_, dxn[sc]], w=[dpg])
                        for kc in range(2):
                            fw.op("pe", lambda e: e.matmul(pp_[:], lhsT=tp[:, kc, :], rhs=pTs[:, kc, cs_(sc)], start=(kc == 0), stop=(kc == 1)),
                                  r=[dp_, dpT[sc]], w=[dpp])
                        s_, ds_ = sg.next()
                        fw.op("act", lambda e: e.activation(out=s_[:], in_=pg[:], func=AF.Sigmoid), r=[dpg], w=[ds_])
                        fw.op("dve", lambda e: e.tensor_tensor(out=s_[:], in0=s_[:], in1=pp_[:], op=ALU.mult), r=[ds_, dpp], w=[ds_])
                        fw.op("dve", lambda e: e.tensor_tensor(out=hT[:, fc, cs_(sc)], in0=s_[:], in1=hT[:, fc, cs_(sc)], op=ALU.add),
                              r=[ds_, dh[fc][sc]], w=[dh[fc][sc]])
                    if after_sc is not None:
                        after_sc(sc)
                fw.barrier()

        def layer0_mixer():
            with contextlib.ExitStack() as st:
                qh = [(sb("qh%d" % i, [128, S], BF16, st), [Dep() for _ in range(4)]) for i in range(2)]
                kh = [(sb("kh%d" % i, [128, S], BF16, st), [Dep() for _ in range(4)]) for i in range(2)]
                vh = [(sb("vh%d" % i, [128, 16, 128], BF16, st), [Dep() for _ in range(4)]) for i in range(2)]
                Eh = [(sb("Eh%d" % i, [128, S], BF16, st), Dep()) for i in range(2)]
                Mt = (sb("Mt", [128, S], BF16, st), Dep())
                stage = sb("stage", [128, S], F32, st)
                dstage = Dep()
                wq = [(sb("wq%d" % i, [128, 8, 384], BF16, st), Dep()) for i in range(2)]
                global_pbuf = [(sb("pbuf%d" % i, [128, 512], BF16, st), Dep()) for i in range(4)]
                nonlocal pbuf
                pbuf = Rot(global_pbuf)
                km = [(sb("km%d" % i, [64, 8], F32, st), Dep()) for i in range(2)]
                kmb = [(sb("kmb%d" % i, [72, 8], BF16, st), Dep()) for i in range(2)]
                gneg16 = sb("gneg16", [128, 128], F32, st)
                gm16 = sb("gm16", [128, 128], F32, st)
                dgm = Dep()
                m816 = sb("m816", [128, 128], F32, st)
                dm8 = Dep()
                nmp16 = sb("nmp16", [128, 16, 72], BF16, st)
                dnmp = Dep()
                dl0 = Dep()
                fw.dma("sp", gneg16[:], IN["c_gneg16"].rearrange("p a b -> p (a b)"), w=[dl0])
                fw.op("dve", lambda e: e.memset(nmp16[:], 0.0), w=[dnmp])
                for i in range(2):
                    fw.op("dve", lambda e: e.memset(kmb[i][0][:], 0.0), w=[kmb[i][1]])
                for i in range(2):
                    fw.op("dve", lambda e: e.memset(vh[i][0][:, :, 64:128], 1.0), w=vh[i][1])
                    fw.op("dve", lambda e: e.memset(kh[i][0][64:128, :], 0.0), w=kh[i][1])
                    fw.op("dve", lambda e: e.memset(qh[i][0][64:128, :], 0.0), w=qh[i][1])
                    fw.dma("pool", kh[i][0][64:72, :], IN["c_blk_moba"][:, :], w=kh[i][1])
                build_M(1, Mt[0], Mt[1], stage, dstage)

                def ldw(pair):
                    t, d_ = wq[pair % 2]
                    base = 0 if pair < 4 else 1536
                    pp = pair % 4
                    load_w3("pool", t, d_, w_in_ab, base + pp * 128, 128, 0)
                    load_w3("pool", t, d_, w_in_ab, base + 512 + pp * 128, 128, 128)
                    load_w3("pool", t, d_, w_in_ab, base + 1024 + pp * 128, 128, 256)

                ldw(0)
                for pair in range(8):
                    moba = pair < 4
                    if pair + 1 < 8:
                        ldw(pair + 1)
                    wt, dw = wq[pair % 2]
                    gq = 0 if moba else 2
                    gk = 1 if moba else 3
                    for sc in range(4):
                        pt, dp = proj_fm(wt, dw, sc, 128, 0)
                        headnorm(pt, dp, 128, gq, [(qh[0][0][0:64, cs_(sc)], qh[0][1][sc], 0), (qh[1][0][0:64, cs_(sc)], qh[1][1][sc], 64)])
                        pt, dp = proj_fm(wt, dw, sc, 128, 128)
                        headnorm(pt, dp, 128, gk, [(kh[0][0][0:64, cs_(sc)], kh[0][1][sc], 0), (kh[1][0][0:64, cs_(sc)], kh[1][1][sc], 64)])
                        if moba:
                            for hh in range(2):
                                kin = kh[hh][0][0:64, cs_(sc)]
                                kin3 = bass.AP(tensor=kin.tensor, offset=kin.offset, ap=[list(kin.ap[0]), [256, 2], [1, 256]])
                                fw.op("dve", lambda e: e.tensor_reduce(out=km[hh][0][0:64, 2 * sc:2 * sc + 2], in_=kin3, axis=AX.X, op=ALU.add),
                                      r=[kh[hh][1][sc]], w=[km[hh][1]])
                        pv, dpv = psA.next()
                        for j in range(4):
                            tt = sc * 4 + j
                            for kc in range(8):
                                fw.op("pe", lambda e: e.matmul(pv[:, j * 128:(j + 1) * 128], lhsT=xnT[:, kc, tt * 128:(tt + 1) * 128],
                                                               rhs=wt[:, kc, 256:384], start=(kc == 0), stop=(kc == 7)),
                                      r=[dw, dxn[sc]], w=[dpv])
                        for hh in range(2):
                            src = pv[:, hh * 64:hh * 64 + 1]
                            src3 = bass.AP(tensor=src.tensor, offset=src.offset, ap=[list(src.ap[0]), [128, 4], [1, 64]])
                            fw.op("act", lambda e: e.activation(out=vh[hh][0][:, sc * 4:sc * 4 + 4, 0:64], in_=src3, func=AF.Copy),
                                  r=[dpv], w=[vh[hh][1][sc]])
                    if pair == 0:
                        dump("q0", qh[0][0][0:64, :], [64, S], qh[0][1])
                        dump("k0", kh[0][0][0:64, :], [64, S], kh[0][1])
                        dump("v0", vh[0][0][:], [128, 16, 128], vh[0][1])
                    if moba:
                        for hh in range(2):
                            qt_, dq_ = qh[hh]
                            fw.op("dve", lambda e: e.tensor_copy(out=kmb[hh][0][0:64, :], in_=km[hh][0][:]), r=[km[hh][1]], w=[kmb[hh][1]])
                            fw.op("dve", lambda e: e.memset(qt_[64:72, 0:1024], 0.0), w=[dq_[0], dq_[1]])
                        pg, dpg = psB.next()
                        for hh in range(2):
                            qt_, dq_ = qh[hh]
                            for j in range(8):
                                qtile = 8 + j
                                i = hh * 8 + j
                                fw.op("pe", lambda e: e.matmul(pg[:, i * 8:i * 8 + 8], lhsT=qt_[0:72, qtile * 128:(qtile + 1) * 128], rhs=kmb[hh][0][0:72, 0:8],
                                                               start=True, stop=True), r=[dq_[qtile // 4], kmb[hh][1]], w=[dpg])
                        fw.op("dve", lambda e: e.tensor_tensor(out=gm16[:, :], in0=pg[:, 0:128], in1=gneg16[:, :], op=ALU.add), r=[dpg, dl0], w=[dgm])
                        for i in range(16):
                            fw.op("dve", lambda e: e.max(out=m816[:, i * 8:i * 8 + 8], in_=gm16[:, i * 8:i * 8 + 8]), r=[dgm], w=[dm8])
                        for i in range(16):
                            own = (8 + i % 8) // 2
                            fw.op("dve", lambda e: e.tensor_scalar(out=nmp16[:, i, 64:64 + own], in0=gm16[:, i * 8:i * 8 + own], scalar1=m816[:, i * 8 + 2:i * 8 + 3],
                                                                   scalar2=NEG, op0=ALU.is_lt, op1=ALU.mult), r=[dgm, dm8], w=[dnmp])
                        for hh in range(2):
                            qt_, dq_ = qh[hh]
                            for half in range(2):
                                p2, dp2 = psB.next()
                                for jj in range(4):
                                    i = hh * 8 + half * 4 + jj
                                    fw.op("pe", lambda e: e.matmul(p2[0:72, jj * 128:(jj + 1) * 128], lhsT=nmp16[:, i, 0:72], rhs=ident_bf[:], start=True, stop=True),
                                          r=[dnmp, dcon], w=[dp2])
                                c0 = (8 + half * 4) * 128
                                fw.op("act", lambda e: e.activation(out=qt_[64:72, c0:c0 + 512], in_=p2[64:72, 0:512], func=AF.Copy),
                                      r=[dp2], w=[dq_[2 + half]])
                    if pair == 4:
                        for hh in range(2):
                            fw.op("dve", lambda e: e.memset(qh[hh][0][64:72, :], 0.0), w=qh[hh][1])
                    items = []
                    for hh in range(2):
                        H = pair * 2 + hh
                        E, dE = Eh[hh]
                        M, dM = (None, None) if moba else Mt
                        build_E(H, E, dE, M, dM, stage, dstage)

                        def cb(qc, po, dpo, hh=hh, pair=pair):
                            rd, drd = f32b.next()
                            (recip_den_dve if qc >= 1 else recip_den)(rd, drd, po, dpo)
                            fw.op("dve", lambda e: e.tensor_tensor(out=oT[hh * 64:hh * 64 + 64, pair, cs_(qc)], in0=po[0:64, :], in1=rd[64:128, :], op=ALU.mult),
                                  r=[dpo, drd], w=[doT[pair][qc]])
                        items += make_items(qh[hh][0], qh[hh][1], 128, kh[hh][0], kh[hh][1], vh[hh][0], vh[hh][1], E, dE, causal_tiles, cb)
                    run_items(items)
                fw.barrier()

        pbuf = None

        def layer1_mixer():
            with contextlib.ExitStack() as st:
                nonlocal pbuf
                pbuf = Rot([(sb("pbuf%d" % i, [128, 512], BF16, st), Dep()) for i in range(4)])
                qa = [(sb("qa%d" % i, [128, S], BF16, st), [Dep() for _ in range(4)]) for i in range(4)]
                ks = (sb("ksT", [128, S], BF16, st), [Dep() for _ in range(4)])
                kw = (sb("kwT", [128, S], BF16, st), [Dep() for _ in range(4)])
                vs = (sb("vsA", [128, 16, 128], BF16, st), [Dep() for _ in range(4)])
                vw = (sb("vwA", [128, 16, 128], BF16, st), [Dep() for _ in range(4)])
                ocmp = [(sb("ocmp%d" % i, [64, S], BF16, st), [Dep() for _ in range(4)]) for i in range(4)]
                tc_ = qa[2]
                tv_ = qa[3]
                kcT = (sb("kcT", [128, 128], BF16, st), Dep())
                vcA = (sb("vcA", [128, 128], BF16, st), Dep())
                EE = [(sb("EE%d" % i, [128, S], BF16, st), Dep()) for i in range(2)]
                EW = [(sb("EW%d" % i, [128, 640], BF16, st), Dep()) for i in range(2)]
                Mw = (sb("Mw", [128, 640], BF16, st), Dep())
                stage = sb("stage", [128, S], F32, st)
                dstage = Dep()
                gsig = (sb("gsig", [96, S], BF16, st), [Dep() for _ in range(4)])
                gsel = sb("gsel", [96, 48, 64], BF16, st)
                ovl = sb("ovl", [128, 33], BF16, st)
                addm = sb("addm", [128, 16, 32], F32, st)
                forced = sb("forced", [128, 16, 32], F32, st)
                imp = (sb("imp", [128, 16, 32], F32, st), [Dep() for _ in range(16)])
                w1s = (sb("w1s", [96, 32, 128], BF16, st), Dep())
                w1 = [w1s, w1s]
                w2 = [(sb("w2_%d" % i, [128, 64], BF16, st), Dep()) for i in range(2)]
                posT = sb("posT", [96, 64], BF16, st)
                b1c = sb("b1c", [128, 2], F32, st)
                b2k = sb("b2k", [64, 1], F32, st)
                b2v = sb("b2v", [128, 64], F32, st)
                cb1 = sb("cb1", [128, 2], F32, st)
                dcb1 = Dep()
                wA = [(sb("wA%d" % i, [128, 8, 128], BF16, st), Dep()) for i in range(3)]
                wQ = [wA[0], wA[1]]
                wG = (sb("wG", [128, 8, 48], BF16, st), Dep())
                tsb = Rot([(sb("tsb%d" % i, [64, 512], BF16, st), Dep()) for i in range(4)])
                sel_t = {}
                for nm_, shp in [("vals", [128, 8, 32]), ("lt", [128, 8, 32]), ("v2", [128, 8, 32]), ("m8a", [128, 8, 8]), ("m8b", [128, 8, 8]), ("rdi", [128, 4])]:
                    sel_t[nm_] = (sb("sel_" + nm_, shp, F32, st), Dep())
                nmp = (sb("nmp1", [128, 8, 96], BF16, st), Dep())
                gl = Dep()
                fw.op("dve", lambda e: e.memset(gsel[:], 0.0), w=[gl])
                fw.op("dve", lambda e: e.memset(gsig[0][:], 0.0), w=gsig[1])
                fw.op("dve", lambda e: e.memset(posT[:], 0.0), w=[gl])
                fw.op("dve", lambda e: e.memset(w1s[0][:], 0.0), w=[w1s[1]])
                fw.op("dve", lambda e: e.memset(kcT[0][:], 0.0), w=[kcT[1]])
                fw.op("dve", lambda e: e.memset(kw[0][64:128, :], 0.0), w=kw[1])
                fw.op("dve", lambda e: e.memset(ks[0][64:128, :], 0.0), w=ks[1])
                for qd in qa:
                    fw.op("dve", lambda e: e.memset(qd[0][64:128, :], 0.0), w=qd[1])
                fw.dma("pool", gsel[0:48, :, :], IN["c_gsel"][:, :, :], w=[gl])
                fw.dma("pool", ovl[0:127, :], IN["c_ovl"][:, :], w=[gl])
                fw.dma("sp", addm[:], IN["c_addmask"][:, :, :], w=[gl])
                fw.dma("sp", forced[:], IN["c_forced"][:, :, :], w=[gl])
                fw.dma("pool", posT[0:64, :], posT_d[:, :], w=[gl])
                fw.dma("sp", b1c[:], b1_d[:, :], w=[gl])
                fw.dma("sp", b2k[:], b2k_d[:, :], w=[gl])
                fw.dma("sp", b2v[:], _dap(b2v_d, 0, [[0, 128], [1, 64]]), w=[gl])
                w1src = [ck_w1.rearrange("(l d) j -> d l j", d=64), cv_w1.rearrange("(l d) j -> d l j", d=64)]
                fw.dma("pool", w2[0][0][:], ck_w2[:, :], w=[w2[0][1]])
                fw.dma("pool", w2[1][0][:], cv_w2[:, :], w=[w2[1][1]])
                fw.dma("pool", ks[0][64:96, :], IN["c_blk_nsa"][:, :], w=ks[1])
                fw.op("dve", lambda e: e.memset(nmp[0][:], 0.0), w=[nmp[1]])
                fw.op("dve", lambda e: e.memset(vs[0][:, :, 64:128], 1.0), w=vs[1])
                fw.op("dve", lambda e: e.memset(vw[0][:, :, 64:128], 1.0), w=vw[1])
                fw.op("dve", lambda e: e.memset(vcA[0][:, 64:128], 1.0), w=[vcA[1]])
                build_M(2, Mw[0], Mw[1], stage, dstage, width=640)
                for i in range(2):
                    fw.dma("pool", w1s[0][0:64, :, :], w1src[i], w=[w1s[1]])
                    pt, dp = psB.next()
                    for l in range(32):
                        fw.op("pe", lambda e: e.matmul(pt[:, 0:1], lhsT=w1[i][0][0:96, l, :], rhs=posT[0:96, i * 32 + l:i * 32 + l + 1],
                                                       start=(l == 0), stop=(l == 31)), r=[w1[i][1], gl], w=[dp])
                    fw.op("dve", lambda e: e.tensor_tensor(out=cb1[:, i:i + 1], in0=pt[:, 0:1], in1=b1c[:, i:i + 1], op=ALU.add),
                          r=[dp, gl], w=[dcb1])
                load_w3("pool", wG[0], wG[1], w_in_nsa, 2560, 48, 0)
                for sc in range(4):
                    pt, dp = proj_fm(wG[0], wG[1], sc, 48, 0)
                    fw.op("act", lambda e: e.activation(out=gsig[0][0:48, cs_(sc)], in_=pt[0:48, :], func=AF.Sigmoid), r=[dp], w=[gsig[1][sc]])

                def gate_bc(h, j, qc):
                    pt, dp = psB.next()
                    fw.op("pe", lambda e: e.matmul(pt[0:64, :], lhsT=gsel[0:96, h * 3 + j, :], rhs=gsig[0][0:96, cs_(qc)], start=True, stop=True),
                          r=[gl, gsig[1][qc]], w=[dp])
                    return pt, dp

                for g in range(4):
                    load_w3("pool", wA[0][0], wA[0][1], w_in_nsa, 1536 + g * 64, 64, 0)
                    load_w3("pool", wA[0][0], wA[0][1], w_in_nsa, 2048 + g * 64, 64, 64)
                    load_w3("pool", wA[1][0], wA[1][1], w_in_nsa, 1024 + g * 64, 64, 0)
                    load_w3("pool", wA[1][0], wA[1][1], w_in_nsa, 1280 + g * 64, 64, 64)
                    load_w3("pool", wA[2][0], wA[2][1], w_in_nsa, 1792 + g * 64, 64, 0)
                    load_w3("pool", wA[2][0], wA[2][1], w_in_nsa, 2304 + g * 64, 64, 64)
                    for sc in range(4):
                        pt, dp = proj_fm(wA[0][0], wA[0][1], sc, 128, 0)
                        headnorm(pt, dp, 128, 5, [(ks[0][0:64, cs_(sc)], ks[1][sc], 0), (kw[0][0:64, cs_(sc)], kw[1][sc], 64)])
                        pt, dp = proj_fm(wA[1][0], wA[1][1], sc, 128, 0)
                        fw.op("act", lambda e: e.activation(out=tc_[0][0:64, cs_(sc)], in_=pt[0:64, :], func=AF.Copy), r=[dp], w=[tc_[1][sc]])
                        fw.op("act", lambda e: e.activation(out=tv_[0][0:64, cs_(sc)], in_=pt[64:128, :], func=AF.Copy), r=[dp], w=[tv_[1][sc]])
                        pv, dpv = psA.next()
                        for j in range(4):
                            tt = sc * 4 + j
                            for kc in range(8):
                                fw.op("pe", lambda e: e.matmul(pv[:, j * 128:(j + 1) * 128], lhsT=xnT[:, kc, tt * 128:(tt + 1) * 128],
                                                               rhs=wA[2][0][:, kc, :], start=(kc == 0), stop=(kc == 7)),
                                      r=[wA[2][1], dxn[sc]], w=[dpv])
                        for hh, vdst in enumerate((vs, vw)):
                            src = pv[:, hh * 64:hh * 64 + 1]
                            src3 = bass.AP(tensor=src.tensor, offset=src.offset, ap=[list(src.ap[0]), [128, 4], [1, 64]])
                            fw.op("act", lambda e: e.activation(out=vdst[0][:, sc * 4:sc * 4 + 4, 0:64], in_=src3, func=AF.Copy),
                                  r=[dpv], w=[vdst[1][sc]])
                    for i, tsrc in enumerate((tc_, tv_)):
                        fw.dma("pool", w1s[0][0:64, :, :], w1src[i], w=[w1s[1]])
                        ph, dph = psA.next()
                        for l in range(32):
                            a = tsrc[0][0:96, l:l + 1]
                            rhs = bass.AP(tensor=a.tensor, offset=a.offset, ap=[list(a.ap[0]), [16, 127]])
                            fw.op("pe", lambda e: e.matmul(ph[:, 0:127], lhsT=w1[i][0][0:96, l, :], rhs=rhs, start=(l == 0), stop=(l == 31)),
                                  r=[w1[i][1]] + tsrc[1], w=[dph])
                        xg, dxg = f32b.next()
                        tg_, dtg = f32b.next()
                        fw.op("act", lambda e: e.activation(out=xg[:, 0:127], in_=ph[:, 0:127], func=AF.Identity, bias=cb1[:, i:i + 1], scale=1.0),
                              r=[dph, dcb1], w=[dxg])
                        fw.op("dve", lambda e: e.tensor_tensor(out=tg_[:, 0:127], in0=xg[:, 0:127], in1=xg[:, 0:127], op=ALU.mult), r=[dxg], w=[dtg])
                        fw.op("dve", lambda e: e.tensor_scalar(out=tg_[:, 0:127], in0=tg_[:, 0:127], scalar1=0.044715, scalar2=1.0, op0=ALU.mult, op1=ALU.add),
                              r=[dtg], w=[dtg])
                        fw.op("dve", lambda e: e.tensor_tensor(out=tg_[:, 0:127], in0=tg_[:, 0:127], in1=xg[:, 0:127], op=ALU.mult), r=[dxg, dtg], w=[dtg])
                        fw.op("act", lambda e: e.activation(out=tg_[:, 0:127], in_=tg_[:, 0:127], func=AF.Sigmoid, scale=1.5957691216), r=[dtg], w=[dtg])
                        gb, dgb = sqb.next()
                        fw.op("dve", lambda e: e.tensor_tensor(out=gb[:, 0:127], in0=tg_[:, 0:127], in1=xg[:, 0:127], op=ALU.mult), r=[dxg, dtg], w=[dgb])
                        if i == 0:
                            pk, dpk = psA.next()
                            fw.op("pe", lambda e: e.matmul(pk[0:64, 0:127], lhsT=w2[0][0][:, :], rhs=gb[:, 0:127], start=True, stop=True),
                                  r=[w2[0][1], dgb], w=[dpk])
                            kf, dkf = f32b.next()
                            fw.op("act", lambda e: e.activation(out=kf[0:64, 0:127], in_=pk[0:64, 0:127], func=AF.Identity, bias=b2k[:, 0:1], scale=1.0),
                                  r=[dpk, gl], w=[dkf])
                            sq, dsq = sqb.next()
                            fw.op("act", lambda e: e.activation(out=sq[0:64, 0:127], in_=kf[0:64, 0:127], func=AF.Square), r=[dkf], w=[dsq])
                            ps2, dp2 = psB.next()
                            fw.op("pe", lambda e: e.matmul(ps2[0:64, 0:127], lhsT=blk_ones[0:128, 0:64], rhs=sq[0:128, 0:127], start=True, stop=True),
                                  r=[dsq, dcon], w=[dp2])
                            rt, drt = f32b.next()
                            fw.op("act", lambda e: e.activation(out=rt[0:64, 0:127], in_=ps2[0:64, 0:127], func=AF.Ln, bias=EPS, scale=1.0 / 64), r=[dp2], w=[drt])
                            fw.op("act", lambda e: e.activation(out=rt[0:64, 0:127], in_=rt[0:64, 0:127], func=AF.Exp, scale=-0.5), r=[drt], w=[drt])
                            fw.op("dve", lambda e: e.scalar_tensor_tensor(out=kcT[0][0:64, 0:127], in0=kf[0:64, 0:127], scalar=hg[0:64, 6:7], in1=rt[0:64, 0:127],
                                                                          op0=ALU.mult, op1=ALU.mult), r=[dkf, drt, dcon], w=[kcT[1]])
                        else:
                            pk, dpk = psA.next()
                            fw.op("pe", lambda e: e.matmul(pk[0:127, 0:64], lhsT=gb[:, 0:127], rhs=w2[1][0][:, :], start=True, stop=True),
                                  r=[w2[1][1], dgb], w=[dpk])
                            fw.op("dve", lambda e: e.tensor_tensor(out=vcA[0][0:127, 0:64], in0=pk[0:127, 0:64], in1=b2v[0:127, :], op=ALU.add),
                                  r=[dpk, gl], w=[vcA[1]])
                    if g == 0:
                        dump("kcT", kcT[0][:], [64, 128], [kcT[1]])
                        dump("vcA", vcA[0][:], [128, 128], [vcA[1]])
                        dump("ksT", ks[0][0:64, :], [64, S], ks[1])
                    for pr in range(2):
                        t, d_ = wQ[pr]
                        load_w3("pool", t, d_, w_in_nsa, (g * 4 + pr * 2) * 64, 128, 0)
                    for pr in range(2):
                        t, d_ = wQ[pr]
                        for sc in range(4):
                            pt, dp = proj_fm(t, d_, sc, 128, 0)
                            a, b_ = qa[pr * 2], qa[pr * 2 + 1]
                            headnorm(pt, dp, 128, 4, [(a[0][0:64, cs_(sc)], a[1][sc], 0), (b_[0][0:64, cs_(sc)], b_[1][sc], 64)])
                    for qd in qa:
                        fw.op("dve", lambda e: e.memset(qd[0][64:96, 0:1024], 0.0), w=[qd[1][0], qd[1][1]])
                    for qt in range(8, 16):
                        fw.op("dve", lambda e: e.memset(imp[0][:, qt, :], 0.0), w=[imp[1][qt]])
                    def build_cmp_E(hl):
                        Ec = EE[hl % 2]
                        build_E(g * 4 + hl, Ec[0], Ec[1], None, None, stage, dstage, pstride=16, off=31, rows=127)

                    build_cmp_E(0)
                    build_cmp_E(1)
                    items = []
                    for hl in range(4):
                        h = g * 4 + hl
                        pair, hh = h // 2, h % 2
                        qt_, dq_ = qa[hl]
                        Ec = EE[hl % 2]
                        for qc in range(4):
                            def post(pb, dpb, qc=qc):
                                if qc < 2:
                                    return
                                pi, dpi = psB.next()
                                for j in range(4):
                                    fw.op("pe", lambda e: e.matmul(pi[:, j * 64:j * 64 + 33], lhsT=pb[0:127, j * 128:(j + 1) * 128], rhs=ovl[0:127, :],
                                                                   start=True, stop=True), r=[dpb, gl], w=[dpi])
                                rdi, drdi = sel_t["rdi"]
                                src = pi[:, 32:33]
                                src3 = bass.AP(tensor=src.tensor, offset=src.offset, ap=[list(src.ap[0]), [64, 4]])
                                fw.op("dve", lambda e: e.reciprocal(out=rdi[:, 0:4], in_=src3), r=[dpi], w=[drdi])
                                for j in range(4):
                                    qtile = qc * 4 + j
                                    fw.op("dve", lambda e: e.scalar_tensor_tensor(out=imp[0][:, qtile, :], in0=pi[:, j * 64:j * 64 + 32], scalar=rdi[:, j:j + 1],
                                                                                  in1=imp[0][:, qtile, :], op0=ALU.mult, op1=ALU.add),
                                          r=[dpi, drdi, imp[1][qtile]], w=[imp[1][qtile]])

                            def cb_cmp(qc, po, dpo, h=h, hl=hl):
                                rd, drd = f32b.next()
                                recip_den(rd, drd, po, dpo)
                                fw.op("dve", lambda e: e.tensor_tensor(out=rd[0:64, :], in0=po[0:64, :], in1=rd[64:128, :], op=ALU.mult), r=[dpo, drd], w=[drd])
                                pgb, dpgb = gate_bc(h, 0, qc)
                                fw.op("dve", lambda e: e.tensor_tensor(out=ocmp[hl][0][0:64, cs_(qc)], in0=rd[0:64, :], in1=pgb[0:64, :], op=ALU.mult),
                                      r=[drd, dpgb], w=[ocmp[hl][1][qc]])
                            items.append(dict(
                                rows=127, n=512, lo=0, hi=512, K=128,
                                lhsT=kcT[0][0:128, 0:127], rhs=qt_[0:128, cs_(qc)], sdeps=[kcT[1], dq_[qc]],
                                E=Ec[0][0:127, cs_(qc)], dE=Ec[1], v=vcA[0][0:127, :], dv=vcA[1],
                                first=True, last=True, cb=cb_cmp, qc=qc, post=post, after=None))
                        if hl + 2 < 4:
                            items[-1]["after"] = (lambda hl=hl: build_cmp_E(hl + 2))
                    run_items(items)
                    vals, dvals = sel_t["vals"]
                    lt, dlt = sel_t["lt"]
                    v2, dv2 = sel_t["v2"]
                    m8a, dm8a = sel_t["m8a"]
                    m8b, dm8b = sel_t["m8b"]
                    impd = [imp[1][q_] for q_ in range(8, 16)]
                    fw.op("dve", lambda e: e.tensor_tensor(out=vals[:, :, :], in0=imp[0][:, 8:16, :], in1=addm[:, 8:16, :], op=ALU.add), r=impd + [gl], w=[dvals])
                    for j in range(8):
                        fw.op("dve", lambda e: e.max(out=m8a[:, j, :], in_=vals[:, j, :]), r=[dvals], w=[dm8a])
                    for j in range(8):
                        fw.op("dve", lambda e: e.tensor_scalar(out=lt[:, j, :], in0=vals[:, j, :], scalar1=m8a[:, j, 7:8], scalar2=None, op0=ALU.is_lt), r=[dvals, dm8a], w=[dlt])
                    fw.op("dve", lambda e: e.tensor_tensor(out=v2[:, :, :], in0=vals[:, :, :], in1=lt[:, :, :], op=ALU.mult), r=[dvals, dlt], w=[dv2])
                    fw.op("dve", lambda e: e.tensor_scalar(out=lt[:, :, :], in0=lt[:, :, :], scalar1=-1.0, scalar2=BIG, op0=ALU.add, op1=ALU.mult), r=[dlt, dv2], w=[dlt])
                    fw.op("dve", lambda e: e.tensor_tensor(out=v2[:, :, :], in0=v2[:, :, :], in1=lt[:, :, :], op=ALU.add), r=[dlt, dv2], w=[dv2])
                    for j in range(8):
                        fw.op("dve", lambda e: e.max(out=m8b[:, j, :], in_=v2[:, j, :]), r=[dv2], w=[dm8b])
                    for j in range(8):
                        fw.op("dve", lambda e: e.tensor_scalar(out=lt[:, j, :], in0=vals[:, j, :], scalar1=m8b[:, j, 4:5], scalar2=None, op0=ALU.is_ge), r=[dvals, dm8b, dlt], w=[dlt])
                    fw.op("dve", lambda e: e.tensor_tensor(out=lt[:, :, :], in0=lt[:, :, :], in1=forced[:, 8:16, :], op=ALU.max), r=[dlt, gl], w=[dlt])
                    fw.op("dve", lambda e: e.tensor_scalar(out=nmp[0][:, :, 64:96], in0=lt[:, :, :], scalar1=-1.0, scalar2=-NEG, op0=ALU.add, op1=ALU.mult),
                          r=[dlt], w=[nmp[1]])
                    for half in range(2):
                        p2, dp2 = psB.next()
                        for jj in range(4):
                            j = half * 4 + jj
                            fw.op("pe", lambda e: e.matmul(p2[0:96, jj * 128:(jj + 1) * 128], lhsT=nmp[0][:, j, 0:96], rhs=ident_bf[:], start=True, stop=True),
                                  r=[nmp[1], dcon], w=[dp2])
                        c0 = (8 + half * 4) * 128
                        for qd in qa:
                            fw.op("act", lambda e: e.activation(out=qd[0][64:96, c0:c0 + 512], in_=p2[64:96, 0:512], func=AF.Copy),
                                  r=[dp2], w=[qd[1][2 + half]])
                    if g == 0:
                        dump("imp", imp[0][:], [128, 16, 32], imp[1])
                        dump("qa0", qa[0][0][:], [96, S], qa[0][1])
                    def build_sw_E(hl):
                        Es, Ew = EE[hl % 2], EW[hl % 2]
                        build_E(g * 4 + hl, Es[0], Es[1], None, None, stage, dstage)
                        fw.op("dve", lambda e: e.tensor_tensor(out=Ew[0][:, :], in0=Es[0][:, 0:640], in1=Mw[0][:, :], op=ALU.add),
                              r=[Es[1], Mw[1]], w=[Ew[1]])

                    build_sw_E(0)
                    build_sw_E(1)
                    items = []
                    for hl in range(4):
                        h = g * 4 + hl
                        pair, hh = h // 2, h % 2
                        Es, Ew = EE[hl % 2], EW[hl % 2]
                        acc = {}

                        def cb_slc(qc, po, dpo, h=h, acc=acc):
                            rd, drd = f32b.next()
                            (recip_den_dve if qc >= 2 else recip_den)(rd, drd, po, dpo)
                            fw.op("dve", lambda e: e.tensor_tensor(out=rd[0:64, :], in0=po[0:64, :], in1=rd[64:128, :], op=ALU.mult), r=[dpo, drd], w=[drd])
                            pgb, dpgb = gate_bc(h, 1, qc)
                            tb, dtb = tsb.next()
                            fw.op("dve", lambda e: e.tensor_tensor(out=tb[:, :], in0=rd[0:64, :], in1=pgb[0:64, :], op=ALU.mult), r=[drd, dpgb], w=[dtb])
                            acc[qc] = (tb, dtb)

                        def cb_win(qc, po, dpo, h=h, pair=pair, hh=hh, hl=hl, acc=acc):
                            rd, drd = f32b.next()
                            a_, da_ = acc[qc]
                            recip_den(rd, drd, po, dpo)
                            fw.op("dve", lambda e: e.tensor_tensor(out=rd[0:64, :], in0=po[0:64, :], in1=rd[64:128, :], op=ALU.mult), r=[dpo, drd], w=[drd])
                            pgb, dpgb = gate_bc(h, 2, qc)
                            tb, dtb = tsb.next()
                            fw.op("dve", lambda e: e.tensor_tensor(out=tb[:, :], in0=rd[0:64, :], in1=pgb[0:64, :], op=ALU.mult), r=[drd, dpgb], w=[dtb])
                            fw.op("dve", lambda e: e.tensor_tensor(out=tb[:, :], in0=tb[:, :], in1=a_[:, :], op=ALU.add), r=[dtb, da_], w=[dtb])
                            fw.op("dve", lambda e: e.tensor_tensor(out=oT[hh * 64:hh * 64 + 64, pair, cs_(qc)], in0=tb[:, :], in1=ocmp[hl][0][0:64, cs_(qc)], op=ALU.add),
                                  r=[dtb, ocmp[hl][1][qc]], w=[doT[pair][qc]])

                        for qc in range(4):
                            items += make_items(qa[hl][0], qa[hl][1], 128, ks[0], ks[1], vs[0], vs[1], Es[0], Es[1], causal_tiles, cb_slc, chunks=[qc])
                            items += make_items(qa[hl][0], qa[hl][1], 128, kw[0], kw[1], vw[0], vw[1], Ew[0], Ew[1], win_tiles, cb_win, chunks=[qc])
                        if hl + 2 < 4:
                            items[-1]["after"] = (lambda hl=hl: build_sw_E(hl + 2))
                    run_items(items)
                pass
                fw.barrier()

        alldh = [d_ for row in dh for d_ in row]
        alldo = [d_ for row in doT for d_ in row]
        with contextlib.ExitStack() as st:
            hT = sb("hT", [128, 8, S], F32, st)
            load_h(xT, [])
            rmsnorm(0)
            dump("xn0", xnT[:], [128, 8, S], dxn)
            fw.barrier()
        if upto >= 1:
            layer0_mixer()
            dump("oT0", oT[:], [128, 8, S], alldo)
        with contextlib.ExitStack() as st:
            hT = sb("hT", [128, 8, S], F32, st)
            load_h(xT, [])
            if upto >= 1:
                out_proj(w_out_ab, st)
                dump("hmix0", hT[:], [128, 8, S], alldh)
                fw.barrier()
            if upto >= 2:
                ffn(0)
                dump("hffn0", hT[:], [128, 8, S], alldh)
            if upto >= 3:
                def after_ple0(sc):
                    if upto >= 4:
                        rmsnorm(3, [sc])
                        store_h(hS, [dhS], [sc])
                ple(0, after_sc=after_ple0)
                dump("h0", hT[:], [128, 8, S], alldh)
            fw.barrier()
            if upto < 4:
                store_h(outT, [])
        if upto >= 4:
            layer1_mixer()
            dump("oT1", oT[:], [128, 8, S], alldo)
            with contextlib.ExitStack() as st:
                hT = sb("hT", [128, 8, S], F32, st)
                load_h(hS, [dhS])
                out_proj(w_out_nsa, st)
                dump("hmix1", hT[:], [128, 8, S], alldh)
                fw.barrier()
                if upto >= 5:
                    ffn(1)
                if upto >= 6:
                    ple(1, after_sc=(lambda sc: store_h(outT, [], [sc])))
                else:
                    store_h(outT, [])
                fw.barrier()
        fw.finish("sp")
        build_nc.stats = (fw.n_ins, fw.n_wait)
    return nc, consts, DBG


def host_inputs(inputs, b, consts):
    f = lambda a: np.ascontiguousarray(np.asarray(a, dtype=np.float32))
    m = {}
    m["xT"] = f(inputs["x"][b].T)
    m["pT"] = f(np.transpose(inputs["p"][:, b], (0, 2, 1)))
    rb = np.asarray(inputs["rel_bias"], np.float32)
    dd = np.maximum(2047 - np.arange(WHL), 0)
    whb = rb[_bucket(dd), :].T.copy()
    whb[:, 2048:] = NEG
    m["whb"] = f(whb)
    gl = []
    for layer in range(2):
        for nm in ("norm_mix", "norm_ffn", "norm_ple"):
            gl.append(np.asarray(inputs[nm][layer], np.float32).reshape(8, 128).T)
    m["gains"] = f(np.concatenate(gl, axis=1))
    t2 = lambda a: np.concatenate([np.asarray(a, np.float32)] * 2)
    hg = np.zeros((128, 8), np.float32)
    hg[:, 0] = t2(inputs["qn_moba"][0])
    hg[:, 1] = t2(inputs["kn_moba"][0])
    hg[:, 2] = t2(inputs["qn_dil"][0])
    hg[:, 3] = t2(inputs["kn_dil"][0])
    hg[:, 4] = t2(inputs["qn_nsa"][0])
    hg[:, 5] = np.concatenate([np.asarray(inputs["kn_slc"][0], np.float32), np.asarray(inputs["kn_win"][0], np.float32)])
    hg[:, 6] = t2(inputs["kn_cmp"][0])
    m["hg"] = hg
    m["posT"] = f(np.concatenate([np.asarray(inputs["cmp_k_pos"][0]).T, np.asarray(inputs["cmp_v_pos"][0]).T], axis=1))
    m["b1c"] = f(np.stack([inputs["cmp_k_b1"][0], inputs["cmp_v_b1"][0]], axis=1))
    m["b2k"] = f(np.asarray(inputs["cmp_k_b2"][0]).reshape(64, 1))
    m["b2v"] = f(np.asarray(inputs["cmp_v_b2"][0]).reshape(1, 64))
    m["w_in_ab"] = f(inputs["w_in_ab"][0])
    m["w_out_ab"] = f(inputs["w_out_ab"][0])
    m["w_in_nsa"] = f(inputs["w_in_nsa"][0])
    m["w_out_nsa"] = f(inputs["w_out_nsa"][0])
    for nm in ("w_ffn_gate", "w_ffn_up", "w_ffn_down", "w_ple_proj", "w_ple_gate"):
        m[nm] = f(inputs[nm])
    m["cmp_k_w1"] = f(inputs["cmp_k_w1"][0])
    m["cmp_k_w2"] = f(inputs["cmp_k_w2"][0])
    m["cmp_v_w1"] = f(inputs["cmp_v_w1"][0])
    m["cmp_v_w2"] = f(inputs["cmp_v_w2"][0])
    for k, v in consts.items():
        m[k] = v
    return m


def kernel(**inputs):
    nc, consts, _ = build_nc()
    in_maps = [host_inputs(inputs, b, consts) for b in range(8)]
    res = run_bass_kernel_spmd(nc, in_maps, core_ids=list(range(8)))
    out = np.stack([np.asarray(r["outT"], np.float32).T for r in res.results], axis=0)
    return np.ascontiguousarray(out.astype(np.float32))
```
